# Optimizing a Trainium2 kernel written in Bass

```python
import math
import jax, jax.numpy as jnp
from jax import lax
import numpy as np

D_MODEL = 1024
BATCH = 4
SEQ = 8192
DEPTH = 2

HEAD_DIM = 64
ROPE_THETA = 10000.0
NORM_EPS = 1e-6
NEG_INF = -1e30
D_FF = 4 * D_MODEL
Q_BLOCK = 128
MAX_POS_OFFSET = 4096

MLA_HEADS = 8
MLA_Q_RANK = 384
MLA_KV_RANK = 256
MLA_NOPE = 64
MLA_ROPE = 32
MLA_V = 64

DIL_CONFIGS = ((128, 1), (512, 4), (2048, 16))
DIL_GROUPS = len(DIL_CONFIGS)
DIL_HEADS = 4

DIFF_HEADS = 4
DIFF_DIM = 64

MOBA_HEADS = 8
MOBA_BLOCK = 256
MOBA_TOPK = 3
MOBA_Q_CHUNK = 32

N_EVEN = (DEPTH + 1) // 2
N_ODD = DEPTH // 2

EVEN_IN = MLA_Q_RANK + MLA_KV_RANK + MLA_ROPE + 3 * DIL_GROUPS * DIL_HEADS * HEAD_DIM
EVEN_OUT = MLA_HEADS * MLA_V + DIL_HEADS * HEAD_DIM
ODD_IN = 3 * DIFF_HEADS * 2 * DIFF_DIM + 3 * MOBA_HEADS * HEAD_DIM
ODD_OUT = DIFF_HEADS * 2 * DIFF_DIM + MOBA_HEADS * HEAD_DIM

kernel_name = "hybrid_mla_dilated_diff_moba_adaln"


def rms_norm(x, w):
    xf = x.astype(jnp.float32)
    y = xf * lax.rsqrt(jnp.mean(xf * xf, axis=-1, keepdims=True) + NORM_EPS)
    return (y * w.astype(jnp.float32)).astype(x.dtype)


def apply_rope(x, positions):
    half = x.shape[-1] // 2
    inv_freq = ROPE_THETA ** (-jnp.arange(half, dtype=jnp.float32) / half)
    ang = positions.astype(jnp.float32)[:, :, None] * inv_freq
    bshape = ang.shape[:2] + (1,) * (x.ndim - 3) + (half,)
    cos = jnp.cos(ang).reshape(bshape)
    sin = jnp.sin(ang).reshape(bshape)
    xf = x.astype(jnp.float32)
    x1, x2 = xf[..., :half], xf[..., half:]
    return jnp.concatenate([x1 * cos - x2 * sin, x2 * cos + x1 * sin], axis=-1).astype(x.dtype)


def dense_causal_attention(q, k, v):
    b, s, h, dk = q.shape
    nq = s // Q_BLOCK
    scale = dk ** -0.5
    qb = q.reshape(b, nq, Q_BLOCK, h, dk).transpose(1, 0, 2, 3, 4)
    kpos = jnp.arange(s)

    def one_block(args):
        i, qi = args
        sc = jnp.einsum('bqhd,bkhd->bhqk', qi, k, preferred_element_type=jnp.float32) * scale
        qpos = i * Q_BLOCK + jnp.arange(Q_BLOCK)
        sc = jnp.where(qpos[:, None] >= kpos[None, :], sc, NEG_INF)
        p = jax.nn.softmax(sc, axis=-1)
        return jnp.einsum('bhqk,bkhe->bqhe', p.astype(v.dtype), v)

    out = lax.map(one_block, (jnp.arange(nq), qb))
    return out.transpose(1, 0, 2, 3, 4).reshape(b, s, h, v.shape[-1])


def diff_causal_attention(q, k, v, lam):
    b, s, h, _, d = q.shape
    nq = s // Q_BLOCK
    scale = d ** -0.5
    qb = q.reshape(b, nq, Q_BLOCK, h, 2, d).transpose(1, 0, 2, 3, 4, 5)
    kpos = jnp.arange(s)

    def one_block(args):
        i, qi = args
        sc = jnp.einsum('bqhmd,bkhmd->bmhqk', qi, k, preferred_element_type=jnp.float32) * scale
        qpos = i * Q_BLOCK + jnp.arange(Q_BLOCK)
        sc = jnp.where(qpos[:, None] >= kpos[None, :], sc, NEG_INF)
        p = jax.nn.softmax(sc, axis=-1)
        pd = p[:, 0] - lam * p[:, 1]
        return jnp.einsum('bhqk,bkhe->bqhe', pd.astype(v.dtype), v)

    out = lax.map(one_block, (jnp.arange(nq), qb))
    return out.transpose(1, 0, 2, 3, 4).reshape(b, s, h, v.shape[-1])


def sliding_window_lse(q, k, v, window):
    n, l, h, d = q.shape
    blk = window
    lp = -(-l // blk) * blk
    pad = ((0, 0), (0, lp - l), (0, 0), (0, 0))
    q, k, v = jnp.pad(q, pad), jnp.pad(k, pad), jnp.pad(v, pad)
    nb = lp // blk
    qb = q.reshape(n, nb, blk, h, d)
    kb = k.reshape(n, nb, blk, h, d)
    vb = v.reshape(n, nb, blk, h, d)
    prev_pad = ((0, 0), (1, 0), (0, 0), (0, 0), (0, 0))
    kk = jnp.concatenate([jnp.pad(kb[:, :-1], prev_pad), kb], axis=2)
    vv = jnp.concatenate([jnp.pad(vb[:, :-1], prev_pad), vb], axis=2)
    sc = jnp.einsum('nbqhd,nbkhd->nbhqk', qb, kk, preferred_element_type=jnp.float32) * d ** -0.5
    qloc = jnp.arange(blk) + blk
    kloc = jnp.arange(2 * blk)
    rel = qloc[:, None] - kloc[None, :]
    band = (rel >= 0) & (rel <= window)
    kabs = jnp.arange(nb)[:, None, None] * blk + kloc[None, None, :] - blk
    mask = band[None] & (kabs >= 0)
    sc = jnp.where(mask[None, :, None], sc, NEG_INF)
    lse = jax.nn.logsumexp(sc, axis=-1)
    p = jnp.exp(sc - lse[..., None])
    out = jnp.einsum('nbhqk,nbkhd->nbqhd', p.astype(v.dtype), vv).reshape(n, lp, h, d)[:, :l]
    lse = lse.transpose(0, 1, 3, 2).reshape(n, lp, h)[:, :l]
    return out, lse


def dilated_group(q, k, v, window, dilation):
    b, s, h, d = q.shape
    m = s // dilation

    def split(t):
        return t.reshape(b, m, dilation, h, d).transpose(0, 2, 1, 3, 4).reshape(b * dilation, m, h, d)

    out, lse = sliding_window_lse(split(q), split(k), split(v), window // dilation)
    out = out.reshape(b, dilation, m, h, d).transpose(0, 2, 1, 3, 4).reshape(b, s, h, d)
    lse = lse.reshape(b, dilation, m, h).transpose(0, 2, 1, 3).reshape(b, s, h)
    return out, lse


def dilated_mixture(q, k, v):
    outs, lses = [], []
    for g, (w, r) in enumerate(DIL_CONFIGS):
        o, l = dilated_group(q[:, :, g], k[:, :, g], v[:, :, g], w, r)
        outs.append(o)
        lses.append(l)
    alpha = jax.nn.softmax(jnp.stack(lses), axis=0)
    out = jnp.sum(alpha[..., None] * jnp.stack(outs).astype(jnp.float32), axis=0)
    return out.astype(q.dtype)


def moba_attention(q, k, v):
    b, s, h, d = q.shape
    blk = MOBA_BLOCK
    sp = -(-s // blk) * blk
    nb = sp // blk
    scale = d ** -0.5
    pad = ((0, 0), (0, sp - s), (0, 0), (0, 0))
    qp, kp, vp = jnp.pad(q, pad), jnp.pad(k, pad), jnp.pad(v, pad)
    qb = qp.reshape(b, nb, blk, h, d)
    kb = kp.reshape(b, nb, blk, h, d)
    vb = vp.reshape(b, nb, blk, h, d)

    s_own = jnp.einsum('bnqhd,bnkhd->bnhqk', qb, kb, preferred_element_type=jnp.float32) * scale
    s_own = jnp.where(jnp.tril(jnp.ones((blk, blk), dtype=bool)), s_own, NEG_INF)
    lse_own = jax.nn.logsumexp(s_own, axis=-1)
    o_own = jnp.einsum('bnhqk,bnkhd->bnqhd', jnp.exp(s_own - lse_own[..., None]).astype(v.dtype), vb)
    o_own = o_own.reshape(b, sp, h, d)[:, :s].astype(jnp.float32)
    lse_own = lse_own.transpose(0, 1, 3, 2).reshape(b, sp, h)[:, :s]

    kmean = jnp.mean(kb.astype(jnp.float32), axis=2)
    gate = jnp.einsum('bshd,bnhd->bhsn', q.astype(jnp.float32), kmean)
    qblk = jnp.arange(s) // blk
    past = jnp.arange(nb)[None, :] < qblk[:, None]
    gate = jnp.where(past, gate, NEG_INF)
    n_sel = min(MOBA_TOPK, nb)
    _, idx = lax.top_k(gate, n_sel)
    valid = idx < qblk[None, None, :, None]

    kbh = kb.transpose(0, 3, 1, 2, 4)
    vbh = vb.transpose(0, 3, 1, 2, 4)
    nc = s // MOBA_Q_CHUNK
    qc_all = q.transpose(0, 2, 1, 3).reshape(b, h, nc, MOBA_Q_CHUNK, d).transpose(2, 0, 1, 3, 4)
    ic_all = idx.reshape(b, h, nc, MOBA_Q_CHUNK, n_sel).transpose(2, 0, 1, 3, 4)
    vc_all = valid.reshape(b, h, nc, MOBA_Q_CHUNK, n_sel).transpose(2, 0, 1, 3, 4)
    bidx = jnp.arange(b)[:, None, None, None]
    hidx = jnp.arange(h)[None, :, None, None]

    def chunk(args):
        qc, ic, vc = args
        kg = kbh[bidx, hidx, ic]
        vg = vbh[bidx, hidx, ic]
        sc = jnp.einsum('bhqd,bhqjkd->bhqjk', qc, kg, preferred_element_type=jnp.float32) * scale
        sc = jnp.where(vc[..., None], sc, NEG_INF).reshape(b, h, MOBA_Q_CHUNK, n_sel * blk)
        lse = jax.nn.logsumexp(sc, axis=-1)
        p = jnp.exp(sc - lse[..., None]).reshape(b, h, MOBA_Q_CHUNK, n_sel, blk)
        o = jnp.einsum('bhqjk,bhqjkd->bhqd', p.astype(vg.dtype), vg)
        return o, lse

    o_sel, lse_sel = lax.map(chunk, (qc_all, ic_all, vc_all))
    o_sel = o_sel.transpose(1, 0, 3, 2, 4).reshape(b, s, h, d).astype(jnp.float32)
    lse_sel = lse_sel.transpose(1, 0, 3, 2).reshape(b, s, h)

    lse_tot = jnp.logaddexp(lse_own, lse_sel)
    out = (jnp.exp(lse_own - lse_tot)[..., None] * o_own
           + jnp.exp(lse_sel - lse_tot)[..., None] * o_sel)
    return out.astype(q.dtype)


def even_mixer(h, positions, w_in, w_out, q_lat_norm, kv_lat_norm, w_uq, w_ukv,
               mla_qn, mla_kn, dil_qn, dil_kn):
    b, s, _ = h.shape
    u = h @ w_in
    o1 = MLA_Q_RANK
    o2 = o1 + MLA_KV_RANK
    o3 = o2 + MLA_ROPE
    c_q, c_kv, k_r, u_dil = u[..., :o1], u[..., o1:o2], u[..., o2:o3], u[..., o3:]

    q = jnp.einsum('bsr,rhd->bshd', rms_norm(c_q, q_lat_norm), w_uq)
    kv = jnp.einsum('bsr,rhd->bshd', rms_norm(c_kv, kv_lat_norm), w_ukv)
    k_nope, v_a = kv[..., :MLA_NOPE], kv[..., MLA_NOPE:]
    k_rope = jnp.broadcast_to(k_r[:, :, None, :], (b, s, MLA_HEADS, MLA_ROPE))
    k = jnp.concatenate([k_nope, k_rope], axis=-1)
    q, k = rms_norm(q, mla_qn), rms_norm(k, mla_kn)
    q = jnp.concatenate([q[..., :MLA_NOPE], apply_rope(q[..., MLA_NOPE:], positions)], axis=-1)
    k = jnp.concatenate([k[..., :MLA_NOPE], apply_rope(k[..., MLA_NOPE:], positions)], axis=-1)
    o_a = dense_causal_attention(q, k, v_a)

    qkv = u_dil.reshape(b, s, 3, DIL_GROUPS, DIL_HEADS, HEAD_DIM)
    qd = apply_rope(rms_norm(qkv[:, :, 0], dil_qn), positions)
    kd = apply_rope(rms_norm(qkv[:, :, 1], dil_kn), positions)
    o_b = dilated_mixture(qd, kd, qkv[:, :, 2])

    o = jnp.concatenate([o_a.reshape(b, s, -1), o_b.reshape(b, s, -1)], axis=-1)
    return o @ w_out


def odd_mixer(h, positions, w_in, w_out, diff_qn, diff_kn, diff_lambda, diff_subln,
              moba_qn, moba_kn, lam_init):
    b, s, _ = h.shape
    u = h @ w_in
    nqk = DIFF_HEADS * 2 * DIFF_DIM

    qc = u[..., :nqk].reshape(b, s, DIFF_HEADS, 2, DIFF_DIM)
    kc = u[..., nqk:2 * nqk].reshape(b, s, DIFF_HEADS, 2, DIFF_DIM)
    vc = u[..., 2 * nqk:3 * nqk].reshape(b, s, DIFF_HEADS, 2 * DIFF_DIM)
    qc = apply_rope(rms_norm(qc, diff_qn), positions)
    kc = apply_rope(rms_norm(kc, diff_kn), positions)
    lv = diff_lambda.astype(jnp.float32)
    lam = jnp.exp(jnp.sum(lv[0] * lv[1])) - jnp.exp(jnp.sum(lv[2] * lv[3])) + lam_init
    o_c = diff_causal_attention(qc, kc, vc, lam)
    o_c = rms_norm(o_c, diff_subln) * (1.0 - lam_init)

    m = u[..., 3 * nqk:].reshape(b, s, 3, MOBA_HEADS, HEAD_DIM)
    qm = apply_rope(rms_norm(m[:, :, 0], moba_qn), positions)
    km = apply_rope(rms_norm(m[:, :, 1], moba_kn), positions)
    o_d = moba_attention(qm, km, m[:, :, 2])

    o = jnp.concatenate([o_c.reshape(b, s, -1), o_d.reshape(b, s, -1)], axis=-1)
    return o @ w_out


def squared_relu_mlp(h, w1, w2):
    a = jax.nn.relu(h @ w1)
    return (a * a) @ w2


def modulate(xn, shift, scale):
    return xn * (1.0 + scale[:, None, :]) + shift[:, None, :]


def setup_inputs(seed: int = 0) -> dict:
    key = jax.random.key(seed)
    ks = jax.random.split(key, 27)
    f32 = jnp.float32

    def nrm(k, shape, scale):
        return jax.random.normal(k, shape, f32) * scale

    def gain(k, shape):
        return 1.0 + 0.1 * jax.random.normal(k, shape, f32)

    offset = jax.random.randint(ks[2], (BATCH, 1), 0, MAX_POS_OFFSET, dtype=jnp.int32)
    positions = offset + jnp.arange(SEQ, dtype=jnp.int32)[None, :]
    return {
        "x": nrm(ks[0], (BATCH, SEQ, D_MODEL), 1.0),
        "c": nrm(ks[1], (BATCH, D_MODEL), 1.0),
        "positions": positions,
        "ada_w": nrm(ks[3], (DEPTH, D_MODEL, 6 * D_MODEL), 0.5 * D_MODEL ** -0.5),
        "ada_b": nrm(ks[4], (DEPTH, 6 * D_MODEL), 0.02),
        "norm_mix": gain(ks[5], (DEPTH, D_MODEL)),
        "norm_mlp": gain(ks[6], (DEPTH, D_MODEL)),
        "mlp_w1": nrm(ks[7], (DEPTH, D_MODEL, D_FF), D_MODEL ** -0.5),
        "mlp_w2": nrm(ks[8], (DEPTH, D_FF, D_MODEL), D_FF ** -0.5),
        "even_w_in": nrm(ks[9], (N_EVEN, D_MODEL, EVEN_IN), D_MODEL ** -0.5),
        "even_w_out": nrm(ks[10], (N_EVEN, EVEN_OUT, D_MODEL), EVEN_OUT ** -0.5),
        "mla_q_lat_norm": gain(ks[11], (N_EVEN, MLA_Q_RANK)),
        "mla_kv_lat_norm": gain(ks[12], (N_EVEN, MLA_KV_RANK)),
        "mla_w_uq": nrm(ks[13], (N_EVEN, MLA_Q_RANK, MLA_HEADS, MLA_NOPE + MLA_ROPE), MLA_Q_RANK ** -0.5),
        "mla_w_ukv": nrm(ks[14], (N_EVEN, MLA_KV_RANK, MLA_HEADS, MLA_NOPE + MLA_V), MLA_KV_RANK ** -0.5),
        "mla_q_norm": gain(ks[15], (N_EVEN, MLA_NOPE + MLA_ROPE)),
        "mla_k_norm": gain(ks[16], (N_EVEN, MLA_NOPE + MLA_ROPE)),
        "dil_q_norm": gain(ks[17], (N_EVEN, HEAD_DIM)),
        "dil_k_norm": gain(ks[18], (N_EVEN, HEAD_DIM)),
        "odd_w_in": nrm(ks[19], (N_ODD, D_MODEL, ODD_IN), D_MODEL ** -0.5),
        "odd_w_out": nrm(ks[20], (N_ODD, ODD_OUT, D_MODEL), ODD_OUT ** -0.5),
        "diff_q_norm": gain(ks[21], (N_ODD, DIFF_DIM)),
        "diff_k_norm": gain(ks[22], (N_ODD, DIFF_DIM)),
        "diff_lambda": nrm(ks[23], (N_ODD, 4, DIFF_DIM), 0.1),
        "diff_subln": gain(ks[24], (N_ODD, 2 * DIFF_DIM)),
        "moba_q_norm": gain(ks[25], (N_ODD, HEAD_DIM)),
        "moba_k_norm": gain(ks[26], (N_ODD, HEAD_DIM)),
    }


def reference(x, c, positions, ada_w, ada_b, norm_mix, norm_mlp, mlp_w1, mlp_w2,
              even_w_in, even_w_out, mla_q_lat_norm, mla_kv_lat_norm, mla_w_uq, mla_w_ukv,
              mla_q_norm, mla_k_norm, dil_q_norm, dil_k_norm,
              odd_w_in, odd_w_out, diff_q_norm, diff_k_norm, diff_lambda, diff_subln,
              moba_q_norm, moba_k_norm):
    cond = jax.nn.silu(c.astype(jnp.float32)).astype(x.dtype)
    for layer in range(DEPTH):
        mod = cond @ ada_w[layer] + ada_b[layer]
        sh1, sc1, g1, sh2, sc2, g2 = jnp.split(mod, 6, axis=-1)
        h = modulate(rms_norm(x, norm_mix[layer]), sh1, sc1)
        i = layer // 2
        if layer % 2 == 0:
            y = even_mixer(h, positions, even_w_in[i], even_w_out[i], mla_q_lat_norm[i],
                           mla_kv_lat_norm[i], mla_w_uq[i], mla_w_ukv[i], mla_q_norm[i],
                           mla_k_norm[i], dil_q_norm[i], dil_k_norm[i])
        else:
            lam_init = 0.8 - 0.6 * math.exp(-0.3 * layer)
            y = odd_mixer(h, positions, odd_w_in[i], odd_w_out[i], diff_q_norm[i], diff_k_norm[i],
                          diff_lambda[i], diff_subln[i], moba_q_norm[i], moba_k_norm[i], lam_init)
        x = x + g1[:, None, :] * y
        h = modulate(rms_norm(x, norm_mlp[layer]), sh2, sc2)
        x = x + g2[:, None, :] * squared_relu_mlp(h, mlp_w1[layer], mlp_w2[layer])
    return x
```

```python
class _Op:
    __slots__ = ("eng", "fn", "deps", "needs_sig", "sem", "val", "is_dma", "idx", "ring_prev")

    def __init__(self, eng, fn, is_dma):
        self.eng = eng
        self.fn = fn
        self.deps = []
        self.needs_sig = False
        self.sem = None
        self.val = 0
        self.is_dma = is_dma
        self.ring_prev = None


class Sched:
    COMPUTE = ("pe", "act", "dve", "pool")

    def __init__(self, nc, ring=8, same_engine_sync=True):
        self.nc = nc
        self.ops = []
        self.last_w = {}
        self.readers = {}
        self.appenders = {}
        self.ring = ring
        self.same_engine_sync = same_engine_sync
        self.engobj = {"pe": nc.tensor, "act": nc.scalar, "dve": nc.vector, "pool": nc.gpsimd,
                       "sp": nc.sync, "pq": nc.gpsimd}
        self.stream = {"pe": "pe", "act": "act", "dve": "dve", "pool": "pool", "sp": "sp", "pq": "pool"}

    def add(self, eng, fn, r=(), w=(), a=()):
        is_dma = eng in ("sp", "pq")
        op = _Op(eng, fn, is_dma)
        deps = {}

        def dep(p, kind):
            if p is op:
                return
            same = (self.stream[p.eng] == self.stream[eng]) and not p.is_dma
            if same:
                if eng == "pe" or not self.same_engine_sync:
                    return
            deps[id(p)] = p

        for x in r:
            p = self.last_w.get(x)
            if p is not None:
                dep(p, "raw")
            for p in self.appenders.get(x, ()):
                dep(p, "raw")
        for x in list(w) + list(a):
            p = self.last_w.get(x)
            if p is not None:
                dep(p, "waw")
            for p in self.readers.get(x, ()):
                dep(p, "war")
        for x in w:
            for p in self.appenders.get(x, ()):
                dep(p, "waw")
        for x in r:
            self.readers.setdefault(x, []).append(op)
        for x in w:
            self.last_w[x] = op
            self.readers[x] = []
            self.appenders[x] = []
        for x in a:
            self.appenders.setdefault(x, []).append(op)
            self.readers[x] = []
        op.deps = list(deps.values())
        for p in op.deps:
            p.needs_sig = True
        self.ops.append(op)
        return op

    def init_emit(self, sems):
        self.sems = sems
        self.cnt = {e: 0 for e in self.COMPUTE}
        self.dcount = {"sp": 0, "pq": 0}
        self.dhist = {"sp": [], "pq": []}
        self.waited = {}
        self.nwaits = 0
        self.nops = 0
        self.barrier_deps = []

    def flush(self):
        ops = self.ops
        lastc = {}
        for op in ops:
            if not op.is_dma:
                lastc[op.eng] = op
        for op in lastc.values():
            op.needs_sig = True
        bd = self.barrier_deps
        first_seen = set()
        for op in ops:
            st = self.stream[op.eng]
            if st not in first_seen:
                first_seen.add(st)
                op.deps = op.deps + [p for p in bd if not (self.stream[p.eng] == st and not p.is_dma)]
            if op.is_dma:
                i = self.dcount[op.eng]
                self.dcount[op.eng] += 1
                op.sem = self.sems[op.eng][i % self.ring]
                op.val = 16 * (i // self.ring + 1)
                if i >= self.ring:
                    op.ring_prev = self.dhist[op.eng][i - self.ring]
                self.dhist[op.eng].append(op)
            elif op.needs_sig:
                self.cnt[op.eng] += 1
                op.sem = self.sems[op.eng]
                op.val = self.cnt[op.eng]
        waited = self.waited
        for op in ops:
            e = self.engobj[op.eng]
            st = self.stream[op.eng]
            need = {}
            plist = list(op.deps)
            if op.ring_prev is not None:
                plist.append(op.ring_prev)
            for p in plist:
                k = id(p.sem)
                if k not in need or need[k][1] < p.val:
                    need[k] = (p.sem, p.val)
            for k, (sem, val) in need.items():
                if waited.get((st, k), 0) < val:
                    e.wait_ge(sem, val)
                    waited[(st, k)] = val
                    self.nwaits += 1
            inst = op.fn(e)
            if op.is_dma:
                inst.then_inc(op.sem, 16)
            elif op.needs_sig:
                inst.then_inc(op.sem, 1)
        self.nops += len(ops)
        nb = list(lastc.values())
        for p in bd:
            if not p.is_dma and p.eng not in lastc:
                nb.append(p)
        for q in ("sp", "pq"):
            nb.extend(self.dhist[q][-self.ring:])
        self.barrier_deps = nb
        self.ops = []
        self.last_w = {}
        self.readers = {}
        self.appenders = {}

    def finish(self, eng="sp"):
        self.flush()
        e = self.engobj[eng]
        st = self.stream[eng]
        for p in self.barrier_deps:
            k = id(p.sem)
            if self.waited.get((st, k), 0) < p.val:
                e.wait_ge(p.sem, p.val)
                self.waited[(st, k)] = p.val
        return {"n_ops": self.nops, "n_waits": self.nwaits, "sig": dict(self.cnt), "dma": dict(self.dcount)}


import math
from contextlib import ExitStack
import numpy as np
import ml_dtypes
import concourse.bass as bass
import concourse.mybir as mybir
from concourse.bass_utils import run_bass_kernel_spmd

F32 = mybir.dt.float32
BF16 = mybir.dt.bfloat16
I32 = mybir.dt.int32
AF = mybir.ActivationFunctionType
ALU = mybir.AluOpType
AX = mybir.AxisListType

D = 1024
SEQ = 8192
NT = SEQ // 128
NCH = SEQ // 512
EPS = 1e-6
EVEN_IN = 2976
ODD_IN = 3072
DFF = 4096
NEGB = -30000.0
TWO_PI = float(2 * np.pi)


class K:
    def __init__(self, nc, nt=NT, phases=None, dbg_out=()):
        self.nc = nc
        self.S = Sched(nc)
        self.nt = nt
        self.phases = phases
        self.dbg_out = dbg_out
        self.es = ExitStack()
        self.din = {}
        self.dscr = {}
        import os as _os
        self.lim = float(_os.environ.get('KLIM', '99'))

    def inp(self, name, shape, dt=F32):
        t = self.nc.dram_tensor(name, list(shape), dt, kind="ExternalInput").ap()
        self.din[name] = t
        return t

    def scr(self, name, shape, dt):
        kind = "ExternalOutput" if name in self.dbg_out else "Internal"
        t = self.nc.dram_tensor(name, list(shape), dt, kind=kind).ap()
        self.dscr[name] = t
        return t

    def sb(self, es, name, shape, dt):
        self.uid = getattr(self, "uid", 0) + 1
        return es.enter_context(self.nc.sbuf_tensor("%s_%d" % (name, self.uid), list(shape), dt))

    def ps(self, es, name, shape, dt):
        self.uid = getattr(self, "uid", 0) + 1
        return es.enter_context(self.nc.psum_tensor("%s_%d" % (name, self.uid), list(shape), dt))

    def act(self, fn, r=(), w=(), a=()):
        return self.S.add("act", fn, r, w, a)

    def dve(self, fn, r=(), w=(), a=()):
        return self.S.add("dve", fn, r, w, a)

    def pool(self, fn, r=(), w=(), a=()):
        return self.S.add("pool", fn, r, w, a)

    def pe(self, fn, r=(), w=(), a=()):
        return self.S.add("pe", fn, r, w, a)

    def ld(self, out, in_, r=(), w=(), a=(), q="sp"):
        return self.S.add(q, lambda e: e.dma_start(out=out, in_=in_), r, w, a)

    def ldc(self, out, in_, r=(), w=(), a=()):
        return self.S.add("pq", lambda e: e.dma_start(out=out, in_=in_), r, w, a)

    def rstd(self, ss, rs, n, rn_ss, rn_rs):
        self.act(lambda e: e.activation(out=rs, in_=ss, func=AF.Sqrt, scale=1.0 / n, bias=self.eps_ap(ss)),
                 r=[rn_ss], w=[rn_rs])
        self.dve(lambda e: e.reciprocal(out=rs, in_=rs), r=[rn_rs], w=[rn_rs])

    def eps_ap(self, like):
        p = like.shape[0]
        return self.epsT[0:p, 0:1]

    def declare(self):
        i = self.inp
        self.x = i("x", [SEQ, D])
        self.posT = i("posT", [128, NT], I32)
        self.cT = i("cT", [128, 8])
        self.ada_w = i("ada_w", [2, D, 6 * D])
        self.ada_bT = i("ada_bT", [128, 96])
        self.nrmT = i("nrmT", [128, 32])
        self.mlp_w1 = i("mlp_w1", [2, D, DFF])
        self.mlp_w2 = i("mlp_w2", [2, DFF, D])
        self.e_w_in = i("e_w_in", [D, EVEN_IN])
        self.e_w_out = i("e_w_out", [768, D])
        self.latT = i("latT", [128, 5])
        self.w_uq = i("w_uq", [384, 768])
        self.w_ukv = i("w_ukv", [256, 1024])
        self.gains = i("gains", [576])
        self.o_w_in = i("o_w_in", [D, ODD_IN])
        self.o_w_out = i("o_w_out", [D, D])
        self.dlam = i("dlam", [256])
        self.subT = i("subT", [64, 2])
        self.ident = i("ident", [128, 128])
        self.invf = i("invf", [48])
        self.cmask = i("cmask", [128, 4 * 512])
        self.dmask = i("dmask", [128, 33 * 512])
        self.kind = i("kind", [32, SEQ])
        self.out = self.nc.dram_tensor("out", [SEQ, D], F32, kind="ExternalOutput").ap()
        s = self.scr
        self.trigd = s("trigd", [128, 2 * NT * 48], F32)
        self.Gd = s("Gd", [128, 4 * D], F32)
        self.x1 = s("x1", [SEQ, D], F32)
        self.xL = s("xL", [SEQ, D], F32)
        self.h2T = s("h2T", [D, SEQ], BF16)
        self.oT = s("oT", [D, SEQ], BF16)
        self.qT_mla = s("qT_mla", [8, 96, SEQ], BF16)
        self.kT_mla = s("kT_mla", [8, 96, SEQ], BF16)
        self.v_mla = s("v_mla", [SEQ, 512], BF16)
        self.qT_dil = s("qT_dil", [12, 64, SEQ], BF16)
        self.kT_dil = s("kT_dil", [12, 64, SEQ], BF16)
        self.v_dil = s("v_dil", [SEQ, 768], BF16)
        self.qT_df = s("qT_df", [8, 64, SEQ], BF16)
        self.kT_df = s("kT_df", [8, 64, SEQ], BF16)
        self.v_df = s("v_df", [SEQ, 512], BF16)
        self.qT_mb = s("qT_mb", [8, 96, SEQ], BF16)
        self.kT_mb = s("kT_mb", [8, 64, SEQ], BF16)
        self.v_mb = s("v_mb", [SEQ, 512], BF16)

    def setup(self):
        nc, S = self.nc, self.S
        g = self.es
        sb = self.sb
        self.epsT = sb(g, "epsT", [128, 1], F32)
        self.identb = sb(g, "identb", [128, 128], BF16)
        self.modT = sb(g, "modT", [128, 96], F32)
        self.aT = sb(g, "aT", [128, 32], F32)
        self.gbc = sb(g, "gbc", [128, 576], F32)
        self.cmb = sb(g, "cmb", [128, 4, 512], BF16)
        self.onesb = sb(g, "onesb", [128, 64], F32)
        self.pool(lambda e: e.memset(self.epsT[:], EPS), w=["epsT"])
        self.pool(lambda e: e.memset(self.onesb[:], 1.0), w=["onesb"])
        self.ldc(self.identb[:], self.ident, w=["identb"])
        self.ldc(self.cmb[:], self.cmask.rearrange("p (a b) -> p a b", a=4), w=["cmb"])
        self.ld(self.gbc[:], self.gains.partition_broadcast(128), w=["gbc"])
        with ExitStack() as es:
            self.trig = sb(es, "trig", [128, 2, NT, 48], F32)
            self.G = sb(es, "G", [128, 4, D], F32)
            condT = sb(es, "condT", [128, 8], F32)
            bT = sb(es, "bT", [128, 96], F32)
            nT = sb(es, "nT", [128, 32], F32)
            stage = sb(es, "adastage", [128, 2, 8, 1024], F32)
            posi = sb(es, "posi", [128, NT], I32)
            posf = sb(es, "posf", [128, NT], F32)
            invb = sb(es, "invb", [128, 48], F32)
            kf = sb(es, "kf", [128, 2, NT, 48], F32)
            ki = sb(es, "ki", [128, 2, NT, 48], I32)
            psmod = self.ps(es, "psmod", [128, 96], F32)
            self.ld(condT[:], self.cT, w=["condT"])
            self.ld(bT[:], self.ada_bT, w=["bT"])
            self.ld(nT[:], self.nrmT, w=["nT"])
            self.ld(posi[:], self.posT, w=["posi"])
            self.ld(invb[:], self.invf.partition_broadcast(128), w=["invb"])
            self.act(lambda e: e.activation(out=condT[:], in_=condT[:], func=AF.Silu), r=["condT"], w=["condT"])
            tr = self.trig
            self.dve(lambda e: e.tensor_copy(out=posf[:], in_=posi[:]), r=["posi"], w=["posf"])
            self.dve(lambda e: e.tensor_tensor(out=tr[:, 0], in0=posf[:].unsqueeze(2).to_broadcast([128, NT, 48]),
                                               in1=invb[:].unsqueeze(1).to_broadcast([128, NT, 48]), op=ALU.mult),
                     r=["posf", "invb"], w=["trig"])
            self.dve(lambda e: e.tensor_scalar(out=tr[:, 1], in0=tr[:, 0], scalar1=float(np.pi / 2), scalar2=None,
                                               op0=ALU.add), r=["trig"], w=["trig"])
            self.dve(lambda e: e.tensor_scalar(out=kf[:], in0=tr[:], scalar1=float(1 / TWO_PI), scalar2=None,
                                               op0=ALU.mult), r=["trig"], w=["kf"])
            self.dve(lambda e: e.tensor_copy(out=ki[:], in_=kf[:]), r=["kf"], w=["ki"])
            self.dve(lambda e: e.tensor_copy(out=kf[:], in_=ki[:]), r=["ki"], w=["kf"])
            self.dve(lambda e: e.scalar_tensor_tensor(out=tr[:], in0=kf[:], scalar=-TWO_PI, in1=tr[:],
                                                      op0=ALU.mult, op1=ALU.add), r=["kf", "trig"], w=["trig"])
            self.dve(lambda e: e.tensor_scalar(out=kf[:], in0=tr[:], scalar1=float(np.pi), scalar2=-TWO_PI,
                                               op0=ALU.is_gt, op1=ALU.mult), r=["trig"], w=["kf"])
            self.dve(lambda e: e.tensor_tensor(out=tr[:], in0=tr[:], in1=kf[:], op=ALU.add), r=["kf", "trig"], w=["trig"])
            self.dve(lambda e: e.tensor_scalar(out=kf[:], in0=tr[:], scalar1=float(-np.pi), scalar2=TWO_PI,
                                               op0=ALU.is_lt, op1=ALU.mult), r=["trig"], w=["kf"])
            self.dve(lambda e: e.tensor_tensor(out=tr[:], in0=tr[:], in1=kf[:], op=ALU.add), r=["kf", "trig"], w=["trig"])
            self.act(lambda e: e.activation(out=tr[:], in_=tr[:], func=AF.Sin), r=["trig"], w=["trig"])
            for l in range(2):
                for cb in range(6):
                    sl = (l * 6 + cb) % 2
                    self.ld(stage[:, sl], self.ada_w[l, :, cb * 1024:(cb + 1) * 1024].rearrange("(k p) n -> p k n", p=128),
                            w=[("adast", sl)])
                    for jj in range(8):
                        col = l * 48 + cb * 8 + jj

                        def mm(e, sl=sl, jj=jj, col=col):
                            for k in range(8):
                                ins = e.matmul(psmod[:, col:col + 1], lhsT=stage[:, sl, k, jj * 128:(jj + 1) * 128],
                                               rhs=condT[:, k:k + 1], start=(k == 0), stop=(k == 7))
                            return ins
                        self.pe(mm, r=[("adast", sl), "condT"], a=["psmod"])
            self.dve(lambda e: e.tensor_tensor(out=self.modT[:], in0=psmod[:], in1=bT[:], op=ALU.add),
                     r=["psmod", "bT"], w=["modT"])
            for l in range(2):
                m = self.modT[:, l * 48:(l + 1) * 48]
                for which, (sc0, sh0) in enumerate(((8, 0), (32, 24))):
                    o = l * 16 + which * 8
                    nsl = nT[:, l * 16 + which * 8: l * 16 + which * 8 + 8]
                    self.dve(lambda e, o=o, m=m, sc0=sc0, nsl=nsl: e.scalar_tensor_tensor(
                        out=self.aT[:, o:o + 8], in0=m[:, sc0:sc0 + 8], scalar=1.0, in1=nsl, op0=ALU.add, op1=ALU.mult),
                        r=["modT", "nT"], w=[("aT", o)])
            identf = sb(es, "identf", [128, 128], F32)
            onesf = sb(es, "onesf", [128, 128], F32)
            dg = sb(es, "dg", [128, 2, 128], F32)
            psg_ = [self.ps(es, "psgate%d" % i, [128, 512], F32) for i in range(2)]
            self.ld(identf[:], self.ident, w=["identf"])
            self.pool(lambda e: e.memset(onesf[:], 1.0), w=["onesf"])
            cnt = 0
            for l in range(2):
                for which, off in enumerate((16, 40)):
                    gi = l * 2 + which
                    for half in range(2):
                        pb = psg_[cnt % 2]
                        for jj in range(4):
                            j = half * 4 + jj
                            col = l * 48 + off + j
                            sl = (cnt * 4 + jj) % 2
                            self.dve(lambda e, sl=sl, col=col: e.tensor_scalar(out=dg[:, sl, :], in0=identf[:], scalar1=self.modT[:, col:col + 1],
                                                                               scalar2=None, op0=ALU.mult), r=["identf", "modT"], w=[("dg", sl)])
                            self.pe(lambda e, sl=sl, jj=jj, pb=pb: e.matmul(pb[:, jj * 128:(jj + 1) * 128], lhsT=onesf[:], rhs=dg[:, sl, :], start=True, stop=True),
                                    r=[("dg", sl), "onesf"], w=[("psgate", cnt % 2, jj)])
                        self.act(lambda e, gi=gi, half=half, pb=pb: e.activation(out=self.G[:, gi, half * 512:(half + 1) * 512], in_=pb[:], func=AF.Copy),
                                 r=[("psgate", cnt % 2, jj) for jj in range(4)], w=[("G", gi, half)])
                        cnt += 1
            self.ldc(self.trigd.rearrange("p (a b) -> p a b", a=2 * NT), self.trig[:].rearrange("p s t f -> p (s t) f"), r=["trig"], w=["trigd"])
            self.ldc(self.Gd.rearrange("p (a b) -> p a b", a=4), self.G[:], r=[("G", gi, hf) for gi in range(4) for hf in range(2)], w=["Gd"])
            self.S.flush()

    def head_post(self, nm, src, nh, hd, goff, rope_lo, half, t, sq, ssh, rt, dst_bf, nm_bf):
        n = nh * hd
        s3 = src.rearrange("p (h d) -> p h d", h=nh)
        sq2 = sq[:, 0:n]
        self.act(lambda e: e.activation(out=sq2, in_=src, func=AF.Square), r=[nm], w=["sq"])
        self.dve(lambda e: e.tensor_reduce(out=ssh[:, 0:nh], in_=sq2.rearrange("p (h d) -> p h d", h=nh), axis=AX.X, op=ALU.add),
                 r=["sq"], w=["ssh"])
        self.rstd(ssh[:, 0:nh], ssh[:, 0:nh], hd, "ssh", "ssh")
        self.dve(lambda e: e.tensor_tensor(out=s3, in0=s3, in1=ssh[:, 0:nh].unsqueeze(2).to_broadcast([128, nh, hd]), op=ALU.mult),
                 r=[nm, "ssh"], w=[nm])
        gb = self.gbc[:, goff:goff + hd]
        self.dve(lambda e: e.tensor_tensor(out=s3, in0=s3, in1=gb.unsqueeze(1).to_broadcast([128, nh, hd]), op=ALU.mult),
                 r=[nm, "gbc"], w=[nm])
        fo = 0 if half == 16 else 16
        sin = self.trig[:, 0, t, fo:fo + half].unsqueeze(1).to_broadcast([128, nh, half])
        cos = self.trig[:, 1, t, fo:fo + half].unsqueeze(1).to_broadcast([128, nh, half])
        x1 = s3[:, :, rope_lo:rope_lo + half]
        x2 = s3[:, :, rope_lo + half:rope_lo + 2 * half]
        m = nh * half
        tA = rt[:, 0, 0:m].rearrange("p (h d) -> p h d", h=nh)
        tB = rt[:, 1, 0:m].rearrange("p (h d) -> p h d", h=nh)
        tC = rt[:, 2, 0:m].rearrange("p (h d) -> p h d", h=nh)
        tD = rt[:, 3, 0:m].rearrange("p (h d) -> p h d", h=nh)
        self.dve(lambda e: e.tensor_tensor(out=tA, in0=x1, in1=cos, op=ALU.mult), r=[nm, "trig"], w=["rtA"])
        self.dve(lambda e: e.tensor_tensor(out=tB, in0=x2, in1=sin, op=ALU.mult), r=[nm, "trig"], w=["rtB"])
        self.dve(lambda e: e.tensor_tensor(out=tC, in0=x2, in1=cos, op=ALU.mult), r=[nm, "trig"], w=["rtC"])
        self.dve(lambda e: e.tensor_tensor(out=tD, in0=x1, in1=sin, op=ALU.mult), r=[nm, "trig"], w=["rtD"])
        self.dve(lambda e: e.tensor_tensor(out=x1, in0=tA, in1=tB, op=ALU.subtract), r=["rtA", "rtB"], w=[nm])
        self.dve(lambda e: e.tensor_tensor(out=x2, in0=tC, in1=tD, op=ALU.add), r=["rtC", "rtD"], w=[nm])
        self.act(lambda e: e.activation(out=dst_bf, in_=src, func=AF.Copy), r=[nm], w=[nm_bf])

    def tr_store(self, nm_bf, src_bf, nh, hd, pstr, pslot, stg, stg_nm, dram, t, hw=None, row0=0):
        hw = hw or hd
        s3 = src_bf.rearrange("p (h d) -> p h d", h=nh)
        done = 0
        while done < nh:
            nb = min(8, nh - done)
            ps = pstr[pslot % 2]
            rn = ("pstr", pslot % 2)

            def tp(e, done=done, nb=nb, ps=ps):
                for i in range(nb):
                    ins = e.transpose(ps[0:hw, i, :], s3[:, done + i, 0:hw], self.identb[:])
                return ins
            self.pe(tp, r=[nm_bf, "identb"], w=[rn])
            self.dve(lambda e, done=done, nb=nb, ps=ps: e.tensor_copy(out=stg[0:hw, done:done + nb, :], in_=ps[0:hw, 0:nb, :]),
                     r=[rn], w=[(stg_nm, done)])
            dst = dram[done:done + nb, row0:row0 + hw, t * 128:(t + 1) * 128].rearrange("h d s -> d h s")
            self.ldc(dst, stg[0:hw, done:done + nb, :], r=[(stg_nm, done)], a=[("dram", id(dram))])
            done += nb
            pslot += 1
        return pslot

    def load_w(self, dst, src, K, rn, n0=0, n1=None, d0=0):
        n1 = n1 if n1 is not None else src.shape[1]
        kc = K // 128
        step = 2 if (n1 - n0) > 1024 else kc
        for k0 in range(0, kc, step):
            k1 = min(kc, k0 + step)
            self.ldc(dst[:, k0:k1, d0:d0 + (n1 - n0)],
                     src[k0 * 128:k1 * 128, n0:n1].rearrange("(k p) n -> p k n", p=128), a=[rn])

    def phaseA(self, layer):
        nc = self.nc
        sb = self.sb
        ncol = EVEN_IN if layer == 0 else ODD_IN
        xin = self.x if layer == 0 else self.xL
        with ExitStack() as es:
            w_in = sb(es, "w_in", [128, 8, ncol], BF16)
            self.trig = sb(es, "trigA", [128, 2, NT, 48], F32)
            self.ld(self.trig[:].rearrange("p s t f -> p (s t) f"), self.trigd.rearrange("p (a b) -> p a b", a=2 * NT), w=["trig"])
            xt = sb(es, "xt", [128, 2, D], F32)
            xs = sb(es, "xs", [128, D], BF16)
            ssx = sb(es, "ssx", [128, 4], F32)
            hT = sb(es, "hT", [128, 8, 128], BF16)
            tmpf = sb(es, "tmpf", [128, D], F32)
            u = sb(es, "u", [128, ncol], F32)
            sq = sb(es, "sq", [128, D], F32)
            ssh = sb(es, "ssh", [128, 16], F32)
            rt = sb(es, "rt", [128, 4, 512], F32)
            qbf = sb(es, "qbf", [128, 2048], BF16)
            vbf = sb(es, "vbf", [128, 1024], BF16)
            stg = [sb(es, "stg%d" % i, [96, 12, 128], BF16) for i in range(4)]
            pT = self.ps(es, "pT", [128, 8, 128], BF16)
            psu = [self.ps(es, "psu%d" % i, [128, 512], F32) for i in range(2)]
            pstr = [self.ps(es, "pstr%d" % i, [128, 8, 128], BF16) for i in range(2)]
            self.load_w(w_in, self.e_w_in if layer == 0 else self.o_w_in, D, "w_in")
            if layer == 0:
                wq_f = sb(es, "wq_f", [128, 3, 768], F32)
                wkv_f = sb(es, "wkv_f", [128, 2, 1024], F32)
                w_uq = sb(es, "w_uqb", [128, 3, 768], BF16)
                w_ukv = sb(es, "w_ukvb", [128, 2, 1024], BF16)
                latTs = sb(es, "latTs", [128, 5], F32)
                latb = sb(es, "latb", [128, 640], BF16)
                latTt = sb(es, "latTt", [128, 5, 128], BF16)
                qm = sb(es, "qm", [128, 768], F32)
                kf_ = sb(es, "kfull", [128, 768], F32)
                psup = [self.ps(es, "psup%d" % i, [128, 512], F32) for i in range(2)]
                self.ld(latTs[:], self.latT, w=["latTs"])
                self.ld(wq_f[:], self.w_uq.rearrange("(k p) n -> p k n", p=128), w=["wq_f"])
                self.ld(wkv_f[:], self.w_ukv.rearrange("(k p) n -> p k n", p=128), w=["wkv_f"])
                self.dve(lambda e: e.tensor_tensor(out=w_uq[:], in0=wq_f[:], in1=latTs[:, 0:3].unsqueeze(2).to_broadcast([128, 3, 768]), op=ALU.mult),
                         r=["wq_f", "latTs"], w=["w_uq"])
                self.dve(lambda e: e.tensor_tensor(out=w_ukv[:], in0=wkv_f[:], in1=latTs[:, 3:5].unsqueeze(2).to_broadcast([128, 2, 1024]), op=ALU.mult),
                         r=["wkv_f", "latTs"], w=["w_ukv"])
            else:
                kmacc = sb(es, "kmacc", [64, 8, 32], F32)
                kmb = sb(es, "kmb", [64, 8, 32], BF16)
                kpart = sb(es, "kpart", [64, 8], F32)
                gsb = sb(es, "gsb", [128, 8, 32], F32)
                top8 = sb(es, "top8", [128, 8, 8], F32)
                biasf = sb(es, "biasf", [128, 8, 32], F32)
                biasb = sb(es, "biasb", [128, 8, 32], BF16)
                psg = self.ps(es, "psg", [128, 8, 32], F32)
                self.pool(lambda e: e.memset(gsb[:], -1e30), w=["gsb"])
                self.pool(lambda e: e.memset(kmacc[:], 0.0), w=["kmacc"])
            pslot = 0
            for t in range(self.nt):
                sl = t % 2
                xtt = xt[:, sl]
                self.ld(xtt, xin[t * 128:(t + 1) * 128, :], w=[("xt", sl)])
                self.act(lambda e, xtt=xtt: e.activation(out=sq[:], in_=xtt, func=AF.Square, accum_out=ssx[:, 0:1]),
                         r=[("xt", sl)], w=["sq", "ssx"])
                self.rstd(ssx[:, 0:1], ssx[:, 1:2], D, "ssx", "rsx")
                self.act(lambda e, xtt=xtt: e.activation(out=xs[:], in_=xtt, func=AF.Copy, scale=ssx[:, 1:2]),
                         r=[("xt", sl), "rsx"], w=["xs"])

                def tpx(e):
                    for j in range(8):
                        ins = e.transpose(pT[:, j, :], xs[:, j * 128:(j + 1) * 128], self.identb[:])
                    return ins
                self.pe(tpx, r=["xs", "identb"], w=["pT"])
                ao = layer * 16
                a_bc = self.aT[:, ao:ao + 8].unsqueeze(2).to_broadcast([128, 8, 128])
                b_bc = self.modT[:, layer * 48:layer * 48 + 8].unsqueeze(2).to_broadcast([128, 8, 128])
                tm3 = tmpf[:].rearrange("p (j s) -> p j s", j=8)
                self.dve(lambda e, a_bc=a_bc, tm3=tm3: e.tensor_tensor(out=tm3, in0=pT[:], in1=a_bc, op=ALU.mult),
                         r=["pT", ("aT", ao)], w=["tmpf"])
                self.dve(lambda e, b_bc=b_bc, tm3=tm3: e.tensor_tensor(out=hT[:], in0=tm3, in1=b_bc, op=ALU.add),
                         r=["tmpf", "modT"], w=["hT"])
                if self.lim < 1:
                    continue
                for gi, c0 in enumerate(range(0, ncol, 512)):
                    c1 = min(ncol, c0 + 512)
                    pb = psu[gi % 2]

                    def mm(e, c0=c0, c1=c1, pb=pb):
                        for j in range(8):
                            ins = e.matmul(pb[:, 0:c1 - c0], lhsT=hT[:, j, :], rhs=w_in[:, j, c0:c1], start=(j == 0), stop=(j == 7))
                        return ins
                    self.pe(mm, r=["hT", "w_in"], w=[("psu", gi % 2)])
                    self.act(lambda e, c0=c0, c1=c1, pb=pb: e.activation(out=u[:, c0:c1], in_=pb[:, 0:c1 - c0], func=AF.Copy),
                             r=[("psu", gi % 2)], w=[("u", gi)])
                if self.lim < 2:
                    continue
                if layer == 0:
                    uall = [("u", gi) for gi in range(6)]
                    self.act(lambda e: e.activation(out=sq[:, 0:384], in_=u[:, 0:384], func=AF.Square, accum_out=ssx[:, 2:3]),
                             r=[("u", 0)], w=["sq", "ssl"])
                    self.act(lambda e: e.activation(out=sq[:, 384:640], in_=u[:, 384:640], func=AF.Square, accum_out=ssx[:, 3:4]),
                             r=[("u", 0), ("u", 1)], w=["sq", "ssl2"])
                    if self.lim < 2.2:
                        continue
                    self.rstd(ssx[:, 2:3], ssx[:, 2:3], 384, "ssl", "ssl")
                    self.rstd(ssx[:, 3:4], ssx[:, 3:4], 256, "ssl2", "ssl2")
                    if self.lim < 2.4:
                        continue
                    self.dve(lambda e: e.tensor_scalar(out=latb[:, 0:384], in0=u[:, 0:384], scalar1=ssx[:, 2:3], scalar2=None, op0=ALU.mult),
                             r=[("u", 0), "ssl"], w=["latb0"])
                    self.dve(lambda e: e.tensor_scalar(out=latb[:, 384:640], in0=u[:, 384:640], scalar1=ssx[:, 3:4], scalar2=None, op0=ALU.mult),
                             r=[("u", 0), ("u", 1), "ssl2"], w=["latb1"])
                    if self.lim < 2.6:
                        continue
                    ps = pstr[pslot % 2]
                    rn = ("pstr", pslot % 2)
                    pslot += 1

                    def tpl(e, ps=ps):
                        for j in range(5):
                            ins = e.transpose(ps[:, j, :], latb[:, j * 128:(j + 1) * 128], self.identb[:])
                        return ins
                    self.pe(tpl, r=["latb0", "latb1", "identb"], w=[rn])
                    if self.lim < 2.8:
                        continue
                    self.dve(lambda e, ps=ps: e.tensor_copy(out=latTt[:], in_=ps[:, 0:5, :]), r=[rn], w=["latTt"])
                    if self.lim < 3:
                        continue
                    for gi, (c0, c1) in enumerate(((0, 512), (512, 768))):
                        def mmq(e, c0=c0, c1=c1, gi=gi):
                            for j in range(3):
                                ins = e.matmul(psup[gi][:, 0:c1 - c0], lhsT=latTt[:, j, :], rhs=w_uq[:, j, c0:c1], start=(j == 0), stop=(j == 2))
                            return ins
                        self.pe(mmq, r=["latTt", "w_uq"], w=[("psup", gi)])
                        self.act(lambda e, c0=c0, c1=c1, gi=gi: e.activation(out=qm[:, c0:c1], in_=psup[gi][:, 0:c1 - c0], func=AF.Copy),
                                 r=[("psup", gi)], w=["qm"] if gi == 0 else [], a=[] if gi == 0 else ["qm"])
                    kf3 = kf_[:].rearrange("p (h d) -> p h d", h=8)
                    vb3 = vbf[:, 0:512].rearrange("p (h d) -> p h d", h=8)
                    if self.lim < 3.2:
                        continue
                    for gi in range(2):
                        def mmk(e, gi=gi):
                            for j in range(2):
                                ins = e.matmul(psup[gi][:], lhsT=latTt[:, 3 + j, :], rhs=w_ukv[:, j, gi * 512:(gi + 1) * 512], start=(j == 0), stop=(j == 1))
                            return ins
                        self.pe(mmk, r=["latTt", "w_ukv"], w=[("psup", gi)])
                        p3 = psup[gi][:].rearrange("p (h d) -> p h d", h=4)
                        self.act(lambda e, gi=gi, p3=p3: e.activation(out=kf3[:, gi * 4:gi * 4 + 4, 0:64], in_=p3[:, :, 0:64], func=AF.Copy),
                                 r=[("psup", gi)], w=["kfull"] if gi == 0 else [], a=[] if gi == 0 else ["kfull"])
                        if self.lim < 3.3:
                            continue
                        self.act(lambda e, gi=gi, p3=p3: e.activation(out=vb3[:, gi * 4:gi * 4 + 4, :], in_=p3[:, :, 64:128], func=AF.Copy),
                                 r=[("psup", gi)], w=["vbf"] if gi == 0 else [], a=[] if gi == 0 else ["vbf"])
                    if self.lim < 3.4:
                        continue
                    self.dve(lambda e: e.tensor_copy(out=kf3[:, :, 64:96], in_=u[:, 640:672].unsqueeze(1).to_broadcast([128, 8, 32])),
                             r=[("u", 1)], a=["kfull"])
                    if self.lim < 4:
                        continue
                    self.head_post("qm", qm[:], 8, 96, 0, 64, 16, t, sq, ssh, rt, qbf[:, 0:768], "qbf0")
                    if self.lim < 5:
                        continue
                    pslot = self.tr_store("qbf0", qbf[:, 0:768], 8, 96, pstr, pslot, stg[0], "stg0", self.qT_mla, t)
                    if self.lim < 6:
                        continue
                    self.head_post("kfull", kf_[:], 8, 96, 96, 64, 16, t, sq, ssh, rt, qbf[:, 768:1536], "qbf1")
                    pslot = self.tr_store("qbf1", qbf[:, 768:1536], 8, 96, pstr, pslot, stg[1], "stg1", self.kT_mla, t)
                    self.ldc(self.v_mla[t * 128:(t + 1) * 128, :], vbf[:, 0:512], r=["vbf"], a=["v_mla"])
                    self.S.add("dve", lambda e: e.tensor_copy(out=qm[:], in_=u[:, 672:1440]), r=uall, w=["qm"])
                    self.head_post("qm", qm[:], 12, 64, 192, 0, 32, t, sq, ssh, rt, qbf[:, 0:768], "qbf0")
                    pslot = self.tr_store("qbf0", qbf[:, 0:768], 12, 64, pstr, pslot, stg[2], "stg2", self.qT_dil, t)
                    self.S.add("dve", lambda e: e.tensor_copy(out=kf_[:], in_=u[:, 1440:2208]), r=uall, w=["kfull"])
                    self.head_post("kfull", kf_[:], 12, 64, 256, 0, 32, t, sq, ssh, rt, qbf[:, 768:1536], "qbf1")
                    pslot = self.tr_store("qbf1", qbf[:, 768:1536], 12, 64, pstr, pslot, stg[3], "stg3", self.kT_dil, t)
                    self.act(lambda e: e.activation(out=vbf[:, 0:768], in_=u[:, 2208:2976], func=AF.Copy), r=uall, w=["vbf"])
                    self.ldc(self.v_dil[t * 128:(t + 1) * 128, :], vbf[:, 0:768], r=["vbf"], a=["v_dil"])
                else:
                    uall = [("u", gi) for gi in range(6)]
                    self.head_post(("u", 0), u[:, 0:512], 8, 64, 320, 0, 32, t, sq, ssh, rt, qbf[:, 0:512], "qbfA")
                    pslot = self.tr_store("qbfA", qbf[:, 0:512], 8, 64, pstr, pslot, stg[0], "stg0", self.qT_df, t)
                    self.head_post(("u", 1), u[:, 512:1024], 8, 64, 384, 0, 32, t, sq, ssh, rt, qbf[:, 512:1024], "qbfB")
                    pslot = self.tr_store("qbfB", qbf[:, 512:1024], 8, 64, pstr, pslot, stg[1], "stg1", self.kT_df, t)
                    self.act(lambda e: e.activation(out=vbf[:, 0:512], in_=u[:, 1024:1536], func=AF.Copy), r=uall, w=["vbf"])
                    self.ldc(self.v_df[t * 128:(t + 1) * 128, :], vbf[:, 0:512], r=["vbf"], a=["v_df"])
                    self.act(lambda e: e.activation(out=vbf[:, 512:1024], in_=u[:, 2560:3072], func=AF.Copy), r=uall, w=["vbf2"])
                    self.ldc(self.v_mb[t * 128:(t + 1) * 128, :], vbf[:, 512:1024], r=["vbf2"], a=["v_mb"])
                    self.head_post(("u", 4), u[:, 2048:2560], 8, 64, 512, 0, 32, t, sq, ssh, rt, qbf[:, 1024:1536], "qbfC")
                    pslot = self.tr_store("qbfC", qbf[:, 1024:1536], 8, 64, pstr, pslot, stg[2], "stg2", self.kT_mb, t)
                    nblk = t // 2
                    self.dve(lambda e: e.tensor_reduce(out=kpart[:], in_=stg[2][0:64, 0:8, :], axis=AX.X, op=ALU.add),
                             r=[("stg2", 0)], w=["kpart"])
                    self.dve(lambda e, nblk=nblk: e.tensor_tensor(out=kmacc[:, :, nblk], in0=kmacc[:, :, nblk], in1=kpart[:], op=ALU.add),
                             r=["kpart", "kmacc"], w=["kmacc"])
                    if t % 2 == 1:
                        self.act(lambda e, nblk=nblk: e.activation(out=kmb[:, :, nblk], in_=kmacc[:, :, nblk], func=AF.Copy, scale=1.0 / 256),
                                 r=["kmacc"], a=["kmb"])
                    self.head_post(("u", 3), u[:, 1536:2048], 8, 64, 448, 0, 32, t, sq, ssh, rt, qbf[:, 1536:2048], "qbfD")
                    pslot = self.tr_store("qbfD", qbf[:, 1536:2048], 8, 64, pstr, pslot, stg[3], "stg3", self.qT_mb, t)
                    if nblk > 0:
                        def mmg(e, nblk=nblk):
                            for h in range(8):
                                ins = e.matmul(psg[:, h, 0:nblk], lhsT=stg[3][0:64, h, :], rhs=kmb[:, h, 0:nblk], start=True, stop=True)
                            return ins
                        self.pe(mmg, r=[("stg3", 0), "kmb"], w=["psg"])
                        self.dve(lambda e, nblk=nblk: e.tensor_copy(out=gsb[:, :, 0:nblk], in_=psg[:, :, 0:nblk]), r=["psg"], w=["gsb"])

                        def mx(e):
                            for h in range(8):
                                ins = e.max(out=top8[:, h, :], in_=gsb[:, h, :])
                            return ins
                        self.dve(mx, r=["gsb"], w=["top8"])
                        self.dve(lambda e: e.tensor_tensor(out=biasf[:], in0=gsb[:], in1=top8[:, :, 2:3].to_broadcast([128, 8, 32]), op=ALU.is_lt),
                                 r=["gsb", "top8"], w=["biasf"])
                        self.dve(lambda e: e.tensor_scalar(out=biasb[:], in0=biasf[:], scalar1=NEGB, scalar2=None, op0=ALU.mult),
                                 r=["biasf"], w=["biasb"])
                    else:
                        self.dve(lambda e: e.memset(biasb[:], NEGB), w=["biasb"])
                    self.dve(lambda e, nblk=nblk: e.memset(biasb[:, :, nblk:nblk + 1], 0.0), r=["biasb"], w=["biasb"])
                    ps = pstr[pslot % 2]
                    rn = ("pstr", pslot % 2)
                    pslot += 1

                    def tpb(e, ps=ps):
                        for h in range(8):
                            ins = e.transpose(ps[0:32, h, :], biasb[:, h, :], self.identb[:])
                        return ins
                    self.pe(tpb, r=["biasb", "identb"], w=[rn])
                    self.dve(lambda e, ps=ps: e.tensor_copy(out=stg[0][0:32, 0:8, :], in_=ps[0:32, 0:8, :]), r=[rn], w=[("stg0", 0)])
                    dst = self.qT_mb[:, 64:96, t * 128:(t + 1) * 128].rearrange("h d s -> d h s")
                    self.ldc(dst, stg[0][0:32, 0:8, :], r=[("stg0", 0)], a=["qT_mb_bias"])
            self.S.flush()

    def attn_tiles(self, tiles, ps_s, P, po, po_nm, sc, cnt):
        n = len(tiles)
        LA = 2
        for i in range(n + LA):
            if i < n:
                kap, qap, q0, N, mk, vs = tiles[i][:6]
                si = (cnt + i) % 3
                pi = (cnt + i) % 4
                self.pe(lambda e, kap=kap, qap=qap, N=N, si=si: e.matmul(ps_s[si][:, 0:N], lhsT=kap, rhs=qap, start=True, stop=True),
                        r=tiles[i][6], w=[("ps_s", si)])
                self.act(lambda e, N=N, si=si, pi=pi: e.activation(out=P[:, pi, 0:N], in_=ps_s[si][:, 0:N], func=AF.Exp, scale=sc),
                         r=[("ps_s", si)], w=[("P", pi)])
                if mk is not None:
                    self.dve(lambda e, N=N, pi=pi, mk=mk: e.tensor_tensor(out=P[:, pi, 0:N], in0=P[:, pi, 0:N], in1=mk, op=ALU.mult),
                             r=[("P", pi), "masks"], w=[("P", pi)])
            j = i - LA
            if j >= 0:
                kap, qap, q0, N, mk, vs = tiles[j][:6]
                pi = (cnt + j) % 4
                for f, vap in enumerate(vs):
                    self.pe(lambda e, f=f, vap=vap, q0=q0, N=N, pi=pi, j=j: e.matmul(po[f][0:65, q0:512], lhsT=vap, rhs=P[:, pi, 0:N],
                                                                                    start=(j == 0), stop=(j == n - 1)),
                            r=[("P", pi)] + tiles[j][7], w=[po_nm[f]] if j == 0 else [], a=[] if j == 0 else [po_nm[f]])
        return cnt + n

    def attn_norm(self, po, po_nm, ep, slot, dst, dst_nm):
        rrow, osb, ps_bc = ep
        self.act(lambda e: e.activation(out=rrow[64:65, slot, :], in_=po[64:65, :], func=AF.Copy), r=[po_nm], w=[("rrow", slot)])
        self.dve(lambda e: e.reciprocal(out=rrow[64:65, slot, :], in_=rrow[64:65, slot, :]), r=[("rrow", slot)], w=[("rrow", slot)])
        self.pe(lambda e: e.matmul(ps_bc[0:64, :], lhsT=self.onesb[64:65, 0:64], rhs=rrow[64:65, slot, :], start=True, stop=True),
                r=[("rrow", slot), "onesb"], w=["ps_bc"])
        self.act(lambda e: e.activation(out=osb[0:64, slot, :], in_=po[0:64, :], func=AF.Copy), r=[po_nm], w=[("osb", slot)])
        self.dve(lambda e: e.tensor_tensor(out=dst, in0=osb[0:64, slot, :], in1=ps_bc[0:64, :], op=ALU.mult),
                 r=[("osb", slot), "ps_bc"], w=[dst_nm])

    def phaseB(self, kind):
        sb = self.sb
        nch = max(1, self.nt // 4)
        ncols = nch * 512
        dil = kind == "dil"
        nheads = {"mla": 8, "dil": 4, "diff": 4, "moba": 8}[kind]
        nm = 2 if kind == "diff" else 1
        parts = 2 if kind == "diff" else 1
        dk = {"mla": 96, "dil": 64, "diff": 64, "moba": 96}[kind]
        sc = float({"mla": 96 ** -0.5, "dil": 0.125, "diff": 0.125, "moba": 0.125}[kind])
        qsrc = {"mla": self.qT_mla, "dil": self.qT_dil, "diff": self.qT_df, "moba": self.qT_mb}[kind]
        ksrc = {"mla": self.kT_mla, "dil": self.kT_dil, "diff": self.kT_df, "moba": self.kT_mb}[kind]
        vsrc = {"mla": self.v_mla, "dil": self.v_dil, "diff": self.v_df, "moba": self.v_mb}[kind]
        row_base = {"mla": 0, "dil": 512, "diff": 0, "moba": 512}[kind]
        lam_init = 0.8 - 0.6 * math.exp(-0.3 * 1)
        with ExitStack() as es:
            if dil:
                qT = sb(es, "qTd", [64, 2, 3, 512], BF16)
                kT = sb(es, "kTd", [64, 3, SEQ], BF16)
                V = sb(es, "Vd", [128, 3, 64, 65], BF16)
                dmb = sb(es, "dmb", [128, 33, 512], BF16)
                self.ldc(dmb[:], self.dmask.rearrange("p (a b) -> p a b", a=33), w=["masks"])
                self.pool(lambda e: e.memset(V[:, :, :, 64:65], 1.0), a=["Vones"])
            else:
                qT = sb(es, "qTa", [96, 2, SEQ], BF16)
                kT = sb(es, "kTa", [96, 2, SEQ], BF16)
                V = sb(es, "Va", [128, 2, parts, 64, 65], BF16)
                self.pool(lambda e: e.memset(V[:, :, :, :, 64:65], 1.0), a=["Vones"])
                if kind == "moba":
                    for sl in range(2):
                        self.ldc(kT[64:96, sl, :], self.kind, a=["Vones"])
            P = sb(es, "Pt", [128, 4, 512], BF16)
            rrow = sb(es, "rrow", [65, 2, 512], F32)
            osb = sb(es, "osb", [64, 2, 512], F32)
            onb = sb(es, "onb", [64, 4, 512], BF16)
            ps_s = [self.ps(es, "ps_s%d" % i, [128, 512], F32) for i in range(3)]
            po = [self.ps(es, "po%d" % i, [128, 512], F32) for i in range(4)]
            ps_bc = self.ps(es, "ps_bc", [64, 512], F32)
            ep = (rrow, osb, ps_bc)
            if kind == "diff":
                nrm = sb(es, "nrm", [64, 4, 512], F32)
                dd = sb(es, "dd", [64, 2, 512], F32)
                sqd = sb(es, "sqd", [64, 2, 512], F32)
                rsd = sb(es, "rsd", [64, 512], F32)
                lamt = sb(es, "lamt", [64, 256], F32)
                lsm = sb(es, "lsm", [64, 8], F32)
                subs = sb(es, "subs", [64, 2], F32)
                self.ld(lamt[:], self.dlam.partition_broadcast(64), w=["lamt"])
                self.ld(subs[:], self.subT, w=["subs"])
                self.dve(lambda e: e.tensor_tensor(out=lamt[:, 0:64], in0=lamt[:, 0:64], in1=lamt[:, 64:128], op=ALU.mult), r=["lamt"], w=["lamt"])
                self.dve(lambda e: e.tensor_tensor(out=lamt[:, 128:192], in0=lamt[:, 128:192], in1=lamt[:, 192:256], op=ALU.mult), r=["lamt"], w=["lamt"])
                self.dve(lambda e: e.tensor_reduce(out=lsm[:, 0:1], in_=lamt[:, 0:64], axis=AX.X, op=ALU.add), r=["lamt"], w=["lsm0"])
                self.dve(lambda e: e.tensor_reduce(out=lsm[:, 1:2], in_=lamt[:, 128:192], axis=AX.X, op=ALU.add), r=["lamt"], w=["lsm1"])
                self.act(lambda e: e.activation(out=lsm[:, 2:4], in_=lsm[:, 0:2], func=AF.Exp), r=["lsm0", "lsm1"], w=["lsm2"])
                self.dve(lambda e: e.tensor_tensor(out=lsm[:, 4:5], in0=lsm[:, 3:4], in1=lsm[:, 2:3], op=ALU.subtract), r=["lsm2"], w=["lsm4"])
                self.dve(lambda e: e.tensor_scalar(out=lsm[:, 5:6], in0=lsm[:, 4:5], scalar1=-lam_init, scalar2=None, op0=ALU.add), r=["lsm4"], w=["neglam"])
                self.dve(lambda e: e.tensor_scalar(out=subs[:], in0=subs[:], scalar1=1.0 - lam_init, scalar2=None, op0=ALU.mult), r=["subs"], w=["subs"])
            cnt = 0
            ecnt = 0
            for h in range(nheads):
                vsl = h % 2
                if dil:
                    for g in range(3):
                        gh = g * 4 + h
                        self.ld(kT[:, g, 0:ncols], ksrc[gh, :, 0:ncols], w=[("kT", g)])
                        self.ld(V[:, g, 0:nch * 4, 0:64], vsrc[0:ncols, gh * 64:gh * 64 + 64].rearrange("(t p) d -> p t d", p=128),
                                r=["Vones"], w=[("V", g)])
                else:
                    for f in range(parts):
                        c0 = h * 64 * parts + f * 64
                        self.ld(V[:, vsl, f, 0:nch * 4, 0:64], vsrc[0:ncols, c0:c0 + 64].rearrange("(t p) d -> p t d", p=128),
                                r=["Vones"], w=[("V", vsl, f)])
                    for m in range(nm):
                        u_ = h * nm + m
                        sl = u_ % 2
                        dq = 96 if kind in ("mla", "moba") else 64
                        dkk = 96 if kind == "mla" else 64
                        self.ld(qT[0:dq, sl, 0:ncols], qsrc[u_, 0:dq, 0:ncols], w=[("qT", sl)])
                        self.ld(kT[0:dkk, sl, 0:ncols], ksrc[u_, 0:dkk, 0:ncols], r=["Vones"], w=[("kT", sl)])
                for c in range(nch):
                    if dil:
                        qs = c % 2
                        for g in range(3):
                            self.ld(qT[:, qs, g, :], qsrc[g * 4 + h, :, c * 512:(c + 1) * 512], w=[("qT", qs, g)])
                    for m in range(nm):
                        u_ = h * nm + m
                        sl = u_ % 2
                        tiles = []
                        if dil:
                            mo = 0
                            for g, W in enumerate((1, 4, 16)):
                                for o in range(W + 4):
                                    kt = 4 * c - W + o
                                    if kt >= 0:
                                        tiles.append((kT[:, g, kt * 128:(kt + 1) * 128], qT[:, qs, g, :], 0, 512, dmb[:, mo + o, :],
                                                      [V[:, g, kt, :]], [("kT", g), ("qT", qs, g)], [("V", g)]))
                                mo += W + 4
                        else:
                            for kt in range(4 * c + 4):
                                j = kt - 4 * c
                                if j < 0:
                                    q0, mk = 0, None
                                else:
                                    q0, mk = 128 * j, self.cmb[:, j, 128 * j:512]
                                N = 512 - q0
                                tiles.append((kT[0:dk, sl, kt * 128:(kt + 1) * 128], qT[0:dk, sl, c * 512 + q0:(c + 1) * 512], q0, N, mk,
                                              [V[:, vsl, f, kt, :] for f in range(parts)], [("kT", sl), ("qT", sl)],
                                              [("V", vsl, f) for f in range(parts)]))
                        if kind == "diff":
                            pidx = [m * 2, m * 2 + 1]
                        else:
                            pidx = [(ecnt % 2) * 2]
                        pos_ = [po[i] for i in pidx]
                        po_nm = [("po", i) for i in pidx]
                        cnt = self.attn_tiles(tiles, ps_s, P, pos_, po_nm, sc, cnt)
                        for f in range(parts):
                            es_ = ecnt % 2
                            if kind == "diff":
                                self.attn_norm(pos_[f], po_nm[f], ep, es_, nrm[:, m * 2 + f, :], ("nrm", m * 2 + f))
                            else:
                                osl = ecnt % 4
                                self.attn_norm(pos_[f], po_nm[f], ep, es_, onb[:, osl, :], ("onb", osl))
                                r0 = row_base + h * 64
                                self.ldc(self.oT[r0:r0 + 64, c * 512:(c + 1) * 512], onb[:, osl, :], r=[("onb", osl)], a=["oT"])
                            ecnt += 1
                    if kind == "diff":
                        for f in range(2):
                            self.dve(lambda e, f=f: e.scalar_tensor_tensor(out=dd[:, f, :], in0=nrm[:, 2 + f, :], scalar=lsm[:, 5:6], in1=nrm[:, f, :],
                                                                           op0=ALU.mult, op1=ALU.add),
                                     r=[("nrm", 2 + f), ("nrm", f), "neglam"], w=[("dd", f)])
                            self.act(lambda e, f=f: e.activation(out=sqd[:, f, :], in_=dd[:, f, :], func=AF.Square), r=[("dd", f)], w=[("sqd", f)])

                        def mms(e):
                            for f in range(2):
                                ins = e.matmul(ps_bc[0:64, :], lhsT=self.onesb[0:64, 0:64], rhs=sqd[:, f, :], start=(f == 0), stop=(f == 1))
                            return ins
                        self.pe(mms, r=[("sqd", 0), ("sqd", 1), "onesb"], w=["ps_bc"])
                        self.act(lambda e: e.activation(out=rsd[:], in_=ps_bc[0:64, :], func=AF.Sqrt, scale=1.0 / 128, bias=self.epsT[0:64, 0:1]),
                                 r=["ps_bc"], w=["rsd"])
                        self.dve(lambda e: e.reciprocal(out=rsd[:], in_=rsd[:]), r=["rsd"], w=["rsd"])
                        for f in range(2):
                            osl = (c * 2 + f) % 4
                            self.dve(lambda e, f=f: e.tensor_tensor(out=dd[:, f, :], in0=dd[:, f, :], in1=rsd[:], op=ALU.mult), r=[("dd", f), "rsd"], w=[("dd", f)])
                            self.dve(lambda e, f=f, osl=osl: e.tensor_scalar(out=onb[:, osl, :], in0=dd[:, f, :], scalar1=subs[:, f:f + 1], scalar2=None, op0=ALU.mult),
                                     r=[("dd", f), "subs"], w=[("onb", osl)])
                            r0 = h * 128 + f * 64
                            self.ldc(self.oT[r0:r0 + 64, c * 512:(c + 1) * 512], onb[:, osl, :], r=[("onb", osl)], a=["oT"])
            self.S.flush()

    def phaseC1(self, layer):
        sb = self.sb
        nk = 6 if layer == 0 else 8
        xin = self.x if layer == 0 else self.xL
        wsrc = self.e_w_out if layer == 0 else self.o_w_out
        with ExitStack() as es:
            w_out = sb(es, "w_out", [128, nk, D], BF16)
            G1 = sb(es, "G1", [128, D], F32)
            xt = sb(es, "xtc", [128, 2, D], F32)
            oTt = sb(es, "oTt", [128, 2, nk, 128], BF16)
            tmpf = sb(es, "tmpfc", [128, D], F32)
            sq = sb(es, "sqc", [128, D], F32)
            ssx = sb(es, "ssxc", [128, 2], F32)
            xs = sb(es, "xsc", [128, D], BF16)
            hT = sb(es, "hTc", [128, 2, 8, 128], BF16)
            psy = [self.ps(es, "psy%d" % i, [128, 512], F32) for i in range(2)]
            pT = self.ps(es, "pTc", [128, 8, 128], BF16)
            self.load_w(w_out, wsrc, nk * 128, "w_out")
            self.ld(G1[:], self.Gd[:, (layer * 2) * D:(layer * 2 + 1) * D], w=["G1"])
            for t in range(self.nt):
                sl = t % 2
                self.ld(xt[:, sl], xin[t * 128:(t + 1) * 128, :], w=[("xt", sl)])
                self.ld(oTt[:, sl], self.oT[0:nk * 128, t * 128:(t + 1) * 128].rearrange("(k p) s -> p k s", p=128), w=[("oTt", sl)])
                for hf in range(2):
                    def mm(e, hf=hf, sl=sl):
                        for k in range(nk):
                            ins = e.matmul(psy[hf][:], lhsT=oTt[:, sl, k, :], rhs=w_out[:, k, hf * 512:(hf + 1) * 512], start=(k == 0), stop=(k == nk - 1))
                        return ins
                    self.pe(mm, r=[("oTt", sl), "w_out"], w=[("psy", hf)])
                    self.dve(lambda e, hf=hf: e.tensor_tensor(out=tmpf[:, hf * 512:(hf + 1) * 512], in0=psy[hf][:], in1=G1[:, hf * 512:(hf + 1) * 512], op=ALU.mult),
                             r=[("psy", hf), "G1"], w=[("tmpf", hf)])
                    self.dve(lambda e, hf=hf, sl=sl: e.tensor_tensor(out=xt[:, sl, hf * 512:(hf + 1) * 512], in0=xt[:, sl, hf * 512:(hf + 1) * 512],
                                                                    in1=tmpf[:, hf * 512:(hf + 1) * 512], op=ALU.add),
                             r=[("tmpf", hf), ("xt", sl)], w=[("xt", sl)])
                self.ldc(self.x1[t * 128:(t + 1) * 128, :], xt[:, sl], r=[("xt", sl)], a=["x1"])
                self.act(lambda e, sl=sl: e.activation(out=sq[:], in_=xt[:, sl], func=AF.Square, accum_out=ssx[:, 0:1]), r=[("xt", sl)], w=["sq", "ssx"])
                self.rstd(ssx[:, 0:1], ssx[:, 1:2], D, "ssx", "rsx")
                self.act(lambda e, sl=sl: e.activation(out=xs[:], in_=xt[:, sl], func=AF.Copy, scale=ssx[:, 1:2]), r=[("xt", sl), "rsx"], w=["xs"])

                def tpx(e):
                    for j in range(8):
                        ins = e.transpose(pT[:, j, :], xs[:, j * 128:(j + 1) * 128], self.identb[:])
                    return ins
                self.pe(tpx, r=["xs", "identb"], w=["pT"])
                ao = layer * 16 + 8
                a_bc = self.aT[:, ao:ao + 8].unsqueeze(2).to_broadcast([128, 8, 128])
                b_bc = self.modT[:, layer * 48 + 24:layer * 48 + 32].unsqueeze(2).to_broadcast([128, 8, 128])
                tm3 = tmpf[:].rearrange("p (j s) -> p j s", j=8)
                self.dve(lambda e, a_bc=a_bc, tm3=tm3: e.tensor_tensor(out=tm3, in0=pT[:], in1=a_bc, op=ALU.mult),
                         r=["pT", ("aT", ao)], w=[("tmpf", 0), ("tmpf", 1)])
                self.dve(lambda e, b_bc=b_bc, tm3=tm3, sl=sl: e.tensor_tensor(out=hT[:, sl], in0=tm3, in1=b_bc, op=ALU.add),
                         r=[("tmpf", 0), ("tmpf", 1), "modT"], w=[("hT", sl)])
                self.ldc(self.h2T[:, t * 128:(t + 1) * 128].rearrange("(k p) s -> p k s", p=128), hT[:, sl], r=[("hT", sl)], a=["h2T"])
            self.S.flush()

    def phaseC2(self, layer):
        sb = self.sb
        dst = self.xL if layer == 0 else self.out
        ng = max(1, self.nt // 2)
        with ExitStack() as es:
            w1 = sb(es, "w1", [128, 8, DFF], BF16)
            w2 = sb(es, "w2", [128, 32, D], BF16)
            G2 = sb(es, "G2", [128, D], F32)
            hTt = sb(es, "hTt", [128, 2, 8, 256], BF16)
            x1t = sb(es, "x1t", [128, 2, 2, D], F32)
            aT = sb(es, "aTt", [128, 32, 256], BF16)
            rl = sb(es, "rl", [128, 2, 512], F32)
            tmp = sb(es, "tmpc2", [128, 2, 512], F32)
            psa = [self.ps(es, "psa%d" % i, [128, 2, 256], F32) for i in range(2)]
            psy = [self.ps(es, "psy2_%d" % i, [128, 512], F32) for i in range(4)]
            self.load_w(w1, self.mlp_w1[layer], D, "w1")
            self.load_w(w2, self.mlp_w2[layer], DFF, "w2")
            self.ld(G2[:], self.Gd[:, (layer * 2 + 1) * D:(layer * 2 + 2) * D], w=["G2"])
            for g in range(ng):
                sl = g % 2
                self.ld(hTt[:, sl], self.h2T[:, g * 256:(g + 1) * 256].rearrange("(k p) s -> p k s", p=128), w=[("hTt", sl)])
                self.ld(x1t[:, sl], self.x1[g * 256:(g + 1) * 256, :].rearrange("(s p) d -> p s d", p=128), w=[("x1t", sl)])
                for fp in range(16):
                    pb = psa[fp % 2]

                    def mm1(e, fp=fp, pb=pb, sl=sl):
                        for ff in range(2):
                            f = fp * 2 + ff
                            for k in range(8):
                                ins = e.matmul(pb[:, ff, :], lhsT=w1[:, k, f * 128:(f + 1) * 128], rhs=hTt[:, sl, k, :], start=(k == 0), stop=(k == 7))
                        return ins
                    self.pe(mm1, r=["w1", ("hTt", sl)], w=[("psa", fp % 2)])
                    rs_ = fp % 2
                    self.act(lambda e, pb=pb, rs_=rs_: e.activation(out=rl[:, rs_, :], in_=pb[:].rearrange("p a b -> p (a b)"), func=AF.Relu),
                             r=[("psa", fp % 2)], w=[("rl", rs_)])
                    self.dve(lambda e, fp=fp, rs_=rs_: e.tensor_tensor(out=aT[:, fp * 2:fp * 2 + 2, :].rearrange("p a b -> p (a b)"), in0=rl[:, rs_, :], in1=rl[:, rs_, :], op=ALU.mult),
                             r=[("rl", rs_)], w=[("aT", fp)])
                for s_ in range(2):
                    for hf in range(2):
                        pi = s_ * 2 + hf

                        def mm2(e, s_=s_, hf=hf, pi=pi):
                            for f in range(32):
                                ins = e.matmul(psy[pi][:], lhsT=aT[:, f, s_ * 128:(s_ + 1) * 128], rhs=w2[:, f, hf * 512:(hf + 1) * 512], start=(f == 0), stop=(f == 31))
                            return ins
                        self.pe(mm2, r=["w2"] + [("aT", fp) for fp in range(16)], w=[("psy", pi)])
                        self.dve(lambda e, hf=hf, pi=pi: e.tensor_tensor(out=tmp[:, hf, :], in0=psy[pi][:], in1=G2[:, hf * 512:(hf + 1) * 512], op=ALU.mult),
                                 r=[("psy", pi), "G2"], w=[("tmp", hf)])
                        self.dve(lambda e, s_=s_, hf=hf, sl=sl: e.tensor_tensor(out=x1t[:, sl, s_, hf * 512:(hf + 1) * 512], in0=x1t[:, sl, s_, hf * 512:(hf + 1) * 512],
                                                                              in1=tmp[:, hf, :], op=ALU.add),
                                 r=[("tmp", hf), ("x1t", sl)], w=[("x1t", sl)])
                self.ldc(dst[g * 256:(g + 1) * 256, :].rearrange("(s p) d -> p s d", p=128), x1t[:, sl], r=[("x1t", sl)], a=["dst"])
            self.S.flush()


def _consts():
    ident = np.eye(128, dtype=np.float32)
    i16 = np.arange(16, dtype=np.float32) / np.float32(16)
    i32 = np.arange(32, dtype=np.float32) / np.float32(32)
    invf = np.concatenate([np.float32(10000.0) ** (-i16), np.float32(10000.0) ** (-i32)]).astype(np.float32)
    k = np.arange(128)[:, None]
    q = np.arange(512)[None, :]
    cmask = np.stack([(q >= 128 * j + k) for j in range(4)], axis=1).astype(np.float32)
    dm = []
    for (w, r) in ((128, 1), (512, 4), (2048, 16)):
        W = w // 128
        for o in range(W + 4):
            rel = q - k + 128 * (W - o)
            dm.append(((rel >= 0) & (rel <= w) & (rel % r == 0)).astype(np.float32))
    dmask = np.stack(dm, axis=1)
    kind = (np.arange(SEQ)[None, :] // 256 == np.arange(32)[:, None]).astype(np.float32)
    return dict(ident=ident, invf=invf, cmask=cmask.reshape(128, -1), dmask=dmask.reshape(128, -1), kind=kind)


def _colT(v, n):
    return np.ascontiguousarray(np.asarray(v, np.float32).reshape(n, 128).T)


def make_in_maps(inp, batches):
    f = lambda a: np.ascontiguousarray(np.asarray(a, dtype=np.float32))
    c = _consts()
    shared = dict(
        ada_w=f(inp["ada_w"]),
        ada_bT=np.ascontiguousarray(np.concatenate([_colT(inp["ada_b"][l], 48) for l in range(2)], axis=1)),
        nrmT=np.ascontiguousarray(np.concatenate(
            [_colT(inp[nm][l], 8) for l in range(2) for nm in ("norm_mix", "norm_mlp")], axis=1)),
        mlp_w1=f(inp["mlp_w1"]), mlp_w2=f(inp["mlp_w2"]),
        e_w_in=f(inp["even_w_in"][0]), e_w_out=f(inp["even_w_out"][0]),
        latT=np.ascontiguousarray(np.concatenate([_colT(inp["mla_q_lat_norm"][0], 3), _colT(inp["mla_kv_lat_norm"][0], 2)], axis=1)),
        w_uq=f(np.asarray(inp["mla_w_uq"][0]).reshape(384, 768)),
        w_ukv=f(np.asarray(inp["mla_w_ukv"][0]).reshape(256, 1024)),
        gains=f(np.concatenate([np.asarray(inp[k][0], np.float32).reshape(-1) for k in
                                ("mla_q_norm", "mla_k_norm", "dil_q_norm", "dil_k_norm",
                                 "diff_q_norm", "diff_k_norm", "moba_q_norm", "moba_k_norm")])),
        o_w_in=f(inp["odd_w_in"][0]), o_w_out=f(inp["odd_w_out"][0]),
        dlam=f(np.asarray(inp["diff_lambda"][0]).reshape(256)),
        subT=np.ascontiguousarray(np.asarray(inp["diff_subln"][0], np.float32).reshape(2, 64).T),
        **c,
    )
    maps = []
    for b in batches:
        m = dict(shared)
        m["x"] = f(inp["x"][b])
        m["posT"] = np.ascontiguousarray(np.asarray(inp["positions"][b], np.int32).reshape(NT, 128).T)
        m["cT"] = _colT(inp["c"][b], 8)
        maps.append(m)
    return maps


def build(nt=NT, phases=None, dbg_out=()):
    phases = ALL_PHASES if phases is None else phases
    nc = bass.Bass("TRN2", target_bir_lowering=False)
    k = K(nc, nt=nt, phases=phases, dbg_out=dbg_out)
    k.declare()
    with ExitStack() as gs:
        k.es = gs
        sems = {}
        for e in Sched.COMPUTE:
            sems[e] = gs.enter_context(nc.semaphore("sem_" + e))
        for q in ("sp", "pq"):
            sems[q] = [gs.enter_context(nc.semaphore("sem_%s%d" % (q, i))) for i in range(k.S.ring)]
        k.S.init_emit(sems)
        k.setup()
        for ph in phases:
            getattr(k, "run_" + ph)()
        stats = k.S.finish()
    return nc, k, stats


def _add_phase_methods():
    K.run_A0 = lambda self: self.phaseA(0)
    K.run_A1 = lambda self: self.phaseA(1)
    K.run_Bmla = lambda self: self.phaseB("mla")
    K.run_C10 = lambda self: self.phaseC1(0)
    K.run_C20 = lambda self: self.phaseC2(0)
    K.run_C11 = lambda self: self.phaseC1(1)
    K.run_C21 = lambda self: self.phaseC2(1)
    K.run_Bdil = lambda self: self.phaseB("dil")
    K.run_Bdiff = lambda self: self.phaseB("diff")
    K.run_Bmoba = lambda self: self.phaseB("moba")


_add_phase_methods()


ALL_PHASES = ("A0", "Bmla", "Bdil", "C10", "C20", "A1", "Bdiff", "Bmoba", "C11", "C21")


def kernel(**inputs):
    nb = int(np.asarray(inputs["x"]).shape[0])
    nc, k, stats = build(nt=NT, phases=ALL_PHASES)
    maps = make_in_maps(inputs, list(range(nb)))
    res = run_bass_kernel_spmd(nc, maps, core_ids=list(range(nb)))
    return np.stack([np.asarray(res.results[b]["out"], dtype=np.float32) for b in range(nb)], axis=0)
```

```python
class _Op:
    __slots__ = ("eng", "fn", "deps", "needs_sig", "sem", "val", "is_dma", "idx", "ring_prev")

    def __init__(self, eng, fn, is_dma):
        self.eng = eng
        self.fn = fn
        self.deps = []
        self.needs_sig = False
        self.sem = None
        self.val = 0
        self.is_dma = is_dma
        self.ring_prev = None


class Sched:
    COMPUTE = ("pe", "act", "dve", "pool")

    def __init__(self, nc, ring=8, same_engine_sync=True):
        self.nc = nc
        self.ops = []
        self.last_w = {}
        self.readers = {}
        self.appenders = {}
        self.ring = ring
        self.same_engine_sync = same_engine_sync
        self.engobj = {"pe": nc.tensor, "act": nc.scalar, "dve": nc.vector, "pool": nc.gpsimd,
                       "sp": nc.sync, "pq": nc.gpsimd}
        self.stream = {"pe": "pe", "act": "act", "dve": "dve", "pool": "pool", "sp": "sp", "pq": "pool"}

    def add(self, eng, fn, r=(), w=(), a=()):
        is_dma = eng in ("sp", "pq")
        op = _Op(eng, fn, is_dma)
        deps = {}

        def dep(p, kind):
            if p is op:
                return
            same = (self.stream[p.eng] == self.stream[eng]) and not p.is_dma
            if same:
                if eng == "pe" or not self.same_engine_sync:
                    return
            deps[id(p)] = p

        for x in r:
            p = self.last_w.get(x)
            if p is not None:
                dep(p, "raw")
            for p in self.appenders.get(x, ()):
                dep(p, "raw")
        for x in list(w) + list(a):
            p = self.last_w.get(x)
            if p is not None:
                dep(p, "waw")
            for p in self.readers.get(x, ()):
                dep(p, "war")
        for x in w:
            for p in self.appenders.get(x, ()):
                dep(p, "waw")
        for x in r:
            self.readers.setdefault(x, []).append(op)
        for x in w:
            self.last_w[x] = op
            self.readers[x] = []
            self.appenders[x] = []
        for x in a:
            self.appenders.setdefault(x, []).append(op)
            self.readers[x] = []
        op.deps = list(deps.values())
        for p in op.deps:
            p.needs_sig = True
        self.ops.append(op)
        return op

    def init_emit(self, sems):
        self.sems = sems
        self.cnt = {e: 0 for e in self.COMPUTE}
        self.dcount = {"sp": 0, "pq": 0}
        self.dhist = {"sp": [], "pq": []}
        self.waited = {}
        self.nwaits = 0
        self.nops = 0
        self.barrier_deps = []

    def flush(self):
        ops = self.ops
        lastc = {}
        for op in ops:
            if not op.is_dma:
                lastc[op.eng] = op
        for op in lastc.values():
            op.needs_sig = True
        bd = self.barrier_deps
        first_seen = set()
        for op in ops:
            st = self.stream[op.eng]
            if st not in first_seen:
                first_seen.add(st)
                op.deps = op.deps + [p for p in bd if not (self.stream[p.eng] == st and not p.is_dma)]
            if op.is_dma:
                i = self.dcount[op.eng]
                self.dcount[op.eng] += 1
                op.sem = self.sems[op.eng][i % self.ring]
                op.val = 16 * (i // self.ring + 1)
                if i >= self.ring:
                    op.ring_prev = self.dhist[op.eng][i - self.ring]
                self.dhist[op.eng].append(op)
            elif op.needs_sig:
                self.cnt[op.eng] += 1
                op.sem = self.sems[op.eng]
                op.val = self.cnt[op.eng]
        waited = self.waited
        for op in ops:
            e = self.engobj[op.eng]
            st = self.stream[op.eng]
            need = {}
            plist = list(op.deps)
            if op.ring_prev is not None:
                plist.append(op.ring_prev)
            for p in plist:
                k = id(p.sem)
                if k not in need or need[k][1] < p.val:
                    need[k] = (p.sem, p.val)
            for k, (sem, val) in need.items():
                if waited.get((st, k), 0) < val:
                    e.wait_ge(sem, val)
                    waited[(st, k)] = val
                    self.nwaits += 1
            inst = op.fn(e)
            if op.is_dma:
                inst.then_inc(op.sem, 16)
            elif op.needs_sig:
                inst.then_inc(op.sem, 1)
        self.nops += len(ops)
        nb = list(lastc.values())
        for p in bd:
            if not p.is_dma and p.eng not in lastc:
                nb.append(p)
        for q in ("sp", "pq"):
            nb.extend(self.dhist[q][-self.ring:])
        self.barrier_deps = nb
        self.ops = []
        self.last_w = {}
        self.readers = {}
        self.appenders = {}

    def finish(self, eng="sp"):
        self.flush()
        e = self.engobj[eng]
        st = self.stream[eng]
        for p in self.barrier_deps:
            k = id(p.sem)
            if self.waited.get((st, k), 0) < p.val:
                e.wait_ge(p.sem, p.val)
                self.waited[(st, k)] = p.val
        return {"n_ops": self.nops, "n_waits": self.nwaits, "sig": dict(self.cnt), "dma": dict(self.dcount)}


import math
from contextlib import ExitStack
import numpy as np
import ml_dtypes
import concourse.bass as bass
import concourse.mybir as mybir
from concourse.bass_utils import run_bass_kernel_spmd

F32 = mybir.dt.float32
BF16 = mybir.dt.bfloat16
I32 = mybir.dt.int32
AF = mybir.ActivationFunctionType
ALU = mybir.AluOpType
AX = mybir.AxisListType

D = 1024
SEQ = 8192
NT = SEQ // 128
NCH = SEQ // 512
EPS = 1e-6
EVEN_IN = 2976
ODD_IN = 3072
DFF = 4096
NEGB = -30000.0
TWO_PI = float(2 * np.pi)


class K:
    def __init__(self, nc, nt=NT, phases=None, dbg_out=()):
        self.nc = nc
        self.S = Sched(nc)
        self.nt = nt
        self.phases = phases
        self.dbg_out = dbg_out
        self.es = ExitStack()
        self.din = {}
        self.dscr = {}
        import os as _os
        self.lim = float(_os.environ.get('KLIM', '99'))

    def inp(self, name, shape, dt=F32):
        t = self.nc.dram_tensor(name, list(shape), dt, kind="ExternalInput").ap()
        self.din[name] = t
        return t

    def scr(self, name, shape, dt):
        kind = "ExternalOutput" if name in self.dbg_out else "Internal"
        t = self.nc.dram_tensor(name, list(shape), dt, kind=kind).ap()
        self.dscr[name] = t
        return t

    def sb(self, es, name, shape, dt):
        self.uid = getattr(self, "uid", 0) + 1
        return es.enter_context(self.nc.sbuf_tensor("%s_%d" % (name, self.uid), list(shape), dt))

    def ps(self, es, name, shape, dt):
        self.uid = getattr(self, "uid", 0) + 1
        return es.enter_context(self.nc.psum_tensor("%s_%d" % (name, self.uid), list(shape), dt))

    def act(self, fn, r=(), w=(), a=()):
        return self.S.add("act", fn, r, w, a)

    def dve(self, fn, r=(), w=(), a=()):
        return self.S.add("dve", fn, r, w, a)

    def pool(self, fn, r=(), w=(), a=()):
        return self.S.add("pool", fn, r, w, a)

    def pe(self, fn, r=(), w=(), a=()):
        return self.S.add("pe", fn, r, w, a)

    def ld(self, out, in_, r=(), w=(), a=(), q="sp"):
        return self.S.add(q, lambda e: e.dma_start(out=out, in_=in_), r, w, a)

    def ldc(self, out, in_, r=(), w=(), a=()):
        return self.S.add("pq", lambda e: e.dma_start(out=out, in_=in_), r, w, a)

    def rstd(self, ss, rs, n, rn_ss, rn_rs):
        self.act(lambda e: e.activation(out=rs, in_=ss, func=AF.Sqrt, scale=1.0 / n, bias=self.eps_ap(ss)),
                 r=[rn_ss], w=[rn_rs])
        self.dve(lambda e: e.reciprocal(out=rs, in_=rs), r=[rn_rs], w=[rn_rs])

    def eps_ap(self, like):
        p = like.shape[0]
        return self.epsT[0:p, 0:1]

    def declare(self):
        i = self.inp
        self.x = i("x", [SEQ, D])
        self.posT = i("posT", [128, NT], I32)
        self.cT = i("cT", [128, 8])
        self.ada_w = i("ada_w", [2, D, 6 * D])
        self.ada_bT = i("ada_bT", [128, 96])
        self.nrmT = i("nrmT", [128, 32])
        self.mlp_w1 = i("mlp_w1", [2, D, DFF])
        self.mlp_w2 = i("mlp_w2", [2, DFF, D])
        self.e_w_in = i("e_w_in", [D, EVEN_IN])
        self.e_w_out = i("e_w_out", [768, D])
        self.latT = i("latT", [128, 5])
        self.w_uq = i("w_uq", [384, 768])
        self.w_ukv = i("w_ukv", [256, 1024])
        self.gains = i("gains", [576])
        self.o_w_in = i("o_w_in", [D, ODD_IN])
        self.o_w_out = i("o_w_out", [D, D])
        self.dlam = i("dlam", [256])
        self.subT = i("subT", [128, 1])
        self.ident = i("ident", [128, 128])
        self.invf = i("invf", [48])
        self.cmask = i("cmask", [128, 4 * 512])
        self.dmask = i("dmask", [128, 33 * 512])
        self.kind = i("kind", [32, SEQ])
        self.out = self.nc.dram_tensor("out", [SEQ, D], F32, kind="ExternalOutput").ap()
        s = self.scr
        self.trigd = s("trigd", [128, 2 * NT * 48], F32)
        self.Gd = s("Gd", [128, 4 * D], F32)
        self.x1 = s("x1", [SEQ, D], F32)
        self.xL = s("xL", [SEQ, D], F32)
        self.h2T = s("h2T", [D, SEQ], BF16)
        self.oT = s("oT", [D, SEQ], BF16)
        self.qT_mla = s("qT_mla", [8, 96, SEQ], BF16)
        self.kT_mla = s("kT_mla", [8, 96, SEQ], BF16)
        self.v_mla = s("v_mla", [SEQ, 512], BF16)
        self.qT_dil = s("qT_dil", [12, 64, SEQ], BF16)
        self.kT_dil = s("kT_dil", [12, 64, SEQ], BF16)
        self.v_dil = s("v_dil", [SEQ, 768], BF16)
        self.qT_df = s("qT_df", [8, 64, SEQ], BF16)
        self.kT_df = s("kT_df", [8, 64, SEQ], BF16)
        self.v_df = s("v_df", [SEQ, 512], BF16)
        self.qT_mb = s("qT_mb", [8, 96, SEQ], BF16)
        self.kT_mb = s("kT_mb", [8, 64, SEQ], BF16)
        self.v_mb = s("v_mb", [SEQ, 512], BF16)

    def setup(self):
        nc, S = self.nc, self.S
        g = self.es
        sb = self.sb
        self.epsT = sb(g, "epsT", [128, 1], F32)
        self.identb = sb(g, "identb", [128, 128], BF16)
        self.modT = sb(g, "modT", [128, 96], F32)
        self.aT = sb(g, "aT", [128, 32], F32)
        self.gbc = sb(g, "gbc", [128, 576], F32)
        self.cmb = sb(g, "cmb", [128, 4, 512], BF16)
        self.onesb = sb(g, "onesb", [128, 64], F32)
        self.pool(lambda e: e.memset(self.epsT[:], EPS), w=["epsT"])
        self.pool(lambda e: e.memset(self.onesb[:], 1.0), w=["onesb"])
        self.ldc(self.identb[:], self.ident, w=["identb"])
        self.ldc(self.cmb[:], self.cmask.rearrange("p (a b) -> p a b", a=4), w=["cmb"])
        self.ld(self.gbc[:], self.gains.partition_broadcast(128), w=["gbc"])
        with ExitStack() as es:
            self.trig = sb(es, "trig", [128, 2, NT, 48], F32)
            self.G = sb(es, "G", [128, 4, D], F32)
            condT = sb(es, "condT", [128, 8], F32)
            bT = sb(es, "bT", [128, 96], F32)
            nT = sb(es, "nT", [128, 32], F32)
            stage = sb(es, "adastage", [128, 2, 8, 1024], F32)
            posi = sb(es, "posi", [128, NT], I32)
            posf = sb(es, "posf", [128, NT], F32)
            invb = sb(es, "invb", [128, 48], F32)
            kf = sb(es, "kf", [128, 2, NT, 48], F32)
            ki = sb(es, "ki", [128, 2, NT, 48], I32)
            psmod = self.ps(es, "psmod", [128, 96], F32)
            self.ld(condT[:], self.cT, w=["condT"])
            self.ld(bT[:], self.ada_bT, w=["bT"])
            self.ld(nT[:], self.nrmT, w=["nT"])
            self.ld(posi[:], self.posT, w=["posi"])
            self.ld(invb[:], self.invf.partition_broadcast(128), w=["invb"])
            self.act(lambda e: e.activation(out=condT[:], in_=condT[:], func=AF.Silu), r=["condT"], w=["condT"])
            tr = self.trig
            self.dve(lambda e: e.tensor_copy(out=posf[:], in_=posi[:]), r=["posi"], w=["posf"])
            self.dve(lambda e: e.tensor_tensor(out=tr[:, 0], in0=posf[:].unsqueeze(2).to_broadcast([128, NT, 48]),
                                               in1=invb[:].unsqueeze(1).to_broadcast([128, NT, 48]), op=ALU.mult),
                     r=["posf", "invb"], w=["trig"])
            self.dve(lambda e: e.tensor_scalar(out=tr[:, 1], in0=tr[:, 0], scalar1=float(np.pi / 2), scalar2=None,
                                               op0=ALU.add), r=["trig"], w=["trig"])
            self.dve(lambda e: e.tensor_scalar(out=kf[:], in0=tr[:], scalar1=float(1 / TWO_PI), scalar2=None,
                                               op0=ALU.mult), r=["trig"], w=["kf"])
            self.dve(lambda e: e.tensor_copy(out=ki[:], in_=kf[:]), r=["kf"], w=["ki"])
            self.dve(lambda e: e.tensor_copy(out=kf[:], in_=ki[:]), r=["ki"], w=["kf"])
            self.dve(lambda e: e.scalar_tensor_tensor(out=tr[:], in0=kf[:], scalar=-TWO_PI, in1=tr[:],
                                                      op0=ALU.mult, op1=ALU.add), r=["kf", "trig"], w=["trig"])
            self.dve(lambda e: e.tensor_scalar(out=kf[:], in0=tr[:], scalar1=float(np.pi), scalar2=-TWO_PI,
                                               op0=ALU.is_gt, op1=ALU.mult), r=["trig"], w=["kf"])
            self.dve(lambda e: e.tensor_tensor(out=tr[:], in0=tr[:], in1=kf[:], op=ALU.add), r=["kf", "trig"], w=["trig"])
            self.dve(lambda e: e.tensor_scalar(out=kf[:], in0=tr[:], scalar1=float(-np.pi), scalar2=TWO_PI,
                                               op0=ALU.is_lt, op1=ALU.mult), r=["trig"], w=["kf"])
            self.dve(lambda e: e.tensor_tensor(out=tr[:], in0=tr[:], in1=kf[:], op=ALU.add), r=["kf", "trig"], w=["trig"])
            self.act(lambda e: e.activation(out=tr[:], in_=tr[:], func=AF.Sin), r=["trig"], w=["trig"])
            for l in range(2):
                for cb in range(6):
                    sl = (l * 6 + cb) % 2
                    self.ld(stage[:, sl], self.ada_w[l, :, cb * 1024:(cb + 1) * 1024].rearrange("(k p) n -> p k n", p=128),
                            w=[("adast", sl)])
                    for jj in range(8):
                        col = l * 48 + cb * 8 + jj

                        def mm(e, sl=sl, jj=jj, col=col):
                            for k in range(8):
                                ins = e.matmul(psmod[:, col:col + 1], lhsT=stage[:, sl, k, jj * 128:(jj + 1) * 128],
                                               rhs=condT[:, k:k + 1], start=(k == 0), stop=(k == 7))
                            return ins
                        self.pe(mm, r=[("adast", sl), "condT"], a=["psmod"])
            self.dve(lambda e: e.tensor_tensor(out=self.modT[:], in0=psmod[:], in1=bT[:], op=ALU.add),
                     r=["psmod", "bT"], w=["modT"])
            for l in range(2):
                m = self.modT[:, l * 48:(l + 1) * 48]
                for which, (sc0, sh0) in enumerate(((8, 0), (32, 24))):
                    o = l * 16 + which * 8
                    nsl = nT[:, l * 16 + which * 8: l * 16 + which * 8 + 8]
                    self.dve(lambda e, o=o, m=m, sc0=sc0, nsl=nsl: e.scalar_tensor_tensor(
                        out=self.aT[:, o:o + 8], in0=m[:, sc0:sc0 + 8], scalar=1.0, in1=nsl, op0=ALU.add, op1=ALU.mult),
                        r=["modT", "nT"], w=[("aT", o)])
            identf = sb(es, "identf", [128, 128], F32)
            onesf = sb(es, "onesf", [128, 128], F32)
            dg = sb(es, "dg", [128, 2, 128], F32)
            psg_ = [self.ps(es, "psgate%d" % i, [128, 512], F32) for i in range(2)]
            self.ld(identf[:], self.ident, w=["identf"])
            self.pool(lambda e: e.memset(onesf[:], 1.0), w=["onesf"])
            cnt = 0
            for l in range(2):
                for which, off in enumerate((16, 40)):
                    gi = l * 2 + which
                    for half in range(2):
                        pb = psg_[cnt % 2]
                        for jj in range(4):
                            j = half * 4 + jj
                            col = l * 48 + off + j
                            sl = (cnt * 4 + jj) % 2
                            self.dve(lambda e, sl=sl, col=col: e.tensor_scalar(out=dg[:, sl, :], in0=identf[:], scalar1=self.modT[:, col:col + 1],
                                                                               scalar2=None, op0=ALU.mult), r=["identf", "modT"], w=[("dg", sl)])
                            self.pe(lambda e, sl=sl, jj=jj, pb=pb: e.matmul(pb[:, jj * 128:(jj + 1) * 128], lhsT=onesf[:], rhs=dg[:, sl, :], start=True, stop=True),
                                    r=[("dg", sl), "onesf"], w=[("psgate", cnt % 2, jj)])
                        self.act(lambda e, gi=gi, half=half, pb=pb: e.activation(out=self.G[:, gi, half * 512:(half + 1) * 512], in_=pb[:], func=AF.Copy),
                                 r=[("psgate", cnt % 2, jj) for jj in range(4)], w=[("G", gi, half)])
                        cnt += 1
            self.ldc(self.trigd.rearrange("p (a b) -> p a b", a=2 * NT), self.trig[:].rearrange("p s t f -> p (s t) f"), r=["trig"], w=["trigd"])
            self.ldc(self.Gd.rearrange("p (a b) -> p a b", a=4), self.G[:], r=[("G", gi, hf) for gi in range(4) for hf in range(2)], w=["Gd"])
            self.S.flush()

    def head_post(self, nm, src, nh, hd, goff, rope_lo, half, t, sq, ssh, rt, dst_bf, nm_bf):
        n = nh * hd
        s3 = src.rearrange("p (h d) -> p h d", h=nh)
        sq2 = sq[:, 0:n]
        self.act(lambda e: e.activation(out=sq2, in_=src, func=AF.Square), r=[nm], w=["sq"])
        self.dve(lambda e: e.tensor_reduce(out=ssh[:, 0:nh], in_=sq2.rearrange("p (h d) -> p h d", h=nh), axis=AX.X, op=ALU.add),
                 r=["sq"], w=["ssh"])
        self.rstd(ssh[:, 0:nh], ssh[:, 0:nh], hd, "ssh", "ssh")
        self.dve(lambda e: e.tensor_tensor(out=s3, in0=s3, in1=ssh[:, 0:nh].unsqueeze(2).to_broadcast([128, nh, hd]), op=ALU.mult),
                 r=[nm, "ssh"], w=[nm])
        gb = self.gbc[:, goff:goff + hd]
        self.dve(lambda e: e.tensor_tensor(out=s3, in0=s3, in1=gb.unsqueeze(1).to_broadcast([128, nh, hd]), op=ALU.mult),
                 r=[nm, "gbc"], w=[nm])
        fo = 0 if half == 16 else 16
        sin = self.trig[:, 0, t, fo:fo + half].unsqueeze(1).to_broadcast([128, nh, half])
        cos = self.trig[:, 1, t, fo:fo + half].unsqueeze(1).to_broadcast([128, nh, half])
        x1 = s3[:, :, rope_lo:rope_lo + half]
        x2 = s3[:, :, rope_lo + half:rope_lo + 2 * half]
        m = nh * half
        tA = rt[:, 0, 0:m].rearrange("p (h d) -> p h d", h=nh)
        tB = rt[:, 1, 0:m].rearrange("p (h d) -> p h d", h=nh)
        tC = rt[:, 2, 0:m].rearrange("p (h d) -> p h d", h=nh)
        tD = rt[:, 3, 0:m].rearrange("p (h d) -> p h d", h=nh)
        self.dve(lambda e: e.tensor_tensor(out=tA, in0=x1, in1=cos, op=ALU.mult), r=[nm, "trig"], w=["rtA"])
        self.dve(lambda e: e.tensor_tensor(out=tB, in0=x2, in1=sin, op=ALU.mult), r=[nm, "trig"], w=["rtB"])
        self.dve(lambda e: e.tensor_tensor(out=tC, in0=x2, in1=cos, op=ALU.mult), r=[nm, "trig"], w=["rtC"])
        self.dve(lambda e: e.tensor_tensor(out=tD, in0=x1, in1=sin, op=ALU.mult), r=[nm, "trig"], w=["rtD"])
        self.dve(lambda e: e.tensor_tensor(out=x1, in0=tA, in1=tB, op=ALU.subtract), r=["rtA", "rtB"], w=[nm])
        self.dve(lambda e: e.tensor_tensor(out=x2, in0=tC, in1=tD, op=ALU.add), r=["rtC", "rtD"], w=[nm])
        self.act(lambda e: e.activation(out=dst_bf, in_=src, func=AF.Copy), r=[nm], w=[nm_bf])

    def tr_store(self, nm_bf, src_bf, nh, hd, pstr, pslot, stg, stg_nm, dram, t, hw=None, row0=0):
        hw = hw or hd
        s3 = src_bf.rearrange("p (h d) -> p h d", h=nh)
        done = 0
        while done < nh:
            nb = min(8, nh - done)
            ps = pstr[pslot % 2]
            rn = ("pstr", pslot % 2)

            def tp(e, done=done, nb=nb, ps=ps):
                for i in range(nb):
                    ins = e.transpose(ps[0:hw, i, :], s3[:, done + i, 0:hw], self.identb[:])
                return ins
            self.pe(tp, r=[nm_bf, "identb"], w=[rn])
            self.dve(lambda e, done=done, nb=nb, ps=ps: e.tensor_copy(out=stg[0:hw, done:done + nb, :], in_=ps[0:hw, 0:nb, :]),
                     r=[rn], w=[(stg_nm, done)])
            dst = dram[done:done + nb, row0:row0 + hw, t * 128:(t + 1) * 128].rearrange("h d s -> d h s")
            self.ldc(dst, stg[0:hw, done:done + nb, :], r=[(stg_nm, done)], a=[("dram", id(dram))])
            done += nb
            pslot += 1
        return pslot

    def load_w(self, dst, src, K, rn, n0=0, n1=None, d0=0):
        n1 = n1 if n1 is not None else src.shape[1]
        kc = K // 128
        step = 2 if (n1 - n0) > 1024 else kc
        for k0 in range(0, kc, step):
            k1 = min(kc, k0 + step)
            self.ldc(dst[:, k0:k1, d0:d0 + (n1 - n0)],
                     src[k0 * 128:k1 * 128, n0:n1].rearrange("(k p) n -> p k n", p=128), a=[rn])

    def phaseA(self, layer):
        nc = self.nc
        sb = self.sb
        ncol = EVEN_IN if layer == 0 else ODD_IN
        xin = self.x if layer == 0 else self.xL
        with ExitStack() as es:
            w_in = sb(es, "w_in", [128, 8, ncol], BF16)
            self.trig = sb(es, "trigA", [128, 2, NT, 48], F32)
            self.ld(self.trig[:].rearrange("p s t f -> p (s t) f"), self.trigd.rearrange("p (a b) -> p a b", a=2 * NT), w=["trig"])
            xt = sb(es, "xt", [128, 2, D], F32)
            xs = sb(es, "xs", [128, D], BF16)
            ssx = sb(es, "ssx", [128, 4], F32)
            hT = sb(es, "hT", [128, 8, 128], BF16)
            tmpf = sb(es, "tmpf", [128, D], F32)
            u = sb(es, "u", [128, ncol], F32)
            sq = sb(es, "sq", [128, D], F32)
            ssh = sb(es, "ssh", [128, 16], F32)
            rt = sb(es, "rt", [128, 4, 512], F32)
            qbf = sb(es, "qbf", [128, 2048], BF16)
            vbf = sb(es, "vbf", [128, 1024], BF16)
            stg = [sb(es, "stg%d" % i, [96, 12, 128], BF16) for i in range(4)]
            pT = self.ps(es, "pT", [128, 8, 128], BF16)
            psu = [self.ps(es, "psu%d" % i, [128, 512], F32) for i in range(2)]
            pstr = [self.ps(es, "pstr%d" % i, [128, 8, 128], BF16) for i in range(2)]
            self.load_w(w_in, self.e_w_in if layer == 0 else self.o_w_in, D, "w_in")
            if layer == 0:
                wq_f = sb(es, "wq_f", [128, 3, 768], F32)
                wkv_f = sb(es, "wkv_f", [128, 2, 1024], F32)
                w_uq = sb(es, "w_uqb", [128, 3, 768], BF16)
                w_ukv = sb(es, "w_ukvb", [128, 2, 1024], BF16)
                latTs = sb(es, "latTs", [128, 5], F32)
                latb = sb(es, "latb", [128, 640], BF16)
                latTt = sb(es, "latTt", [128, 5, 128], BF16)
                qm = sb(es, "qm", [128, 768], F32)
                kf_ = sb(es, "kfull", [128, 768], F32)
                psup = [self.ps(es, "psup%d" % i, [128, 512], F32) for i in range(2)]
                self.ld(latTs[:], self.latT, w=["latTs"])
                self.ld(wq_f[:], self.w_uq.rearrange("(k p) n -> p k n", p=128), w=["wq_f"])
                self.ld(wkv_f[:], self.w_ukv.rearrange("(k p) n -> p k n", p=128), w=["wkv_f"])
                self.dve(lambda e: e.tensor_tensor(out=w_uq[:], in0=wq_f[:], in1=latTs[:, 0:3].unsqueeze(2).to_broadcast([128, 3, 768]), op=ALU.mult),
                         r=["wq_f", "latTs"], w=["w_uq"])
                self.dve(lambda e: e.tensor_tensor(out=w_ukv[:], in0=wkv_f[:], in1=latTs[:, 3:5].unsqueeze(2).to_broadcast([128, 2, 1024]), op=ALU.mult),
                         r=["wkv_f", "latTs"], w=["w_ukv"])
            else:
                kmacc = sb(es, "kmacc", [64, 8, 32], F32)
                kmb = sb(es, "kmb", [64, 8, 32], BF16)
                kpart = sb(es, "kpart", [64, 8], F32)
                gsb = sb(es, "gsb", [128, 8, 32], F32)
                top8 = sb(es, "top8", [128, 8, 8], F32)
                biasf = sb(es, "biasf", [128, 8, 32], F32)
                biasb = sb(es, "biasb", [128, 8, 32], BF16)
                psg = self.ps(es, "psg", [128, 8, 32], F32)
                self.pool(lambda e: e.memset(gsb[:], -1e30), w=["gsb"])
                self.pool(lambda e: e.memset(kmacc[:], 0.0), w=["kmacc"])
            pslot = 0
            for t in range(self.nt):
                sl = t % 2
                xtt = xt[:, sl]
                self.ld(xtt, xin[t * 128:(t + 1) * 128, :], w=[("xt", sl)])
                self.act(lambda e, xtt=xtt: e.activation(out=sq[:], in_=xtt, func=AF.Square, accum_out=ssx[:, 0:1]),
                         r=[("xt", sl)], w=["sq", "ssx"])
                self.rstd(ssx[:, 0:1], ssx[:, 1:2], D, "ssx", "rsx")
                self.act(lambda e, xtt=xtt: e.activation(out=xs[:], in_=xtt, func=AF.Copy, scale=ssx[:, 1:2]),
                         r=[("xt", sl), "rsx"], w=["xs"])

                def tpx(e):
                    for j in range(8):
                        ins = e.transpose(pT[:, j, :], xs[:, j * 128:(j + 1) * 128], self.identb[:])
                    return ins
                self.pe(tpx, r=["xs", "identb"], w=["pT"])
                ao = layer * 16
                a_bc = self.aT[:, ao:ao + 8].unsqueeze(2).to_broadcast([128, 8, 128])
                b_bc = self.modT[:, layer * 48:layer * 48 + 8].unsqueeze(2).to_broadcast([128, 8, 128])
                tm3 = tmpf[:].rearrange("p (j s) -> p j s", j=8)
                self.dve(lambda e, a_bc=a_bc, tm3=tm3: e.tensor_tensor(out=tm3, in0=pT[:], in1=a_bc, op=ALU.mult),
                         r=["pT", ("aT", ao)], w=["tmpf"])
                self.dve(lambda e, b_bc=b_bc, tm3=tm3: e.tensor_tensor(out=hT[:], in0=tm3, in1=b_bc, op=ALU.add),
                         r=["tmpf", "modT"], w=["hT"])
                if self.lim < 1:
                    continue
                for gi, c0 in enumerate(range(0, ncol, 512)):
                    c1 = min(ncol, c0 + 512)
                    pb = psu[gi % 2]

                    def mm(e, c0=c0, c1=c1, pb=pb):
                        for j in range(8):
                            ins = e.matmul(pb[:, 0:c1 - c0], lhsT=hT[:, j, :], rhs=w_in[:, j, c0:c1], start=(j == 0), stop=(j == 7))
                        return ins
                    self.pe(mm, r=["hT", "w_in"], w=[("psu", gi % 2)])
                    self.act(lambda e, c0=c0, c1=c1, pb=pb: e.activation(out=u[:, c0:c1], in_=pb[:, 0:c1 - c0], func=AF.Copy),
                             r=[("psu", gi % 2)], w=[("u", gi)])
                if self.lim < 2:
                    continue
                if layer == 0:
                    uall = [("u", gi) for gi in range(6)]
                    self.act(lambda e: e.activation(out=sq[:, 0:384], in_=u[:, 0:384], func=AF.Square, accum_out=ssx[:, 2:3]),
                             r=[("u", 0)], w=["sq", "ssl"])
                    self.act(lambda e: e.activation(out=sq[:, 384:640], in_=u[:, 384:640], func=AF.Square, accum_out=ssx[:, 3:4]),
                             r=[("u", 0), ("u", 1)], w=["sq", "ssl2"])
                    if self.lim < 2.2:
                        continue
                    self.rstd(ssx[:, 2:3], ssx[:, 2:3], 384, "ssl", "ssl")
                    self.rstd(ssx[:, 3:4], ssx[:, 3:4], 256, "ssl2", "ssl2")
                    if self.lim < 2.4:
                        continue
                    self.dve(lambda e: e.tensor_scalar(out=latb[:, 0:384], in0=u[:, 0:384], scalar1=ssx[:, 2:3], scalar2=None, op0=ALU.mult),
                             r=[("u", 0), "ssl"], w=["latb0"])
                    self.dve(lambda e: e.tensor_scalar(out=latb[:, 384:640], in0=u[:, 384:640], scalar1=ssx[:, 3:4], scalar2=None, op0=ALU.mult),
                             r=[("u", 0), ("u", 1), "ssl2"], w=["latb1"])
                    if self.lim < 2.6:
                        continue
                    ps = pstr[pslot % 2]
                    rn = ("pstr", pslot % 2)
                    pslot += 1

                    def tpl(e, ps=ps):
                        for j in range(5):
                            ins = e.transpose(ps[:, j, :], latb[:, j * 128:(j + 1) * 128], self.identb[:])
                        return ins
                    self.pe(tpl, r=["latb0", "latb1", "identb"], w=[rn])
                    if self.lim < 2.8:
                        continue
                    self.dve(lambda e, ps=ps: e.tensor_copy(out=latTt[:], in_=ps[:, 0:5, :]), r=[rn], w=["latTt"])
                    if self.lim < 3:
                        continue
                    for gi, (c0, c1) in enumerate(((0, 512), (512, 768))):
                        def mmq(e, c0=c0, c1=c1, gi=gi):
                            for j in range(3):
                                ins = e.matmul(psup[gi][:, 0:c1 - c0], lhsT=latTt[:, j, :], rhs=w_uq[:, j, c0:c1], start=(j == 0), stop=(j == 2))
                            return ins
                        self.pe(mmq, r=["latTt", "w_uq"], w=[("psup", gi)])
                        self.act(lambda e, c0=c0, c1=c1, gi=gi: e.activation(out=qm[:, c0:c1], in_=psup[gi][:, 0:c1 - c0], func=AF.Copy),
                                 r=[("psup", gi)], w=["qm"] if gi == 0 else [], a=[] if gi == 0 else ["qm"])
                    kf3 = kf_[:].rearrange("p (h d) -> p h d", h=8)
                    vb3 = vbf[:, 0:512].rearrange("p (h d) -> p h d", h=8)
                    if self.lim < 3.2:
                        continue
                    for gi in range(2):
                        def mmk(e, gi=gi):
                            for j in range(2):
                                ins = e.matmul(psup[gi][:], lhsT=latTt[:, 3 + j, :], rhs=w_ukv[:, j, gi * 512:(gi + 1) * 512], start=(j == 0), stop=(j == 1))
                            return ins
                        self.pe(mmk, r=["latTt", "w_ukv"], w=[("psup", gi)])
                        p3 = psup[gi][:].rearrange("p (h d) -> p h d", h=4)
                        self.act(lambda e, gi=gi, p3=p3: e.activation(out=kf3[:, gi * 4:gi * 4 + 4, 0:64], in_=p3[:, :, 0:64], func=AF.Copy),
                                 r=[("psup", gi)], w=["kfull"] if gi == 0 else [], a=[] if gi == 0 else ["kfull"])
                        if self.lim < 3.3:
                            continue
                        self.act(lambda e, gi=gi, p3=p3: e.activation(out=vb3[:, gi * 4:gi * 4 + 4, :], in_=p3[:, :, 64:128], func=AF.Copy),
                                 r=[("psup", gi)], w=["vbf"] if gi == 0 else [], a=[] if gi == 0 else ["vbf"])
                    if self.lim < 3.4:
                        continue
                    self.dve(lambda e: e.tensor_copy(out=kf3[:, :, 64:96], in_=u[:, 640:672].unsqueeze(1).to_broadcast([128, 8, 32])),
                             r=[("u", 1)], a=["kfull"])
                    if self.lim < 4:
                        continue
                    self.head_post("qm", qm[:], 8, 96, 0, 64, 16, t, sq, ssh, rt, qbf[:, 0:768], "qbf0")
                    if self.lim < 5:
                        continue
                    pslot = self.tr_store("qbf0", qbf[:, 0:768], 8, 96, pstr, pslot, stg[0], "stg0", self.qT_mla, t)
                    if self.lim < 6:
                        continue
                    self.head_post("kfull", kf_[:], 8, 96, 96, 64, 16, t, sq, ssh, rt, qbf[:, 768:1536], "qbf1")
                    pslot = self.tr_store("qbf1", qbf[:, 768:1536], 8, 96, pstr, pslot, stg[1], "stg1", self.kT_mla, t)
                    self.ldc(self.v_mla[t * 128:(t + 1) * 128, :], vbf[:, 0:512], r=["vbf"], a=["v_mla"])
                    self.S.add("dve", lambda e: e.tensor_copy(out=qm[:], in_=u[:, 672:1440]), r=uall, w=["qm"])
                    self.head_post("qm", qm[:], 12, 64, 192, 0, 32, t, sq, ssh, rt, qbf[:, 0:768], "qbf0")
                    pslot = self.tr_store("qbf0", qbf[:, 0:768], 12, 64, pstr, pslot, stg[2], "stg2", self.qT_dil, t)
                    self.S.add("dve", lambda e: e.tensor_copy(out=kf_[:], in_=u[:, 1440:2208]), r=uall, w=["kfull"])
                    self.head_post("kfull", kf_[:], 12, 64, 256, 0, 32, t, sq, ssh, rt, qbf[:, 768:1536], "qbf1")
                    pslot = self.tr_store("qbf1", qbf[:, 768:1536], 12, 64, pstr, pslot, stg[3], "stg3", self.kT_dil, t)
                    self.act(lambda e: e.activation(out=vbf[:, 0:768], in_=u[:, 2208:2976], func=AF.Copy), r=uall, w=["vbf"])
                    self.ldc(self.v_dil[t * 128:(t + 1) * 128, :], vbf[:, 0:768], r=["vbf"], a=["v_dil"])
                else:
                    uall = [("u", gi) for gi in range(6)]
                    self.head_post(("u", 0), u[:, 0:512], 8, 64, 320, 0, 32, t, sq, ssh, rt, qbf[:, 0:512], "qbfA")
                    pslot = self.tr_store("qbfA", qbf[:, 0:512], 8, 64, pstr, pslot, stg[0], "stg0", self.qT_df, t)
                    self.head_post(("u", 1), u[:, 512:1024], 8, 64, 384, 0, 32, t, sq, ssh, rt, qbf[:, 512:1024], "qbfB")
                    pslot = self.tr_store("qbfB", qbf[:, 512:1024], 8, 64, pstr, pslot, stg[1], "stg1", self.kT_df, t)
                    self.act(lambda e: e.activation(out=vbf[:, 0:512], in_=u[:, 1024:1536], func=AF.Copy), r=uall, w=["vbf"])
                    self.ldc(self.v_df[t * 128:(t + 1) * 128, :], vbf[:, 0:512], r=["vbf"], a=["v_df"])
                    self.act(lambda e: e.activation(out=vbf[:, 512:1024], in_=u[:, 2560:3072], func=AF.Copy), r=uall, w=["vbf2"])
                    self.ldc(self.v_mb[t * 128:(t + 1) * 128, :], vbf[:, 512:1024], r=["vbf2"], a=["v_mb"])
                    self.head_post(("u", 4), u[:, 2048:2560], 8, 64, 512, 0, 32, t, sq, ssh, rt, qbf[:, 1024:1536], "qbfC")
                    pslot = self.tr_store("qbfC", qbf[:, 1024:1536], 8, 64, pstr, pslot, stg[2], "stg2", self.kT_mb, t)
                    nblk = t // 2
                    self.dve(lambda e: e.tensor_reduce(out=kpart[:], in_=stg[2][0:64, 0:8, :], axis=AX.X, op=ALU.add),
                             r=[("stg2", 0)], w=["kpart"])
                    self.dve(lambda e, nblk=nblk: e.tensor_tensor(out=kmacc[:, :, nblk], in0=kmacc[:, :, nblk], in1=kpart[:], op=ALU.add),
                             r=["kpart", "kmacc"], w=["kmacc"])
                    if t % 2 == 1:
                        self.act(lambda e, nblk=nblk: e.activation(out=kmb[:, :, nblk], in_=kmacc[:, :, nblk], func=AF.Copy, scale=1.0 / 256),
                                 r=["kmacc"], a=["kmb"])
                    self.head_post(("u", 3), u[:, 1536:2048], 8, 64, 448, 0, 32, t, sq, ssh, rt, qbf[:, 1536:2048], "qbfD")
                    pslot = self.tr_store("qbfD", qbf[:, 1536:2048], 8, 64, pstr, pslot, stg[3], "stg3", self.qT_mb, t)
                    if nblk > 0:
                        def mmg(e, nblk=nblk):
                            for h in range(8):
                                ins = e.matmul(psg[:, h, 0:nblk], lhsT=stg[3][0:64, h, :], rhs=kmb[:, h, 0:nblk], start=True, stop=True)
                            return ins
                        self.pe(mmg, r=[("stg3", 0), "kmb"], w=["psg"])
                        self.dve(lambda e, nblk=nblk: e.tensor_copy(out=gsb[:, :, 0:nblk], in_=psg[:, :, 0:nblk]), r=["psg"], w=["gsb"])

                        def mx(e):
                            for h in range(8):
                                ins = e.max(out=top8[:, h, :], in_=gsb[:, h, :])
                            return ins
                        self.dve(mx, r=["gsb"], w=["top8"])
                        self.dve(lambda e: e.tensor_tensor(out=biasf[:], in0=gsb[:], in1=top8[:, :, 2:3].to_broadcast([128, 8, 32]), op=ALU.is_lt),
                                 r=["gsb", "top8"], w=["biasf"])
                        self.dve(lambda e: e.tensor_scalar(out=biasb[:], in0=biasf[:], scalar1=NEGB, scalar2=None, op0=ALU.mult),
                                 r=["biasf"], w=["biasb"])
                    else:
                        self.dve(lambda e: e.memset(biasb[:], NEGB), w=["biasb"])
                    self.dve(lambda e, nblk=nblk: e.memset(biasb[:, :, nblk:nblk + 1], 0.0), r=["biasb"], w=["biasb"])
                    ps = pstr[pslot % 2]
                    rn = ("pstr", pslot % 2)
                    pslot += 1

                    def tpb(e, ps=ps):
                        for h in range(8):
                            ins = e.transpose(ps[0:32, h, :], biasb[:, h, :], self.identb[:])
                        return ins
                    self.pe(tpb, r=["biasb", "identb"], w=[rn])
                    self.dve(lambda e, ps=ps: e.tensor_copy(out=stg[0][0:32, 0:8, :], in_=ps[0:32, 0:8, :]), r=[rn], w=[("stg0", 0)])
                    dst = self.qT_mb[:, 64:96, t * 128:(t + 1) * 128].rearrange("h d s -> d h s")
                    self.ldc(dst, stg[0][0:32, 0:8, :], r=[("stg0", 0)], a=["qT_mb_bias"])
            self.S.flush()

    def attn_tiles(self, tiles, ps_s, P, po, po_nm, sc, cnt, hook=None, hook_at=5, acc=None, acc_nm=None, vrows=65):
        n = len(tiles)
        ns = len(ps_s)
        npb = P.shape[1]
        LA = min(ns - 1, 3)
        for i in range(n + LA):
            if i < n:
                kap, qap, q0, N, mk, vs = tiles[i][:6]
                si = (cnt + i) % ns
                pi = (cnt + i) % npb
                self.pe(lambda e, kap=kap, qap=qap, N=N, si=si: e.matmul(ps_s[si][:, 0:N], lhsT=kap, rhs=qap, start=True, stop=True),
                        r=tiles[i][6], w=[("ps_s", si)])
                self.act(lambda e, N=N, si=si, pi=pi: e.activation(out=P[:, pi, 0:N], in_=ps_s[si][:, 0:N], func=AF.Exp, scale=sc),
                         r=[("ps_s", si)], w=[("P", pi)])
                if mk is not None:
                    self.dve(lambda e, N=N, pi=pi, mk=mk: e.tensor_tensor(out=P[:, pi, 0:N], in0=P[:, pi, 0:N], in1=mk, op=ALU.mult),
                             r=[("P", pi), "masks"], w=[("P", pi)])
                if acc is not None:
                    if i == 0:
                        self.dve(lambda e, N=N, pi=pi, q0=q0: e.tensor_copy(out=acc[:, q0:512], in_=P[:, pi, 0:N]), r=[("P", pi)], w=[acc_nm])
                    else:
                        self.dve(lambda e, N=N, pi=pi, q0=q0: e.tensor_tensor(out=acc[:, q0:512], in0=acc[:, q0:512], in1=P[:, pi, 0:N], op=ALU.add),
                                 r=[("P", pi), acc_nm], w=[acc_nm])
            if hook is not None and i == min(hook_at, n + LA - 1):
                hook()
                hook = None
            j = i - LA
            if j >= 0:
                kap, qap, q0, N, mk, vs = tiles[j][:6]
                pi = (cnt + j) % npb
                for f, vap in enumerate(vs):
                    self.pe(lambda e, f=f, vap=vap, q0=q0, N=N, pi=pi, j=j: e.matmul(po[f][0:vrows, q0:512], lhsT=vap, rhs=P[:, pi, 0:N],
                                                                                    start=(j == 0), stop=(j == n - 1)),
                            r=[("P", pi)] + tiles[j][7], w=[po_nm[f]] if j == 0 else [], a=[] if j == 0 else [po_nm[f]])
        if hook is not None:
            hook()
        return cnt + n

    def attn_norm(self, po, po_nm, ep, slot, dst, dst_nm):
        rrow, osb, ps_bc = ep
        self.act(lambda e: e.activation(out=rrow[64:65, slot, :], in_=po[64:65, :], func=AF.Ln), r=[po_nm], w=[("rrow", slot)])
        self.act(lambda e: e.activation(out=rrow[64:65, slot, :], in_=rrow[64:65, slot, :], func=AF.Exp, scale=-1.0), r=[("rrow", slot)], w=[("rrow", slot)])
        self.pe(lambda e: e.matmul(ps_bc[0:64, :], lhsT=self.onesb[64:65, 0:64], rhs=rrow[64:65, slot, :], start=True, stop=True),
                r=[("rrow", slot), "onesb"], w=["ps_bc"])
        self.act(lambda e: e.activation(out=osb[0:64, slot, :], in_=po[0:64, :], func=AF.Copy), r=[po_nm], w=[("osb", slot)])
        self.dve(lambda e: e.tensor_tensor(out=dst, in0=osb[0:64, slot, :], in1=ps_bc[0:64, :], op=ALU.mult),
                 r=[("osb", slot), "ps_bc"], w=[dst_nm])

    def phaseB(self, kind):
        sb = self.sb
        nch = max(1, self.nt // 4)
        ncols = nch * 512
        dil = kind == "dil"
        nheads = {"mla": 8, "dil": 4, "diff": 4, "moba": 8}[kind]
        nm = 2 if kind == "diff" else 1
        parts = 1
        isdf = kind == "diff"
        dk = {"mla": 96, "dil": 64, "diff": 64, "moba": 96}[kind]
        sc = float({"mla": 96 ** -0.5, "dil": 0.125, "diff": 0.125, "moba": 0.125}[kind])
        qsrc = {"mla": self.qT_mla, "dil": self.qT_dil, "diff": self.qT_df, "moba": self.qT_mb}[kind]
        ksrc = {"mla": self.kT_mla, "dil": self.kT_dil, "diff": self.kT_df, "moba": self.kT_mb}[kind]
        vsrc = {"mla": self.v_mla, "dil": self.v_dil, "diff": self.v_df, "moba": self.v_mb}[kind]
        row_base = {"mla": 0, "dil": 512, "diff": 0, "moba": 512}[kind]
        lam_init = 0.8 - 0.6 * math.exp(-0.3 * 1)
        with ExitStack() as es:
            P_dummy = sb(es, "Pdummy", [128, 2], F32)
            if dil:
                qT = sb(es, "qTd", [64, 2, 3, 512], BF16)
                kT = sb(es, "kTd", [64, 3, SEQ], BF16)
                V = sb(es, "Vd", [128, 3, 64, 65], BF16)
                dmb = sb(es, "dmb", [128, 33, 512], BF16)
                self.ldc(dmb[:], self.dmask.rearrange("p (a b) -> p a b", a=33), w=["masks"])
                self.pool(lambda e: e.memset(V[:, :, :, 64:65], 1.0), a=["Vones"])
            else:
                qT = sb(es, "qTa", [96, 2, SEQ], BF16)
                kT = sb(es, "kTa", [96, 2, SEQ], BF16)
                vw = 128 if isdf else 65
                V = sb(es, "Va", [128, 2, parts, 64, vw], BF16)
                if isdf:
                    self.pool(lambda e: e.memset(P_dummy[:], 0.0), a=["Vones"])
                else:
                    self.pool(lambda e: e.memset(V[:, :, :, :, 64:65], 1.0), a=["Vones"])
                if kind == "moba":
                    for sl in range(2):
                        self.ldc(kT[64:96, sl, :], self.kind, a=["Vones"])
            n_po = 2
            n_s = 7 - n_po
            P = sb(es, "Pt", [128, n_s + 2, 512], BF16)
            rrow = sb(es, "rrow", [65, 2, 512], F32)
            osb = sb(es, "osb", [64, 2, 512], F32)
            onb = sb(es, "onb", [64, 4, 512], BF16)
            onbd = sb(es, "onbd", [128, 2, 512], BF16)
            ps_s = [self.ps(es, "ps_s%d" % i, [128, 512], F32) for i in range(n_s)]
            po = [self.ps(es, "po%d" % i, [128, 512], F32) for i in range(n_po)]
            ps_bc = self.ps(es, "ps_bc", [128, 512], F32)
            ep = (rrow, osb, ps_bc)
            if isdf:
                acc = sb(es, "acc", [128, 2, 512], F32)
                rcp = sb(es, "rcp", [128, 512], F32)
                nrm = sb(es, "nrm", [128, 2, 512], F32)
                dd = sb(es, "dd", [128, 512], F32)
                sqd = sb(es, "sqd", [128, 512], F32)
                rsd = sb(es, "rsd", [128, 512], F32)
                onesf = sb(es, "onesfB", [128, 128], F32)
                lamt = sb(es, "lamt", [128, 256], F32)
                lsm = sb(es, "lsm", [128, 8], F32)
                subs = sb(es, "subs", [128, 1], F32)
                self.pool(lambda e: e.memset(onesf[:], 1.0), w=["onesfB"])
                self.ld(lamt[:], self.dlam.partition_broadcast(128), w=["lamt"])
                self.ld(subs[:], self.subT, w=["subs"])
                self.dve(lambda e: e.tensor_tensor(out=lamt[:, 0:64], in0=lamt[:, 0:64], in1=lamt[:, 64:128], op=ALU.mult), r=["lamt"], w=["lamt"])
                self.dve(lambda e: e.tensor_tensor(out=lamt[:, 128:192], in0=lamt[:, 128:192], in1=lamt[:, 192:256], op=ALU.mult), r=["lamt"], w=["lamt"])
                self.dve(lambda e: e.tensor_reduce(out=lsm[:, 0:1], in_=lamt[:, 0:64], axis=AX.X, op=ALU.add), r=["lamt"], w=["lsm0"])
                self.dve(lambda e: e.tensor_reduce(out=lsm[:, 1:2], in_=lamt[:, 128:192], axis=AX.X, op=ALU.add), r=["lamt"], w=["lsm1"])
                self.act(lambda e: e.activation(out=lsm[:, 2:4], in_=lsm[:, 0:2], func=AF.Exp), r=["lsm0", "lsm1"], w=["lsm2"])
                self.dve(lambda e: e.tensor_tensor(out=lsm[:, 4:5], in0=lsm[:, 3:4], in1=lsm[:, 2:3], op=ALU.subtract), r=["lsm2"], w=["lsm4"])
                self.dve(lambda e: e.tensor_scalar(out=lsm[:, 5:6], in0=lsm[:, 4:5], scalar1=-lam_init, scalar2=None, op0=ALU.add), r=["lsm4"], w=["neglam"])
                self.dve(lambda e: e.tensor_scalar(out=subs[:], in0=subs[:], scalar1=1.0 - lam_init, scalar2=None, op0=ALU.mult), r=["subs"], w=["subs"])
            cnt = 0
            ecnt = 0
            pending = [None]
            for h in range(nheads):
                vsl = h % 2
                if dil:
                    for g in range(3):
                        gh = g * 4 + h
                        self.ld(kT[:, g, 0:ncols], ksrc[gh, :, 0:ncols], w=[("kT", g)])
                        self.ld(V[:, g, 0:nch * 4, 0:64], vsrc[0:ncols, gh * 64:gh * 64 + 64].rearrange("(t p) d -> p t d", p=128),
                                r=["Vones"], w=[("V", g)])
                else:
                    for f in range(parts):
                        vd = 128 if isdf else 64
                        c0 = h * vd
                        self.ld(V[:, vsl, f, 0:nch * 4, 0:vd], vsrc[0:ncols, c0:c0 + vd].rearrange("(t p) d -> p t d", p=128),
                                r=["Vones"], w=[("V", vsl, f)])
                    for m in range(nm):
                        u_ = h * nm + m
                        sl = u_ % 2
                        dq = 96 if kind in ("mla", "moba") else 64
                        dkk = 96 if kind == "mla" else 64
                        self.ld(qT[0:dq, sl, 0:ncols], qsrc[u_, 0:dq, 0:ncols], w=[("qT", sl)])
                        self.ld(kT[0:dkk, sl, 0:ncols], ksrc[u_, 0:dkk, 0:ncols], r=["Vones"], w=[("kT", sl)])
                for c in range(nch):
                    if dil:
                        qs = c % 2
                        for g in range(3):
                            self.ld(qT[:, qs, g, :], qsrc[g * 4 + h, :, c * 512:(c + 1) * 512], w=[("qT", qs, g)])
                    for m in range(nm):
                        u_ = h * nm + m
                        sl = u_ % 2
                        tiles = []
                        if dil:
                            mo = 0
                            for g, W in enumerate((1, 4, 16)):
                                for o in range(W + 4):
                                    kt = 4 * c - W + o
                                    if kt >= 0:
                                        tiles.append((kT[:, g, kt * 128:(kt + 1) * 128], qT[:, qs, g, :], 0, 512, dmb[:, mo + o, :],
                                                      [V[:, g, kt, :]], [("kT", g), ("qT", qs, g)], [("V", g)]))
                                mo += W + 4
                        else:
                            for kt in range(4 * c + 4):
                                j = kt - 4 * c
                                if j < 0:
                                    q0, mk = 0, None
                                else:
                                    q0, mk = 128 * j, self.cmb[:, j, 128 * j:512]
                                N = 512 - q0
                                tiles.append((kT[0:dk, sl, kt * 128:(kt + 1) * 128], qT[0:dk, sl, c * 512 + q0:(c + 1) * 512], q0, N, mk,
                                              [V[:, vsl, f, kt, :] for f in range(parts)], [("kT", sl), ("qT", sl)],
                                              [("V", vsl, f) for f in range(parts)]))
                        pidx = [ecnt % 2]
                        pos_ = [po[i] for i in pidx]
                        po_nm = [("po", i) for i in pidx]
                        if isdf:
                            cnt = self.attn_tiles(tiles, ps_s, P, pos_, po_nm, sc, cnt, hook=pending[0], acc=acc[:, m, :], acc_nm=("acc", m), vrows=128)
                        else:
                            cnt = self.attn_tiles(tiles, ps_s, P, pos_, po_nm, sc, cnt, hook=pending[0])

                        def epilogue(pos_=pos_, po_nm=po_nm, ecnt0=ecnt, m=m, c=c, h=h):
                            if not isdf:
                                es_ = ecnt0 % 2
                                osl = ecnt0 % 4
                                self.attn_norm(pos_[0], po_nm[0], ep, es_, onb[:, osl, :], ("onb", osl))
                                r0 = row_base + h * 64
                                self.ldc(self.oT[r0:r0 + 64, c * 512:(c + 1) * 512], onb[:, osl, :], r=[("onb", osl)], a=["oT"])
                                return
                            self.pe(lambda e: e.matmul(ps_bc[:, :], lhsT=onesf[:], rhs=acc[:, m, :], start=True, stop=True), r=[("acc", m), "onesfB"], w=["ps_bc"])
                            self.act(lambda e: e.activation(out=rcp[:], in_=ps_bc[:, :], func=AF.Ln), r=["ps_bc"], w=["rcp"])
                            self.act(lambda e: e.activation(out=rcp[:], in_=rcp[:], func=AF.Exp, scale=-1.0), r=["rcp"], w=["rcp"])
                            self.dve(lambda e: e.tensor_tensor(out=nrm[:, m, :], in0=pos_[0][:, :], in1=rcp[:], op=ALU.mult), r=[po_nm[0], "rcp"], w=[("nrm", m)])
                            if m == 1:
                                self.dve(lambda e: e.scalar_tensor_tensor(out=dd[:], in0=nrm[:, 1, :], scalar=lsm[:, 5:6], in1=nrm[:, 0, :], op0=ALU.mult, op1=ALU.add),
                                         r=[("nrm", 0), ("nrm", 1), "neglam"], w=["dd"])
                                self.act(lambda e: e.activation(out=sqd[:], in_=dd[:], func=AF.Square), r=["dd"], w=["sqd"])
                                self.pe(lambda e: e.matmul(ps_bc[:, :], lhsT=onesf[:], rhs=sqd[:], start=True, stop=True), r=["sqd", "onesfB"], w=["ps_bc"])
                                self.act(lambda e: e.activation(out=rsd[:], in_=ps_bc[:, :], func=AF.Ln, scale=1.0 / 128, bias=self.epsT[:, 0:1]), r=["ps_bc"], w=["rsd"])
                                self.act(lambda e: e.activation(out=rsd[:], in_=rsd[:], func=AF.Exp, scale=-0.5), r=["rsd"], w=["rsd"])
                                osl = c % 2
                                ob = onbd[:, osl, :]
                                self.dve(lambda e, ob=ob: e.scalar_tensor_tensor(out=ob, in0=dd[:], scalar=subs[:, 0:1], in1=rsd[:], op0=ALU.mult, op1=ALU.mult),
                                         r=["dd", "rsd", "subs"], w=[("onbd", osl)])
                                self.ldc(self.oT[h * 128:(h + 1) * 128, c * 512:(c + 1) * 512], ob, r=[("onbd", osl)], a=["oT"])
                        pending[0] = epilogue
                        ecnt += parts
            if pending[0] is not None:
                pending[0]()
            self.S.flush()

    def phaseC1(self, layer):
        sb = self.sb
        nk = 6 if layer == 0 else 8
        xin = self.x if layer == 0 else self.xL
        wsrc = self.e_w_out if layer == 0 else self.o_w_out
        with ExitStack() as es:
            w_out = sb(es, "w_out", [128, nk, D], BF16)
            G1 = sb(es, "G1", [128, D], F32)
            xt = sb(es, "xtc", [128, 2, D], F32)
            oTt = sb(es, "oTt", [128, 2, nk, 128], BF16)
            tmpf = sb(es, "tmpfc", [128, D], F32)
            sq = sb(es, "sqc", [128, D], F32)
            ssx = sb(es, "ssxc", [128, 2], F32)
            xs = sb(es, "xsc", [128, D], BF16)
            hT = sb(es, "hTc", [128, 2, 8, 128], BF16)
            psy = [self.ps(es, "psy%d" % i, [128, 512], F32) for i in range(2)]
            pT = self.ps(es, "pTc", [128, 8, 128], BF16)
            self.load_w(w_out, wsrc, nk * 128, "w_out")
            self.ld(G1[:], self.Gd[:, (layer * 2) * D:(layer * 2 + 1) * D], w=["G1"])
            for t in range(self.nt):
                sl = t % 2
                self.ld(xt[:, sl], xin[t * 128:(t + 1) * 128, :], w=[("xt", sl)])
                self.ld(oTt[:, sl], self.oT[0:nk * 128, t * 128:(t + 1) * 128].rearrange("(k p) s -> p k s", p=128), w=[("oTt", sl)])
                for hf in range(2):
                    def mm(e, hf=hf, sl=sl):
                        for k in range(nk):
                            ins = e.matmul(psy[hf][:], lhsT=oTt[:, sl, k, :], rhs=w_out[:, k, hf * 512:(hf + 1) * 512], start=(k == 0), stop=(k == nk - 1))
                        return ins
                    self.pe(mm, r=[("oTt", sl), "w_out"], w=[("psy", hf)])
                    self.dve(lambda e, hf=hf: e.tensor_tensor(out=tmpf[:, hf * 512:(hf + 1) * 512], in0=psy[hf][:], in1=G1[:, hf * 512:(hf + 1) * 512], op=ALU.mult),
                             r=[("psy", hf), "G1"], w=[("tmpf", hf)])
                    self.dve(lambda e, hf=hf, sl=sl: e.tensor_tensor(out=xt[:, sl, hf * 512:(hf + 1) * 512], in0=xt[:, sl, hf * 512:(hf + 1) * 512],
                                                                    in1=tmpf[:, hf * 512:(hf + 1) * 512], op=ALU.add),
                             r=[("tmpf", hf), ("xt", sl)], w=[("xt", sl)])
                self.ldc(self.x1[t * 128:(t + 1) * 128, :], xt[:, sl], r=[("xt", sl)], a=["x1"])
                self.act(lambda e, sl=sl: e.activation(out=sq[:], in_=xt[:, sl], func=AF.Square, accum_out=ssx[:, 0:1]), r=[("xt", sl)], w=["sq", "ssx"])
                self.rstd(ssx[:, 0:1], ssx[:, 1:2], D, "ssx", "rsx")
                self.act(lambda e, sl=sl: e.activation(out=xs[:], in_=xt[:, sl], func=AF.Copy, scale=ssx[:, 1:2]), r=[("xt", sl), "rsx"], w=["xs"])

                def tpx(e):
                    for j in range(8):
                        ins = e.transpose(pT[:, j, :], xs[:, j * 128:(j + 1) * 128], self.identb[:])
                    return ins
                self.pe(tpx, r=["xs", "identb"], w=["pT"])
                ao = layer * 16 + 8
                a_bc = self.aT[:, ao:ao + 8].unsqueeze(2).to_broadcast([128, 8, 128])
                b_bc = self.modT[:, layer * 48 + 24:layer * 48 + 32].unsqueeze(2).to_broadcast([128, 8, 128])
                tm3 = tmpf[:].rearrange("p (j s) -> p j s", j=8)
                self.dve(lambda e, a_bc=a_bc, tm3=tm3: e.tensor_tensor(out=tm3, in0=pT[:], in1=a_bc, op=ALU.mult),
                         r=["pT", ("aT", ao)], w=[("tmpf", 0), ("tmpf", 1)])
                self.dve(lambda e, b_bc=b_bc, tm3=tm3, sl=sl: e.tensor_tensor(out=hT[:, sl], in0=tm3, in1=b_bc, op=ALU.add),
                         r=[("tmpf", 0), ("tmpf", 1), "modT"], w=[("hT", sl)])
                self.ldc(self.h2T[:, t * 128:(t + 1) * 128].rearrange("(k p) s -> p k s", p=128), hT[:, sl], r=[("hT", sl)], a=["h2T"])
            self.S.flush()

    def phaseC2(self, layer):
        sb = self.sb
        dst = self.xL if layer == 0 else self.out
        ng = max(1, self.nt // 2)
        with ExitStack() as es:
            w1 = sb(es, "w1", [128, 8, DFF], BF16)
            w2 = sb(es, "w2", [128, 32, D], BF16)
            G2 = sb(es, "G2", [128, D], F32)
            hTt = sb(es, "hTt", [128, 2, 8, 256], BF16)
            x1t = sb(es, "x1t", [128, 2, 2, D], F32)
            aT = sb(es, "aTt", [128, 32, 256], BF16)
            rl = sb(es, "rl", [128, 2, 512], F32)
            tmp = sb(es, "tmpc2", [128, 2, 512], F32)
            psa = [self.ps(es, "psa%d" % i, [128, 2, 256], F32) for i in range(2)]
            psy = [self.ps(es, "psy2_%d" % i, [128, 512], F32) for i in range(4)]
            self.load_w(w1, self.mlp_w1[layer], D, "w1")
            self.load_w(w2, self.mlp_w2[layer], DFF, "w2")
            self.ld(G2[:], self.Gd[:, (layer * 2 + 1) * D:(layer * 2 + 2) * D], w=["G2"])
            for g in range(ng):
                sl = g % 2
                self.ld(hTt[:, sl], self.h2T[:, g * 256:(g + 1) * 256].rearrange("(k p) s -> p k s", p=128), w=[("hTt", sl)])
                self.ld(x1t[:, sl], self.x1[g * 256:(g + 1) * 256, :].rearrange("(s p) d -> p s d", p=128), w=[("x1t", sl)])
                for fp in range(16):
                    pb = psa[fp % 2]

                    def mm1(e, fp=fp, pb=pb, sl=sl):
                        for ff in range(2):
                            f = fp * 2 + ff
                            for k in range(8):
                                ins = e.matmul(pb[:, ff, :], lhsT=w1[:, k, f * 128:(f + 1) * 128], rhs=hTt[:, sl, k, :], start=(k == 0), stop=(k == 7))
                        return ins
                    self.pe(mm1, r=["w1", ("hTt", sl)], w=[("psa", fp % 2)])
                    rs_ = fp % 2
                    self.act(lambda e, pb=pb, rs_=rs_: e.activation(out=rl[:, rs_, :], in_=pb[:].rearrange("p a b -> p (a b)"), func=AF.Relu),
                             r=[("psa", fp % 2)], w=[("rl", rs_)])
                    self.dve(lambda e, fp=fp, rs_=rs_: e.tensor_tensor(out=aT[:, fp * 2:fp * 2 + 2, :].rearrange("p a b -> p (a b)"), in0=rl[:, rs_, :], in1=rl[:, rs_, :], op=ALU.mult),
                             r=[("rl", rs_)], w=[("aT", fp)])
                for s_ in range(2):
                    for hf in range(2):
                        pi = s_ * 2 + hf

                        def mm2(e, s_=s_, hf=hf, pi=pi):
                            for f in range(32):
                                ins = e.matmul(psy[pi][:], lhsT=aT[:, f, s_ * 128:(s_ + 1) * 128], rhs=w2[:, f, hf * 512:(hf + 1) * 512], start=(f == 0), stop=(f == 31))
                            return ins
                        self.pe(mm2, r=["w2"] + [("aT", fp) for fp in range(16)], w=[("psy", pi)])
                        self.dve(lambda e, hf=hf, pi=pi: e.tensor_tensor(out=tmp[:, hf, :], in0=psy[pi][:], in1=G2[:, hf * 512:(hf + 1) * 512], op=ALU.mult),
                                 r=[("psy", pi), "G2"], w=[("tmp", hf)])
                        self.dve(lambda e, s_=s_, hf=hf, sl=sl: e.tensor_tensor(out=x1t[:, sl, s_, hf * 512:(hf + 1) * 512], in0=x1t[:, sl, s_, hf * 512:(hf + 1) * 512],
                                                                              in1=tmp[:, hf, :], op=ALU.add),
                                 r=[("tmp", hf), ("x1t", sl)], w=[("x1t", sl)])
                self.ldc(dst[g * 256:(g + 1) * 256, :].rearrange("(s p) d -> p s d", p=128), x1t[:, sl], r=[("x1t", sl)], a=["dst"])
            self.S.flush()


def _consts():
    ident = np.eye(128, dtype=np.float32)
    i16 = np.arange(16, dtype=np.float32) / np.float32(16)
    i32 = np.arange(32, dtype=np.float32) / np.float32(32)
    invf = np.concatenate([np.float32(10000.0) ** (-i16), np.float32(10000.0) ** (-i32)]).astype(np.float32)
    k = np.arange(128)[:, None]
    q = np.arange(512)[None, :]
    cmask = np.stack([(q >= 128 * j + k) for j in range(4)], axis=1).astype(np.float32)
    dm = []
    for (w, r) in ((128, 1), (512, 4), (2048, 16)):
        W = w // 128
        for o in range(W + 4):
            rel = q - k + 128 * (W - o)
            dm.append(((rel >= 0) & (rel <= w) & (rel % r == 0)).astype(np.float32))
    dmask = np.stack(dm, axis=1)
    kind = (np.arange(SEQ)[None, :] // 256 == np.arange(32)[:, None]).astype(np.float32)
    return dict(ident=ident, invf=invf, cmask=cmask.reshape(128, -1), dmask=dmask.reshape(128, -1), kind=kind)


def _colT(v, n):
    return np.ascontiguousarray(np.asarray(v, np.float32).reshape(n, 128).T)


def make_in_maps(inp, batches):
    f = lambda a: np.ascontiguousarray(np.asarray(a, dtype=np.float32))
    c = _consts()
    shared = dict(
        ada_w=f(inp["ada_w"]),
        ada_bT=np.ascontiguousarray(np.concatenate([_colT(inp["ada_b"][l], 48) for l in range(2)], axis=1)),
        nrmT=np.ascontiguousarray(np.concatenate(
            [_colT(inp[nm][l], 8) for l in range(2) for nm in ("norm_mix", "norm_mlp")], axis=1)),
        mlp_w1=f(inp["mlp_w1"]), mlp_w2=f(inp["mlp_w2"]),
        e_w_in=f(inp["even_w_in"][0]), e_w_out=f(inp["even_w_out"][0]),
        latT=np.ascontiguousarray(np.concatenate([_colT(inp["mla_q_lat_norm"][0], 3), _colT(inp["mla_kv_lat_norm"][0], 2)], axis=1)),
        w_uq=f(np.asarray(inp["mla_w_uq"][0]).reshape(384, 768)),
        w_ukv=f(np.asarray(inp["mla_w_ukv"][0]).reshape(256, 1024)),
        gains=f(np.concatenate([np.asarray(inp[k][0], np.float32).reshape(-1) for k in
                                ("mla_q_norm", "mla_k_norm", "dil_q_norm", "dil_k_norm",
                                 "diff_q_norm", "diff_k_norm", "moba_q_norm", "moba_k_norm")])),
        o_w_in=f(inp["odd_w_in"][0]), o_w_out=f(inp["odd_w_out"][0]),
        dlam=f(np.asarray(inp["diff_lambda"][0]).reshape(256)),
        subT=np.ascontiguousarray(np.asarray(inp["diff_subln"][0], np.float32).reshape(128, 1)),
        **c,
    )
    maps = []
    for b in batches:
        m = dict(shared)
        m["x"] = f(inp["x"][b])
        m["posT"] = np.ascontiguousarray(np.asarray(inp["positions"][b], np.int32).reshape(NT, 128).T)
        m["cT"] = _colT(inp["c"][b], 8)
        maps.append(m)
    return maps


def build(nt=NT, phases=None, dbg_out=()):
    phases = ALL_PHASES if phases is None else phases
    nc = bass.Bass("TRN2", target_bir_lowering=False)
    k = K(nc, nt=nt, phases=phases, dbg_out=dbg_out)
    k.declare()
    with ExitStack() as gs:
        k.es = gs
        sems = {}
        for e in Sched.COMPUTE:
            sems[e] = gs.enter_context(nc.semaphore("sem_" + e))
        for q in ("sp", "pq"):
            sems[q] = [gs.enter_context(nc.semaphore("sem_%s%d" % (q, i))) for i in range(k.S.ring)]
        k.S.init_emit(sems)
        k.setup()
        for ph in phases:
            getattr(k, "run_" + ph)()
        stats = k.S.finish()
    return nc, k, stats


def _add_phase_methods():
    K.run_A0 = lambda self: self.phaseA(0)
    K.run_A1 = lambda self: self.phaseA(1)
    K.run_Bmla = lambda self: self.phaseB("mla")
    K.run_C10 = lambda self: self.phaseC1(0)
    K.run_C20 = lambda self: self.phaseC2(0)
    K.run_C11 = lambda self: self.phaseC1(1)
    K.run_C21 = lambda self: self.phaseC2(1)
    K.run_Bdil = lambda self: self.phaseB("dil")
    K.run_Bdiff = lambda self: self.phaseB("diff")
    K.run_Bmoba = lambda self: self.phaseB("moba")


_add_phase_methods()


ALL_PHASES = ("A0", "Bmla", "Bdil", "C10", "C20", "A1", "Bdiff", "Bmoba", "C11", "C21")


def kernel(**inputs):
    nb = int(np.asarray(inputs["x"]).shape[0])
    nc, k, stats = build(nt=NT, phases=ALL_PHASES)
    maps = make_in_maps(inputs, list(range(nb)))
    res = run_bass_kernel_spmd(nc, maps, core_ids=list(range(nb)))
    return np.stack([np.asarray(res.results[b]["out"], dtype=np.float32) for b in range(nb)], axis=0)
```

```python
class _Op:
    __slots__ = ("eng", "fn", "deps", "needs_sig", "sem", "val", "is_dma", "idx", "ring_prev")

    def __init__(self, eng, fn, is_dma):
        self.eng = eng
        self.fn = fn
        self.deps = []
        self.needs_sig = False
        self.sem = None
        self.val = 0
        self.is_dma = is_dma
        self.ring_prev = None


class Sched:
    COMPUTE = ("pe", "act", "dve", "pool")

    def __init__(self, nc, ring=8, same_engine_sync=True):
        self.nc = nc
        self.ops = []
        self.last_w = {}
        self.readers = {}
        self.appenders = {}
        self.ring = ring
        self.same_engine_sync = same_engine_sync
        self.engobj = {"pe": nc.tensor, "act": nc.scalar, "dve": nc.vector, "pool": nc.gpsimd,
                       "sp": nc.sync, "pq": nc.gpsimd}
        self.stream = {"pe": "pe", "act": "act", "dve": "dve", "pool": "pool", "sp": "sp", "pq": "pool"}

    def add(self, eng, fn, r=(), w=(), a=()):
        is_dma = eng in ("sp", "pq")
        op = _Op(eng, fn, is_dma)
        deps = {}

        def dep(p, kind):
            if p is op:
                return
            same = (self.stream[p.eng] == self.stream[eng]) and not p.is_dma
            if same:
                if eng == "pe" or not self.same_engine_sync:
                    return
            deps[id(p)] = p

        for x in r:
            p = self.last_w.get(x)
            if p is not None:
                dep(p, "raw")
            for p in self.appenders.get(x, ()):
                dep(p, "raw")
        for x in list(w) + list(a):
            p = self.last_w.get(x)
            if p is not None:
                dep(p, "waw")
            for p in self.readers.get(x, ()):
                dep(p, "war")
        for x in w:
            for p in self.appenders.get(x, ()):
                dep(p, "waw")
        for x in r:
            self.readers.setdefault(x, []).append(op)
        for x in w:
            self.last_w[x] = op
            self.readers[x] = []
            self.appenders[x] = []
        for x in a:
            self.appenders.setdefault(x, []).append(op)
            self.readers[x] = []
        op.deps = list(deps.values())
        for p in op.deps:
            p.needs_sig = True
        self.ops.append(op)
        return op

    def init_emit(self, sems):
        self.sems = sems
        self.cnt = {e: 0 for e in self.COMPUTE}
        self.dcount = {"sp": 0, "pq": 0}
        self.dhist = {"sp": [], "pq": []}
        self.waited = {}
        self.nwaits = 0
        self.nops = 0
        self.barrier_deps = []

    def flush(self):
        ops = self.ops
        lastc = {}
        for op in ops:
            if not op.is_dma:
                lastc[op.eng] = op
        for op in lastc.values():
            op.needs_sig = True
        bd = self.barrier_deps
        first_seen = set()
        for op in ops:
            st = self.stream[op.eng]
            if st not in first_seen:
                first_seen.add(st)
                op.deps = op.deps + [p for p in bd if not (self.stream[p.eng] == st and not p.is_dma)]
            if op.is_dma:
                i = self.dcount[op.eng]
                self.dcount[op.eng] += 1
                op.sem = self.sems[op.eng][i % self.ring]
                op.val = 16 * (i // self.ring + 1)
                if i >= self.ring:
                    op.ring_prev = self.dhist[op.eng][i - self.ring]
                self.dhist[op.eng].append(op)
            elif op.needs_sig:
                self.cnt[op.eng] += 1
                op.sem = self.sems[op.eng]
                op.val = self.cnt[op.eng]
        waited = self.waited
        for op in ops:
            e = self.engobj[op.eng]
            st = self.stream[op.eng]
            need = {}
            plist = list(op.deps)
            if op.ring_prev is not None:
                plist.append(op.ring_prev)
            for p in plist:
                k = id(p.sem)
                if k not in need or need[k][1] < p.val:
                    need[k] = (p.sem, p.val)
            for k, (sem, val) in need.items():
                if waited.get((st, k), 0) < val:
                    e.wait_ge(sem, val)
                    waited[(st, k)] = val
                    self.nwaits += 1
            inst = op.fn(e)
            if op.is_dma:
                inst.then_inc(op.sem, 16)
            elif op.needs_sig:
                inst.then_inc(op.sem, 1)
        self.nops += len(ops)
        nb = list(lastc.values())
        for p in bd:
            if not p.is_dma and p.eng not in lastc:
                nb.append(p)
        for q in ("sp", "pq"):
            nb.extend(self.dhist[q][-self.ring:])
        self.barrier_deps = nb
        self.ops = []
        self.last_w = {}
        self.readers = {}
        self.appenders = {}

    def finish(self, eng="sp"):
        self.flush()
        e = self.engobj[eng]
        st = self.stream[eng]
        for p in self.barrier_deps:
            k = id(p.sem)
            if self.waited.get((st, k), 0) < p.val:
                e.wait_ge(p.sem, p.val)
                self.waited[(st, k)] = p.val
        return {"n_ops": self.nops, "n_waits": self.nwaits, "sig": dict(self.cnt), "dma": dict(self.dcount)}


import math
from contextlib import ExitStack
import numpy as np
import ml_dtypes
import concourse.bass as bass
import concourse.mybir as mybir
from concourse.bass_utils import run_bass_kernel_spmd

F32 = mybir.dt.float32
BF16 = mybir.dt.bfloat16
I32 = mybir.dt.int32
AF = mybir.ActivationFunctionType
ALU = mybir.AluOpType
AX = mybir.AxisListType

D = 1024
SEQ = 8192
NT = SEQ // 128
NCH = SEQ // 512
EPS = 1e-6
EVEN_IN = 2976
ODD_IN = 3072
DFF = 4096
NEGB = -30000.0
TWO_PI = float(2 * np.pi)


class K:
    def __init__(self, nc, nt=NT, phases=None, dbg_out=()):
        self.nc = nc
        self.S = Sched(nc)
        self.nt = nt
        self.phases = phases
        self.dbg_out = dbg_out
        self.es = ExitStack()
        self.din = {}
        self.dscr = {}
        import os as _os
        self.lim = float(_os.environ.get('KLIM', '99'))

    def inp(self, name, shape, dt=F32):
        t = self.nc.dram_tensor(name, list(shape), dt, kind="ExternalInput").ap()
        self.din[name] = t
        return t

    def scr(self, name, shape, dt):
        kind = "ExternalOutput" if name in self.dbg_out else "Internal"
        t = self.nc.dram_tensor(name, list(shape), dt, kind=kind).ap()
        self.dscr[name] = t
        return t

    def sb(self, es, name, shape, dt):
        self.uid = getattr(self, "uid", 0) + 1
        return es.enter_context(self.nc.sbuf_tensor("%s_%d" % (name, self.uid), list(shape), dt))

    def ps(self, es, name, shape, dt):
        self.uid = getattr(self, "uid", 0) + 1
        return es.enter_context(self.nc.psum_tensor("%s_%d" % (name, self.uid), list(shape), dt))

    def act(self, fn, r=(), w=(), a=()):
        return self.S.add("act", fn, r, w, a)

    def dve(self, fn, r=(), w=(), a=()):
        return self.S.add("dve", fn, r, w, a)

    def pool(self, fn, r=(), w=(), a=()):
        return self.S.add("pool", fn, r, w, a)

    def pe(self, fn, r=(), w=(), a=()):
        return self.S.add("pe", fn, r, w, a)

    def ld(self, out, in_, r=(), w=(), a=(), q="sp"):
        return self.S.add(q, lambda e: e.dma_start(out=out, in_=in_), r, w, a)

    def ldc(self, out, in_, r=(), w=(), a=()):
        return self.S.add("pq", lambda e: e.dma_start(out=out, in_=in_), r, w, a)

    def rstd(self, ss, rs, n, rn_ss, rn_rs):
        self.act(lambda e: e.activation(out=rs, in_=ss, func=AF.Sqrt, scale=1.0 / n, bias=self.eps_ap(ss)),
                 r=[rn_ss], w=[rn_rs])
        self.dve(lambda e: e.reciprocal(out=rs, in_=rs), r=[rn_rs], w=[rn_rs])

    def eps_ap(self, like):
        p = like.shape[0]
        return self.epsT[0:p, 0:1]

    def declare(self):
        i = self.inp
        self.x = i("x", [SEQ, D])
        self.posT = i("posT", [128, NT], I32)
        self.cT = i("cT", [128, 8])
        self.ada_w = i("ada_w", [2, D, 6 * D])
        self.ada_bT = i("ada_bT", [128, 96])
        self.nrmT = i("nrmT", [128, 32])
        self.mlp_w1 = i("mlp_w1", [2, D, DFF])
        self.mlp_w2 = i("mlp_w2", [2, DFF, D])
        self.e_w_in = i("e_w_in", [D, EVEN_IN])
        self.e_w_out = i("e_w_out", [768, D])
        self.latT = i("latT", [128, 5])
        self.w_uq = i("w_uq", [384, 768])
        self.w_ukv = i("w_ukv", [256, 1024])
        self.gains = i("gains", [576])
        self.o_w_in = i("o_w_in", [D, ODD_IN])
        self.o_w_out = i("o_w_out", [D, D])
        self.dlam = i("dlam", [256])
        self.subT = i("subT", [128, 1])
        self.ident = i("ident", [128, 128])
        self.invf = i("invf", [48])
        self.cmask = i("cmask", [128, 4 * 512])
        self.dmask = i("dmask", [128, 33 * 512])
        self.kind = i("kind", [32, SEQ])
        self.out = self.nc.dram_tensor("out", [SEQ, D], F32, kind="ExternalOutput").ap()
        s = self.scr
        self.trigd = s("trigd", [128, 2 * NT * 48], F32)
        self.Gd = s("Gd", [128, 4 * D], F32)
        self.x1 = s("x1", [SEQ, D], F32)
        self.xL = s("xL", [SEQ, D], F32)
        self.h2T = s("h2T", [D, SEQ], BF16)
        self.oT = s("oT", [D, SEQ], BF16)
        self.qT_mla = s("qT_mla", [8, 96, SEQ], BF16)
        self.kT_mla = s("kT_mla", [8, 96, SEQ], BF16)
        self.v_mla = s("v_mla", [SEQ, 512], BF16)
        self.qT_dil = s("qT_dil", [12, 64, SEQ], BF16)
        self.kT_dil = s("kT_dil", [12, 64, SEQ], BF16)
        self.v_dil = s("v_dil", [SEQ, 768], BF16)
        self.qT_df = s("qT_df", [8, 64, SEQ], BF16)
        self.kT_df = s("kT_df", [8, 64, SEQ], BF16)
        self.v_df = s("v_df", [SEQ, 512], BF16)
        self.qT_mb = s("qT_mb", [8, 96, SEQ], BF16)
        self.kT_mb = s("kT_mb", [8, 64, SEQ], BF16)
        self.v_mb = s("v_mb", [SEQ, 512], BF16)

    def setup(self):
        nc, S = self.nc, self.S
        g = self.es
        sb = self.sb
        self.epsT = sb(g, "epsT", [128, 1], F32)
        self.identb = sb(g, "identb", [128, 128], BF16)
        self.modT = sb(g, "modT", [128, 96], F32)
        self.aT = sb(g, "aT", [128, 32], F32)
        self.gbc = sb(g, "gbc", [128, 576], F32)
        self.cmb = sb(g, "cmb", [128, 4, 512], BF16)
        self.onesb = sb(g, "onesb", [128, 64], F32)
        self.pool(lambda e: e.memset(self.epsT[:], EPS), w=["epsT"])
        self.pool(lambda e: e.memset(self.onesb[:], 1.0), w=["onesb"])
        self.ldc(self.identb[:], self.ident, w=["identb"])
        self.ldc(self.cmb[:], self.cmask.rearrange("p (a b) -> p a b", a=4), w=["cmb"])
        self.ld(self.gbc[:], self.gains.partition_broadcast(128), w=["gbc"])
        with ExitStack() as es:
            self.trig = sb(es, "trig", [128, 2, NT, 48], F32)
            self.G = sb(es, "G", [128, 4, D], F32)
            condT = sb(es, "condT", [128, 8], F32)
            bT = sb(es, "bT", [128, 96], F32)
            nT = sb(es, "nT", [128, 32], F32)
            stage = sb(es, "adastage", [128, 2, 8, 1024], F32)
            posi = sb(es, "posi", [128, NT], I32)
            posf = sb(es, "posf", [128, NT], F32)
            invb = sb(es, "invb", [128, 48], F32)
            kf = sb(es, "kf", [128, 2, NT, 48], F32)
            ki = sb(es, "ki", [128, 2, NT, 48], I32)
            psmod = self.ps(es, "psmod", [128, 96], F32)
            self.ld(condT[:], self.cT, w=["condT"])
            self.ld(bT[:], self.ada_bT, w=["bT"])
            self.ld(nT[:], self.nrmT, w=["nT"])
            self.ld(posi[:], self.posT, w=["posi"])
            self.ld(invb[:], self.invf.partition_broadcast(128), w=["invb"])
            self.act(lambda e: e.activation(out=condT[:], in_=condT[:], func=AF.Silu), r=["condT"], w=["condT"])
            tr = self.trig
            self.dve(lambda e: e.tensor_copy(out=posf[:], in_=posi[:]), r=["posi"], w=["posf"])
            self.dve(lambda e: e.tensor_tensor(out=tr[:, 0], in0=posf[:].unsqueeze(2).to_broadcast([128, NT, 48]),
                                               in1=invb[:].unsqueeze(1).to_broadcast([128, NT, 48]), op=ALU.mult),
                     r=["posf", "invb"], w=["trig"])
            self.dve(lambda e: e.tensor_scalar(out=tr[:, 1], in0=tr[:, 0], scalar1=float(np.pi / 2), scalar2=None,
                                               op0=ALU.add), r=["trig"], w=["trig"])
            self.dve(lambda e: e.tensor_scalar(out=kf[:], in0=tr[:], scalar1=float(1 / TWO_PI), scalar2=None,
                                               op0=ALU.mult), r=["trig"], w=["kf"])
            self.dve(lambda e: e.tensor_copy(out=ki[:], in_=kf[:]), r=["kf"], w=["ki"])
            self.dve(lambda e: e.tensor_copy(out=kf[:], in_=ki[:]), r=["ki"], w=["kf"])
            self.dve(lambda e: e.scalar_tensor_tensor(out=tr[:], in0=kf[:], scalar=-TWO_PI, in1=tr[:],
                                                      op0=ALU.mult, op1=ALU.add), r=["kf", "trig"], w=["trig"])
            self.dve(lambda e: e.tensor_scalar(out=kf[:], in0=tr[:], scalar1=float(np.pi), scalar2=-TWO_PI,
                                               op0=ALU.is_gt, op1=ALU.mult), r=["trig"], w=["kf"])
            self.dve(lambda e: e.tensor_tensor(out=tr[:], in0=tr[:], in1=kf[:], op=ALU.add), r=["kf", "trig"], w=["trig"])
            self.dve(lambda e: e.tensor_scalar(out=kf[:], in0=tr[:], scalar1=float(-np.pi), scalar2=TWO_PI,
                                               op0=ALU.is_lt, op1=ALU.mult), r=["trig"], w=["kf"])
            self.dve(lambda e: e.tensor_tensor(out=tr[:], in0=tr[:], in1=kf[:], op=ALU.add), r=["kf", "trig"], w=["trig"])
            self.act(lambda e: e.activation(out=tr[:], in_=tr[:], func=AF.Sin), r=["trig"], w=["trig"])
            for l in range(2):
                for cb in range(6):
                    sl = (l * 6 + cb) % 2
                    self.ld(stage[:, sl], self.ada_w[l, :, cb * 1024:(cb + 1) * 1024].rearrange("(k p) n -> p k n", p=128),
                            w=[("adast", sl)])
                    for jj in range(8):
                        col = l * 48 + cb * 8 + jj

                        def mm(e, sl=sl, jj=jj, col=col):
                            for k in range(8):
                                ins = e.matmul(psmod[:, col:col + 1], lhsT=stage[:, sl, k, jj * 128:(jj + 1) * 128],
                                               rhs=condT[:, k:k + 1], start=(k == 0), stop=(k == 7))
                            return ins
                        self.pe(mm, r=[("adast", sl), "condT"], a=["psmod"])
            self.dve(lambda e: e.tensor_tensor(out=self.modT[:], in0=psmod[:], in1=bT[:], op=ALU.add),
                     r=["psmod", "bT"], w=["modT"])
            for l in range(2):
                m = self.modT[:, l * 48:(l + 1) * 48]
                for which, (sc0, sh0) in enumerate(((8, 0), (32, 24))):
                    o = l * 16 + which * 8
                    nsl = nT[:, l * 16 + which * 8: l * 16 + which * 8 + 8]
                    self.dve(lambda e, o=o, m=m, sc0=sc0, nsl=nsl: e.scalar_tensor_tensor(
                        out=self.aT[:, o:o + 8], in0=m[:, sc0:sc0 + 8], scalar=1.0, in1=nsl, op0=ALU.add, op1=ALU.mult),
                        r=["modT", "nT"], w=[("aT", o)])
            identf = sb(es, "identf", [128, 128], F32)
            onesf = sb(es, "onesf", [128, 128], F32)
            dg = sb(es, "dg", [128, 2, 128], F32)
            psg_ = [self.ps(es, "psgate%d" % i, [128, 512], F32) for i in range(2)]
            self.ld(identf[:], self.ident, w=["identf"])
            self.pool(lambda e: e.memset(onesf[:], 1.0), w=["onesf"])
            cnt = 0
            for l in range(2):
                for which, off in enumerate((16, 40)):
                    gi = l * 2 + which
                    for half in range(2):
                        pb = psg_[cnt % 2]
                        for jj in range(4):
                            j = half * 4 + jj
                            col = l * 48 + off + j
                            sl = (cnt * 4 + jj) % 2
                            self.dve(lambda e, sl=sl, col=col: e.tensor_scalar(out=dg[:, sl, :], in0=identf[:], scalar1=self.modT[:, col:col + 1],
                                                                               scalar2=None, op0=ALU.mult), r=["identf", "modT"], w=[("dg", sl)])
                            self.pe(lambda e, sl=sl, jj=jj, pb=pb: e.matmul(pb[:, jj * 128:(jj + 1) * 128], lhsT=onesf[:], rhs=dg[:, sl, :], start=True, stop=True),
                                    r=[("dg", sl), "onesf"], w=[("psgate", cnt % 2, jj)])
                        self.act(lambda e, gi=gi, half=half, pb=pb: e.activation(out=self.G[:, gi, half * 512:(half + 1) * 512], in_=pb[:], func=AF.Copy),
                                 r=[("psgate", cnt % 2, jj) for jj in range(4)], w=[("G", gi, half)])
                        cnt += 1
            self.ldc(self.trigd.rearrange("p (a b) -> p a b", a=2 * NT), self.trig[:].rearrange("p s t f -> p (s t) f"), r=["trig"], w=["trigd"])
            self.ldc(self.Gd.rearrange("p (a b) -> p a b", a=4), self.G[:], r=[("G", gi, hf) for gi in range(4) for hf in range(2)], w=["Gd"])
            self.S.flush()

    def head_post(self, nm, src, nh, hd, goff, rope_lo, half, tg, tg_nm, sq, sq_nm, ssh, ssh_nm, rt, rt_nm, dst_bf, nm_bf):
        n = nh * hd
        s3 = src.rearrange("p (h d) -> p h d", h=nh)
        sq2 = sq[:, 0:n]
        self.act(lambda e: e.activation(out=sq2, in_=src, func=AF.Square), r=[nm], w=[sq_nm])
        yield
        self.dve(lambda e: e.tensor_reduce(out=ssh[:, 0:nh], in_=sq2.rearrange("p (h d) -> p h d", h=nh), axis=AX.X, op=ALU.add),
                 r=[sq_nm], w=[ssh_nm])
        yield
        self.act(lambda e: e.activation(out=ssh[:, 0:nh], in_=ssh[:, 0:nh], func=AF.Sqrt, scale=1.0 / hd, bias=self.epsT[:, 0:1]),
                 r=[ssh_nm], w=[ssh_nm])
        yield
        self.dve(lambda e: e.reciprocal(out=ssh[:, 0:nh], in_=ssh[:, 0:nh]), r=[ssh_nm], w=[ssh_nm])
        yield
        self.dve(lambda e: e.tensor_tensor(out=s3, in0=s3, in1=ssh[:, 0:nh].unsqueeze(2).to_broadcast([128, nh, hd]), op=ALU.mult),
                 r=[nm, ssh_nm], w=[nm])
        yield
        gb = self.gbc[:, goff:goff + hd]
        self.dve(lambda e: e.tensor_tensor(out=s3, in0=s3, in1=gb.unsqueeze(1).to_broadcast([128, nh, hd]), op=ALU.mult),
                 r=[nm, "gbc"], w=[nm])
        yield
        fo = 0 if half == 16 else 16
        sin = tg[:, 0, fo:fo + half].unsqueeze(1).to_broadcast([128, nh, half])
        cos = tg[:, 1, fo:fo + half].unsqueeze(1).to_broadcast([128, nh, half])
        x1 = s3[:, :, rope_lo:rope_lo + half]
        x2 = s3[:, :, rope_lo + half:rope_lo + 2 * half]
        m = nh * half
        tA = rt[:, 0, 0:m].rearrange("p (h d) -> p h d", h=nh)
        tB = rt[:, 1, 0:m].rearrange("p (h d) -> p h d", h=nh)
        tC = rt[:, 2, 0:m].rearrange("p (h d) -> p h d", h=nh)
        tD = rt[:, 3, 0:m].rearrange("p (h d) -> p h d", h=nh)
        rA, rB, rC, rD = [(rt_nm, i) for i in range(4)]
        self.dve(lambda e: e.tensor_tensor(out=tA, in0=x1, in1=cos, op=ALU.mult), r=[nm, tg_nm], w=[rA])
        self.dve(lambda e: e.tensor_tensor(out=tB, in0=x2, in1=sin, op=ALU.mult), r=[nm, tg_nm], w=[rB])
        yield
        self.dve(lambda e: e.tensor_tensor(out=tC, in0=x2, in1=cos, op=ALU.mult), r=[nm, tg_nm], w=[rC])
        self.dve(lambda e: e.tensor_tensor(out=tD, in0=x1, in1=sin, op=ALU.mult), r=[nm, tg_nm], w=[rD])
        yield
        self.dve(lambda e: e.tensor_tensor(out=x1, in0=tA, in1=tB, op=ALU.subtract), r=[rA, rB], w=[nm])
        yield
        self.dve(lambda e: e.tensor_tensor(out=x2, in0=tC, in1=tD, op=ALU.add), r=[rC, rD], w=[nm])
        yield
        self.act(lambda e: e.activation(out=dst_bf, in_=src, func=AF.Copy), r=[nm], w=[nm_bf])
        yield

    def tr_store(self, nm_bf, src_bf, nh, hd, pstr, pcnt, stg, stg_nm, dram, t, hw=None, row0=0):
        hw = hw or hd
        s3 = src_bf.rearrange("p (h d) -> p h d", h=nh)
        done = 0
        while done < nh:
            nb = min(8, nh - done)
            ps = pstr[pcnt[0] % 2]
            rn = ("pstr", pcnt[0] % 2)
            pcnt[0] += 1

            def tp(e, done=done, nb=nb, ps=ps):
                for i in range(nb):
                    ins = e.transpose(ps[0:hw, i, :], s3[:, done + i, 0:hw], self.identb[:])
                return ins
            self.pe(tp, r=[nm_bf, "identb"], w=[rn])
            self.dve(lambda e, done=done, nb=nb, ps=ps: e.tensor_copy(out=stg[0:hw, done:done + nb, :], in_=ps[0:hw, 0:nb, :]),
                     r=[rn], w=[(stg_nm, done)])
            yield
            dst = dram[done:done + nb, row0:row0 + hw, t * 128:(t + 1) * 128].rearrange("h d s -> d h s")
            self.ldc(dst, stg[0:hw, done:done + nb, :], r=[(stg_nm, done)], a=[("dram", id(dram))])
            done += nb

    def load_w(self, dst, src, K, rn, n0=0, n1=None, d0=0):
        n1 = n1 if n1 is not None else src.shape[1]
        kc = K // 128
        step = 2 if (n1 - n0) > 1024 else kc
        for k0 in range(0, kc, step):
            k1 = min(kc, k0 + step)
            self.ldc(dst[:, k0:k1, d0:d0 + (n1 - n0)],
                     src[k0 * 128:k1 * 128, n0:n1].rearrange("(k p) n -> p k n", p=128), a=[rn])

    def phaseA(self, layer):
        sb = self.sb
        NI = 2
        ncol = EVEN_IN if layer == 0 else ODD_IN
        xin = self.x if layer == 0 else self.xL
        with ExitStack() as es:
            w_in = sb(es, "w_in", [128, 8, ncol], BF16)
            trg = sb(es, "trg", [128, NI, 2, 48], F32)
            xt = sb(es, "xt", [128, NI, D], F32)
            xs = sb(es, "xs", [128, NI, D], BF16)
            ssx = sb(es, "ssx", [128, NI, 4], F32)
            hT = sb(es, "hT", [128, NI, 8, 128], BF16)
            tmpf = sb(es, "tmpf", [128, NI, D], F32)
            u = sb(es, "u", [128, NI, ncol], F32)
            sq = sb(es, "sq", [128, NI, D], F32)
            ssh = sb(es, "ssh", [128, NI, 16], F32)
            rt = sb(es, "rt", [128, NI, 4, 512], F32)
            qbf = sb(es, "qbf", [128, NI, 2048], BF16)
            vbf = sb(es, "vbf", [128, NI, 1024], BF16)
            stg = [sb(es, "stg%d" % i, [96, NI, 12, 128], BF16) for i in range(4)]
            pT = self.ps(es, "pT", [128, 8, 128], BF16)
            psu = [self.ps(es, "psu%d" % i, [128, 512], F32) for i in range(2)]
            pstr = [self.ps(es, "pstr%d" % i, [128, 8, 128], BF16) for i in range(2)]
            self.load_w(w_in, self.e_w_in if layer == 0 else self.o_w_in, D, "w_in")
            if layer == 0:
                w_uq = sb(es, "w_uqb", [128, 3, 768], BF16)
                w_ukv = sb(es, "w_ukvb", [128, 2, 1024], BF16)
                latTs = sb(es, "latTs", [128, 5], F32)
                latb = sb(es, "latb", [128, NI, 640], BF16)
                latTt = sb(es, "latTt", [128, NI, 5, 128], BF16)
                qm = sb(es, "qm", [128, NI, 768], F32)
                kf_ = sb(es, "kfull", [128, NI, 768], F32)
                psup = [self.ps(es, "psup%d" % i, [128, 512], F32) for i in range(2)]
                self.ld(latTs[:], self.latT, w=["latTs"])
                self.ldc(w_uq[:], self.w_uq.rearrange("(k p) n -> p k n", p=128), w=["w_uq"])
                self.ldc(w_ukv[:], self.w_ukv.rearrange("(k p) n -> p k n", p=128), w=["w_ukv"])
                self.dve(lambda e: e.tensor_tensor(out=w_uq[:], in0=w_uq[:], in1=latTs[:, 0:3].unsqueeze(2).to_broadcast([128, 3, 768]), op=ALU.mult),
                         r=["w_uq", "latTs"], w=["w_uq"])
                self.dve(lambda e: e.tensor_tensor(out=w_ukv[:], in0=w_ukv[:], in1=latTs[:, 3:5].unsqueeze(2).to_broadcast([128, 2, 1024]), op=ALU.mult),
                         r=["w_ukv", "latTs"], w=["w_ukv"])
            else:
                kmacc = sb(es, "kmacc", [64, 8, 32], F32)
                kmb = sb(es, "kmb", [64, 8, 32], BF16)
                kpart = sb(es, "kpart", [64, NI, 8], F32)
                gsb = sb(es, "gsb", [128, NI, 8, 32], F32)
                top8 = sb(es, "top8", [128, NI, 8, 8], F32)
                biasf = sb(es, "biasf", [128, NI, 8, 32], F32)
                biasb = sb(es, "biasb", [128, NI, 8, 32], BF16)
                psg = self.ps(es, "psg", [128, 8, 32], F32)
                self.pool(lambda e: e.memset(gsb[:], -1e30), w=[("gsb", i) for i in range(NI)])
                self.pool(lambda e: e.memset(kmacc[:], 0.0), w=["kmacc"])
            pcnt = [0]
            trg_d = self.trigd.rearrange("p (s t f) -> p s t f", s=2, t=NT)

            def body(t, sl):
                R = lambda nm: (nm, sl)
                xtt = xt[:, sl]
                u_ = u[:, sl]
                sq_ = sq[:, sl]
                ssx_ = ssx[:, sl]
                ssh_ = ssh[:, sl]
                rt_ = rt[:, sl]
                qbf_ = qbf[:, sl]
                vbf_ = vbf[:, sl]
                tg = trg[:, sl]
                self.ld(xtt, xin[t * 128:(t + 1) * 128, :], w=[R("xt")])
                self.ld(tg, trg_d[:, :, t, :], w=[R("tg")])
                yield
                self.act(lambda e: e.activation(out=sq_, in_=xtt, func=AF.Square, accum_out=ssx_[:, 0:1]), r=[R("xt")], w=[R("sq"), R("ssx")])
                yield
                self.act(lambda e: e.activation(out=ssx_[:, 1:2], in_=ssx_[:, 0:1], func=AF.Sqrt, scale=1.0 / D, bias=self.epsT[:, 0:1]), r=[R("ssx")], w=[R("rsx")])
                yield
                self.dve(lambda e: e.reciprocal(out=ssx_[:, 1:2], in_=ssx_[:, 1:2]), r=[R("rsx")], w=[R("rsx")])
                yield
                self.act(lambda e: e.activation(out=xs[:, sl], in_=xtt, func=AF.Copy, scale=ssx_[:, 1:2]), r=[R("xt"), R("rsx")], w=[R("xs")])
                yield

                def tpx(e):
                    for j in range(8):
                        ins = e.transpose(pT[:, j, :], xs[:, sl, j * 128:(j + 1) * 128], self.identb[:])
                    return ins
                self.pe(tpx, r=[R("xs"), "identb"], w=["pT"])
                ao = layer * 16
                a_bc = self.aT[:, ao:ao + 8].unsqueeze(2).to_broadcast([128, 8, 128])
                b_bc = self.modT[:, layer * 48:layer * 48 + 8].unsqueeze(2).to_broadcast([128, 8, 128])
                tm3 = tmpf[:, sl].rearrange("p (j s) -> p j s", j=8)
                self.dve(lambda e: e.tensor_tensor(out=tm3, in0=pT[:], in1=a_bc, op=ALU.mult), r=["pT", ("aT", ao)], w=[R("tmpf")])
                yield
                self.dve(lambda e: e.tensor_tensor(out=hT[:, sl], in0=tm3, in1=b_bc, op=ALU.add), r=[R("tmpf"), "modT"], w=[R("hT")])
                yield
                ngrp = (ncol + 511) // 512
                for gi, c0 in enumerate(range(0, ncol, 512)):
                    c1 = min(ncol, c0 + 512)
                    pb = psu[gi % 2]

                    def mm(e, c0=c0, c1=c1, pb=pb):
                        for j in range(8):
                            ins = e.matmul(pb[:, 0:c1 - c0], lhsT=hT[:, sl, j, :], rhs=w_in[:, j, c0:c1], start=(j == 0), stop=(j == 7))
                        return ins
                    self.pe(mm, r=[R("hT"), "w_in"], w=[("psu", gi % 2)])
                    self.act(lambda e, c0=c0, c1=c1, pb=pb: e.activation(out=u_[:, c0:c1], in_=pb[:, 0:c1 - c0], func=AF.Copy),
                             r=[("psu", gi % 2)], w=[("u", sl, gi)])
                    yield
                uall = [("u", sl, gi) for gi in range(ngrp)]
                U = lambda gi: ("u", sl, gi)
                if layer == 0:
                    latb_ = latb[:, sl]
                    qm_ = qm[:, sl]
                    kfs = kf_[:, sl]
                    self.act(lambda e: e.activation(out=sq_[:, 0:384], in_=u_[:, 0:384], func=AF.Square, accum_out=ssx_[:, 2:3]), r=[U(0)], w=[R("sq"), R("ssl")])
                    self.act(lambda e: e.activation(out=sq_[:, 384:640], in_=u_[:, 384:640], func=AF.Square, accum_out=ssx_[:, 3:4]), r=[U(0), U(1), R("sq")], w=[R("sq"), R("ssl2")])
                    yield
                    self.act(lambda e: e.activation(out=ssx_[:, 2:3], in_=ssx_[:, 2:3], func=AF.Sqrt, scale=1.0 / 384, bias=self.epsT[:, 0:1]), r=[R("ssl")], w=[R("ssl")])
                    self.act(lambda e: e.activation(out=ssx_[:, 3:4], in_=ssx_[:, 3:4], func=AF.Sqrt, scale=1.0 / 256, bias=self.epsT[:, 0:1]), r=[R("ssl2")], w=[R("ssl2")])
                    yield
                    self.dve(lambda e: e.reciprocal(out=ssx_[:, 2:4], in_=ssx_[:, 2:4]), r=[R("ssl"), R("ssl2")], w=[R("ssl"), R("ssl2")])
                    yield
                    self.dve(lambda e: e.tensor_scalar(out=latb_[:, 0:384], in0=u_[:, 0:384], scalar1=ssx_[:, 2:3], scalar2=None, op0=ALU.mult), r=[U(0), R("ssl")], w=[R("latb0")])
                    self.dve(lambda e: e.tensor_scalar(out=latb_[:, 384:640], in0=u_[:, 384:640], scalar1=ssx_[:, 3:4], scalar2=None, op0=ALU.mult), r=[U(0), U(1), R("ssl2")], w=[R("latb1")])
                    yield
                    ps = pstr[pcnt[0] % 2]
                    rn = ("pstr", pcnt[0] % 2)
                    pcnt[0] += 1

                    def tpl(e, ps=ps):
                        for j in range(5):
                            ins = e.transpose(ps[:, j, :], latb_[:, j * 128:(j + 1) * 128], self.identb[:])
                        return ins
                    self.pe(tpl, r=[R("latb0"), R("latb1"), "identb"], w=[rn])
                    self.dve(lambda e, ps=ps: e.tensor_copy(out=latTt[:, sl], in_=ps[:, 0:5, :]), r=[rn], w=[R("latTt")])
                    yield
                    for gi, (c0, c1) in enumerate(((0, 512), (512, 768))):
                        def mmq(e, c0=c0, c1=c1, gi=gi):
                            for j in range(3):
                                ins = e.matmul(psup[gi][:, 0:c1 - c0], lhsT=latTt[:, sl, j, :], rhs=w_uq[:, j, c0:c1], start=(j == 0), stop=(j == 2))
                            return ins
                        self.pe(mmq, r=[R("latTt"), "w_uq"], w=[("psup", gi)])
                        self.act(lambda e, c0=c0, c1=c1, gi=gi: e.activation(out=qm_[:, c0:c1], in_=psup[gi][:, 0:c1 - c0], func=AF.Copy),
                                 r=[("psup", gi)], w=[R("qm")] if gi == 0 else [], a=[] if gi == 0 else [R("qm")])
                        yield
                    kf3 = kfs.rearrange("p (h d) -> p h d", h=8)
                    vb3 = vbf_[:, 0:512].rearrange("p (h d) -> p h d", h=8)
                    for gi in range(2):
                        def mmk(e, gi=gi):
                            for j in range(2):
                                ins = e.matmul(psup[gi][:], lhsT=latTt[:, sl, 3 + j, :], rhs=w_ukv[:, j, gi * 512:(gi + 1) * 512], start=(j == 0), stop=(j == 1))
                            return ins
                        self.pe(mmk, r=[R("latTt"), "w_ukv"], w=[("psup", gi)])
                        p3 = psup[gi][:].rearrange("p (h d) -> p h d", h=4)
                        self.act(lambda e, gi=gi, p3=p3: e.activation(out=kf3[:, gi * 4:gi * 4 + 4, 0:64], in_=p3[:, :, 0:64], func=AF.Copy),
                                 r=[("psup", gi)], w=[R("kfull")] if gi == 0 else [], a=[] if gi == 0 else [R("kfull")])
                        self.act(lambda e, gi=gi, p3=p3: e.activation(out=vb3[:, gi * 4:gi * 4 + 4, :], in_=p3[:, :, 64:128], func=AF.Copy),
                                 r=[("psup", gi)], w=[R("vbf")] if gi == 0 else [], a=[] if gi == 0 else [R("vbf")])
                        yield
                    self.dve(lambda e: e.tensor_copy(out=kf3[:, :, 64:96], in_=u_[:, 640:672].unsqueeze(1).to_broadcast([128, 8, 32])), r=[U(1)], a=[R("kfull")])
                    yield
                    yield from self.head_post(R("qm"), qm_, 8, 96, 0, 64, 16, tg, R("tg"), sq_, R("sq"), ssh_, R("ssh"), rt_, R("rt"), qbf_[:, 0:768], R("qbf0"))
                    yield from self.tr_store(R("qbf0"), qbf_[:, 0:768], 8, 96, pstr, pcnt, stg[0][:, sl], R("stg0"), self.qT_mla, t)
                    yield from self.head_post(R("kfull"), kfs, 8, 96, 96, 64, 16, tg, R("tg"), sq_, R("sq"), ssh_, R("ssh"), rt_, R("rt"), qbf_[:, 768:1536], R("qbf1"))
                    yield from self.tr_store(R("qbf1"), qbf_[:, 768:1536], 8, 96, pstr, pcnt, stg[1][:, sl], R("stg1"), self.kT_mla, t)
                    self.ldc(self.v_mla[t * 128:(t + 1) * 128, :], vbf_[:, 0:512], r=[R("vbf")], a=["v_mla"])
                    yield
                    self.dve(lambda e: e.tensor_copy(out=qm_, in_=u_[:, 672:1440]), r=uall, w=[R("qm")])
                    yield
                    yield from self.head_post(R("qm"), qm_, 12, 64, 192, 0, 32, tg, R("tg"), sq_, R("sq"), ssh_, R("ssh"), rt_, R("rt"), qbf_[:, 0:768], R("qbf0"))
                    yield from self.tr_store(R("qbf0"), qbf_[:, 0:768], 12, 64, pstr, pcnt, stg[2][:, sl], R("stg2"), self.qT_dil, t)
                    self.dve(lambda e: e.tensor_copy(out=kfs, in_=u_[:, 1440:2208]), r=uall, w=[R("kfull")])
                    yield
                    yield from self.head_post(R("kfull"), kfs, 12, 64, 256, 0, 32, tg, R("tg"), sq_, R("sq"), ssh_, R("ssh"), rt_, R("rt"), qbf_[:, 768:1536], R("qbf1"))
                    yield from self.tr_store(R("qbf1"), qbf_[:, 768:1536], 12, 64, pstr, pcnt, stg[3][:, sl], R("stg3"), self.kT_dil, t)
                    self.act(lambda e: e.activation(out=vbf_[:, 0:768], in_=u_[:, 2208:2976], func=AF.Copy), r=uall, w=[R("vbf")])
                    yield
                    self.ldc(self.v_dil[t * 128:(t + 1) * 128, :], vbf_[:, 0:768], r=[R("vbf")], a=["v_dil"])
                    yield
                else:
                    yield from self.head_post(U(0), u_[:, 0:512], 8, 64, 320, 0, 32, tg, R("tg"), sq_, R("sq"), ssh_, R("ssh"), rt_, R("rt"), qbf_[:, 0:512], R("qbfA"))
                    yield from self.tr_store(R("qbfA"), qbf_[:, 0:512], 8, 64, pstr, pcnt, stg[0][:, sl], R("stg0"), self.qT_df, t)
                    yield from self.head_post(U(1), u_[:, 512:1024], 8, 64, 384, 0, 32, tg, R("tg"), sq_, R("sq"), ssh_, R("ssh"), rt_, R("rt"), qbf_[:, 512:1024], R("qbfB"))
                    yield from self.tr_store(R("qbfB"), qbf_[:, 512:1024], 8, 64, pstr, pcnt, stg[1][:, sl], R("stg1"), self.kT_df, t)
                    self.act(lambda e: e.activation(out=vbf_[:, 0:512], in_=u_[:, 1024:1536], func=AF.Copy), r=uall, w=[R("vbf")])
                    yield
                    self.ldc(self.v_df[t * 128:(t + 1) * 128, :], vbf_[:, 0:512], r=[R("vbf")], a=["v_df"])
                    self.act(lambda e: e.activation(out=vbf_[:, 512:1024], in_=u_[:, 2560:3072], func=AF.Copy), r=uall, w=[R("vbf2")])
                    yield
                    self.ldc(self.v_mb[t * 128:(t + 1) * 128, :], vbf_[:, 512:1024], r=[R("vbf2")], a=["v_mb"])
                    yield
                    yield from self.head_post(U(4), u_[:, 2048:2560], 8, 64, 512, 0, 32, tg, R("tg"), sq_, R("sq"), ssh_, R("ssh"), rt_, R("rt"), qbf_[:, 1024:1536], R("qbfC"))
                    yield from self.tr_store(R("qbfC"), qbf_[:, 1024:1536], 8, 64, pstr, pcnt, stg[2][:, sl], R("stg2"), self.kT_mb, t)
                    nblk = t // 2
                    kp = kpart[:, sl]
                    self.dve(lambda e: e.tensor_reduce(out=kp, in_=stg[2][0:64, sl, 0:8, :], axis=AX.X, op=ALU.add), r=[(R("stg2"), 0)], w=[R("kpart")])
                    yield
                    self.dve(lambda e: e.tensor_tensor(out=kmacc[:, :, nblk], in0=kmacc[:, :, nblk], in1=kp, op=ALU.add), r=[R("kpart"), "kmacc"], w=["kmacc"])
                    yield
                    if t % 2 == 1:
                        self.act(lambda e: e.activation(out=kmb[:, :, nblk], in_=kmacc[:, :, nblk], func=AF.Copy, scale=1.0 / 256), r=["kmacc"], a=["kmb"])
                        yield
                    yield from self.head_post(U(3), u_[:, 1536:2048], 8, 64, 448, 0, 32, tg, R("tg"), sq_, R("sq"), ssh_, R("ssh"), rt_, R("rt"), qbf_[:, 1536:2048], R("qbfD"))
                    yield from self.tr_store(R("qbfD"), qbf_[:, 1536:2048], 8, 64, pstr, pcnt, stg[3][:, sl], R("stg3"), self.qT_mb, t)
                    gs_ = gsb[:, sl]
                    t8 = top8[:, sl]
                    bf_ = biasf[:, sl]
                    bb_ = biasb[:, sl]
                    if nblk > 0:
                        def mmg(e):
                            for h in range(8):
                                ins = e.matmul(psg[:, h, 0:nblk], lhsT=stg[3][0:64, sl, h, :], rhs=kmb[:, h, 0:nblk], start=True, stop=True)
                            return ins
                        self.pe(mmg, r=[(R("stg3"), 0), "kmb"], w=["psg"])
                        self.dve(lambda e: e.tensor_copy(out=gs_[:, :, 0:nblk], in_=psg[:, :, 0:nblk]), r=["psg"], w=[R("gsb")])
                        yield

                        def mx(e):
                            for h in range(8):
                                ins = e.max(out=t8[:, h, :], in_=gs_[:, h, :])
                            return ins
                        self.dve(mx, r=[R("gsb")], w=[R("top8")])
                        yield
                        self.dve(lambda e: e.tensor_tensor(out=bf_, in0=gs_, in1=t8[:, :, 2:3].to_broadcast([128, 8, 32]), op=ALU.is_lt), r=[R("gsb"), R("top8")], w=[R("biasf")])
                        yield
                        self.dve(lambda e: e.tensor_scalar(out=bb_, in0=bf_, scalar1=NEGB, scalar2=None, op0=ALU.mult), r=[R("biasf")], w=[R("biasb")])
                        yield
                    else:
                        self.dve(lambda e: e.memset(bb_, NEGB), w=[R("biasb")])
                        yield
                    self.dve(lambda e: e.memset(bb_[:, :, nblk:nblk + 1], 0.0), r=[R("biasb")], w=[R("biasb")])
                    yield
                    ps = pstr[pcnt[0] % 2]
                    rn = ("pstr", pcnt[0] % 2)
                    pcnt[0] += 1

                    def tpb(e, ps=ps):
                        for h in range(8):
                            ins = e.transpose(ps[0:32, h, :], bb_[:, h, :], self.identb[:])
                        return ins
                    self.pe(tpb, r=[R("biasb"), "identb"], w=[rn])
                    self.dve(lambda e, ps=ps: e.tensor_copy(out=stg[0][0:32, sl, 0:8, :], in_=ps[0:32, 0:8, :]), r=[rn], w=[(R("stg0"), 0)])
                    yield
                    dst = self.qT_mb[:, 64:96, t * 128:(t + 1) * 128].rearrange("h d s -> d h s")
                    self.ldc(dst, stg[0][0:32, sl, 0:8, :], r=[(R("stg0"), 0)], a=["qT_mb_bias"])
                    yield

            import os as _os
            STAG = int(_os.environ.get("KSTAG", "35"))
            active = []
            next_t = 0
            while next_t < self.nt or active:
                if next_t < self.nt and len(active) < NI and (not active or active[-1][1] >= STAG):
                    active.append([body(next_t, next_t % NI), 0])
                    next_t += 1
                for a_ in list(active):
                    try:
                        next(a_[0])
                        a_[1] += 1
                    except StopIteration:
                        active.remove(a_)
            self.S.flush()

    def attn_tiles(self, tiles, ps_s, P, po, po_nm, sc, cnt, hook=None, hook_at=5, acc=None, acc_nm=None, vrows=65):
        n = len(tiles)
        ns = len(ps_s)
        npb = P.shape[1]
        LA = min(ns - 1, 3)
        for i in range(n + LA):
            if i < n:
                kap, qap, q0, N, mk, vs = tiles[i][:6]
                si = (cnt + i) % ns
                pi = (cnt + i) % npb
                self.pe(lambda e, kap=kap, qap=qap, N=N, si=si: e.matmul(ps_s[si][:, 0:N], lhsT=kap, rhs=qap, start=True, stop=True),
                        r=tiles[i][6], w=[("ps_s", si)])
                self.act(lambda e, N=N, si=si, pi=pi: e.activation(out=P[:, pi, 0:N], in_=ps_s[si][:, 0:N], func=AF.Exp, scale=sc),
                         r=[("ps_s", si)], w=[("P", pi)])
                if mk is not None:
                    self.dve(lambda e, N=N, pi=pi, mk=mk: e.tensor_tensor(out=P[:, pi, 0:N], in0=P[:, pi, 0:N], in1=mk, op=ALU.mult),
                             r=[("P", pi), "masks"], w=[("P", pi)])
                if acc is not None:
                    if i == 0:
                        self.dve(lambda e, N=N, pi=pi, q0=q0: e.tensor_copy(out=acc[:, q0:512], in_=P[:, pi, 0:N]), r=[("P", pi)], w=[acc_nm])
                    else:
                        self.dve(lambda e, N=N, pi=pi, q0=q0: e.tensor_tensor(out=acc[:, q0:512], in0=acc[:, q0:512], in1=P[:, pi, 0:N], op=ALU.add),
                                 r=[("P", pi), acc_nm], w=[acc_nm])
            if hook is not None and i == min(hook_at, n + LA - 1):
                hook()
                hook = None
            j = i - LA
            if j >= 0:
                kap, qap, q0, N, mk, vs = tiles[j][:6]
                pi = (cnt + j) % npb
                for f, vap in enumerate(vs):
                    self.pe(lambda e, f=f, vap=vap, q0=q0, N=N, pi=pi, j=j: e.matmul(po[f][0:vrows, q0:512], lhsT=vap, rhs=P[:, pi, 0:N],
                                                                                    start=(j == 0), stop=(j == n - 1)),
                            r=[("P", pi)] + tiles[j][7], w=[po_nm[f]] if j == 0 else [], a=[] if j == 0 else [po_nm[f]])
        if hook is not None:
            hook()
        return cnt + n

    def attn_norm(self, po, po_nm, ep, slot, dst, dst_nm):
        rrow, osb, ps_bc = ep
        self.act(lambda e: e.activation(out=rrow[64:65, slot, :], in_=po[64:65, :], func=AF.Ln), r=[po_nm], w=[("rrow", slot)])
        self.act(lambda e: e.activation(out=rrow[64:65, slot, :], in_=rrow[64:65, slot, :], func=AF.Exp, scale=-1.0), r=[("rrow", slot)], w=[("rrow", slot)])
        self.pe(lambda e: e.matmul(ps_bc[0:64, :], lhsT=self.onesb[64:65, 0:64], rhs=rrow[64:65, slot, :], start=True, stop=True),
                r=[("rrow", slot), "onesb"], w=["ps_bc"])
        self.act(lambda e: e.activation(out=osb[0:64, slot, :], in_=po[0:64, :], func=AF.Copy), r=[po_nm], w=[("osb", slot)])
        self.dve(lambda e: e.tensor_tensor(out=dst, in0=osb[0:64, slot, :], in1=ps_bc[0:64, :], op=ALU.mult),
                 r=[("osb", slot), "ps_bc"], w=[dst_nm])

    def phaseB(self, kind):
        sb = self.sb
        nch = max(1, self.nt // 4)
        ncols = nch * 512
        dil = kind == "dil"
        nheads = {"mla": 8, "dil": 4, "diff": 4, "moba": 8}[kind]
        nm = 2 if kind == "diff" else 1
        parts = 1
        isdf = kind == "diff"
        dk = {"mla": 96, "dil": 64, "diff": 64, "moba": 96}[kind]
        sc = float({"mla": 96 ** -0.5, "dil": 0.125, "diff": 0.125, "moba": 0.125}[kind])
        qsrc = {"mla": self.qT_mla, "dil": self.qT_dil, "diff": self.qT_df, "moba": self.qT_mb}[kind]
        ksrc = {"mla": self.kT_mla, "dil": self.kT_dil, "diff": self.kT_df, "moba": self.kT_mb}[kind]
        vsrc = {"mla": self.v_mla, "dil": self.v_dil, "diff": self.v_df, "moba": self.v_mb}[kind]
        row_base = {"mla": 0, "dil": 512, "diff": 0, "moba": 512}[kind]
        lam_init = 0.8 - 0.6 * math.exp(-0.3 * 1)
        with ExitStack() as es:
            P_dummy = sb(es, "Pdummy", [128, 2], F32)
            if dil:
                qT = sb(es, "qTd", [64, 2, 3, 512], BF16)
                kT = sb(es, "kTd", [64, 3, SEQ], BF16)
                V = sb(es, "Vd", [128, 3, 64, 65], BF16)
                dmb = sb(es, "dmb", [128, 33, 512], BF16)
                self.ldc(dmb[:], self.dmask.rearrange("p (a b) -> p a b", a=33), w=["masks"])
                self.pool(lambda e: e.memset(V[:, :, :, 64:65], 1.0), a=["Vones"])
            else:
                qT = sb(es, "qTa", [96, 2, SEQ], BF16)
                kT = sb(es, "kTa", [96, 2, SEQ], BF16)
                vw = 128 if isdf else 65
                V = sb(es, "Va", [128, 2, parts, 64, vw], BF16)
                if isdf:
                    self.pool(lambda e: e.memset(P_dummy[:], 0.0), a=["Vones"])
                else:
                    self.pool(lambda e: e.memset(V[:, :, :, :, 64:65], 1.0), a=["Vones"])
                if kind == "moba":
                    for sl in range(2):
                        self.ldc(kT[64:96, sl, :], self.kind, a=["Vones"])
            n_po = 2
            n_s = 7 - n_po
            P = sb(es, "Pt", [128, n_s + 2, 512], BF16)
            rrow = sb(es, "rrow", [65, 2, 512], F32)
            osb = sb(es, "osb", [64, 2, 512], F32)
            onb = sb(es, "onb", [64, 4, 512], BF16)
            onbd = sb(es, "onbd", [128, 2, 512], BF16)
            ps_s = [self.ps(es, "ps_s%d" % i, [128, 512], F32) for i in range(n_s)]
            po = [self.ps(es, "po%d" % i, [128, 512], F32) for i in range(n_po)]
            ps_bc = self.ps(es, "ps_bc", [128, 512], F32)
            ep = (rrow, osb, ps_bc)
            if isdf:
                acc = sb(es, "acc", [128, 2, 512], F32)
                rcp = sb(es, "rcp", [128, 512], F32)
                nrm = sb(es, "nrm", [128, 2, 512], F32)
                dd = sb(es, "dd", [128, 512], F32)
                sqd = sb(es, "sqd", [128, 512], F32)
                rsd = sb(es, "rsd", [128, 512], F32)
                onesf = sb(es, "onesfB", [128, 128], F32)
                lamt = sb(es, "lamt", [128, 256], F32)
                lsm = sb(es, "lsm", [128, 8], F32)
                subs = sb(es, "subs", [128, 1], F32)
                self.pool(lambda e: e.memset(onesf[:], 1.0), w=["onesfB"])
                self.ld(lamt[:], self.dlam.partition_broadcast(128), w=["lamt"])
                self.ld(subs[:], self.subT, w=["subs"])
                self.dve(lambda e: e.tensor_tensor(out=lamt[:, 0:64], in0=lamt[:, 0:64], in1=lamt[:, 64:128], op=ALU.mult), r=["lamt"], w=["lamt"])
                self.dve(lambda e: e.tensor_tensor(out=lamt[:, 128:192], in0=lamt[:, 128:192], in1=lamt[:, 192:256], op=ALU.mult), r=["lamt"], w=["lamt"])
                self.dve(lambda e: e.tensor_reduce(out=lsm[:, 0:1], in_=lamt[:, 0:64], axis=AX.X, op=ALU.add), r=["lamt"], w=["lsm0"])
                self.dve(lambda e: e.tensor_reduce(out=lsm[:, 1:2], in_=lamt[:, 128:192], axis=AX.X, op=ALU.add), r=["lamt"], w=["lsm1"])
                self.act(lambda e: e.activation(out=lsm[:, 2:4], in_=lsm[:, 0:2], func=AF.Exp), r=["lsm0", "lsm1"], w=["lsm2"])
                self.dve(lambda e: e.tensor_tensor(out=lsm[:, 4:5], in0=lsm[:, 3:4], in1=lsm[:, 2:3], op=ALU.subtract), r=["lsm2"], w=["lsm4"])
                self.dve(lambda e: e.tensor_scalar(out=lsm[:, 5:6], in0=lsm[:, 4:5], scalar1=-lam_init, scalar2=None, op0=ALU.add), r=["lsm4"], w=["neglam"])
                self.dve(lambda e: e.tensor_scalar(out=subs[:], in0=subs[:], scalar1=1.0 - lam_init, scalar2=None, op0=ALU.mult), r=["subs"], w=["subs"])
            cnt = 0
            ecnt = 0
            pending = [None]
            for h in range(nheads):
                vsl = h % 2
                if dil:
                    for g in range(3):
                        gh = g * 4 + h
                        self.ld(kT[:, g, 0:ncols], ksrc[gh, :, 0:ncols], w=[("kT", g)])
                        self.ld(V[:, g, 0:nch * 4, 0:64], vsrc[0:ncols, gh * 64:gh * 64 + 64].rearrange("(t p) d -> p t d", p=128),
                                r=["Vones"], w=[("V", g)])
                else:
                    for f in range(parts):
                        vd = 128 if isdf else 64
                        c0 = h * vd
                        self.ld(V[:, vsl, f, 0:nch * 4, 0:vd], vsrc[0:ncols, c0:c0 + vd].rearrange("(t p) d -> p t d", p=128),
                                r=["Vones"], w=[("V", vsl, f)])
                    for m in range(nm):
                        u_ = h * nm + m
                        sl = u_ % 2
                        dq = 96 if kind in ("mla", "moba") else 64
                        dkk = 96 if kind == "mla" else 64
                        self.ld(qT[0:dq, sl, 0:ncols], qsrc[u_, 0:dq, 0:ncols], w=[("qT", sl)])
                        self.ld(kT[0:dkk, sl, 0:ncols], ksrc[u_, 0:dkk, 0:ncols], r=["Vones"], w=[("kT", sl)])
                for c in range(nch):
                    if dil:
                        qs = c % 2
                        for g in range(3):
                            self.ld(qT[:, qs, g, :], qsrc[g * 4 + h, :, c * 512:(c + 1) * 512], w=[("qT", qs, g)])
                    for m in range(nm):
                        u_ = h * nm + m
                        sl = u_ % 2
                        tiles = []
                        if dil:
                            mo = 0
                            for g, W in enumerate((1, 4, 16)):
                                for o in range(W + 4):
                                    kt = 4 * c - W + o
                                    if kt >= 0:
                                        tiles.append((kT[:, g, kt * 128:(kt + 1) * 128], qT[:, qs, g, :], 0, 512, dmb[:, mo + o, :],
                                                      [V[:, g, kt, :]], [("kT", g), ("qT", qs, g)], [("V", g)]))
                                mo += W + 4
                        else:
                            for kt in range(4 * c + 4):
                                j = kt - 4 * c
                                if j < 0:
                                    q0, mk = 0, None
                                else:
                                    q0, mk = 128 * j, self.cmb[:, j, 128 * j:512]
                                N = 512 - q0
                                tiles.append((kT[0:dk, sl, kt * 128:(kt + 1) * 128], qT[0:dk, sl, c * 512 + q0:(c + 1) * 512], q0, N, mk,
                                              [V[:, vsl, f, kt, :] for f in range(parts)], [("kT", sl), ("qT", sl)],
                                              [("V", vsl, f) for f in range(parts)]))
                        pidx = [ecnt % 2]
                        pos_ = [po[i] for i in pidx]
                        po_nm = [("po", i) for i in pidx]
                        if isdf:
                            cnt = self.attn_tiles(tiles, ps_s, P, pos_, po_nm, sc, cnt, hook=pending[0], acc=acc[:, m, :], acc_nm=("acc", m), vrows=128)
                        else:
                            cnt = self.attn_tiles(tiles, ps_s, P, pos_, po_nm, sc, cnt, hook=pending[0])

                        def epilogue(pos_=pos_, po_nm=po_nm, ecnt0=ecnt, m=m, c=c, h=h):
                            if not isdf:
                                es_ = ecnt0 % 2
                                osl = ecnt0 % 4
                                self.attn_norm(pos_[0], po_nm[0], ep, es_, onb[:, osl, :], ("onb", osl))
                                r0 = row_base + h * 64
                                self.ldc(self.oT[r0:r0 + 64, c * 512:(c + 1) * 512], onb[:, osl, :], r=[("onb", osl)], a=["oT"])
                                return
                            self.pe(lambda e: e.matmul(ps_bc[:, :], lhsT=onesf[:], rhs=acc[:, m, :], start=True, stop=True), r=[("acc", m), "onesfB"], w=["ps_bc"])
                            self.act(lambda e: e.activation(out=rcp[:], in_=ps_bc[:, :], func=AF.Ln), r=["ps_bc"], w=["rcp"])
                            self.act(lambda e: e.activation(out=rcp[:], in_=rcp[:], func=AF.Exp, scale=-1.0), r=["rcp"], w=["rcp"])
                            self.dve(lambda e: e.tensor_tensor(out=nrm[:, m, :], in0=pos_[0][:, :], in1=rcp[:], op=ALU.mult), r=[po_nm[0], "rcp"], w=[("nrm", m)])
                            if m == 1:
                                self.dve(lambda e: e.scalar_tensor_tensor(out=dd[:], in0=nrm[:, 1, :], scalar=lsm[:, 5:6], in1=nrm[:, 0, :], op0=ALU.mult, op1=ALU.add),
                                         r=[("nrm", 0), ("nrm", 1), "neglam"], w=["dd"])
                                self.act(lambda e: e.activation(out=sqd[:], in_=dd[:], func=AF.Square), r=["dd"], w=["sqd"])
                                self.pe(lambda e: e.matmul(ps_bc[:, :], lhsT=onesf[:], rhs=sqd[:], start=True, stop=True), r=["sqd", "onesfB"], w=["ps_bc"])
                                self.act(lambda e: e.activation(out=rsd[:], in_=ps_bc[:, :], func=AF.Ln, scale=1.0 / 128, bias=self.epsT[:, 0:1]), r=["ps_bc"], w=["rsd"])
                                self.act(lambda e: e.activation(out=rsd[:], in_=rsd[:], func=AF.Exp, scale=-0.5), r=["rsd"], w=["rsd"])
                                osl = c % 2
                                ob = onbd[:, osl, :]
                                self.dve(lambda e, ob=ob: e.scalar_tensor_tensor(out=ob, in0=dd[:], scalar=subs[:, 0:1], in1=rsd[:], op0=ALU.mult, op1=ALU.mult),
                                         r=["dd", "rsd", "subs"], w=[("onbd", osl)])
                                self.ldc(self.oT[h * 128:(h + 1) * 128, c * 512:(c + 1) * 512], ob, r=[("onbd", osl)], a=["oT"])
                        pending[0] = epilogue
                        ecnt += parts
            if pending[0] is not None:
                pending[0]()
            self.S.flush()

    def phaseC1(self, layer):
        sb = self.sb
        nk = 6 if layer == 0 else 8
        xin = self.x if layer == 0 else self.xL
        wsrc = self.e_w_out if layer == 0 else self.o_w_out
        with ExitStack() as es:
            w_out = sb(es, "w_out", [128, nk, D], BF16)
            G1 = sb(es, "G1", [128, D], F32)
            xt = sb(es, "xtc", [128, 2, D], F32)
            oTt = sb(es, "oTt", [128, 2, nk, 128], BF16)
            tmpf = sb(es, "tmpfc", [128, D], F32)
            sq = sb(es, "sqc", [128, D], F32)
            ssx = sb(es, "ssxc", [128, 2], F32)
            xs = sb(es, "xsc", [128, D], BF16)
            hT = sb(es, "hTc", [128, 2, 8, 128], BF16)
            psy = [self.ps(es, "psy%d" % i, [128, 512], F32) for i in range(2)]
            pT = self.ps(es, "pTc", [128, 8, 128], BF16)
            self.load_w(w_out, wsrc, nk * 128, "w_out")
            self.ld(G1[:], self.Gd[:, (layer * 2) * D:(layer * 2 + 1) * D], w=["G1"])
            for t in range(self.nt):
                sl = t % 2
                self.ld(xt[:, sl], xin[t * 128:(t + 1) * 128, :], w=[("xt", sl)])
                self.ld(oTt[:, sl], self.oT[0:nk * 128, t * 128:(t + 1) * 128].rearrange("(k p) s -> p k s", p=128), w=[("oTt", sl)])
                for hf in range(2):
                    def mm(e, hf=hf, sl=sl):
                        for k in range(nk):
                            ins = e.matmul(psy[hf][:], lhsT=oTt[:, sl, k, :], rhs=w_out[:, k, hf * 512:(hf + 1) * 512], start=(k == 0), stop=(k == nk - 1))
                        return ins
                    self.pe(mm, r=[("oTt", sl), "w_out"], w=[("psy", hf)])
                    self.dve(lambda e, hf=hf: e.tensor_tensor(out=tmpf[:, hf * 512:(hf + 1) * 512], in0=psy[hf][:], in1=G1[:, hf * 512:(hf + 1) * 512], op=ALU.mult),
                             r=[("psy", hf), "G1"], w=[("tmpf", hf)])
                    self.dve(lambda e, hf=hf, sl=sl: e.tensor_tensor(out=xt[:, sl, hf * 512:(hf + 1) * 512], in0=xt[:, sl, hf * 512:(hf + 1) * 512],
                                                                    in1=tmpf[:, hf * 512:(hf + 1) * 512], op=ALU.add),
                             r=[("tmpf", hf), ("xt", sl)], w=[("xt", sl)])
                self.ldc(self.x1[t * 128:(t + 1) * 128, :], xt[:, sl], r=[("xt", sl)], a=["x1"])
                self.act(lambda e, sl=sl: e.activation(out=sq[:], in_=xt[:, sl], func=AF.Square, accum_out=ssx[:, 0:1]), r=[("xt", sl)], w=["sq", "ssx"])
                self.rstd(ssx[:, 0:1], ssx[:, 1:2], D, "ssx", "rsx")
                self.act(lambda e, sl=sl: e.activation(out=xs[:], in_=xt[:, sl], func=AF.Copy, scale=ssx[:, 1:2]), r=[("xt", sl), "rsx"], w=["xs"])

                def tpx(e):
                    for j in range(8):
                        ins = e.transpose(pT[:, j, :], xs[:, j * 128:(j + 1) * 128], self.identb[:])
                    return ins
                self.pe(tpx, r=["xs", "identb"], w=["pT"])
                ao = layer * 16 + 8
                a_bc = self.aT[:, ao:ao + 8].unsqueeze(2).to_broadcast([128, 8, 128])
                b_bc = self.modT[:, layer * 48 + 24:layer * 48 + 32].unsqueeze(2).to_broadcast([128, 8, 128])
                tm3 = tmpf[:].rearrange("p (j s) -> p j s", j=8)
                self.dve(lambda e, a_bc=a_bc, tm3=tm3: e.tensor_tensor(out=tm3, in0=pT[:], in1=a_bc, op=ALU.mult),
                         r=["pT", ("aT", ao)], w=[("tmpf", 0), ("tmpf", 1)])
                self.dve(lambda e, b_bc=b_bc, tm3=tm3, sl=sl: e.tensor_tensor(out=hT[:, sl], in0=tm3, in1=b_bc, op=ALU.add),
                         r=[("tmpf", 0), ("tmpf", 1), "modT"], w=[("hT", sl)])
                self.ldc(self.h2T[:, t * 128:(t + 1) * 128].rearrange("(k p) s -> p k s", p=128), hT[:, sl], r=[("hT", sl)], a=["h2T"])
            self.S.flush()

    def phaseC2(self, layer):
        sb = self.sb
        dst = self.xL if layer == 0 else self.out
        ng = max(1, self.nt // 2)
        with ExitStack() as es:
            w1 = sb(es, "w1", [128, 8, DFF], BF16)
            w2 = sb(es, "w2", [128, 32, D], BF16)
            G2 = sb(es, "G2", [128, D], F32)
            hTt = sb(es, "hTt", [128, 2, 8, 256], BF16)
            x1t = sb(es, "x1t", [128, 2, 2, D], F32)
            aT = sb(es, "aTt", [128, 32, 256], BF16)
            rl = sb(es, "rl", [128, 2, 512], F32)
            tmp = sb(es, "tmpc2", [128, 2, 512], F32)
            psa = [self.ps(es, "psa%d" % i, [128, 2, 256], F32) for i in range(2)]
            psy = [self.ps(es, "psy2_%d" % i, [128, 512], F32) for i in range(4)]
            self.load_w(w1, self.mlp_w1[layer], D, "w1")
            self.load_w(w2, self.mlp_w2[layer], DFF, "w2")
            self.ld(G2[:], self.Gd[:, (layer * 2 + 1) * D:(layer * 2 + 2) * D], w=["G2"])
            for g in range(ng):
                sl = g % 2
                self.ld(hTt[:, sl], self.h2T[:, g * 256:(g + 1) * 256].rearrange("(k p) s -> p k s", p=128), w=[("hTt", sl)])
                self.ld(x1t[:, sl], self.x1[g * 256:(g + 1) * 256, :].rearrange("(s p) d -> p s d", p=128), w=[("x1t", sl)])
                for fp in range(16):
                    pb = psa[fp % 2]

                    def mm1(e, fp=fp, pb=pb, sl=sl):
                        for ff in range(2):
                            f = fp * 2 + ff
                            for k in range(8):
                                ins = e.matmul(pb[:, ff, :], lhsT=w1[:, k, f * 128:(f + 1) * 128], rhs=hTt[:, sl, k, :], start=(k == 0), stop=(k == 7))
                        return ins
                    self.pe(mm1, r=["w1", ("hTt", sl)], w=[("psa", fp % 2)])
                    rs_ = fp % 2
                    self.act(lambda e, pb=pb, rs_=rs_: e.activation(out=rl[:, rs_, :], in_=pb[:].rearrange("p a b -> p (a b)"), func=AF.Relu),
                             r=[("psa", fp % 2)], w=[("rl", rs_)])
                    self.dve(lambda e, fp=fp, rs_=rs_: e.tensor_tensor(out=aT[:, fp * 2:fp * 2 + 2, :].rearrange("p a b -> p (a b)"), in0=rl[:, rs_, :], in1=rl[:, rs_, :], op=ALU.mult),
                             r=[("rl", rs_)], w=[("aT", fp)])
                for s_ in range(2):
                    for hf in range(2):
                        pi = s_ * 2 + hf

                        def mm2(e, s_=s_, hf=hf, pi=pi):
                            for f in range(32):
                                ins = e.matmul(psy[pi][:], lhsT=aT[:, f, s_ * 128:(s_ + 1) * 128], rhs=w2[:, f, hf * 512:(hf + 1) * 512], start=(f == 0), stop=(f == 31))
                            return ins
                        self.pe(mm2, r=["w2"] + [("aT", fp) for fp in range(16)], w=[("psy", pi)])
                        self.dve(lambda e, hf=hf, pi=pi: e.tensor_tensor(out=tmp[:, hf, :], in0=psy[pi][:], in1=G2[:, hf * 512:(hf + 1) * 512], op=ALU.mult),
                                 r=[("psy", pi), "G2"], w=[("tmp", hf)])
                        self.dve(lambda e, s_=s_, hf=hf, sl=sl: e.tensor_tensor(out=x1t[:, sl, s_, hf * 512:(hf + 1) * 512], in0=x1t[:, sl, s_, hf * 512:(hf + 1) * 512],
                                                                              in1=tmp[:, hf, :], op=ALU.add),
                                 r=[("tmp", hf), ("x1t", sl)], w=[("x1t", sl)])
                self.ldc(dst[g * 256:(g + 1) * 256, :].rearrange("(s p) d -> p s d", p=128), x1t[:, sl], r=[("x1t", sl)], a=["dst"])
            self.S.flush()


def _consts():
    ident = np.eye(128, dtype=np.float32)
    i16 = np.arange(16, dtype=np.float32) / np.float32(16)
    i32 = np.arange(32, dtype=np.float32) / np.float32(32)
    invf = np.concatenate([np.float32(10000.0) ** (-i16), np.float32(10000.0) ** (-i32)]).astype(np.float32)
    k = np.arange(128)[:, None]
    q = np.arange(512)[None, :]
    cmask = np.stack([(q >= 128 * j + k) for j in range(4)], axis=1).astype(np.float32)
    dm = []
    for (w, r) in ((128, 1), (512, 4), (2048, 16)):
        W = w // 128
        for o in range(W + 4):
            rel = q - k + 128 * (W - o)
            dm.append(((rel >= 0) & (rel <= w) & (rel % r == 0)).astype(np.float32))
    dmask = np.stack(dm, axis=1)
    kind = (np.arange(SEQ)[None, :] // 256 == np.arange(32)[:, None]).astype(np.float32)
    return dict(ident=ident, invf=invf, cmask=cmask.reshape(128, -1), dmask=dmask.reshape(128, -1), kind=kind)


def _colT(v, n):
    return np.ascontiguousarray(np.asarray(v, np.float32).reshape(n, 128).T)


def make_in_maps(inp, batches):
    f = lambda a: np.ascontiguousarray(np.asarray(a, dtype=np.float32))
    c = _consts()
    shared = dict(
        ada_w=f(inp["ada_w"]),
        ada_bT=np.ascontiguousarray(np.concatenate([_colT(inp["ada_b"][l], 48) for l in range(2)], axis=1)),
        nrmT=np.ascontiguousarray(np.concatenate(
            [_colT(inp[nm][l], 8) for l in range(2) for nm in ("norm_mix", "norm_mlp")], axis=1)),
        mlp_w1=f(inp["mlp_w1"]), mlp_w2=f(inp["mlp_w2"]),
        e_w_in=f(inp["even_w_in"][0]), e_w_out=f(inp["even_w_out"][0]),
        latT=np.ascontiguousarray(np.concatenate([_colT(inp["mla_q_lat_norm"][0], 3), _colT(inp["mla_kv_lat_norm"][0], 2)], axis=1)),
        w_uq=f(np.asarray(inp["mla_w_uq"][0]).reshape(384, 768)),
        w_ukv=f(np.asarray(inp["mla_w_ukv"][0]).reshape(256, 1024)),
        gains=f(np.concatenate([np.asarray(inp[k][0], np.float32).reshape(-1) for k in
                                ("mla_q_norm", "mla_k_norm", "dil_q_norm", "dil_k_norm",
                                 "diff_q_norm", "diff_k_norm", "moba_q_norm", "moba_k_norm")])),
        o_w_in=f(inp["odd_w_in"][0]), o_w_out=f(inp["odd_w_out"][0]),
        dlam=f(np.asarray(inp["diff_lambda"][0]).reshape(256)),
        subT=np.ascontiguousarray(np.asarray(inp["diff_subln"][0], np.float32).reshape(128, 1)),
        **c,
    )
    maps = []
    for b in batches:
        m = dict(shared)
        m["x"] = f(inp["x"][b])
        m["posT"] = np.ascontiguousarray(np.asarray(inp["positions"][b], np.int32).reshape(NT, 128).T)
        m["cT"] = _colT(inp["c"][b], 8)
        maps.append(m)
    return maps


def build(nt=NT, phases=None, dbg_out=()):
    phases = ALL_PHASES if phases is None else phases
    nc = bass.Bass("TRN2", target_bir_lowering=False)
    k = K(nc, nt=nt, phases=phases, dbg_out=dbg_out)
    k.declare()
    with ExitStack() as gs:
        k.es = gs
        sems = {}
        for e in Sched.COMPUTE:
            sems[e] = gs.enter_context(nc.semaphore("sem_" + e))
        for q in ("sp", "pq"):
            sems[q] = [gs.enter_context(nc.semaphore("sem_%s%d" % (q, i))) for i in range(k.S.ring)]
        k.S.init_emit(sems)
        k.setup()
        for ph in phases:
            getattr(k, "run_" + ph)()
        stats = k.S.finish()
    return nc, k, stats


def _add_phase_methods():
    K.run_A0 = lambda self: self.phaseA(0)
    K.run_A1 = lambda self: self.phaseA(1)
    K.run_Bmla = lambda self: self.phaseB("mla")
    K.run_C10 = lambda self: self.phaseC1(0)
    K.run_C20 = lambda self: self.phaseC2(0)
    K.run_C11 = lambda self: self.phaseC1(1)
    K.run_C21 = lambda self: self.phaseC2(1)
    K.run_Bdil = lambda self: self.phaseB("dil")
    K.run_Bdiff = lambda self: self.phaseB("diff")
    K.run_Bmoba = lambda self: self.phaseB("moba")


_add_phase_methods()


ALL_PHASES = ("A0", "Bmla", "Bdil", "C10", "C20", "A1", "Bdiff", "Bmoba", "C11", "C21")


def kernel(**inputs):
    nb = int(np.asarray(inputs["x"]).shape[0])
    nc, k, stats = build(nt=NT, phases=ALL_PHASES)
    maps = make_in_maps(inputs, list(range(nb)))
    res = run_bass_kernel_spmd(nc, maps, core_ids=list(range(nb)))
    return np.stack([np.asarray(res.results[b]["out"], dtype=np.float32) for b in range(nb)], axis=0)
```

```python
class _Op:
    __slots__ = ("eng", "fn", "deps", "needs_sig", "sem", "val", "is_dma", "idx", "ring_prev")

    def __init__(self, eng, fn, is_dma):
        self.eng = eng
        self.fn = fn
        self.deps = []
        self.needs_sig = False
        self.sem = None
        self.val = 0
        self.is_dma = is_dma
        self.ring_prev = None


class Sched:
    COMPUTE = ("pe", "act", "dve", "pool")

    def __init__(self, nc, ring=8, same_engine_sync=True):
        self.nc = nc
        self.ops = []
        self.last_w = {}
        self.readers = {}
        self.appenders = {}
        self.ring = ring
        self.same_engine_sync = same_engine_sync
        self.engobj = {"pe": nc.tensor, "act": nc.scalar, "dve": nc.vector, "pool": nc.gpsimd,
                       "sp": nc.sync, "pq": nc.gpsimd}
        self.stream = {"pe": "pe", "act": "act", "dve": "dve", "pool": "pool", "sp": "sp", "pq": "pool"}

    def add(self, eng, fn, r=(), w=(), a=()):
        is_dma = eng in ("sp", "pq")
        op = _Op(eng, fn, is_dma)
        deps = {}

        def dep(p, kind):
            if p is op:
                return
            same = (self.stream[p.eng] == self.stream[eng]) and not p.is_dma
            if same:
                if eng == "pe" or not self.same_engine_sync:
                    return
            deps[id(p)] = p

        for x in r:
            p = self.last_w.get(x)
            if p is not None:
                dep(p, "raw")
            for p in self.appenders.get(x, ()):
                dep(p, "raw")
        for x in list(w) + list(a):
            p = self.last_w.get(x)
            if p is not None:
                dep(p, "waw")
            for p in self.readers.get(x, ()):
                dep(p, "war")
        for x in w:
            for p in self.appenders.get(x, ()):
                dep(p, "waw")
        for x in r:
            self.readers.setdefault(x, []).append(op)
        for x in w:
            self.last_w[x] = op
            self.readers[x] = []
            self.appenders[x] = []
        for x in a:
            self.appenders.setdefault(x, []).append(op)
            self.readers[x] = []
        op.deps = list(deps.values())
        for p in op.deps:
            p.needs_sig = True
        self.ops.append(op)
        return op

    def init_emit(self, sems):
        self.sems = sems
        self.cnt = {e: 0 for e in self.COMPUTE}
        self.dcount = {"sp": 0, "pq": 0}
        self.dhist = {"sp": [], "pq": []}
        self.waited = {}
        self.nwaits = 0
        self.nops = 0
        self.barrier_deps = []

    def flush(self):
        ops = self.ops
        lastc = {}
        for op in ops:
            if not op.is_dma:
                lastc[op.eng] = op
        for op in lastc.values():
            op.needs_sig = True
        bd = self.barrier_deps
        first_seen = set()
        for op in ops:
            st = self.stream[op.eng]
            if st not in first_seen:
                first_seen.add(st)
                op.deps = op.deps + [p for p in bd if not (self.stream[p.eng] == st and not p.is_dma)]
            if op.is_dma:
                i = self.dcount[op.eng]
                self.dcount[op.eng] += 1
                op.sem = self.sems[op.eng][i % self.ring]
                op.val = 16 * (i // self.ring + 1)
                if i >= self.ring:
                    op.ring_prev = self.dhist[op.eng][i - self.ring]
                self.dhist[op.eng].append(op)
            elif op.needs_sig:
                self.cnt[op.eng] += 1
                op.sem = self.sems[op.eng]
                op.val = self.cnt[op.eng]
        waited = self.waited
        for op in ops:
            e = self.engobj[op.eng]
            st = self.stream[op.eng]
            need = {}
            plist = list(op.deps)
            if op.ring_prev is not None:
                plist.append(op.ring_prev)
            for p in plist:
                k = id(p.sem)
                if k not in need or need[k][1] < p.val:
                    need[k] = (p.sem, p.val)
            for k, (sem, val) in need.items():
                if waited.get((st, k), 0) < val:
                    e.wait_ge(sem, val)
                    waited[(st, k)] = val
                    self.nwaits += 1
            inst = op.fn(e)
            if op.is_dma:
                inst.then_inc(op.sem, 16)
            elif op.needs_sig:
                inst.then_inc(op.sem, 1)
        self.nops += len(ops)
        nb = list(lastc.values())
        for p in bd:
            if not p.is_dma and p.eng not in lastc:
                nb.append(p)
        for q in ("sp", "pq"):
            nb.extend(self.dhist[q][-self.ring:])
        self.barrier_deps = nb
        self.ops = []
        self.last_w = {}
        self.readers = {}
        self.appenders = {}

    def finish(self, eng="sp"):
        self.flush()
        e = self.engobj[eng]
        st = self.stream[eng]
        for p in self.barrier_deps:
            k = id(p.sem)
            if self.waited.get((st, k), 0) < p.val:
                e.wait_ge(p.sem, p.val)
                self.waited[(st, k)] = p.val
        return {"n_ops": self.nops, "n_waits": self.nwaits, "sig": dict(self.cnt), "dma": dict(self.dcount)}


import math
from contextlib import ExitStack
import numpy as np
import ml_dtypes
import concourse.bass as bass
import concourse.mybir as mybir
from concourse.bass_utils import run_bass_kernel_spmd

F32 = mybir.dt.float32
BF16 = mybir.dt.bfloat16
I32 = mybir.dt.int32
AF = mybir.ActivationFunctionType
ALU = mybir.AluOpType
AX = mybir.AxisListType

D = 1024
SEQ = 8192
NT = SEQ // 128
NCH = SEQ // 512
EPS = 1e-6
EVEN_IN = 2976
ODD_IN = 3072
DFF = 4096
NEGB = -30000.0
TWO_PI = float(2 * np.pi)


class K:
    def __init__(self, nc, nt=NT, phases=None, dbg_out=()):
        self.nc = nc
        self.S = Sched(nc)
        self.nt = nt
        self.phases = phases
        self.dbg_out = dbg_out
        self.es = ExitStack()
        self.din = {}
        self.dscr = {}
        import os as _os
        self.lim = float(_os.environ.get('KLIM', '99'))

    def inp(self, name, shape, dt=F32):
        t = self.nc.dram_tensor(name, list(shape), dt, kind="ExternalInput").ap()
        self.din[name] = t
        return t

    def scr(self, name, shape, dt):
        kind = "ExternalOutput" if name in self.dbg_out else "Internal"
        t = self.nc.dram_tensor(name, list(shape), dt, kind=kind).ap()
        self.dscr[name] = t
        return t

    def sb(self, es, name, shape, dt):
        self.uid = getattr(self, "uid", 0) + 1
        return es.enter_context(self.nc.sbuf_tensor("%s_%d" % (name, self.uid), list(shape), dt))

    def ps(self, es, name, shape, dt):
        self.uid = getattr(self, "uid", 0) + 1
        return es.enter_context(self.nc.psum_tensor("%s_%d" % (name, self.uid), list(shape), dt))

    def act(self, fn, r=(), w=(), a=()):
        return self.S.add("act", fn, r, w, a)

    def dve(self, fn, r=(), w=(), a=()):
        return self.S.add("dve", fn, r, w, a)

    def pool(self, fn, r=(), w=(), a=()):
        return self.S.add("pool", fn, r, w, a)

    def pe(self, fn, r=(), w=(), a=()):
        return self.S.add("pe", fn, r, w, a)

    def ld(self, out, in_, r=(), w=(), a=(), q="sp"):
        return self.S.add(q, lambda e: e.dma_start(out=out, in_=in_), r, w, a)

    def ldc(self, out, in_, r=(), w=(), a=()):
        return self.S.add("pq", lambda e: e.dma_start(out=out, in_=in_), r, w, a)

    def rstd(self, ss, rs, n, rn_ss, rn_rs):
        self.act(lambda e: e.activation(out=rs, in_=ss, func=AF.Sqrt, scale=1.0 / n, bias=self.eps_ap(ss)),
                 r=[rn_ss], w=[rn_rs])
        self.dve(lambda e: e.reciprocal(out=rs, in_=rs), r=[rn_rs], w=[rn_rs])

    def eps_ap(self, like):
        p = like.shape[0]
        return self.epsT[0:p, 0:1]

    def declare(self):
        i = self.inp
        self.x = i("x", [SEQ, D])
        self.posT = i("posT", [128, NT], I32)
        self.cT = i("cT", [128, 8])
        self.ada_w = i("ada_w", [2, D, 6 * D])
        self.ada_bT = i("ada_bT", [128, 96])
        self.nrmT = i("nrmT", [128, 32])
        self.mlp_w1 = i("mlp_w1", [2, D, DFF])
        self.mlp_w2 = i("mlp_w2", [2, DFF, D])
        self.e_w_in = i("e_w_in", [D, EVEN_IN])
        self.e_w_out = i("e_w_out", [768, D])
        self.latT = i("latT", [128, 5])
        self.w_uq = i("w_uq", [384, 768])
        self.w_ukv = i("w_ukv", [256, 1024])
        self.gains = i("gains", [576])
        self.o_w_in = i("o_w_in", [D, ODD_IN])
        self.o_w_out = i("o_w_out", [D, D])
        self.dlam = i("dlam", [256])
        self.subT = i("subT", [128, 1])
        self.ident = i("ident", [128, 128])
        self.invf = i("invf", [48])
        self.cmask = i("cmask", [128, 4 * 512])
        self.dmask = i("dmask", [128, 33 * 512])
        self.kind = i("kind", [32, SEQ])
        self.out = self.nc.dram_tensor("out", [SEQ, D], F32, kind="ExternalOutput").ap()
        s = self.scr
        self.trigd = s("trigd", [128, 2 * NT * 48], F32)
        self.Gd = s("Gd", [128, 4 * D], F32)
        self.x1 = s("x1", [SEQ, D], F32)
        self.xL = s("xL", [SEQ, D], F32)
        self.h2T = s("h2T", [D, SEQ], BF16)
        self.oT = s("oT", [D, SEQ], BF16)
        self.qT_mla = s("qT_mla", [8, 96, SEQ], BF16)
        self.kT_mla = s("kT_mla", [8, 96, SEQ], BF16)
        self.v_mla = s("v_mla", [SEQ, 512], BF16)
        self.qT_dil = s("qT_dil", [12, 64, SEQ], BF16)
        self.kT_dil = s("kT_dil", [12, 64, SEQ], BF16)
        self.v_dil = s("v_dil", [SEQ, 768], BF16)
        self.qT_df = s("qT_df", [8, 64, SEQ], BF16)
        self.kT_df = s("kT_df", [8, 64, SEQ], BF16)
        self.v_df = s("v_df", [SEQ, 512], BF16)
        self.qT_mb = s("qT_mb", [8, 96, SEQ], BF16)
        self.kT_mb = s("kT_mb", [8, 64, SEQ], BF16)
        self.v_mb = s("v_mb", [SEQ, 512], BF16)

    def setup(self):
        nc, S = self.nc, self.S
        g = self.es
        sb = self.sb
        self.epsT = sb(g, "epsT", [128, 1], F32)
        self.identb = sb(g, "identb", [128, 128], BF16)
        self.modT = sb(g, "modT", [128, 96], F32)
        self.aT = sb(g, "aT", [128, 32], F32)
        self.gbc = sb(g, "gbc", [128, 576], F32)
        self.cmb = sb(g, "cmb", [128, 4, 512], BF16)
        self.onesb = sb(g, "onesb", [128, 64], F32)
        self.pool(lambda e: e.memset(self.epsT[:], EPS), w=["epsT"])
        self.pool(lambda e: e.memset(self.onesb[:], 1.0), w=["onesb"])
        self.ldc(self.identb[:], self.ident, w=["identb"])
        self.ldc(self.cmb[:], self.cmask.rearrange("p (a b) -> p a b", a=4), w=["cmb"])
        self.ld(self.gbc[:], self.gains.partition_broadcast(128), w=["gbc"])
        with ExitStack() as es:
            self.trig = sb(es, "trig", [128, 2, NT, 48], F32)
            self.G = sb(es, "G", [128, 4, D], F32)
            condT = sb(es, "condT", [128, 8], F32)
            bT = sb(es, "bT", [128, 96], F32)
            nT = sb(es, "nT", [128, 32], F32)
            stage = sb(es, "adastage", [128, 2, 8, 1024], F32)
            posi = sb(es, "posi", [128, NT], I32)
            posf = sb(es, "posf", [128, NT], F32)
            invb = sb(es, "invb", [128, 48], F32)
            kf = sb(es, "kf", [128, 2, NT, 48], F32)
            ki = sb(es, "ki", [128, 2, NT, 48], I32)
            psmod = self.ps(es, "psmod", [128, 96], F32)
            self.ld(condT[:], self.cT, w=["condT"])
            self.ld(bT[:], self.ada_bT, w=["bT"])
            self.ld(nT[:], self.nrmT, w=["nT"])
            self.ld(posi[:], self.posT, w=["posi"])
            self.ld(invb[:], self.invf.partition_broadcast(128), w=["invb"])
            self.act(lambda e: e.activation(out=condT[:], in_=condT[:], func=AF.Silu), r=["condT"], w=["condT"])
            tr = self.trig
            self.dve(lambda e: e.tensor_copy(out=posf[:], in_=posi[:]), r=["posi"], w=["posf"])
            self.dve(lambda e: e.tensor_tensor(out=tr[:, 0], in0=posf[:].unsqueeze(2).to_broadcast([128, NT, 48]),
                                               in1=invb[:].unsqueeze(1).to_broadcast([128, NT, 48]), op=ALU.mult),
                     r=["posf", "invb"], w=["trig"])
            self.dve(lambda e: e.tensor_scalar(out=tr[:, 1], in0=tr[:, 0], scalar1=float(np.pi / 2), scalar2=None,
                                               op0=ALU.add), r=["trig"], w=["trig"])
            self.dve(lambda e: e.tensor_scalar(out=kf[:], in0=tr[:], scalar1=float(1 / TWO_PI), scalar2=None,
                                               op0=ALU.mult), r=["trig"], w=["kf"])
            self.dve(lambda e: e.tensor_copy(out=ki[:], in_=kf[:]), r=["kf"], w=["ki"])
            self.dve(lambda e: e.tensor_copy(out=kf[:], in_=ki[:]), r=["ki"], w=["kf"])
            self.dve(lambda e: e.scalar_tensor_tensor(out=tr[:], in0=kf[:], scalar=-TWO_PI, in1=tr[:],
                                                      op0=ALU.mult, op1=ALU.add), r=["kf", "trig"], w=["trig"])
            self.dve(lambda e: e.tensor_scalar(out=kf[:], in0=tr[:], scalar1=float(np.pi), scalar2=-TWO_PI,
                                               op0=ALU.is_gt, op1=ALU.mult), r=["trig"], w=["kf"])
            self.dve(lambda e: e.tensor_tensor(out=tr[:], in0=tr[:], in1=kf[:], op=ALU.add), r=["kf", "trig"], w=["trig"])
            self.dve(lambda e: e.tensor_scalar(out=kf[:], in0=tr[:], scalar1=float(-np.pi), scalar2=TWO_PI,
                                               op0=ALU.is_lt, op1=ALU.mult), r=["trig"], w=["kf"])
            self.dve(lambda e: e.tensor_tensor(out=tr[:], in0=tr[:], in1=kf[:], op=ALU.add), r=["kf", "trig"], w=["trig"])
            self.act(lambda e: e.activation(out=tr[:], in_=tr[:], func=AF.Sin), r=["trig"], w=["trig"])
            for l in range(2):
                for cb in range(6):
                    sl = (l * 6 + cb) % 2
                    self.ld(stage[:, sl], self.ada_w[l, :, cb * 1024:(cb + 1) * 1024].rearrange("(k p) n -> p k n", p=128),
                            w=[("adast", sl)])
                    for jj in range(8):
                        col = l * 48 + cb * 8 + jj

                        def mm(e, sl=sl, jj=jj, col=col):
                            for k in range(8):
                                ins = e.matmul(psmod[:, col:col + 1], lhsT=stage[:, sl, k, jj * 128:(jj + 1) * 128],
                                               rhs=condT[:, k:k + 1], start=(k == 0), stop=(k == 7))
                            return ins
                        self.pe(mm, r=[("adast", sl), "condT"], a=["psmod"])
            self.dve(lambda e: e.tensor_tensor(out=self.modT[:], in0=psmod[:], in1=bT[:], op=ALU.add),
                     r=["psmod", "bT"], w=["modT"])
            for l in range(2):
                m = self.modT[:, l * 48:(l + 1) * 48]
                for which, (sc0, sh0) in enumerate(((8, 0), (32, 24))):
                    o = l * 16 + which * 8
                    nsl = nT[:, l * 16 + which * 8: l * 16 + which * 8 + 8]
                    self.dve(lambda e, o=o, m=m, sc0=sc0, nsl=nsl: e.scalar_tensor_tensor(
                        out=self.aT[:, o:o + 8], in0=m[:, sc0:sc0 + 8], scalar=1.0, in1=nsl, op0=ALU.add, op1=ALU.mult),
                        r=["modT", "nT"], w=[("aT", o)])
            identf = sb(es, "identf", [128, 128], F32)
            onesf = sb(es, "onesf", [128, 128], F32)
            dg = sb(es, "dg", [128, 2, 128], F32)
            psg_ = [self.ps(es, "psgate%d" % i, [128, 512], F32) for i in range(2)]
            self.ld(identf[:], self.ident, w=["identf"])
            self.pool(lambda e: e.memset(onesf[:], 1.0), w=["onesf"])
            cnt = 0
            for l in range(2):
                for which, off in enumerate((16, 40)):
                    gi = l * 2 + which
                    for half in range(2):
                        pb = psg_[cnt % 2]
                        for jj in range(4):
                            j = half * 4 + jj
                            col = l * 48 + off + j
                            sl = (cnt * 4 + jj) % 2
                            self.dve(lambda e, sl=sl, col=col: e.tensor_scalar(out=dg[:, sl, :], in0=identf[:], scalar1=self.modT[:, col:col + 1],
                                                                               scalar2=None, op0=ALU.mult), r=["identf", "modT"], w=[("dg", sl)])
                            self.pe(lambda e, sl=sl, jj=jj, pb=pb: e.matmul(pb[:, jj * 128:(jj + 1) * 128], lhsT=onesf[:], rhs=dg[:, sl, :], start=True, stop=True),
                                    r=[("dg", sl), "onesf"], w=[("psgate", cnt % 2, jj)])
                        self.act(lambda e, gi=gi, half=half, pb=pb: e.activation(out=self.G[:, gi, half * 512:(half + 1) * 512], in_=pb[:], func=AF.Copy),
                                 r=[("psgate", cnt % 2, jj) for jj in range(4)], w=[("G", gi, half)])
                        cnt += 1
            self.ldc(self.trigd.rearrange("p (a b) -> p a b", a=2 * NT), self.trig[:].rearrange("p s t f -> p (s t) f"), r=["trig"], w=["trigd"])
            self.ldc(self.Gd.rearrange("p (a b) -> p a b", a=4), self.G[:], r=[("G", gi, hf) for gi in range(4) for hf in range(2)], w=["Gd"])
            self.S.flush()

    def head_post(self, nm, src, nh, hd, goff, rope_lo, half, tg, tg_nm, sq, sq_nm, ssh, ssh_nm, rt, rt_nm, dst_bf, nm_bf):
        n = nh * hd
        nms = list(nm) if isinstance(nm, list) else [nm]
        s3 = src.rearrange("p (h d) -> p h d", h=nh)
        sq2 = sq[:, 0:n]
        self.act(lambda e: e.activation(out=sq2, in_=src, func=AF.Square), r=nms, w=[sq_nm])
        yield
        self.dve(lambda e: e.tensor_reduce(out=ssh[:, 0:nh], in_=sq2.rearrange("p (h d) -> p h d", h=nh), axis=AX.X, op=ALU.add),
                 r=[sq_nm], w=[ssh_nm])
        yield
        self.act(lambda e: e.activation(out=ssh[:, 0:nh], in_=ssh[:, 0:nh], func=AF.Sqrt, scale=1.0 / hd, bias=self.epsT[:, 0:1]),
                 r=[ssh_nm], w=[ssh_nm])
        yield
        self.dve(lambda e: e.reciprocal(out=ssh[:, 0:nh], in_=ssh[:, 0:nh]), r=[ssh_nm], w=[ssh_nm])
        yield
        self.dve(lambda e: e.tensor_tensor(out=s3, in0=s3, in1=ssh[:, 0:nh].unsqueeze(2).to_broadcast([128, nh, hd]), op=ALU.mult),
                 r=nms + [ssh_nm], w=nms)
        yield
        gb = self.gbc[:, goff:goff + hd]
        self.dve(lambda e: e.tensor_tensor(out=s3, in0=s3, in1=gb.unsqueeze(1).to_broadcast([128, nh, hd]), op=ALU.mult),
                 r=nms + ["gbc"], w=nms)
        yield
        fo = 0 if half == 16 else 16
        sin = tg[:, 0, fo:fo + half].unsqueeze(1).to_broadcast([128, nh, half])
        cos = tg[:, 1, fo:fo + half].unsqueeze(1).to_broadcast([128, nh, half])
        x1 = s3[:, :, rope_lo:rope_lo + half]
        x2 = s3[:, :, rope_lo + half:rope_lo + 2 * half]
        m = nh * half
        tA = rt[:, 0, 0:m].rearrange("p (h d) -> p h d", h=nh)
        tB = rt[:, 1, 0:m].rearrange("p (h d) -> p h d", h=nh)
        tC = rt[:, 2, 0:m].rearrange("p (h d) -> p h d", h=nh)
        tD = rt[:, 3, 0:m].rearrange("p (h d) -> p h d", h=nh)
        rA, rB, rC, rD = [(rt_nm, i) for i in range(4)]
        self.dve(lambda e: e.tensor_tensor(out=tA, in0=x1, in1=cos, op=ALU.mult), r=nms + [tg_nm], w=[rA])
        self.dve(lambda e: e.tensor_tensor(out=tB, in0=x2, in1=sin, op=ALU.mult), r=nms + [tg_nm], w=[rB])
        yield
        self.dve(lambda e: e.tensor_tensor(out=tC, in0=x2, in1=cos, op=ALU.mult), r=nms + [tg_nm], w=[rC])
        self.dve(lambda e: e.tensor_tensor(out=tD, in0=x1, in1=sin, op=ALU.mult), r=nms + [tg_nm], w=[rD])
        yield
        self.dve(lambda e: e.tensor_tensor(out=x1, in0=tA, in1=tB, op=ALU.subtract), r=[rA, rB], w=nms)
        yield
        self.dve(lambda e: e.tensor_tensor(out=x2, in0=tC, in1=tD, op=ALU.add), r=[rC, rD], w=nms)
        yield
        self.act(lambda e: e.activation(out=dst_bf, in_=src, func=AF.Copy), r=nms, w=[nm_bf])
        yield

    def tr_store(self, nm_bf, src_bf, nh, hd, pstr, pcnt, stg, stg_nm, dram, t, hw=None, row0=0):
        hw = hw or hd
        s3 = src_bf.rearrange("p (h d) -> p h d", h=nh)
        done = 0
        while done < nh:
            nb = min(8, nh - done)
            ps = pstr[pcnt[0] % 2]
            rn = ("pstr", pcnt[0] % 2)
            pcnt[0] += 1

            def tp(e, done=done, nb=nb, ps=ps):
                for i in range(nb):
                    ins = e.transpose(ps[0:hw, i, :], s3[:, done + i, 0:hw], self.identb[:])
                return ins
            self.pe(tp, r=[nm_bf, "identb"], w=[rn])
            self.dve(lambda e, done=done, nb=nb, ps=ps: e.tensor_copy(out=stg[0:hw, done:done + nb, :], in_=ps[0:hw, 0:nb, :]),
                     r=[rn], w=[(stg_nm, done)])
            yield
            dst = dram[done:done + nb, row0:row0 + hw, t * 128:(t + 1) * 128].rearrange("h d s -> d h s")
            self.ldc(dst, stg[0:hw, done:done + nb, :], r=[(stg_nm, done)], a=[("dram", id(dram))])
            done += nb

    def load_w(self, dst, src, K, rn, n0=0, n1=None, d0=0):
        n1 = n1 if n1 is not None else src.shape[1]
        kc = K // 128
        step = 2 if (n1 - n0) > 1024 else kc
        for k0 in range(0, kc, step):
            k1 = min(kc, k0 + step)
            self.ldc(dst[:, k0:k1, d0:d0 + (n1 - n0)],
                     src[k0 * 128:k1 * 128, n0:n1].rearrange("(k p) n -> p k n", p=128), a=[rn])

    def phaseA(self, layer):
        sb = self.sb
        NI = 2
        ncol = EVEN_IN if layer == 0 else ODD_IN
        xin = self.x if layer == 0 else self.xL
        with ExitStack() as es:
            w_in = sb(es, "w_in", [128, 8, ncol], BF16)
            trg = sb(es, "trg", [128, NI, 2, 48], F32)
            xt = sb(es, "xt", [128, NI, D], F32)
            xs = sb(es, "xs", [128, NI, D], BF16)
            ssx = sb(es, "ssx", [128, NI, 4], F32)
            hT = sb(es, "hT", [128, NI, 8, 128], BF16)
            tmpf = sb(es, "tmpf", [128, NI, D], F32)
            u = sb(es, "u", [128, NI, ncol], F32)
            sq = sb(es, "sq", [128, NI, D], F32)
            ssh = sb(es, "ssh", [128, NI, 16], F32)
            rt = sb(es, "rt", [128, NI, 4, 512], F32)
            qbf = sb(es, "qbf", [128, NI, 2048], BF16)
            vbf = sb(es, "vbf", [128, NI, 1024], BF16)
            stg = [sb(es, "stg%d" % i, [96, NI, 12, 128], BF16) for i in range(4)]
            pT = self.ps(es, "pT", [128, 8, 128], BF16)
            psu = [self.ps(es, "psu%d" % i, [128, 512], F32) for i in range(2)]
            pstr = [self.ps(es, "pstr%d" % i, [128, 8, 128], BF16) for i in range(2)]
            self.load_w(w_in, self.e_w_in if layer == 0 else self.o_w_in, D, "w_in")
            if layer == 0:
                w_uq = sb(es, "w_uqb", [128, 3, 768], BF16)
                w_ukv = sb(es, "w_ukvb", [128, 2, 1024], BF16)
                latTs = sb(es, "latTs", [128, 5], F32)
                latb = sb(es, "latb", [128, NI, 640], BF16)
                latTt = sb(es, "latTt", [128, NI, 5, 128], BF16)
                qm = sb(es, "qm", [128, NI, 768], F32)
                kf_ = sb(es, "kfull", [128, NI, 768], F32)
                psup = [self.ps(es, "psup%d" % i, [128, 512], F32) for i in range(2)]
                self.ld(latTs[:], self.latT, w=["latTs"])
                self.ldc(w_uq[:], self.w_uq.rearrange("(k p) n -> p k n", p=128), w=["w_uq"])
                self.ldc(w_ukv[:], self.w_ukv.rearrange("(k p) n -> p k n", p=128), w=["w_ukv"])
                self.dve(lambda e: e.tensor_tensor(out=w_uq[:], in0=w_uq[:], in1=latTs[:, 0:3].unsqueeze(2).to_broadcast([128, 3, 768]), op=ALU.mult),
                         r=["w_uq", "latTs"], w=["w_uq"])
                self.dve(lambda e: e.tensor_tensor(out=w_ukv[:], in0=w_ukv[:], in1=latTs[:, 3:5].unsqueeze(2).to_broadcast([128, 2, 1024]), op=ALU.mult),
                         r=["w_ukv", "latTs"], w=["w_ukv"])
            else:
                kmacc = sb(es, "kmacc", [64, 8, 32], F32)
                kmb = sb(es, "kmb", [64, 8, 32], BF16)
                kpart = sb(es, "kpart", [64, NI, 8], F32)
                gsb = sb(es, "gsb", [128, NI, 8, 32], F32)
                top8 = sb(es, "top8", [128, NI, 8, 8], F32)
                biasf = sb(es, "biasf", [128, NI, 8, 32], F32)
                biasb = sb(es, "biasb", [128, NI, 8, 32], BF16)
                psg = self.ps(es, "psg", [128, 8, 32], F32)
                self.pool(lambda e: e.memset(gsb[:], -1e30), w=[("gsb", i) for i in range(NI)])
                self.pool(lambda e: e.memset(kmacc[:], 0.0), w=["kmacc"])
            pcnt = [0]
            trg_d = self.trigd.rearrange("p (s t f) -> p s t f", s=2, t=NT)

            def body(t, sl):
                R = lambda nm: (nm, sl)
                xtt = xt[:, sl]
                u_ = u[:, sl]
                sq_ = sq[:, sl]
                ssx_ = ssx[:, sl]
                ssh_ = ssh[:, sl]
                rt_ = rt[:, sl]
                qbf_ = qbf[:, sl]
                vbf_ = vbf[:, sl]
                tg = trg[:, sl]
                self.ld(xtt, xin[t * 128:(t + 1) * 128, :], w=[R("xt")])
                self.ld(tg, trg_d[:, :, t, :], w=[R("tg")])
                yield
                self.act(lambda e: e.activation(out=sq_, in_=xtt, func=AF.Square, accum_out=ssx_[:, 0:1]), r=[R("xt")], w=[R("sq"), R("ssx")])
                yield
                self.act(lambda e: e.activation(out=ssx_[:, 1:2], in_=ssx_[:, 0:1], func=AF.Sqrt, scale=1.0 / D, bias=self.epsT[:, 0:1]), r=[R("ssx")], w=[R("rsx")])
                yield
                self.dve(lambda e: e.reciprocal(out=ssx_[:, 1:2], in_=ssx_[:, 1:2]), r=[R("rsx")], w=[R("rsx")])
                yield
                self.act(lambda e: e.activation(out=xs[:, sl], in_=xtt, func=AF.Copy, scale=ssx_[:, 1:2]), r=[R("xt"), R("rsx")], w=[R("xs")])
                yield

                def tpx(e):
                    for j in range(8):
                        ins = e.transpose(pT[:, j, :], xs[:, sl, j * 128:(j + 1) * 128], self.identb[:])
                    return ins
                self.pe(tpx, r=[R("xs"), "identb"], w=["pT"])
                ao = layer * 16
                a_bc = self.aT[:, ao:ao + 8].unsqueeze(2).to_broadcast([128, 8, 128])
                b_bc = self.modT[:, layer * 48:layer * 48 + 8].unsqueeze(2).to_broadcast([128, 8, 128])
                tm3 = tmpf[:, sl].rearrange("p (j s) -> p j s", j=8)
                self.dve(lambda e: e.tensor_tensor(out=tm3, in0=pT[:], in1=a_bc, op=ALU.mult), r=["pT", ("aT", ao)], w=[R("tmpf")])
                yield
                self.dve(lambda e: e.tensor_tensor(out=hT[:, sl], in0=tm3, in1=b_bc, op=ALU.add), r=[R("tmpf"), "modT"], w=[R("hT")])
                yield
                ngrp = (ncol + 511) // 512
                for gi, c0 in enumerate(range(0, ncol, 512)):
                    c1 = min(ncol, c0 + 512)
                    pb = psu[gi % 2]

                    def mm(e, c0=c0, c1=c1, pb=pb):
                        for j in range(8):
                            ins = e.matmul(pb[:, 0:c1 - c0], lhsT=hT[:, sl, j, :], rhs=w_in[:, j, c0:c1], start=(j == 0), stop=(j == 7))
                        return ins
                    self.pe(mm, r=[R("hT"), "w_in"], w=[("psu", gi % 2)])
                    self.act(lambda e, c0=c0, c1=c1, pb=pb: e.activation(out=u_[:, c0:c1], in_=pb[:, 0:c1 - c0], func=AF.Copy),
                             r=[("psu", gi % 2)], w=[("u", sl, gi)])
                    yield
                uall = [("u", sl, gi) for gi in range(ngrp)]
                U = lambda gi: ("u", sl, gi)
                if layer == 0:
                    latb_ = latb[:, sl]
                    qm_ = qm[:, sl]
                    kfs = kf_[:, sl]
                    self.act(lambda e: e.activation(out=sq_[:, 0:384], in_=u_[:, 0:384], func=AF.Square, accum_out=ssx_[:, 2:3]), r=[U(0)], w=[R("sq"), R("ssl")])
                    self.act(lambda e: e.activation(out=sq_[:, 384:640], in_=u_[:, 384:640], func=AF.Square, accum_out=ssx_[:, 3:4]), r=[U(0), U(1), R("sq")], w=[R("sq"), R("ssl2")])
                    yield
                    self.act(lambda e: e.activation(out=ssx_[:, 2:3], in_=ssx_[:, 2:3], func=AF.Sqrt, scale=1.0 / 384, bias=self.epsT[:, 0:1]), r=[R("ssl")], w=[R("ssl")])
                    self.act(lambda e: e.activation(out=ssx_[:, 3:4], in_=ssx_[:, 3:4], func=AF.Sqrt, scale=1.0 / 256, bias=self.epsT[:, 0:1]), r=[R("ssl2")], w=[R("ssl2")])
                    yield
                    self.dve(lambda e: e.reciprocal(out=ssx_[:, 2:4], in_=ssx_[:, 2:4]), r=[R("ssl"), R("ssl2")], w=[R("ssl"), R("ssl2")])
                    yield
                    self.dve(lambda e: e.tensor_scalar(out=latb_[:, 0:384], in0=u_[:, 0:384], scalar1=ssx_[:, 2:3], scalar2=None, op0=ALU.mult), r=[U(0), R("ssl")], w=[R("latb0")])
                    self.dve(lambda e: e.tensor_scalar(out=latb_[:, 384:640], in0=u_[:, 384:640], scalar1=ssx_[:, 3:4], scalar2=None, op0=ALU.mult), r=[U(0), U(1), R("ssl2")], w=[R("latb1")])
                    yield
                    ps = pstr[pcnt[0] % 2]
                    rn = ("pstr", pcnt[0] % 2)
                    pcnt[0] += 1

                    def tpl(e, ps=ps):
                        for j in range(5):
                            ins = e.transpose(ps[:, j, :], latb_[:, j * 128:(j + 1) * 128], self.identb[:])
                        return ins
                    self.pe(tpl, r=[R("latb0"), R("latb1"), "identb"], w=[rn])
                    self.dve(lambda e, ps=ps: e.tensor_copy(out=latTt[:, sl], in_=ps[:, 0:5, :]), r=[rn], w=[R("latTt")])
                    yield
                    for gi, (c0, c1) in enumerate(((0, 512), (512, 768))):
                        def mmq(e, c0=c0, c1=c1, gi=gi):
                            for j in range(3):
                                ins = e.matmul(psup[gi][:, 0:c1 - c0], lhsT=latTt[:, sl, j, :], rhs=w_uq[:, j, c0:c1], start=(j == 0), stop=(j == 2))
                            return ins
                        self.pe(mmq, r=[R("latTt"), "w_uq"], w=[("psup", gi)])
                        self.act(lambda e, c0=c0, c1=c1, gi=gi: e.activation(out=qm_[:, c0:c1], in_=psup[gi][:, 0:c1 - c0], func=AF.Copy),
                                 r=[("psup", gi)], w=[R("qm")] if gi == 0 else [], a=[] if gi == 0 else [R("qm")])
                        yield
                    kf3 = kfs.rearrange("p (h d) -> p h d", h=8)
                    vb3 = vbf_[:, 0:512].rearrange("p (h d) -> p h d", h=8)
                    for gi in range(2):
                        def mmk(e, gi=gi):
                            for j in range(2):
                                ins = e.matmul(psup[gi][:], lhsT=latTt[:, sl, 3 + j, :], rhs=w_ukv[:, j, gi * 512:(gi + 1) * 512], start=(j == 0), stop=(j == 1))
                            return ins
                        self.pe(mmk, r=[R("latTt"), "w_ukv"], w=[("psup", gi)])
                        p3 = psup[gi][:].rearrange("p (h d) -> p h d", h=4)
                        self.act(lambda e, gi=gi, p3=p3: e.activation(out=kf3[:, gi * 4:gi * 4 + 4, 0:64], in_=p3[:, :, 0:64], func=AF.Copy),
                                 r=[("psup", gi)], w=[R("kfull")] if gi == 0 else [], a=[] if gi == 0 else [R("kfull")])
                        self.act(lambda e, gi=gi, p3=p3: e.activation(out=vb3[:, gi * 4:gi * 4 + 4, :], in_=p3[:, :, 64:128], func=AF.Copy),
                                 r=[("psup", gi)], w=[R("vbf")] if gi == 0 else [], a=[] if gi == 0 else [R("vbf")])
                        yield
                    self.dve(lambda e: e.tensor_copy(out=kf3[:, :, 64:96], in_=u_[:, 640:672].unsqueeze(1).to_broadcast([128, 8, 32])), r=[U(1)], a=[R("kfull")])
                    yield
                    yield from self.head_post(R("qm"), qm_, 8, 96, 0, 64, 16, tg, R("tg"), sq_, R("sq"), ssh_, R("ssh"), rt_, R("rt"), qbf_[:, 0:768], R("qbf0"))
                    yield from self.tr_store(R("qbf0"), qbf_[:, 0:768], 8, 96, pstr, pcnt, stg[0][:, sl], R("stg0"), self.qT_mla, t)
                    yield from self.head_post(R("kfull"), kfs, 8, 96, 96, 64, 16, tg, R("tg"), sq_, R("sq"), ssh_, R("ssh"), rt_, R("rt"), qbf_[:, 768:1536], R("qbf1"))
                    yield from self.tr_store(R("qbf1"), qbf_[:, 768:1536], 8, 96, pstr, pcnt, stg[1][:, sl], R("stg1"), self.kT_mla, t)
                    self.ldc(self.v_mla[t * 128:(t + 1) * 128, :], vbf_[:, 0:512], r=[R("vbf")], a=["v_mla"])
                    yield
                    yield from self.head_post([U(1), U(2)], u_[:, 672:1440], 12, 64, 192, 0, 32, tg, R("tg"), sq_, R("sq"), ssh_, R("ssh"), rt_, R("rt"), qbf_[:, 0:768], R("qbf0"))
                    yield from self.tr_store(R("qbf0"), qbf_[:, 0:768], 12, 64, pstr, pcnt, stg[2][:, sl], R("stg2"), self.qT_dil, t)
                    yield from self.head_post([U(2), U(3), U(4)], u_[:, 1440:2208], 12, 64, 256, 0, 32, tg, R("tg"), sq_, R("sq"), ssh_, R("ssh"), rt_, R("rt"), qbf_[:, 768:1536], R("qbf1"))
                    yield from self.tr_store(R("qbf1"), qbf_[:, 768:1536], 12, 64, pstr, pcnt, stg[3][:, sl], R("stg3"), self.kT_dil, t)
                    self.act(lambda e: e.activation(out=vbf_[:, 0:768], in_=u_[:, 2208:2976], func=AF.Copy), r=uall, w=[R("vbf")])
                    yield
                    self.ldc(self.v_dil[t * 128:(t + 1) * 128, :], vbf_[:, 0:768], r=[R("vbf")], a=["v_dil"])
                    yield
                else:
                    yield from self.head_post(U(0), u_[:, 0:512], 8, 64, 320, 0, 32, tg, R("tg"), sq_, R("sq"), ssh_, R("ssh"), rt_, R("rt"), qbf_[:, 0:512], R("qbfA"))
                    yield from self.tr_store(R("qbfA"), qbf_[:, 0:512], 8, 64, pstr, pcnt, stg[0][:, sl], R("stg0"), self.qT_df, t)
                    yield from self.head_post(U(1), u_[:, 512:1024], 8, 64, 384, 0, 32, tg, R("tg"), sq_, R("sq"), ssh_, R("ssh"), rt_, R("rt"), qbf_[:, 512:1024], R("qbfB"))
                    yield from self.tr_store(R("qbfB"), qbf_[:, 512:1024], 8, 64, pstr, pcnt, stg[1][:, sl], R("stg1"), self.kT_df, t)
                    self.act(lambda e: e.activation(out=vbf_[:, 0:512], in_=u_[:, 1024:1536], func=AF.Copy), r=uall, w=[R("vbf")])
                    yield
                    self.ldc(self.v_df[t * 128:(t + 1) * 128, :], vbf_[:, 0:512], r=[R("vbf")], a=["v_df"])
                    self.act(lambda e: e.activation(out=vbf_[:, 512:1024], in_=u_[:, 2560:3072], func=AF.Copy), r=uall, w=[R("vbf2")])
                    yield
                    self.ldc(self.v_mb[t * 128:(t + 1) * 128, :], vbf_[:, 512:1024], r=[R("vbf2")], a=["v_mb"])
                    yield
                    yield from self.head_post(U(4), u_[:, 2048:2560], 8, 64, 512, 0, 32, tg, R("tg"), sq_, R("sq"), ssh_, R("ssh"), rt_, R("rt"), qbf_[:, 1024:1536], R("qbfC"))
                    yield from self.tr_store(R("qbfC"), qbf_[:, 1024:1536], 8, 64, pstr, pcnt, stg[2][:, sl], R("stg2"), self.kT_mb, t)
                    nblk = t // 2
                    kp = kpart[:, sl]
                    self.dve(lambda e: e.tensor_reduce(out=kp, in_=stg[2][0:64, sl, 0:8, :], axis=AX.X, op=ALU.add), r=[(R("stg2"), 0)], w=[R("kpart")])
                    yield
                    self.dve(lambda e: e.tensor_tensor(out=kmacc[:, :, nblk], in0=kmacc[:, :, nblk], in1=kp, op=ALU.add), r=[R("kpart"), "kmacc"], w=["kmacc"])
                    yield
                    if t % 2 == 1:
                        self.act(lambda e: e.activation(out=kmb[:, :, nblk], in_=kmacc[:, :, nblk], func=AF.Copy, scale=1.0 / 256), r=["kmacc"], a=["kmb"])
                        yield
                    yield from self.head_post(U(3), u_[:, 1536:2048], 8, 64, 448, 0, 32, tg, R("tg"), sq_, R("sq"), ssh_, R("ssh"), rt_, R("rt"), qbf_[:, 1536:2048], R("qbfD"))
                    yield from self.tr_store(R("qbfD"), qbf_[:, 1536:2048], 8, 64, pstr, pcnt, stg[3][:, sl], R("stg3"), self.qT_mb, t)
                    gs_ = gsb[:, sl]
                    t8 = top8[:, sl]
                    bf_ = biasf[:, sl]
                    bb_ = biasb[:, sl]
                    if nblk > 0:
                        def mmg(e):
                            for h in range(8):
                                ins = e.matmul(psg[:, h, 0:nblk], lhsT=stg[3][0:64, sl, h, :], rhs=kmb[:, h, 0:nblk], start=True, stop=True)
                            return ins
                        self.pe(mmg, r=[(R("stg3"), 0), "kmb"], w=["psg"])
                        self.dve(lambda e: e.tensor_copy(out=gs_[:, :, 0:nblk], in_=psg[:, :, 0:nblk]), r=["psg"], w=[R("gsb")])
                        yield

                        def mx(e):
                            for h in range(8):
                                ins = e.max(out=t8[:, h, :], in_=gs_[:, h, :])
                            return ins
                        self.dve(mx, r=[R("gsb")], w=[R("top8")])
                        yield
                        self.dve(lambda e: e.tensor_tensor(out=bf_, in0=gs_, in1=t8[:, :, 2:3].to_broadcast([128, 8, 32]), op=ALU.is_lt), r=[R("gsb"), R("top8")], w=[R("biasf")])
                        yield
                        self.dve(lambda e: e.tensor_scalar(out=bb_, in0=bf_, scalar1=NEGB, scalar2=None, op0=ALU.mult), r=[R("biasf")], w=[R("biasb")])
                        yield
                    else:
                        self.dve(lambda e: e.memset(bb_, NEGB), w=[R("biasb")])
                        yield
                    self.dve(lambda e: e.memset(bb_[:, :, nblk:nblk + 1], 0.0), r=[R("biasb")], w=[R("biasb")])
                    yield
                    ps = pstr[pcnt[0] % 2]
                    rn = ("pstr", pcnt[0] % 2)
                    pcnt[0] += 1

                    def tpb(e, ps=ps):
                        for h in range(8):
                            ins = e.transpose(ps[0:32, h, :], bb_[:, h, :], self.identb[:])
                        return ins
                    self.pe(tpb, r=[R("biasb"), "identb"], w=[rn])
                    self.dve(lambda e, ps=ps: e.tensor_copy(out=stg[0][0:32, sl, 0:8, :], in_=ps[0:32, 0:8, :]), r=[rn], w=[(R("stg0"), 0)])
                    yield
                    dst = self.qT_mb[:, 64:96, t * 128:(t + 1) * 128].rearrange("h d s -> d h s")
                    self.ldc(dst, stg[0][0:32, sl, 0:8, :], r=[(R("stg0"), 0)], a=["qT_mb_bias"])
                    yield

            import os as _os
            STAG = int(_os.environ.get("KSTAG", "35"))
            active = []
            next_t = 0
            while next_t < self.nt or active:
                if next_t < self.nt and len(active) < NI and (not active or active[-1][1] >= STAG):
                    active.append([body(next_t, next_t % NI), 0])
                    next_t += 1
                for a_ in list(active):
                    try:
                        next(a_[0])
                        a_[1] += 1
                    except StopIteration:
                        active.remove(a_)
            self.S.flush()

    def attn_tiles(self, tiles, ps_s, P, po, po_nm, sc, cnt, hook=None, hook_at=5, acc=None, acc_nm=None, vrows=65):
        n = len(tiles)
        ns = len(ps_s)
        npb = P.shape[1]
        import os as _os
        LA = min(ns - 1, int(_os.environ.get('KLA', '3')))
        for i in range(n + LA):
            if i < n:
                kap, qap, q0, N, mk, vs = tiles[i][:6]
                si = (cnt + i) % ns
                pi = (cnt + i) % npb
                self.pe(lambda e, kap=kap, qap=qap, N=N, si=si: e.matmul(ps_s[si][:, 0:N], lhsT=kap, rhs=qap, start=True, stop=True),
                        r=tiles[i][6], w=[("ps_s", si)])
                self.act(lambda e, N=N, si=si, pi=pi: e.activation(out=P[:, pi, 0:N], in_=ps_s[si][:, 0:N], func=AF.Exp, scale=sc),
                         r=[("ps_s", si)], w=[("P", pi)])
                if mk is not None:
                    self.dve(lambda e, N=N, pi=pi, mk=mk: e.tensor_tensor(out=P[:, pi, 0:N], in0=P[:, pi, 0:N], in1=mk, op=ALU.mult),
                             r=[("P", pi), "masks"], w=[("P", pi)])
                if acc is not None:
                    if i == 0:
                        self.dve(lambda e, N=N, pi=pi, q0=q0: e.tensor_copy(out=acc[:, q0:512], in_=P[:, pi, 0:N]), r=[("P", pi)], w=[acc_nm])
                    else:
                        self.dve(lambda e, N=N, pi=pi, q0=q0: e.tensor_tensor(out=acc[:, q0:512], in0=acc[:, q0:512], in1=P[:, pi, 0:N], op=ALU.add),
                                 r=[("P", pi), acc_nm], w=[acc_nm])
            if hook is not None and i == min(hook_at, n + LA - 1):
                hook()
                hook = None
            j = i - LA
            if j >= 0:
                kap, qap, q0, N, mk, vs = tiles[j][:6]
                pi = (cnt + j) % npb
                for f, vap in enumerate(vs):
                    self.pe(lambda e, f=f, vap=vap, q0=q0, N=N, pi=pi, j=j: e.matmul(po[f][0:vrows, q0:512], lhsT=vap, rhs=P[:, pi, 0:N],
                                                                                    start=(j == 0), stop=(j == n - 1)),
                            r=[("P", pi)] + tiles[j][7], w=[po_nm[f]] if j == 0 else [], a=[] if j == 0 else [po_nm[f]])
                nd = getattr(self, "ndummy", 0)
                if nd:
                    def dm(e):
                        for _ in range(nd):
                            ins = e.matmul(self.ps_dummy[:, 0:128], lhsT=self.identb[:], rhs=self.identb[:], start=True, stop=True)
                        return ins
                    self.pe(dm, r=["identb"], a=["psdummy"])
        if hook is not None:
            hook()
        return cnt + n

    def attn_norm(self, po, po_nm, ep, slot, dst, dst_nm):
        rrow, osb, ps_bc = ep
        self.act(lambda e: e.activation(out=rrow[64:65, slot, :], in_=po[64:65, :], func=AF.Ln), r=[po_nm], w=[("rrow", slot)])
        self.act(lambda e: e.activation(out=rrow[64:65, slot, :], in_=rrow[64:65, slot, :], func=AF.Exp, scale=-1.0), r=[("rrow", slot)], w=[("rrow", slot)])
        self.pe(lambda e: e.matmul(ps_bc[0:64, :], lhsT=self.onesb[64:65, 0:64], rhs=rrow[64:65, slot, :], start=True, stop=True),
                r=[("rrow", slot), "onesb"], w=["ps_bc"])
        self.act(lambda e: e.activation(out=osb[0:64, slot, :], in_=po[0:64, :], func=AF.Copy), r=[po_nm], w=[("osb", slot)])
        self.dve(lambda e: e.tensor_tensor(out=dst, in0=osb[0:64, slot, :], in1=ps_bc[0:64, :], op=ALU.mult),
                 r=[("osb", slot), "ps_bc"], w=[dst_nm])

    def phaseB(self, kind):
        sb = self.sb
        nch = max(1, self.nt // 4)
        ncols = nch * 512
        dil = kind == "dil"
        nheads = {"mla": 8, "dil": 4, "diff": 4, "moba": 8}[kind]
        nm = 2 if kind == "diff" else 1
        parts = 1
        isdf = kind == "diff"
        dk = {"mla": 96, "dil": 64, "diff": 64, "moba": 96}[kind]
        sc = float({"mla": 96 ** -0.5, "dil": 0.125, "diff": 0.125, "moba": 0.125}[kind])
        qsrc = {"mla": self.qT_mla, "dil": self.qT_dil, "diff": self.qT_df, "moba": self.qT_mb}[kind]
        ksrc = {"mla": self.kT_mla, "dil": self.kT_dil, "diff": self.kT_df, "moba": self.kT_mb}[kind]
        vsrc = {"mla": self.v_mla, "dil": self.v_dil, "diff": self.v_df, "moba": self.v_mb}[kind]
        row_base = {"mla": 0, "dil": 512, "diff": 0, "moba": 512}[kind]
        lam_init = 0.8 - 0.6 * math.exp(-0.3 * 1)
        with ExitStack() as es:
            P_dummy = sb(es, "Pdummy", [128, 2], F32)
            if dil:
                qT = sb(es, "qTd", [64, 2, 3, 512], BF16)
                kT = sb(es, "kTd", [64, 3, SEQ], BF16)
                V = sb(es, "Vd", [128, 3, 64, 65], BF16)
                dmb = sb(es, "dmb", [128, 33, 512], BF16)
                self.ldc(dmb[:], self.dmask.rearrange("p (a b) -> p a b", a=33), w=["masks"])
                self.pool(lambda e: e.memset(V[:, :, :, 64:65], 1.0), a=["Vones"])
            else:
                qT = sb(es, "qTa", [96, 2, SEQ], BF16)
                kT = sb(es, "kTa", [96, 2, SEQ], BF16)
                vw = 128 if isdf else 65
                V = sb(es, "Va", [128, 2, parts, 64, vw], BF16)
                if isdf:
                    self.pool(lambda e: e.memset(P_dummy[:], 0.0), a=["Vones"])
                else:
                    self.pool(lambda e: e.memset(V[:, :, :, :, 64:65], 1.0), a=["Vones"])
                if kind == "moba":
                    for sl in range(2):
                        self.ldc(kT[64:96, sl, :], self.kind, a=["Vones"])
            import os as _os
            self.ndummy = int(_os.environ.get("KDUMMY", "0"))
            n_po = 2
            n_s = 7 - n_po - (1 if self.ndummy else 0)
            self.ps_dummy = self.ps(es, "psdummy", [128, 512], F32) if self.ndummy else None
            P = sb(es, "Pt", [128, n_s + 2, 512], BF16)
            rrow = sb(es, "rrow", [65, 2, 512], F32)
            osb = sb(es, "osb", [64, 2, 512], F32)
            onb = sb(es, "onb", [64, 4, 512], BF16)
            onbd = sb(es, "onbd", [128, 2, 512], BF16)
            ps_s = [self.ps(es, "ps_s%d" % i, [128, 512], F32) for i in range(n_s)]
            po = [self.ps(es, "po%d" % i, [128, 512], F32) for i in range(n_po)]
            ps_bc = self.ps(es, "ps_bc", [128, 512], F32)
            ep = (rrow, osb, ps_bc)
            if isdf:
                acc = sb(es, "acc", [128, 2, 512], F32)
                rcp = sb(es, "rcp", [128, 512], F32)
                nrm = sb(es, "nrm", [128, 2, 512], F32)
                dd = sb(es, "dd", [128, 512], F32)
                sqd = sb(es, "sqd", [128, 512], F32)
                rsd = sb(es, "rsd", [128, 512], F32)
                onesf = sb(es, "onesfB", [128, 128], F32)
                lamt = sb(es, "lamt", [128, 256], F32)
                lsm = sb(es, "lsm", [128, 8], F32)
                subs = sb(es, "subs", [128, 1], F32)
                self.pool(lambda e: e.memset(onesf[:], 1.0), w=["onesfB"])
                self.ld(lamt[:], self.dlam.partition_broadcast(128), w=["lamt"])
                self.ld(subs[:], self.subT, w=["subs"])
                self.dve(lambda e: e.tensor_tensor(out=lamt[:, 0:64], in0=lamt[:, 0:64], in1=lamt[:, 64:128], op=ALU.mult), r=["lamt"], w=["lamt"])
                self.dve(lambda e: e.tensor_tensor(out=lamt[:, 128:192], in0=lamt[:, 128:192], in1=lamt[:, 192:256], op=ALU.mult), r=["lamt"], w=["lamt"])
                self.dve(lambda e: e.tensor_reduce(out=lsm[:, 0:1], in_=lamt[:, 0:64], axis=AX.X, op=ALU.add), r=["lamt"], w=["lsm0"])
                self.dve(lambda e: e.tensor_reduce(out=lsm[:, 1:2], in_=lamt[:, 128:192], axis=AX.X, op=ALU.add), r=["lamt"], w=["lsm1"])
                self.act(lambda e: e.activation(out=lsm[:, 2:4], in_=lsm[:, 0:2], func=AF.Exp), r=["lsm0", "lsm1"], w=["lsm2"])
                self.dve(lambda e: e.tensor_tensor(out=lsm[:, 4:5], in0=lsm[:, 3:4], in1=lsm[:, 2:3], op=ALU.subtract), r=["lsm2"], w=["lsm4"])
                self.dve(lambda e: e.tensor_scalar(out=lsm[:, 5:6], in0=lsm[:, 4:5], scalar1=-lam_init, scalar2=None, op0=ALU.add), r=["lsm4"], w=["neglam"])
                self.dve(lambda e: e.tensor_scalar(out=subs[:], in0=subs[:], scalar1=1.0 - lam_init, scalar2=None, op0=ALU.mult), r=["subs"], w=["subs"])
            cnt = 0
            ecnt = 0
            pending = [None]
            for h in range(nheads):
                vsl = h % 2
                if dil:
                    for g in range(3):
                        gh = g * 4 + h
                        self.ld(kT[:, g, 0:ncols], ksrc[gh, :, 0:ncols], w=[("kT", g)])
                        self.ld(V[:, g, 0:nch * 4, 0:64], vsrc[0:ncols, gh * 64:gh * 64 + 64].rearrange("(t p) d -> p t d", p=128),
                                r=["Vones"], w=[("V", g)])
                else:
                    for f in range(parts):
                        vd = 128 if isdf else 64
                        c0 = h * vd
                        self.ld(V[:, vsl, f, 0:nch * 4, 0:vd], vsrc[0:ncols, c0:c0 + vd].rearrange("(t p) d -> p t d", p=128),
                                r=["Vones"], w=[("V", vsl, f)])
                    for m in range(nm):
                        u_ = h * nm + m
                        sl = u_ % 2
                        dq = 96 if kind in ("mla", "moba") else 64
                        dkk = 96 if kind == "mla" else 64
                        self.ld(qT[0:dq, sl, 0:ncols], qsrc[u_, 0:dq, 0:ncols], w=[("qT", sl)])
                        self.ld(kT[0:dkk, sl, 0:ncols], ksrc[u_, 0:dkk, 0:ncols], r=["Vones"], w=[("kT", sl)])
                for c in range(nch):
                    if dil:
                        qs = c % 2
                        for g in range(3):
                            self.ld(qT[:, qs, g, :], qsrc[g * 4 + h, :, c * 512:(c + 1) * 512], w=[("qT", qs, g)])
                    for m in range(nm):
                        u_ = h * nm + m
                        sl = u_ % 2
                        tiles = []
                        if dil:
                            mo = 0
                            for g, W in enumerate((1, 4, 16)):
                                for o in range(W + 4):
                                    kt = 4 * c - W + o
                                    if kt >= 0:
                                        tiles.append((kT[:, g, kt * 128:(kt + 1) * 128], qT[:, qs, g, :], 0, 512, dmb[:, mo + o, :],
                                                      [V[:, g, kt, :]], [("kT", g), ("qT", qs, g)], [("V", g)]))
                                mo += W + 4
                        else:
                            for kt in range(4 * c + 4):
                                j = kt - 4 * c
                                if j < 0:
                                    q0, mk = 0, None
                                else:
                                    q0, mk = 128 * j, self.cmb[:, j, 128 * j:512]
                                N = 512 - q0
                                tiles.append((kT[0:dk, sl, kt * 128:(kt + 1) * 128], qT[0:dk, sl, c * 512 + q0:(c + 1) * 512], q0, N, mk,
                                              [V[:, vsl, f, kt, :] for f in range(parts)], [("kT", sl), ("qT", sl)],
                                              [("V", vsl, f) for f in range(parts)]))
                        pidx = [ecnt % 2]
                        pos_ = [po[i] for i in pidx]
                        po_nm = [("po", i) for i in pidx]
                        if isdf:
                            cnt = self.attn_tiles(tiles, ps_s, P, pos_, po_nm, sc, cnt, hook=pending[0], acc=acc[:, m, :], acc_nm=("acc", m), vrows=128)
                        else:
                            cnt = self.attn_tiles(tiles, ps_s, P, pos_, po_nm, sc, cnt, hook=pending[0])

                        def epilogue(pos_=pos_, po_nm=po_nm, ecnt0=ecnt, m=m, c=c, h=h):
                            if not isdf:
                                es_ = ecnt0 % 2
                                osl = ecnt0 % 4
                                self.attn_norm(pos_[0], po_nm[0], ep, es_, onb[:, osl, :], ("onb", osl))
                                r0 = row_base + h * 64
                                self.ldc(self.oT[r0:r0 + 64, c * 512:(c + 1) * 512], onb[:, osl, :], r=[("onb", osl)], a=["oT"])
                                return
                            self.pe(lambda e: e.matmul(ps_bc[:, :], lhsT=onesf[:], rhs=acc[:, m, :], start=True, stop=True), r=[("acc", m), "onesfB"], w=["ps_bc"])
                            self.act(lambda e: e.activation(out=rcp[:], in_=ps_bc[:, :], func=AF.Ln), r=["ps_bc"], w=["rcp"])
                            self.act(lambda e: e.activation(out=rcp[:], in_=rcp[:], func=AF.Exp, scale=-1.0), r=["rcp"], w=["rcp"])
                            self.dve(lambda e: e.tensor_tensor(out=nrm[:, m, :], in0=pos_[0][:, :], in1=rcp[:], op=ALU.mult), r=[po_nm[0], "rcp"], w=[("nrm", m)])
                            if m == 1:
                                self.dve(lambda e: e.scalar_tensor_tensor(out=dd[:], in0=nrm[:, 1, :], scalar=lsm[:, 5:6], in1=nrm[:, 0, :], op0=ALU.mult, op1=ALU.add),
                                         r=[("nrm", 0), ("nrm", 1), "neglam"], w=["dd"])
                                self.act(lambda e: e.activation(out=sqd[:], in_=dd[:], func=AF.Square), r=["dd"], w=["sqd"])
                                self.pe(lambda e: e.matmul(ps_bc[:, :], lhsT=onesf[:], rhs=sqd[:], start=True, stop=True), r=["sqd", "onesfB"], w=["ps_bc"])
                                self.act(lambda e: e.activation(out=rsd[:], in_=ps_bc[:, :], func=AF.Ln, scale=1.0 / 128, bias=self.epsT[:, 0:1]), r=["ps_bc"], w=["rsd"])
                                self.act(lambda e: e.activation(out=rsd[:], in_=rsd[:], func=AF.Exp, scale=-0.5), r=["rsd"], w=["rsd"])
                                osl = c % 2
                                ob = onbd[:, osl, :]
                                self.dve(lambda e, ob=ob: e.scalar_tensor_tensor(out=ob, in0=dd[:], scalar=subs[:, 0:1], in1=rsd[:], op0=ALU.mult, op1=ALU.mult),
                                         r=["dd", "rsd", "subs"], w=[("onbd", osl)])
                                self.ldc(self.oT[h * 128:(h + 1) * 128, c * 512:(c + 1) * 512], ob, r=[("onbd", osl)], a=["oT"])
                        pending[0] = epilogue
                        ecnt += parts
            if pending[0] is not None:
                pending[0]()
            self.S.flush()

    def phaseC1(self, layer):
        sb = self.sb
        nk = 6 if layer == 0 else 8
        xin = self.x if layer == 0 else self.xL
        wsrc = self.e_w_out if layer == 0 else self.o_w_out
        with ExitStack() as es:
            w_out = sb(es, "w_out", [128, nk, D], BF16)
            G1 = sb(es, "G1", [128, D], F32)
            xt = sb(es, "xtc", [128, 2, D], F32)
            oTt = sb(es, "oTt", [128, 2, nk, 128], BF16)
            tmpf = sb(es, "tmpfc", [128, D], F32)
            sq = sb(es, "sqc", [128, D], F32)
            ssx = sb(es, "ssxc", [128, 2], F32)
            xs = sb(es, "xsc", [128, D], BF16)
            hT = sb(es, "hTc", [128, 2, 8, 128], BF16)
            psy = [self.ps(es, "psy%d" % i, [128, 512], F32) for i in range(2)]
            pT = self.ps(es, "pTc", [128, 8, 128], BF16)
            self.load_w(w_out, wsrc, nk * 128, "w_out")
            self.ld(G1[:], self.Gd[:, (layer * 2) * D:(layer * 2 + 1) * D], w=["G1"])
            for t in range(self.nt):
                sl = t % 2
                self.ld(xt[:, sl], xin[t * 128:(t + 1) * 128, :], w=[("xt", sl)])
                self.ld(oTt[:, sl], self.oT[0:nk * 128, t * 128:(t + 1) * 128].rearrange("(k p) s -> p k s", p=128), w=[("oTt", sl)])
                for hf in range(2):
                    def mm(e, hf=hf, sl=sl):
                        for k in range(nk):
                            ins = e.matmul(psy[hf][:], lhsT=oTt[:, sl, k, :], rhs=w_out[:, k, hf * 512:(hf + 1) * 512], start=(k == 0), stop=(k == nk - 1))
                        return ins
                    self.pe(mm, r=[("oTt", sl), "w_out"], w=[("psy", hf)])
                    self.dve(lambda e, hf=hf: e.tensor_tensor(out=tmpf[:, hf * 512:(hf + 1) * 512], in0=psy[hf][:], in1=G1[:, hf * 512:(hf + 1) * 512], op=ALU.mult),
                             r=[("psy", hf), "G1"], w=[("tmpf", hf)])
                    self.dve(lambda e, hf=hf, sl=sl: e.tensor_tensor(out=xt[:, sl, hf * 512:(hf + 1) * 512], in0=xt[:, sl, hf * 512:(hf + 1) * 512],
                                                                    in1=tmpf[:, hf * 512:(hf + 1) * 512], op=ALU.add),
                             r=[("tmpf", hf), ("xt", sl)], w=[("xt", sl)])
                self.ldc(self.x1[t * 128:(t + 1) * 128, :], xt[:, sl], r=[("xt", sl)], a=["x1"])
                self.act(lambda e, sl=sl: e.activation(out=sq[:], in_=xt[:, sl], func=AF.Square, accum_out=ssx[:, 0:1]), r=[("xt", sl)], w=["sq", "ssx"])
                self.rstd(ssx[:, 0:1], ssx[:, 1:2], D, "ssx", "rsx")
                self.act(lambda e, sl=sl: e.activation(out=xs[:], in_=xt[:, sl], func=AF.Copy, scale=ssx[:, 1:2]), r=[("xt", sl), "rsx"], w=["xs"])

                def tpx(e):
                    for j in range(8):
                        ins = e.transpose(pT[:, j, :], xs[:, j * 128:(j + 1) * 128], self.identb[:])
                    return ins
                self.pe(tpx, r=["xs", "identb"], w=["pT"])
                ao = layer * 16 + 8
                a_bc = self.aT[:, ao:ao + 8].unsqueeze(2).to_broadcast([128, 8, 128])
                b_bc = self.modT[:, layer * 48 + 24:layer * 48 + 32].unsqueeze(2).to_broadcast([128, 8, 128])
                tm3 = tmpf[:].rearrange("p (j s) -> p j s", j=8)
                self.dve(lambda e, a_bc=a_bc, tm3=tm3: e.tensor_tensor(out=tm3, in0=pT[:], in1=a_bc, op=ALU.mult),
                         r=["pT", ("aT", ao)], w=[("tmpf", 0), ("tmpf", 1)])
                self.dve(lambda e, b_bc=b_bc, tm3=tm3, sl=sl: e.tensor_tensor(out=hT[:, sl], in0=tm3, in1=b_bc, op=ALU.add),
                         r=[("tmpf", 0), ("tmpf", 1), "modT"], w=[("hT", sl)])
                self.ldc(self.h2T[:, t * 128:(t + 1) * 128].rearrange("(k p) s -> p k s", p=128), hT[:, sl], r=[("hT", sl)], a=["h2T"])
            self.S.flush()

    def phaseC2(self, layer):
        sb = self.sb
        dst = self.xL if layer == 0 else self.out
        ng = max(1, self.nt // 2)
        with ExitStack() as es:
            w1 = sb(es, "w1", [128, 8, DFF], BF16)
            w2 = sb(es, "w2", [128, 32, D], BF16)
            G2 = sb(es, "G2", [128, D], F32)
            hTt = sb(es, "hTt", [128, 2, 8, 256], BF16)
            x1t = sb(es, "x1t", [128, 2, 2, D], F32)
            aT = sb(es, "aTt", [128, 32, 256], BF16)
            rl = sb(es, "rl", [128, 2, 512], F32)
            tmp = sb(es, "tmpc2", [128, 2, 512], F32)
            psa = [self.ps(es, "psa%d" % i, [128, 2, 256], F32) for i in range(2)]
            psy = [self.ps(es, "psy2_%d" % i, [128, 512], F32) for i in range(4)]
            self.load_w(w1, self.mlp_w1[layer], D, "w1")
            self.load_w(w2, self.mlp_w2[layer], DFF, "w2")
            self.ld(G2[:], self.Gd[:, (layer * 2 + 1) * D:(layer * 2 + 2) * D], w=["G2"])
            for g in range(ng):
                sl = g % 2
                self.ld(hTt[:, sl], self.h2T[:, g * 256:(g + 1) * 256].rearrange("(k p) s -> p k s", p=128), w=[("hTt", sl)])
                self.ld(x1t[:, sl], self.x1[g * 256:(g + 1) * 256, :].rearrange("(s p) d -> p s d", p=128), w=[("x1t", sl)])
                for fp in range(16):
                    pb = psa[fp % 2]

                    def mm1(e, fp=fp, pb=pb, sl=sl):
                        for ff in range(2):
                            f = fp * 2 + ff
                            for k in range(8):
                                ins = e.matmul(pb[:, ff, :], lhsT=w1[:, k, f * 128:(f + 1) * 128], rhs=hTt[:, sl, k, :], start=(k == 0), stop=(k == 7))
                        return ins
                    self.pe(mm1, r=["w1", ("hTt", sl)], w=[("psa", fp % 2)])
                    rs_ = fp % 2
                    self.act(lambda e, pb=pb, rs_=rs_: e.activation(out=rl[:, rs_, :], in_=pb[:].rearrange("p a b -> p (a b)"), func=AF.Relu),
                             r=[("psa", fp % 2)], w=[("rl", rs_)])
                    self.dve(lambda e, fp=fp, rs_=rs_: e.tensor_tensor(out=aT[:, fp * 2:fp * 2 + 2, :].rearrange("p a b -> p (a b)"), in0=rl[:, rs_, :], in1=rl[:, rs_, :], op=ALU.mult),
                             r=[("rl", rs_)], w=[("aT", fp)])
                for s_ in range(2):
                    for hf in range(2):
                        pi = s_ * 2 + hf

                        def mm2(e, s_=s_, hf=hf, pi=pi):
                            for f in range(32):
                                ins = e.matmul(psy[pi][:], lhsT=aT[:, f, s_ * 128:(s_ + 1) * 128], rhs=w2[:, f, hf * 512:(hf + 1) * 512], start=(f == 0), stop=(f == 31))
                            return ins
                        self.pe(mm2, r=["w2"] + [("aT", fp) for fp in range(16)], w=[("psy", pi)])
                        self.dve(lambda e, hf=hf, pi=pi: e.tensor_tensor(out=tmp[:, hf, :], in0=psy[pi][:], in1=G2[:, hf * 512:(hf + 1) * 512], op=ALU.mult),
                                 r=[("psy", pi), "G2"], w=[("tmp", hf)])
                        self.dve(lambda e, s_=s_, hf=hf, sl=sl: e.tensor_tensor(out=x1t[:, sl, s_, hf * 512:(hf + 1) * 512], in0=x1t[:, sl, s_, hf * 512:(hf + 1) * 512],
                                                                              in1=tmp[:, hf, :], op=ALU.add),
                                 r=[("tmp", hf), ("x1t", sl)], w=[("x1t", sl)])
                self.ldc(dst[g * 256:(g + 1) * 256, :].rearrange("(s p) d -> p s d", p=128), x1t[:, sl], r=[("x1t", sl)], a=["dst"])
            self.S.flush()


def _consts():
    ident = np.eye(128, dtype=np.float32)
    i16 = np.arange(16, dtype=np.float32) / np.float32(16)
    i32 = np.arange(32, dtype=np.float32) / np.float32(32)
    invf = np.concatenate([np.float32(10000.0) ** (-i16), np.float32(10000.0) ** (-i32)]).astype(np.float32)
    k = np.arange(128)[:, None]
    q = np.arange(512)[None, :]
    cmask = np.stack([(q >= 128 * j + k) for j in range(4)], axis=1).astype(np.float32)
    dm = []
    for (w, r) in ((128, 1), (512, 4), (2048, 16)):
        W = w // 128
        for o in range(W + 4):
            rel = q - k + 128 * (W - o)
            dm.append(((rel >= 0) & (rel <= w) & (rel % r == 0)).astype(np.float32))
    dmask = np.stack(dm, axis=1)
    kind = (np.arange(SEQ)[None, :] // 256 == np.arange(32)[:, None]).astype(np.float32)
    return dict(ident=ident, invf=invf, cmask=cmask.reshape(128, -1), dmask=dmask.reshape(128, -1), kind=kind)


def _colT(v, n):
    return np.ascontiguousarray(np.asarray(v, np.float32).reshape(n, 128).T)


def make_in_maps(inp, batches):
    f = lambda a: np.ascontiguousarray(np.asarray(a, dtype=np.float32))
    c = _consts()
    shared = dict(
        ada_w=f(inp["ada_w"]),
        ada_bT=np.ascontiguousarray(np.concatenate([_colT(inp["ada_b"][l], 48) for l in range(2)], axis=1)),
        nrmT=np.ascontiguousarray(np.concatenate(
            [_colT(inp[nm][l], 8) for l in range(2) for nm in ("norm_mix", "norm_mlp")], axis=1)),
        mlp_w1=f(inp["mlp_w1"]), mlp_w2=f(inp["mlp_w2"]),
        e_w_in=f(inp["even_w_in"][0]), e_w_out=f(inp["even_w_out"][0]),
        latT=np.ascontiguousarray(np.concatenate([_colT(inp["mla_q_lat_norm"][0], 3), _colT(inp["mla_kv_lat_norm"][0], 2)], axis=1)),
        w_uq=f(np.asarray(inp["mla_w_uq"][0]).reshape(384, 768)),
        w_ukv=f(np.asarray(inp["mla_w_ukv"][0]).reshape(256, 1024)),
        gains=f(np.concatenate([np.asarray(inp[k][0], np.float32).reshape(-1) for k in
                                ("mla_q_norm", "mla_k_norm", "dil_q_norm", "dil_k_norm",
                                 "diff_q_norm", "diff_k_norm", "moba_q_norm", "moba_k_norm")])),
        o_w_in=f(inp["odd_w_in"][0]), o_w_out=f(inp["odd_w_out"][0]),
        dlam=f(np.asarray(inp["diff_lambda"][0]).reshape(256)),
        subT=np.ascontiguousarray(np.asarray(inp["diff_subln"][0], np.float32).reshape(128, 1)),
        **c,
    )
    maps = []
    for b in batches:
        m = dict(shared)
        m["x"] = f(inp["x"][b])
        m["posT"] = np.ascontiguousarray(np.asarray(inp["positions"][b], np.int32).reshape(NT, 128).T)
        m["cT"] = _colT(inp["c"][b], 8)
        maps.append(m)
    return maps


def build(nt=NT, phases=None, dbg_out=()):
    phases = ALL_PHASES if phases is None else phases
    nc = bass.Bass("TRN2", target_bir_lowering=False)
    k = K(nc, nt=nt, phases=phases, dbg_out=dbg_out)
    k.declare()
    with ExitStack() as gs:
        k.es = gs
        sems = {}
        for e in Sched.COMPUTE:
            sems[e] = gs.enter_context(nc.semaphore("sem_" + e))
        for q in ("sp", "pq"):
            sems[q] = [gs.enter_context(nc.semaphore("sem_%s%d" % (q, i))) for i in range(k.S.ring)]
        k.S.init_emit(sems)
        k.setup()
        for ph in phases:
            getattr(k, "run_" + ph)()
        stats = k.S.finish()
    return nc, k, stats


def _add_phase_methods():
    K.run_A0 = lambda self: self.phaseA(0)
    K.run_A1 = lambda self: self.phaseA(1)
    K.run_Bmla = lambda self: self.phaseB("mla")
    K.run_C10 = lambda self: self.phaseC1(0)
    K.run_C20 = lambda self: self.phaseC2(0)
    K.run_C11 = lambda self: self.phaseC1(1)
    K.run_C21 = lambda self: self.phaseC2(1)
    K.run_Bdil = lambda self: self.phaseB("dil")
    K.run_Bdiff = lambda self: self.phaseB("diff")
    K.run_Bmoba = lambda self: self.phaseB("moba")


_add_phase_methods()


ALL_PHASES = ("A0", "Bmla", "Bdil", "C10", "C20", "A1", "Bdiff", "Bmoba", "C11", "C21")


def kernel(**inputs):
    nb = int(np.asarray(inputs["x"]).shape[0])
    nc, k, stats = build(nt=NT, phases=ALL_PHASES)
    maps = make_in_maps(inputs, list(range(nb)))
    res = run_bass_kernel_spmd(nc, maps, core_ids=list(range(nb)))
    return np.stack([np.asarray(res.results[b]["out"], dtype=np.float32) for b in range(nb)], axis=0)
```

```python
class _Op:
    __slots__ = ("eng", "fn", "deps", "needs_sig", "sem", "val", "is_dma", "idx", "ring_prev")

    def __init__(self, eng, fn, is_dma):
        self.eng = eng
        self.fn = fn
        self.deps = []
        self.needs_sig = False
        self.sem = None
        self.val = 0
        self.is_dma = is_dma
        self.ring_prev = None


class Sched:
    COMPUTE = ("pe", "act", "dve", "pool")

    def __init__(self, nc, ring=8, same_engine_sync=True):
        self.nc = nc
        self.ops = []
        self.last_w = {}
        self.readers = {}
        self.appenders = {}
        self.ring = ring
        self.same_engine_sync = same_engine_sync
        self.engobj = {"pe": nc.tensor, "act": nc.scalar, "dve": nc.vector, "pool": nc.gpsimd,
                       "sp": nc.sync, "pq": nc.gpsimd}
        self.stream = {"pe": "pe", "act": "act", "dve": "dve", "pool": "pool", "sp": "sp", "pq": "pool"}

    def add(self, eng, fn, r=(), w=(), a=()):
        is_dma = eng in ("sp", "pq")
        op = _Op(eng, fn, is_dma)
        deps = {}

        def dep(p, kind):
            if p is op:
                return
            same = (self.stream[p.eng] == self.stream[eng]) and not p.is_dma
            if same:
                if eng == "pe" or not self.same_engine_sync:
                    return
            deps[id(p)] = p

        for x in r:
            p = self.last_w.get(x)
            if p is not None:
                dep(p, "raw")
            for p in self.appenders.get(x, ()):
                dep(p, "raw")
        for x in list(w) + list(a):
            p = self.last_w.get(x)
            if p is not None:
                dep(p, "waw")
            for p in self.readers.get(x, ()):
                dep(p, "war")
        for x in w:
            for p in self.appenders.get(x, ()):
                dep(p, "waw")
        for x in r:
            self.readers.setdefault(x, []).append(op)
        for x in w:
            self.last_w[x] = op
            self.readers[x] = []
            self.appenders[x] = []
        for x in a:
            self.appenders.setdefault(x, []).append(op)
            self.readers[x] = []
        op.deps = list(deps.values())
        for p in op.deps:
            p.needs_sig = True
        self.ops.append(op)
        return op

    def init_emit(self, sems):
        self.sems = sems
        self.cnt = {e: 0 for e in self.COMPUTE}
        self.dcount = {"sp": 0, "pq": 0}
        self.dhist = {"sp": [], "pq": []}
        self.waited = {}
        self.nwaits = 0
        self.nops = 0
        self.barrier_deps = []

    def flush(self):
        ops = self.ops
        lastc = {}
        for op in ops:
            if not op.is_dma:
                lastc[op.eng] = op
        for op in lastc.values():
            op.needs_sig = True
        bd = self.barrier_deps
        first_seen = set()
        for op in ops:
            st = self.stream[op.eng]
            if st not in first_seen:
                first_seen.add(st)
                op.deps = op.deps + [p for p in bd if not (self.stream[p.eng] == st and not p.is_dma)]
            if op.is_dma:
                i = self.dcount[op.eng]
                self.dcount[op.eng] += 1
                op.sem = self.sems[op.eng][i % self.ring]
                op.val = 16 * (i // self.ring + 1)
                if i >= self.ring:
                    op.ring_prev = self.dhist[op.eng][i - self.ring]
                self.dhist[op.eng].append(op)
            elif op.needs_sig:
                self.cnt[op.eng] += 1
                op.sem = self.sems[op.eng]
                op.val = self.cnt[op.eng]
        waited = self.waited
        for op in ops:
            e = self.engobj[op.eng]
            st = self.stream[op.eng]
            need = {}
            plist = list(op.deps)
            if op.ring_prev is not None:
                plist.append(op.ring_prev)
            for p in plist:
                k = id(p.sem)
                if k not in need or need[k][1] < p.val:
                    need[k] = (p.sem, p.val)
            for k, (sem, val) in need.items():
                if waited.get((st, k), 0) < val:
                    e.wait_ge(sem, val)
                    waited[(st, k)] = val
                    self.nwaits += 1
            inst = op.fn(e)
            if op.is_dma:
                inst.then_inc(op.sem, 16)
            elif op.needs_sig:
                inst.then_inc(op.sem, 1)
        self.nops += len(ops)
        nb = list(lastc.values())
        for p in bd:
            if not p.is_dma and p.eng not in lastc:
                nb.append(p)
        for q in ("sp", "pq"):
            nb.extend(self.dhist[q][-self.ring:])
        self.barrier_deps = nb
        self.ops = []
        self.last_w = {}
        self.readers = {}
        self.appenders = {}

    def finish(self, eng="sp"):
        self.flush()
        e = self.engobj[eng]
        st = self.stream[eng]
        for p in self.barrier_deps:
            k = id(p.sem)
            if self.waited.get((st, k), 0) < p.val:
                e.wait_ge(p.sem, p.val)
                self.waited[(st, k)] = p.val
        return {"n_ops": self.nops, "n_waits": self.nwaits, "sig": dict(self.cnt), "dma": dict(self.dcount)}


import math
from contextlib import ExitStack
import numpy as np
import ml_dtypes
import concourse.bass as bass
import concourse.mybir as mybir
from concourse.bass_utils import run_bass_kernel_spmd

F32 = mybir.dt.float32
BF16 = mybir.dt.bfloat16
I32 = mybir.dt.int32
AF = mybir.ActivationFunctionType
ALU = mybir.AluOpType
AX = mybir.AxisListType

D = 1024
SEQ = 8192
NT = SEQ // 128
NCH = SEQ // 512
EPS = 1e-6
EVEN_IN = 2976
ODD_IN = 3072
DFF = 4096
NEGB = -30000.0
TWO_PI = float(2 * np.pi)


class K:
    def __init__(self, nc, nt=NT, phases=None, dbg_out=()):
        self.nc = nc
        self.S = Sched(nc)
        self.nt = nt
        self.phases = phases
        self.dbg_out = dbg_out
        self.es = ExitStack()
        self.din = {}
        self.dscr = {}
        import os as _os
        self.lim = float(_os.environ.get('KLIM', '99'))

    def inp(self, name, shape, dt=F32):
        t = self.nc.dram_tensor(name, list(shape), dt, kind="ExternalInput").ap()
        self.din[name] = t
        return t

    def scr(self, name, shape, dt):
        kind = "ExternalOutput" if name in self.dbg_out else "Internal"
        t = self.nc.dram_tensor(name, list(shape), dt, kind=kind).ap()
        self.dscr[name] = t
        return t

    def sb(self, es, name, shape, dt):
        self.uid = getattr(self, "uid", 0) + 1
        return es.enter_context(self.nc.sbuf_tensor("%s_%d" % (name, self.uid), list(shape), dt))

    def ps(self, es, name, shape, dt):
        self.uid = getattr(self, "uid", 0) + 1
        return es.enter_context(self.nc.psum_tensor("%s_%d" % (name, self.uid), list(shape), dt))

    def act(self, fn, r=(), w=(), a=()):
        return self.S.add("act", fn, r, w, a)

    def dve(self, fn, r=(), w=(), a=()):
        return self.S.add("dve", fn, r, w, a)

    def pool(self, fn, r=(), w=(), a=()):
        return self.S.add("pool", fn, r, w, a)

    def pe(self, fn, r=(), w=(), a=()):
        return self.S.add("pe", fn, r, w, a)

    def ld(self, out, in_, r=(), w=(), a=(), q="sp"):
        return self.S.add(q, lambda e: e.dma_start(out=out, in_=in_), r, w, a)

    def ldc(self, out, in_, r=(), w=(), a=()):
        return self.S.add("pq", lambda e: e.dma_start(out=out, in_=in_), r, w, a)

    def rstd(self, ss, rs, n, rn_ss, rn_rs):
        self.act(lambda e: e.activation(out=rs, in_=ss, func=AF.Sqrt, scale=1.0 / n, bias=self.eps_ap(ss)),
                 r=[rn_ss], w=[rn_rs])
        self.dve(lambda e: e.reciprocal(out=rs, in_=rs), r=[rn_rs], w=[rn_rs])

    def eps_ap(self, like):
        p = like.shape[0]
        return self.epsT[0:p, 0:1]

    def declare(self):
        i = self.inp
        self.x = i("x", [SEQ, D])
        self.posT = i("posT", [128, NT], I32)
        self.cT = i("cT", [128, 8])
        self.ada_w = i("ada_w", [2, D, 6 * D])
        self.ada_bT = i("ada_bT", [128, 96])
        self.nrmT = i("nrmT", [128, 32])
        self.mlp_w1 = i("mlp_w1", [2, D, DFF])
        self.mlp_w2 = i("mlp_w2", [2, DFF, D])
        self.e_w_in = i("e_w_in", [D, EVEN_IN])
        self.e_w_out = i("e_w_out", [768, D])
        self.latT = i("latT", [128, 5])
        self.w_uq = i("w_uq", [384, 768])
        self.w_ukv = i("w_ukv", [256, 1024])
        self.gains = i("gains", [576])
        self.o_w_in = i("o_w_in", [D, ODD_IN])
        self.o_w_out = i("o_w_out", [D, D])
        self.dlam = i("dlam", [256])
        self.subT = i("subT", [128, 1])
        self.ident = i("ident", [128, 128])
        self.invf = i("invf", [48])
        self.cmask = i("cmask", [128, 4 * 512])
        self.dmask = i("dmask", [128, 33 * 512])
        self.kind = i("kind", [32, SEQ])
        self.out = self.nc.dram_tensor("out", [SEQ, D], F32, kind="ExternalOutput").ap()
        s = self.scr
        self.trigd = s("trigd", [128, 2 * NT * 48], F32)
        self.Gd = s("Gd", [128, 4 * D], F32)
        self.x1 = s("x1", [SEQ, D], F32)
        self.xL = s("xL", [SEQ, D], F32)
        self.h2T = s("h2T", [D, SEQ], BF16)
        self.oT = s("oT", [D, SEQ], BF16)
        self.qT_mla = s("qT_mla", [8, 96, SEQ], BF16)
        self.kT_mla = s("kT_mla", [8, 96, SEQ], BF16)
        self.v_mla = s("v_mla", [SEQ, 512], BF16)
        self.qT_dil = s("qT_dil", [12, 64, SEQ], BF16)
        self.kT_dil = s("kT_dil", [12, 64, SEQ], BF16)
        self.v_dil = s("v_dil", [SEQ, 768], BF16)
        self.qT_df = s("qT_df", [8, 64, SEQ], BF16)
        self.kT_df = s("kT_df", [8, 64, SEQ], BF16)
        self.v_df = s("v_df", [SEQ, 512], BF16)
        self.qT_mb = s("qT_mb", [8, 96, SEQ], BF16)
        self.kT_mb = s("kT_mb", [8, 64, SEQ], BF16)
        self.v_mb = s("v_mb", [SEQ, 512], BF16)

    def setup(self):
        nc, S = self.nc, self.S
        g = self.es
        sb = self.sb
        self.epsT = sb(g, "epsT", [128, 1], F32)
        self.identb = sb(g, "identb", [128, 128], BF16)
        self.modT = sb(g, "modT", [128, 96], F32)
        self.aT = sb(g, "aT", [128, 32], F32)
        self.gbc = sb(g, "gbc", [128, 576], F32)
        self.cmb = sb(g, "cmb", [128, 4, 512], BF16)
        self.onesb = sb(g, "onesb", [128, 64], F32)
        self.pool(lambda e: e.memset(self.epsT[:], EPS), w=["epsT"])
        self.pool(lambda e: e.memset(self.onesb[:], 1.0), w=["onesb"])
        self.ldc(self.identb[:], self.ident, w=["identb"])
        self.ldc(self.cmb[:], self.cmask.rearrange("p (a b) -> p a b", a=4), w=["cmb"])
        self.ld(self.gbc[:], self.gains.partition_broadcast(128), w=["gbc"])
        with ExitStack() as es:
            self.trig = sb(es, "trig", [128, 2, NT, 48], F32)
            self.G = sb(es, "G", [128, 4, D], F32)
            condT = sb(es, "condT", [128, 8], F32)
            bT = sb(es, "bT", [128, 96], F32)
            nT = sb(es, "nT", [128, 32], F32)
            stage = sb(es, "adastage", [128, 2, 8, 1024], F32)
            posi = sb(es, "posi", [128, NT], I32)
            posf = sb(es, "posf", [128, NT], F32)
            invb = sb(es, "invb", [128, 48], F32)
            kf = sb(es, "kf", [128, 2, NT, 48], F32)
            ki = sb(es, "ki", [128, 2, NT, 48], I32)
            psmod = self.ps(es, "psmod", [128, 96], F32)
            self.ld(condT[:], self.cT, w=["condT"])
            self.ld(bT[:], self.ada_bT, w=["bT"])
            self.ld(nT[:], self.nrmT, w=["nT"])
            self.ld(posi[:], self.posT, w=["posi"])
            self.ld(invb[:], self.invf.partition_broadcast(128), w=["invb"])
            self.act(lambda e: e.activation(out=condT[:], in_=condT[:], func=AF.Silu), r=["condT"], w=["condT"])
            tr = self.trig
            self.dve(lambda e: e.tensor_copy(out=posf[:], in_=posi[:]), r=["posi"], w=["posf"])
            self.dve(lambda e: e.tensor_tensor(out=tr[:, 0], in0=posf[:].unsqueeze(2).to_broadcast([128, NT, 48]),
                                               in1=invb[:].unsqueeze(1).to_broadcast([128, NT, 48]), op=ALU.mult),
                     r=["posf", "invb"], w=["trig"])
            self.dve(lambda e: e.tensor_scalar(out=tr[:, 1], in0=tr[:, 0], scalar1=float(np.pi / 2), scalar2=None,
                                               op0=ALU.add), r=["trig"], w=["trig"])
            self.dve(lambda e: e.tensor_scalar(out=kf[:], in0=tr[:], scalar1=float(1 / TWO_PI), scalar2=None,
                                               op0=ALU.mult), r=["trig"], w=["kf"])
            self.dve(lambda e: e.tensor_copy(out=ki[:], in_=kf[:]), r=["kf"], w=["ki"])
            self.dve(lambda e: e.tensor_copy(out=kf[:], in_=ki[:]), r=["ki"], w=["kf"])
            self.dve(lambda e: e.scalar_tensor_tensor(out=tr[:], in0=kf[:], scalar=-TWO_PI, in1=tr[:],
                                                      op0=ALU.mult, op1=ALU.add), r=["kf", "trig"], w=["trig"])
            self.dve(lambda e: e.tensor_scalar(out=kf[:], in0=tr[:], scalar1=float(np.pi), scalar2=-TWO_PI,
                                               op0=ALU.is_gt, op1=ALU.mult), r=["trig"], w=["kf"])
            self.dve(lambda e: e.tensor_tensor(out=tr[:], in0=tr[:], in1=kf[:], op=ALU.add), r=["kf", "trig"], w=["trig"])
            self.dve(lambda e: e.tensor_scalar(out=kf[:], in0=tr[:], scalar1=float(-np.pi), scalar2=TWO_PI,
                                               op0=ALU.is_lt, op1=ALU.mult), r=["trig"], w=["kf"])
            self.dve(lambda e: e.tensor_tensor(out=tr[:], in0=tr[:], in1=kf[:], op=ALU.add), r=["kf", "trig"], w=["trig"])
            self.act(lambda e: e.activation(out=tr[:], in_=tr[:], func=AF.Sin), r=["trig"], w=["trig"])
            for l in range(2):
                for cb in range(6):
                    sl = (l * 6 + cb) % 2
                    self.ld(stage[:, sl], self.ada_w[l, :, cb * 1024:(cb + 1) * 1024].rearrange("(k p) n -> p k n", p=128),
                            w=[("adast", sl)])
                    for jj in range(8):
                        col = l * 48 + cb * 8 + jj

                        def mm(e, sl=sl, jj=jj, col=col):
                            for k in range(8):
                                ins = e.matmul(psmod[:, col:col + 1], lhsT=stage[:, sl, k, jj * 128:(jj + 1) * 128],
                                               rhs=condT[:, k:k + 1], start=(k == 0), stop=(k == 7))
                            return ins
                        self.pe(mm, r=[("adast", sl), "condT"], a=["psmod"])
            self.dve(lambda e: e.tensor_tensor(out=self.modT[:], in0=psmod[:], in1=bT[:], op=ALU.add),
                     r=["psmod", "bT"], w=["modT"])
            for l in range(2):
                m = self.modT[:, l * 48:(l + 1) * 48]
                for which, (sc0, sh0) in enumerate(((8, 0), (32, 24))):
                    o = l * 16 + which * 8
                    nsl = nT[:, l * 16 + which * 8: l * 16 + which * 8 + 8]
                    self.dve(lambda e, o=o, m=m, sc0=sc0, nsl=nsl: e.scalar_tensor_tensor(
                        out=self.aT[:, o:o + 8], in0=m[:, sc0:sc0 + 8], scalar=1.0, in1=nsl, op0=ALU.add, op1=ALU.mult),
                        r=["modT", "nT"], w=[("aT", o)])
            identf = sb(es, "identf", [128, 128], F32)
            onesf = sb(es, "onesf", [128, 128], F32)
            dg = sb(es, "dg", [128, 2, 128], F32)
            psg_ = [self.ps(es, "psgate%d" % i, [128, 512], F32) for i in range(2)]
            self.ld(identf[:], self.ident, w=["identf"])
            self.pool(lambda e: e.memset(onesf[:], 1.0), w=["onesf"])
            cnt = 0
            for l in range(2):
                for which, off in enumerate((16, 40)):
                    gi = l * 2 + which
                    for half in range(2):
                        pb = psg_[cnt % 2]
                        for jj in range(4):
                            j = half * 4 + jj
                            col = l * 48 + off + j
                            sl = (cnt * 4 + jj) % 2
                            self.dve(lambda e, sl=sl, col=col: e.tensor_scalar(out=dg[:, sl, :], in0=identf[:], scalar1=self.modT[:, col:col + 1],
                                                                               scalar2=None, op0=ALU.mult), r=["identf", "modT"], w=[("dg", sl)])
                            self.pe(lambda e, sl=sl, jj=jj, pb=pb: e.matmul(pb[:, jj * 128:(jj + 1) * 128], lhsT=onesf[:], rhs=dg[:, sl, :], start=True, stop=True),
                                    r=[("dg", sl), "onesf"], w=[("psgate", cnt % 2, jj)])
                        self.act(lambda e, gi=gi, half=half, pb=pb: e.activation(out=self.G[:, gi, half * 512:(half + 1) * 512], in_=pb[:], func=AF.Copy),
                                 r=[("psgate", cnt % 2, jj) for jj in range(4)], w=[("G", gi, half)])
                        cnt += 1
            self.ldc(self.trigd.rearrange("p (a b) -> p a b", a=2 * NT), self.trig[:].rearrange("p s t f -> p (s t) f"), r=["trig"], w=["trigd"])
            self.ldc(self.Gd.rearrange("p (a b) -> p a b", a=4), self.G[:], r=[("G", gi, hf) for gi in range(4) for hf in range(2)], w=["Gd"])
            self.S.flush()

    def head_post(self, nm, src, nh, hd, goff, rope_lo, half, tg, tg_nm, sq, sq_nm, ssh, ssh_nm, rt, rt_nm, dst_bf, nm_bf):
        n = nh * hd
        nms = list(nm) if isinstance(nm, list) else [nm]
        s3 = src.rearrange("p (h d) -> p h d", h=nh)
        sq2 = sq[:, 0:n]
        self.act(lambda e: e.activation(out=sq2, in_=src, func=AF.Square), r=nms, w=[sq_nm])
        yield
        self.dve(lambda e: e.tensor_reduce(out=ssh[:, 0:nh], in_=sq2.rearrange("p (h d) -> p h d", h=nh), axis=AX.X, op=ALU.add),
                 r=[sq_nm], w=[ssh_nm])
        yield
        self.act(lambda e: e.activation(out=ssh[:, 0:nh], in_=ssh[:, 0:nh], func=AF.Sqrt, scale=1.0 / hd, bias=self.epsT[:, 0:1]),
                 r=[ssh_nm], w=[ssh_nm])
        yield
        self.dve(lambda e: e.reciprocal(out=ssh[:, 0:nh], in_=ssh[:, 0:nh]), r=[ssh_nm], w=[ssh_nm])
        yield
        self.dve(lambda e: e.tensor_tensor(out=s3, in0=s3, in1=ssh[:, 0:nh].unsqueeze(2).to_broadcast([128, nh, hd]), op=ALU.mult),
                 r=nms + [ssh_nm], w=nms)
        yield
        gb = self.gbc[:, goff:goff + hd]
        self.dve(lambda e: e.tensor_tensor(out=s3, in0=s3, in1=gb.unsqueeze(1).to_broadcast([128, nh, hd]), op=ALU.mult),
                 r=nms + ["gbc"], w=nms)
        yield
        fo = 0 if half == 16 else 16
        sin = tg[:, 0, fo:fo + half].unsqueeze(1).to_broadcast([128, nh, half])
        cos = tg[:, 1, fo:fo + half].unsqueeze(1).to_broadcast([128, nh, half])
        x1 = s3[:, :, rope_lo:rope_lo + half]
        x2 = s3[:, :, rope_lo + half:rope_lo + 2 * half]
        m = nh * half
        tA = rt[:, 0, 0:m].rearrange("p (h d) -> p h d", h=nh)
        tB = rt[:, 1, 0:m].rearrange("p (h d) -> p h d", h=nh)
        tC = rt[:, 2, 0:m].rearrange("p (h d) -> p h d", h=nh)
        tD = rt[:, 3, 0:m].rearrange("p (h d) -> p h d", h=nh)
        rA, rB, rC, rD = [(rt_nm, i) for i in range(4)]
        self.dve(lambda e: e.tensor_tensor(out=tA, in0=x1, in1=cos, op=ALU.mult), r=nms + [tg_nm], w=[rA])
        self.dve(lambda e: e.tensor_tensor(out=tB, in0=x2, in1=sin, op=ALU.mult), r=nms + [tg_nm], w=[rB])
        yield
        self.dve(lambda e: e.tensor_tensor(out=tC, in0=x2, in1=cos, op=ALU.mult), r=nms + [tg_nm], w=[rC])
        self.dve(lambda e: e.tensor_tensor(out=tD, in0=x1, in1=sin, op=ALU.mult), r=nms + [tg_nm], w=[rD])
        yield
        self.dve(lambda e: e.tensor_tensor(out=x1, in0=tA, in1=tB, op=ALU.subtract), r=[rA, rB], w=nms)
        yield
        self.dve(lambda e: e.tensor_tensor(out=x2, in0=tC, in1=tD, op=ALU.add), r=[rC, rD], w=nms)
        yield
        self.act(lambda e: e.activation(out=dst_bf, in_=src, func=AF.Copy), r=nms, w=[nm_bf])
        yield

    def tr_store(self, nm_bf, src_bf, nh, hd, pstr, pcnt, stg, stg_nm, dram, t, hw=None, row0=0):
        hw = hw or hd
        s3 = src_bf.rearrange("p (h d) -> p h d", h=nh)
        done = 0
        while done < nh:
            nb = min(8, nh - done)
            ps = pstr[pcnt[0] % 2]
            rn = ("pstr", pcnt[0] % 2)
            pcnt[0] += 1

            def tp(e, done=done, nb=nb, ps=ps):
                for i in range(nb):
                    ins = e.transpose(ps[0:hw, i, :], s3[:, done + i, 0:hw], self.identb[:])
                return ins
            self.pe(tp, r=[nm_bf, "identb"], w=[rn])
            self.dve(lambda e, done=done, nb=nb, ps=ps: e.tensor_copy(out=stg[0:hw, done:done + nb, :], in_=ps[0:hw, 0:nb, :]),
                     r=[rn], w=[(stg_nm, done)])
            yield
            dst = dram[done:done + nb, row0:row0 + hw, t * 128:(t + 1) * 128].rearrange("h d s -> d h s")
            self.ldc(dst, stg[0:hw, done:done + nb, :], r=[(stg_nm, done)], a=[("dram", id(dram))])
            done += nb

    def load_w(self, dst, src, K, rn, n0=0, n1=None, d0=0):
        n1 = n1 if n1 is not None else src.shape[1]
        kc = K // 128
        step = 2 if (n1 - n0) > 1024 else kc
        for k0 in range(0, kc, step):
            k1 = min(kc, k0 + step)
            self.ldc(dst[:, k0:k1, d0:d0 + (n1 - n0)],
                     src[k0 * 128:k1 * 128, n0:n1].rearrange("(k p) n -> p k n", p=128), a=[rn])

    def phaseA(self, layer):
        sb = self.sb
        NI = 2
        ncol = EVEN_IN if layer == 0 else ODD_IN
        xin = self.x if layer == 0 else self.xL
        with ExitStack() as es:
            w_in = sb(es, "w_in", [128, 8, ncol], BF16)
            trg = sb(es, "trg", [128, NI, 2, 48], F32)
            xt = sb(es, "xt", [128, NI, D], F32)
            xs = sb(es, "xs", [128, NI, D], BF16)
            ssx = sb(es, "ssx", [128, NI, 4], F32)
            hT = sb(es, "hT", [128, NI, 8, 128], BF16)
            tmpf = sb(es, "tmpf", [128, NI, D], F32)
            u = sb(es, "u", [128, NI, ncol], F32)
            sq = sb(es, "sq", [128, NI, D], F32)
            ssh = sb(es, "ssh", [128, NI, 16], F32)
            rt = sb(es, "rt", [128, NI, 4, 512], F32)
            qbf = sb(es, "qbf", [128, NI, 2048], BF16)
            vbf = sb(es, "vbf", [128, NI, 1024], BF16)
            stg = [sb(es, "stg%d" % i, [96, NI, 12, 128], BF16) for i in range(4)]
            pT = self.ps(es, "pT", [128, 8, 128], BF16)
            psu = [self.ps(es, "psu%d" % i, [128, 512], F32) for i in range(2)]
            pstr = [self.ps(es, "pstr%d" % i, [128, 8, 128], BF16) for i in range(2)]
            self.load_w(w_in, self.e_w_in if layer == 0 else self.o_w_in, D, "w_in")
            if layer == 0:
                w_uq = sb(es, "w_uqb", [128, 3, 768], BF16)
                w_ukv = sb(es, "w_ukvb", [128, 2, 1024], BF16)
                latTs = sb(es, "latTs", [128, 5], F32)
                latb = sb(es, "latb", [128, NI, 640], BF16)
                latTt = sb(es, "latTt", [128, NI, 5, 128], BF16)
                qm = sb(es, "qm", [128, NI, 768], F32)
                kf_ = sb(es, "kfull", [128, NI, 768], F32)
                psup = [self.ps(es, "psup%d" % i, [128, 512], F32) for i in range(2)]
                self.ld(latTs[:], self.latT, w=["latTs"])
                self.ldc(w_uq[:], self.w_uq.rearrange("(k p) n -> p k n", p=128), w=["w_uq"])
                self.ldc(w_ukv[:], self.w_ukv.rearrange("(k p) n -> p k n", p=128), w=["w_ukv"])
                self.dve(lambda e: e.tensor_tensor(out=w_uq[:], in0=w_uq[:], in1=latTs[:, 0:3].unsqueeze(2).to_broadcast([128, 3, 768]), op=ALU.mult),
                         r=["w_uq", "latTs"], w=["w_uq"])
                self.dve(lambda e: e.tensor_tensor(out=w_ukv[:], in0=w_ukv[:], in1=latTs[:, 3:5].unsqueeze(2).to_broadcast([128, 2, 1024]), op=ALU.mult),
                         r=["w_ukv", "latTs"], w=["w_ukv"])
            else:
                kmacc = sb(es, "kmacc", [64, 8, 32], F32)
                kmb = sb(es, "kmb", [64, 8, 32], BF16)
                kpart = sb(es, "kpart", [64, NI, 8], F32)
                gsb = sb(es, "gsb", [128, NI, 8, 32], F32)
                top8 = sb(es, "top8", [128, NI, 8, 8], F32)
                biasf = sb(es, "biasf", [128, NI, 8, 32], F32)
                biasb = sb(es, "biasb", [128, NI, 8, 32], BF16)
                psg = self.ps(es, "psg", [128, 8, 32], F32)
                self.pool(lambda e: e.memset(gsb[:], -1e30), w=[("gsb", i) for i in range(NI)])
                self.pool(lambda e: e.memset(kmacc[:], 0.0), w=["kmacc"])
            pcnt = [0]
            trg_d = self.trigd.rearrange("p (s t f) -> p s t f", s=2, t=NT)

            def body(t, sl):
                R = lambda nm: (nm, sl)
                xtt = xt[:, sl]
                u_ = u[:, sl]
                sq_ = sq[:, sl]
                ssx_ = ssx[:, sl]
                ssh_ = ssh[:, sl]
                rt_ = rt[:, sl]
                qbf_ = qbf[:, sl]
                vbf_ = vbf[:, sl]
                tg = trg[:, sl]
                self.ld(xtt, xin[t * 128:(t + 1) * 128, :], w=[R("xt")])
                self.ld(tg, trg_d[:, :, t, :], w=[R("tg")])
                yield
                self.act(lambda e: e.activation(out=sq_, in_=xtt, func=AF.Square, accum_out=ssx_[:, 0:1]), r=[R("xt")], w=[R("sq"), R("ssx")])
                yield
                self.act(lambda e: e.activation(out=ssx_[:, 1:2], in_=ssx_[:, 0:1], func=AF.Sqrt, scale=1.0 / D, bias=self.epsT[:, 0:1]), r=[R("ssx")], w=[R("rsx")])
                yield
                self.dve(lambda e: e.reciprocal(out=ssx_[:, 1:2], in_=ssx_[:, 1:2]), r=[R("rsx")], w=[R("rsx")])
                yield
                self.act(lambda e: e.activation(out=xs[:, sl], in_=xtt, func=AF.Copy, scale=ssx_[:, 1:2]), r=[R("xt"), R("rsx")], w=[R("xs")])
                yield

                def tpx(e):
                    for j in range(8):
                        ins = e.transpose(pT[:, j, :], xs[:, sl, j * 128:(j + 1) * 128], self.identb[:])
                    return ins
                self.pe(tpx, r=[R("xs"), "identb"], w=["pT"])
                ao = layer * 16
                a_bc = self.aT[:, ao:ao + 8].unsqueeze(2).to_broadcast([128, 8, 128])
                b_bc = self.modT[:, layer * 48:layer * 48 + 8].unsqueeze(2).to_broadcast([128, 8, 128])
                tm3 = tmpf[:, sl].rearrange("p (j s) -> p j s", j=8)
                self.dve(lambda e: e.tensor_tensor(out=tm3, in0=pT[:], in1=a_bc, op=ALU.mult), r=["pT", ("aT", ao)], w=[R("tmpf")])
                yield
                self.dve(lambda e: e.tensor_tensor(out=hT[:, sl], in0=tm3, in1=b_bc, op=ALU.add), r=[R("tmpf"), "modT"], w=[R("hT")])
                yield
                ngrp = (ncol + 511) // 512
                for gi, c0 in enumerate(range(0, ncol, 512)):
                    c1 = min(ncol, c0 + 512)
                    pb = psu[gi % 2]

                    def mm(e, c0=c0, c1=c1, pb=pb):
                        for j in range(8):
                            ins = e.matmul(pb[:, 0:c1 - c0], lhsT=hT[:, sl, j, :], rhs=w_in[:, j, c0:c1], start=(j == 0), stop=(j == 7))
                        return ins
                    self.pe(mm, r=[R("hT"), "w_in"], w=[("psu", gi % 2)])
                    self.act(lambda e, c0=c0, c1=c1, pb=pb: e.activation(out=u_[:, c0:c1], in_=pb[:, 0:c1 - c0], func=AF.Copy),
                             r=[("psu", gi % 2)], w=[("u", sl, gi)])
                    yield
                uall = [("u", sl, gi) for gi in range(ngrp)]
                U = lambda gi: ("u", sl, gi)
                if layer == 0:
                    latb_ = latb[:, sl]
                    qm_ = qm[:, sl]
                    kfs = kf_[:, sl]
                    self.act(lambda e: e.activation(out=sq_[:, 0:384], in_=u_[:, 0:384], func=AF.Square, accum_out=ssx_[:, 2:3]), r=[U(0)], w=[R("sq"), R("ssl")])
                    self.act(lambda e: e.activation(out=sq_[:, 384:640], in_=u_[:, 384:640], func=AF.Square, accum_out=ssx_[:, 3:4]), r=[U(0), U(1), R("sq")], w=[R("sq"), R("ssl2")])
                    yield
                    self.act(lambda e: e.activation(out=ssx_[:, 2:3], in_=ssx_[:, 2:3], func=AF.Sqrt, scale=1.0 / 384, bias=self.epsT[:, 0:1]), r=[R("ssl")], w=[R("ssl")])
                    self.act(lambda e: e.activation(out=ssx_[:, 3:4], in_=ssx_[:, 3:4], func=AF.Sqrt, scale=1.0 / 256, bias=self.epsT[:, 0:1]), r=[R("ssl2")], w=[R("ssl2")])
                    yield
                    self.dve(lambda e: e.reciprocal(out=ssx_[:, 2:4], in_=ssx_[:, 2:4]), r=[R("ssl"), R("ssl2")], w=[R("ssl"), R("ssl2")])
                    yield
                    self.dve(lambda e: e.tensor_scalar(out=latb_[:, 0:384], in0=u_[:, 0:384], scalar1=ssx_[:, 2:3], scalar2=None, op0=ALU.mult), r=[U(0), R("ssl")], w=[R("latb0")])
                    self.dve(lambda e: e.tensor_scalar(out=latb_[:, 384:640], in0=u_[:, 384:640], scalar1=ssx_[:, 3:4], scalar2=None, op0=ALU.mult), r=[U(0), U(1), R("ssl2")], w=[R("latb1")])
                    yield
                    ps = pstr[pcnt[0] % 2]
                    rn = ("pstr", pcnt[0] % 2)
                    pcnt[0] += 1

                    def tpl(e, ps=ps):
                        for j in range(5):
                            ins = e.transpose(ps[:, j, :], latb_[:, j * 128:(j + 1) * 128], self.identb[:])
                        return ins
                    self.pe(tpl, r=[R("latb0"), R("latb1"), "identb"], w=[rn])
                    self.dve(lambda e, ps=ps: e.tensor_copy(out=latTt[:, sl], in_=ps[:, 0:5, :]), r=[rn], w=[R("latTt")])
                    yield
                    for gi, (c0, c1) in enumerate(((0, 512), (512, 768))):
                        def mmq(e, c0=c0, c1=c1, gi=gi):
                            for j in range(3):
                                ins = e.matmul(psup[gi][:, 0:c1 - c0], lhsT=latTt[:, sl, j, :], rhs=w_uq[:, j, c0:c1], start=(j == 0), stop=(j == 2))
                            return ins
                        self.pe(mmq, r=[R("latTt"), "w_uq"], w=[("psup", gi)])
                        self.act(lambda e, c0=c0, c1=c1, gi=gi: e.activation(out=qm_[:, c0:c1], in_=psup[gi][:, 0:c1 - c0], func=AF.Copy),
                                 r=[("psup", gi)], w=[R("qm")] if gi == 0 else [], a=[] if gi == 0 else [R("qm")])
                        yield
                    kf3 = kfs.rearrange("p (h d) -> p h d", h=8)
                    vb3 = vbf_[:, 0:512].rearrange("p (h d) -> p h d", h=8)
                    for gi in range(2):
                        def mmk(e, gi=gi):
                            for j in range(2):
                                ins = e.matmul(psup[gi][:], lhsT=latTt[:, sl, 3 + j, :], rhs=w_ukv[:, j, gi * 512:(gi + 1) * 512], start=(j == 0), stop=(j == 1))
                            return ins
                        self.pe(mmk, r=[R("latTt"), "w_ukv"], w=[("psup", gi)])
                        p3 = psup[gi][:].rearrange("p (h d) -> p h d", h=4)
                        self.act(lambda e, gi=gi, p3=p3: e.activation(out=kf3[:, gi * 4:gi * 4 + 4, 0:64], in_=p3[:, :, 0:64], func=AF.Copy),
                                 r=[("psup", gi)], w=[R("kfull")] if gi == 0 else [], a=[] if gi == 0 else [R("kfull")])
                        self.act(lambda e, gi=gi, p3=p3: e.activation(out=vb3[:, gi * 4:gi * 4 + 4, :], in_=p3[:, :, 64:128], func=AF.Copy),
                                 r=[("psup", gi)], w=[R("vbf")] if gi == 0 else [], a=[] if gi == 0 else [R("vbf")])
                        yield
                    self.dve(lambda e: e.tensor_copy(out=kf3[:, :, 64:96], in_=u_[:, 640:672].unsqueeze(1).to_broadcast([128, 8, 32])), r=[U(1)], a=[R("kfull")])
                    yield
                    yield from self.head_post(R("qm"), qm_, 8, 96, 0, 64, 16, tg, R("tg"), sq_, R("sq"), ssh_, R("ssh"), rt_, R("rt"), qbf_[:, 0:768], R("qbf0"))
                    yield from self.tr_store(R("qbf0"), qbf_[:, 0:768], 8, 96, pstr, pcnt, stg[0][:, sl], R("stg0"), self.qT_mla, t)
                    yield from self.head_post(R("kfull"), kfs, 8, 96, 96, 64, 16, tg, R("tg"), sq_, R("sq"), ssh_, R("ssh"), rt_, R("rt"), qbf_[:, 768:1536], R("qbf1"))
                    yield from self.tr_store(R("qbf1"), qbf_[:, 768:1536], 8, 96, pstr, pcnt, stg[1][:, sl], R("stg1"), self.kT_mla, t)
                    self.ldc(self.v_mla[t * 128:(t + 1) * 128, :], vbf_[:, 0:512], r=[R("vbf")], a=["v_mla"])
                    yield
                    yield from self.head_post([U(1), U(2)], u_[:, 672:1440], 12, 64, 192, 0, 32, tg, R("tg"), sq_, R("sq"), ssh_, R("ssh"), rt_, R("rt"), qbf_[:, 0:768], R("qbf0"))
                    yield from self.tr_store(R("qbf0"), qbf_[:, 0:768], 12, 64, pstr, pcnt, stg[2][:, sl], R("stg2"), self.qT_dil, t)
                    yield from self.head_post([U(2), U(3), U(4)], u_[:, 1440:2208], 12, 64, 256, 0, 32, tg, R("tg"), sq_, R("sq"), ssh_, R("ssh"), rt_, R("rt"), qbf_[:, 768:1536], R("qbf1"))
                    yield from self.tr_store(R("qbf1"), qbf_[:, 768:1536], 12, 64, pstr, pcnt, stg[3][:, sl], R("stg3"), self.kT_dil, t)
                    self.act(lambda e: e.activation(out=vbf_[:, 0:768], in_=u_[:, 2208:2976], func=AF.Copy), r=uall, w=[R("vbf")])
                    yield
                    self.ldc(self.v_dil[t * 128:(t + 1) * 128, :], vbf_[:, 0:768], r=[R("vbf")], a=["v_dil"])
                    yield
                else:
                    yield from self.head_post(U(0), u_[:, 0:512], 8, 64, 320, 0, 32, tg, R("tg"), sq_, R("sq"), ssh_, R("ssh"), rt_, R("rt"), qbf_[:, 0:512], R("qbfA"))
                    yield from self.tr_store(R("qbfA"), qbf_[:, 0:512], 8, 64, pstr, pcnt, stg[0][:, sl], R("stg0"), self.qT_df, t)
                    yield from self.head_post(U(1), u_[:, 512:1024], 8, 64, 384, 0, 32, tg, R("tg"), sq_, R("sq"), ssh_, R("ssh"), rt_, R("rt"), qbf_[:, 512:1024], R("qbfB"))
                    yield from self.tr_store(R("qbfB"), qbf_[:, 512:1024], 8, 64, pstr, pcnt, stg[1][:, sl], R("stg1"), self.kT_df, t)
                    self.act(lambda e: e.activation(out=vbf_[:, 0:512], in_=u_[:, 1024:1536], func=AF.Copy), r=uall, w=[R("vbf")])
                    yield
                    self.ldc(self.v_df[t * 128:(t + 1) * 128, :], vbf_[:, 0:512], r=[R("vbf")], a=["v_df"])
                    self.act(lambda e: e.activation(out=vbf_[:, 512:1024], in_=u_[:, 2560:3072], func=AF.Copy), r=uall, w=[R("vbf2")])
                    yield
                    self.ldc(self.v_mb[t * 128:(t + 1) * 128, :], vbf_[:, 512:1024], r=[R("vbf2")], a=["v_mb"])
                    yield
                    yield from self.head_post(U(4), u_[:, 2048:2560], 8, 64, 512, 0, 32, tg, R("tg"), sq_, R("sq"), ssh_, R("ssh"), rt_, R("rt"), qbf_[:, 1024:1536], R("qbfC"))
                    yield from self.tr_store(R("qbfC"), qbf_[:, 1024:1536], 8, 64, pstr, pcnt, stg[2][:, sl], R("stg2"), self.kT_mb, t)
                    nblk = t // 2
                    kp = kpart[:, sl]
                    self.dve(lambda e: e.tensor_reduce(out=kp, in_=stg[2][0:64, sl, 0:8, :], axis=AX.X, op=ALU.add), r=[(R("stg2"), 0)], w=[R("kpart")])
                    yield
                    self.dve(lambda e: e.tensor_tensor(out=kmacc[:, :, nblk], in0=kmacc[:, :, nblk], in1=kp, op=ALU.add), r=[R("kpart"), "kmacc"], w=["kmacc"])
                    yield
                    if t % 2 == 1:
                        self.act(lambda e: e.activation(out=kmb[:, :, nblk], in_=kmacc[:, :, nblk], func=AF.Copy, scale=1.0 / 256), r=["kmacc"], a=["kmb"])
                        yield
                    yield from self.head_post(U(3), u_[:, 1536:2048], 8, 64, 448, 0, 32, tg, R("tg"), sq_, R("sq"), ssh_, R("ssh"), rt_, R("rt"), qbf_[:, 1536:2048], R("qbfD"))
                    yield from self.tr_store(R("qbfD"), qbf_[:, 1536:2048], 8, 64, pstr, pcnt, stg[3][:, sl], R("stg3"), self.qT_mb, t)
                    gs_ = gsb[:, sl]
                    t8 = top8[:, sl]
                    bf_ = biasf[:, sl]
                    bb_ = biasb[:, sl]
                    if nblk > 0:
                        def mmg(e):
                            for h in range(8):
                                ins = e.matmul(psg[:, h, 0:nblk], lhsT=stg[3][0:64, sl, h, :], rhs=kmb[:, h, 0:nblk], start=True, stop=True)
                            return ins
                        self.pe(mmg, r=[(R("stg3"), 0), "kmb"], w=["psg"])
                        self.dve(lambda e: e.tensor_copy(out=gs_[:, :, 0:nblk], in_=psg[:, :, 0:nblk]), r=["psg"], w=[R("gsb")])
                        yield

                        def mx(e):
                            for h in range(8):
                                ins = e.max(out=t8[:, h, :], in_=gs_[:, h, :])
                            return ins
                        self.dve(mx, r=[R("gsb")], w=[R("top8")])
                        yield
                        self.dve(lambda e: e.tensor_tensor(out=bf_, in0=gs_, in1=t8[:, :, 2:3].to_broadcast([128, 8, 32]), op=ALU.is_lt), r=[R("gsb"), R("top8")], w=[R("biasf")])
                        yield
                        self.dve(lambda e: e.tensor_scalar(out=bb_, in0=bf_, scalar1=NEGB, scalar2=None, op0=ALU.mult), r=[R("biasf")], w=[R("biasb")])
                        yield
                    else:
                        self.dve(lambda e: e.memset(bb_, NEGB), w=[R("biasb")])
                        yield
                    self.dve(lambda e: e.memset(bb_[:, :, nblk:nblk + 1], 0.0), r=[R("biasb")], w=[R("biasb")])
                    yield
                    ps = pstr[pcnt[0] % 2]
                    rn = ("pstr", pcnt[0] % 2)
                    pcnt[0] += 1

                    def tpb(e, ps=ps):
                        for h in range(8):
                            ins = e.transpose(ps[0:32, h, :], bb_[:, h, :], self.identb[:])
                        return ins
                    self.pe(tpb, r=[R("biasb"), "identb"], w=[rn])
                    self.dve(lambda e, ps=ps: e.tensor_copy(out=stg[0][0:32, sl, 0:8, :], in_=ps[0:32, 0:8, :]), r=[rn], w=[(R("stg0"), 0)])
                    yield
                    dst = self.qT_mb[:, 64:96, t * 128:(t + 1) * 128].rearrange("h d s -> d h s")
                    self.ldc(dst, stg[0][0:32, sl, 0:8, :], r=[(R("stg0"), 0)], a=["qT_mb_bias"])
                    yield

            import os as _os
            STAG = int(_os.environ.get("KSTAG", "35"))
            active = []
            next_t = 0
            while next_t < self.nt or active:
                if next_t < self.nt and len(active) < NI and (not active or active[-1][1] >= STAG):
                    active.append([body(next_t, next_t % NI), 0])
                    next_t += 1
                for a_ in list(active):
                    try:
                        next(a_[0])
                        a_[1] += 1
                    except StopIteration:
                        active.remove(a_)
            self.S.flush()

    def attn_tiles(self, tiles, ps_s, P, po, po_nm, sc, cnt, hook=None, hook_at=5, acc=None, acc_nm=None, vrows=65, pair=False):
        n = len(tiles)
        ns = len(ps_s)
        npb = P.shape[1]
        import os as _os
        LA = min(ns - 1, int(_os.environ.get('KLA', '3')))
        for i in range(n + LA):
            if i < n:
                kap, qap, q0, N, mk, vs = tiles[i][:6]
                si = (cnt + i) % ns
                pi = (cnt + i) % npb
                if not pair:
                    self.pe(lambda e, kap=kap, qap=qap, N=N, si=si: e.matmul(ps_s[si][:, 0:N], lhsT=kap, rhs=qap, start=True, stop=True),
                            r=tiles[i][6], w=[("ps_s", si)])
                elif i % 2 == 0:
                    grp = [i] + ([i + 1] if i + 1 < n else [])

                    def qk2(e, grp=grp):
                        for ii in grp:
                            kap2, qap2, _, N2 = tiles[ii][:4]
                            ins = e.matmul(ps_s[(cnt + ii) % ns][:, 0:N2], lhsT=kap2, rhs=qap2, start=True, stop=True)
                        return ins
                    rr = []
                    for ii in grp:
                        rr += tiles[ii][6]
                    self.pe(qk2, r=rr, w=[("ps_s", (cnt + ii) % ns) for ii in grp])
                self.act(lambda e, N=N, si=si, pi=pi: e.activation(out=P[:, pi, 0:N], in_=ps_s[si][:, 0:N], func=AF.Exp, scale=sc),
                         r=[("ps_s", si)], w=[("P", pi)])
                if mk is not None:
                    self.dve(lambda e, N=N, pi=pi, mk=mk: e.tensor_tensor(out=P[:, pi, 0:N], in0=P[:, pi, 0:N], in1=mk, op=ALU.mult),
                             r=[("P", pi), "masks"], w=[("P", pi)])
                if acc is not None:
                    if i == 0:
                        self.dve(lambda e, N=N, pi=pi, q0=q0: e.tensor_copy(out=acc[:, q0:512], in_=P[:, pi, 0:N]), r=[("P", pi)], w=[acc_nm])
                    else:
                        self.dve(lambda e, N=N, pi=pi, q0=q0: e.tensor_tensor(out=acc[:, q0:512], in0=acc[:, q0:512], in1=P[:, pi, 0:N], op=ALU.add),
                                 r=[("P", pi), acc_nm], w=[acc_nm])
            if hook is not None and i == min(hook_at, n + LA - 1):
                hook()
                hook = None
            j = i - LA
            if j >= 0:
                kap, qap, q0, N, mk, vs = tiles[j][:6]
                pi = (cnt + j) % npb
                for f, vap in enumerate(vs):
                    self.pe(lambda e, f=f, vap=vap, q0=q0, N=N, pi=pi, j=j: e.matmul(po[f][0:vrows, q0:512], lhsT=vap, rhs=P[:, pi, 0:N],
                                                                                    start=(j == 0), stop=(j == n - 1)),
                            r=[("P", pi)] + tiles[j][7], w=[po_nm[f]] if j == 0 else [], a=[] if j == 0 else [po_nm[f]])
                nd = getattr(self, "ndummy", 0)
                if nd:
                    def dm(e):
                        for _ in range(nd):
                            ins = e.matmul(self.ps_dummy[:, 0:128], lhsT=self.identb[:], rhs=self.identb[:], start=True, stop=True)
                        return ins
                    self.pe(dm, r=["identb"], a=["psdummy"])
        if hook is not None:
            hook()
        return cnt + n

    def attn_norm(self, po, po_nm, ep, slot, dst, dst_nm):
        rrow, osb, ps_bc = ep
        self.act(lambda e: e.activation(out=rrow[64:65, slot, :], in_=po[64:65, :], func=AF.Ln), r=[po_nm], w=[("rrow", slot)])
        self.act(lambda e: e.activation(out=rrow[64:65, slot, :], in_=rrow[64:65, slot, :], func=AF.Exp, scale=-1.0), r=[("rrow", slot)], w=[("rrow", slot)])
        self.pe(lambda e: e.matmul(ps_bc[0:64, :], lhsT=self.onesb[64:65, 0:64], rhs=rrow[64:65, slot, :], start=True, stop=True),
                r=[("rrow", slot), "onesb"], w=["ps_bc"])
        self.act(lambda e: e.activation(out=osb[0:64, slot, :], in_=po[0:64, :], func=AF.Copy), r=[po_nm], w=[("osb", slot)])
        self.dve(lambda e: e.tensor_tensor(out=dst, in0=osb[0:64, slot, :], in1=ps_bc[0:64, :], op=ALU.mult),
                 r=[("osb", slot), "ps_bc"], w=[dst_nm])

    def phaseB(self, kind):
        sb = self.sb
        nch = max(1, self.nt // 4)
        ncols = nch * 512
        dil = kind == "dil"
        nheads = {"mla": 8, "dil": 4, "diff": 4, "moba": 8}[kind]
        nm = 2 if kind == "diff" else 1
        parts = 1
        isdf = kind == "diff"
        dk = {"mla": 96, "dil": 64, "diff": 64, "moba": 96}[kind]
        sc = float({"mla": 96 ** -0.5, "dil": 0.125, "diff": 0.125, "moba": 0.125}[kind])
        qsrc = {"mla": self.qT_mla, "dil": self.qT_dil, "diff": self.qT_df, "moba": self.qT_mb}[kind]
        ksrc = {"mla": self.kT_mla, "dil": self.kT_dil, "diff": self.kT_df, "moba": self.kT_mb}[kind]
        vsrc = {"mla": self.v_mla, "dil": self.v_dil, "diff": self.v_df, "moba": self.v_mb}[kind]
        row_base = {"mla": 0, "dil": 512, "diff": 0, "moba": 512}[kind]
        lam_init = 0.8 - 0.6 * math.exp(-0.3 * 1)
        with ExitStack() as es:
            P_dummy = sb(es, "Pdummy", [128, 2], F32)
            if dil:
                qT = sb(es, "qTd", [128, 2, 3, 512], BF16)
                kT = sb(es, "kTd", [128, 3, SEQ // 2], BF16)
                V = sb(es, "Vd", [128, 3, 64, 65], BF16)
                dmb = sb(es, "dmb", [128, 33, 512], BF16)
                self.ldc(dmb[:], self.dmask.rearrange("p (a b) -> p a b", a=33), w=["masks"])
                self.pool(lambda e: e.memset(V[:, :, :, 64:65], 1.0), a=["Vones"])
            else:
                if isdf:
                    qT = sb(es, "qTp", [128, 2, SEQ], BF16)
                    kT = sb(es, "kTp", [128, 2, SEQ // 2], BF16)
                else:
                    qT = sb(es, "qTa", [96, 2, SEQ], BF16)
                    kT = sb(es, "kTa", [96, 2, SEQ], BF16)
                vw = 128 if isdf else 65
                V = sb(es, "Va", [128, 2, parts, 64, vw], BF16)
                if isdf:
                    self.pool(lambda e: e.memset(P_dummy[:], 0.0), a=["Vones"])
                else:
                    self.pool(lambda e: e.memset(V[:, :, :, :, 64:65], 1.0), a=["Vones"])
                if kind == "moba":
                    for sl in range(2):
                        self.ldc(kT[64:96, sl, :], self.kind, a=["Vones"])
            import os as _os
            self.ndummy = 0
            n_po = 2
            n_s = 7 - n_po - (1 if self.ndummy else 0)
            self.ps_dummy = self.ps(es, "psdummy", [128, 512], F32) if self.ndummy else None
            P = sb(es, "Pt", [128, n_s + 2, 512], BF16)
            rrow = sb(es, "rrow", [65, 2, 512], F32)
            osb = sb(es, "osb", [64, 2, 512], F32)
            onb = sb(es, "onb", [64, 4, 512], BF16)
            onbd = sb(es, "onbd", [128, 2, 512], BF16)
            ps_s = [self.ps(es, "ps_s%d" % i, [128, 512], F32) for i in range(n_s)]
            po = [self.ps(es, "po%d" % i, [128, 512], F32) for i in range(n_po)]
            ps_bc = self.ps(es, "ps_bc", [128, 512], F32)
            ep = (rrow, osb, ps_bc)
            if isdf:
                acc = sb(es, "acc", [128, 2, 512], F32)
                rcp = sb(es, "rcp", [128, 512], F32)
                nrm = sb(es, "nrm", [128, 2, 512], F32)
                dd = sb(es, "dd", [128, 512], F32)
                sqd = sb(es, "sqd", [128, 512], F32)
                rsd = sb(es, "rsd", [128, 512], F32)
                onesf = sb(es, "onesfB", [128, 128], F32)
                lamt = sb(es, "lamt", [128, 256], F32)
                lsm = sb(es, "lsm", [128, 8], F32)
                subs = sb(es, "subs", [128, 1], F32)
                self.pool(lambda e: e.memset(onesf[:], 1.0), w=["onesfB"])
                self.ld(lamt[:], self.dlam.partition_broadcast(128), w=["lamt"])
                self.ld(subs[:], self.subT, w=["subs"])
                self.dve(lambda e: e.tensor_tensor(out=lamt[:, 0:64], in0=lamt[:, 0:64], in1=lamt[:, 64:128], op=ALU.mult), r=["lamt"], w=["lamt"])
                self.dve(lambda e: e.tensor_tensor(out=lamt[:, 128:192], in0=lamt[:, 128:192], in1=lamt[:, 192:256], op=ALU.mult), r=["lamt"], w=["lamt"])
                self.dve(lambda e: e.tensor_reduce(out=lsm[:, 0:1], in_=lamt[:, 0:64], axis=AX.X, op=ALU.add), r=["lamt"], w=["lsm0"])
                self.dve(lambda e: e.tensor_reduce(out=lsm[:, 1:2], in_=lamt[:, 128:192], axis=AX.X, op=ALU.add), r=["lamt"], w=["lsm1"])
                self.act(lambda e: e.activation(out=lsm[:, 2:4], in_=lsm[:, 0:2], func=AF.Exp), r=["lsm0", "lsm1"], w=["lsm2"])
                self.dve(lambda e: e.tensor_tensor(out=lsm[:, 4:5], in0=lsm[:, 3:4], in1=lsm[:, 2:3], op=ALU.subtract), r=["lsm2"], w=["lsm4"])
                self.dve(lambda e: e.tensor_scalar(out=lsm[:, 5:6], in0=lsm[:, 4:5], scalar1=-lam_init, scalar2=None, op0=ALU.add), r=["lsm4"], w=["neglam"])
                self.dve(lambda e: e.tensor_scalar(out=subs[:], in0=subs[:], scalar1=1.0 - lam_init, scalar2=None, op0=ALU.mult), r=["subs"], w=["subs"])
            cnt = 0
            ecnt = 0
            pending = [None]
            for h in range(nheads):
                vsl = h % 2
                if dil:
                    for g in range(3):
                        gh = g * 4 + h
                        for hf in range(2):
                            self.ld(kT[hf * 64:(hf + 1) * 64, g, 0:ncols // 2].rearrange("d (j s) -> d j s", s=128),
                                    ksrc[gh, :, 0:ncols].rearrange("d (j two s) -> d j two s", two=2, s=128)[:, :, hf, :],
                                    w=[("kT", g)] if hf == 0 else [], a=[] if hf == 0 else [("kT", g)])
                        self.ld(V[:, g, 0:nch * 4, 0:64], vsrc[0:ncols, gh * 64:gh * 64 + 64].rearrange("(t p) d -> p t d", p=128),
                                r=["Vones"], w=[("V", g)])
                else:
                    for f in range(parts):
                        vd = 128 if isdf else 64
                        c0 = h * vd
                        self.ld(V[:, vsl, f, 0:nch * 4, 0:vd], vsrc[0:ncols, c0:c0 + vd].rearrange("(t p) d -> p t d", p=128),
                                r=["Vones"], w=[("V", vsl, f)])
                    for m in range(nm):
                        u_ = h * nm + m
                        sl = u_ % 2
                        dq = 96 if kind in ("mla", "moba") else 64
                        dkk = 96 if kind == "mla" else 64
                        if isdf:
                            for hf in range(2):
                                self.ld(qT[hf * 64:(hf + 1) * 64, sl, 0:ncols], qsrc[u_, 0:64, 0:ncols], w=[("qT", sl)] if hf == 0 else [], a=[] if hf == 0 else [("qT", sl)])
                                self.ld(kT[hf * 64:(hf + 1) * 64, sl, 0:ncols // 2].rearrange("d (j s) -> d j s", s=128),
                                        ksrc[u_, 0:64, 0:ncols].rearrange("d (j two s) -> d j two s", two=2, s=128)[:, :, hf, :],
                                        w=[("kT", sl)] if hf == 0 else [], a=[] if hf == 0 else [("kT", sl)])
                        else:
                            self.ld(qT[0:dq, sl, 0:ncols], qsrc[u_, 0:dq, 0:ncols], w=[("qT", sl)])
                            self.ld(kT[0:dkk, sl, 0:ncols], ksrc[u_, 0:dkk, 0:ncols], r=["Vones"], w=[("kT", sl)])
                for c in range(nch):
                    if dil:
                        qs = c % 2
                        for g in range(3):
                            for hf in range(2):
                                self.ld(qT[hf * 64:(hf + 1) * 64, qs, g, :], qsrc[g * 4 + h, :, c * 512:(c + 1) * 512],
                                        w=[("qT", qs, g)] if hf == 0 else [], a=[] if hf == 0 else [("qT", qs, g)])
                    for m in range(nm):
                        u_ = h * nm + m
                        sl = u_ % 2
                        tiles = []
                        if dil:
                            mo = 0
                            for g, W in enumerate((1, 4, 16)):
                                for o in range(W + 4):
                                    kt = 4 * c - W + o
                                    if kt >= 0:
                                        hf = kt % 2
                                        tiles.append((kT[hf * 64:(hf + 1) * 64, g, (kt // 2) * 128:(kt // 2 + 1) * 128], qT[hf * 64:(hf + 1) * 64, qs, g, :], 0, 512, dmb[:, mo + o, :],
                                                      [V[:, g, kt, :]], [("kT", g), ("qT", qs, g)], [("V", g)]))
                                mo += W + 4
                        else:
                            for kt in range(4 * c + 4):
                                j = kt - 4 * c
                                if j < 0:
                                    q0, mk = 0, None
                                else:
                                    q0, mk = 128 * j, self.cmb[:, j, 128 * j:512]
                                N = 512 - q0
                                if isdf:
                                    hf = kt % 2
                                    kap_ = kT[hf * 64:(hf + 1) * 64, sl, (kt // 2) * 128:(kt // 2 + 1) * 128]
                                    qap_ = qT[hf * 64:(hf + 1) * 64, sl, c * 512 + q0:(c + 1) * 512]
                                else:
                                    kap_ = kT[0:dk, sl, kt * 128:(kt + 1) * 128]
                                    qap_ = qT[0:dk, sl, c * 512 + q0:(c + 1) * 512]
                                tiles.append((kap_, qap_, q0, N, mk,
                                              [V[:, vsl, f, kt, :] for f in range(parts)], [("kT", sl), ("qT", sl)],
                                              [("V", vsl, f) for f in range(parts)]))
                        pidx = [ecnt % 2]
                        pos_ = [po[i] for i in pidx]
                        po_nm = [("po", i) for i in pidx]
                        if isdf:
                            cnt = self.attn_tiles(tiles, ps_s, P, pos_, po_nm, sc, cnt, hook=pending[0], acc=acc[:, m, :], acc_nm=("acc", m), vrows=128, pair=True)
                        else:
                            cnt = self.attn_tiles(tiles, ps_s, P, pos_, po_nm, sc, cnt, hook=pending[0], pair=dil)

                        def epilogue(pos_=pos_, po_nm=po_nm, ecnt0=ecnt, m=m, c=c, h=h):
                            if not isdf:
                                es_ = ecnt0 % 2
                                osl = ecnt0 % 4
                                self.attn_norm(pos_[0], po_nm[0], ep, es_, onb[:, osl, :], ("onb", osl))
                                r0 = row_base + h * 64
                                self.ldc(self.oT[r0:r0 + 64, c * 512:(c + 1) * 512], onb[:, osl, :], r=[("onb", osl)], a=["oT"])
                                return
                            self.pe(lambda e: e.matmul(ps_bc[:, :], lhsT=onesf[:], rhs=acc[:, m, :], start=True, stop=True), r=[("acc", m), "onesfB"], w=["ps_bc"])
                            self.act(lambda e: e.activation(out=rcp[:], in_=ps_bc[:, :], func=AF.Ln), r=["ps_bc"], w=["rcp"])
                            self.act(lambda e: e.activation(out=rcp[:], in_=rcp[:], func=AF.Exp, scale=-1.0), r=["rcp"], w=["rcp"])
                            self.dve(lambda e: e.tensor_tensor(out=nrm[:, m, :], in0=pos_[0][:, :], in1=rcp[:], op=ALU.mult), r=[po_nm[0], "rcp"], w=[("nrm", m)])
                            if m == 1:
                                self.dve(lambda e: e.scalar_tensor_tensor(out=dd[:], in0=nrm[:, 1, :], scalar=lsm[:, 5:6], in1=nrm[:, 0, :], op0=ALU.mult, op1=ALU.add),
                                         r=[("nrm", 0), ("nrm", 1), "neglam"], w=["dd"])
                                self.act(lambda e: e.activation(out=sqd[:], in_=dd[:], func=AF.Square), r=["dd"], w=["sqd"])
                                self.pe(lambda e: e.matmul(ps_bc[:, :], lhsT=onesf[:], rhs=sqd[:], start=True, stop=True), r=["sqd", "onesfB"], w=["ps_bc"])
                                self.act(lambda e: e.activation(out=rsd[:], in_=ps_bc[:, :], func=AF.Ln, scale=1.0 / 128, bias=self.epsT[:, 0:1]), r=["ps_bc"], w=["rsd"])
                                self.act(lambda e: e.activation(out=rsd[:], in_=rsd[:], func=AF.Exp, scale=-0.5), r=["rsd"], w=["rsd"])
                                osl = c % 2
                                ob = onbd[:, osl, :]
                                self.dve(lambda e, ob=ob: e.scalar_tensor_tensor(out=ob, in0=dd[:], scalar=subs[:, 0:1], in1=rsd[:], op0=ALU.mult, op1=ALU.mult),
                                         r=["dd", "rsd", "subs"], w=[("onbd", osl)])
                                self.ldc(self.oT[h * 128:(h + 1) * 128, c * 512:(c + 1) * 512], ob, r=[("onbd", osl)], a=["oT"])
                        pending[0] = epilogue
                        ecnt += parts
            if pending[0] is not None:
                pending[0]()
            self.S.flush()

    def phaseC1(self, layer):
        sb = self.sb
        nk = 6 if layer == 0 else 8
        xin = self.x if layer == 0 else self.xL
        wsrc = self.e_w_out if layer == 0 else self.o_w_out
        with ExitStack() as es:
            w_out = sb(es, "w_out", [128, nk, D], BF16)
            G1 = sb(es, "G1", [128, D], F32)
            xt = sb(es, "xtc", [128, 2, D], F32)
            oTt = sb(es, "oTt", [128, 2, nk, 128], BF16)
            tmpf = sb(es, "tmpfc", [128, D], F32)
            sq = sb(es, "sqc", [128, D], F32)
            ssx = sb(es, "ssxc", [128, 2], F32)
            xs = sb(es, "xsc", [128, D], BF16)
            hT = sb(es, "hTc", [128, 2, 8, 128], BF16)
            psy = [self.ps(es, "psy%d" % i, [128, 512], F32) for i in range(2)]
            pT = self.ps(es, "pTc", [128, 8, 128], BF16)
            self.load_w(w_out, wsrc, nk * 128, "w_out")
            self.ld(G1[:], self.Gd[:, (layer * 2) * D:(layer * 2 + 1) * D], w=["G1"])
            for t in range(self.nt):
                sl = t % 2
                self.ld(xt[:, sl], xin[t * 128:(t + 1) * 128, :], w=[("xt", sl)])
                self.ld(oTt[:, sl], self.oT[0:nk * 128, t * 128:(t + 1) * 128].rearrange("(k p) s -> p k s", p=128), w=[("oTt", sl)])
                for hf in range(2):
                    def mm(e, hf=hf, sl=sl):
                        for k in range(nk):
                            ins = e.matmul(psy[hf][:], lhsT=oTt[:, sl, k, :], rhs=w_out[:, k, hf * 512:(hf + 1) * 512], start=(k == 0), stop=(k == nk - 1))
                        return ins
                    self.pe(mm, r=[("oTt", sl), "w_out"], w=[("psy", hf)])
                    self.dve(lambda e, hf=hf: e.tensor_tensor(out=tmpf[:, hf * 512:(hf + 1) * 512], in0=psy[hf][:], in1=G1[:, hf * 512:(hf + 1) * 512], op=ALU.mult),
                             r=[("psy", hf), "G1"], w=[("tmpf", hf)])
                    self.dve(lambda e, hf=hf, sl=sl: e.tensor_tensor(out=xt[:, sl, hf * 512:(hf + 1) * 512], in0=xt[:, sl, hf * 512:(hf + 1) * 512],
                                                                    in1=tmpf[:, hf * 512:(hf + 1) * 512], op=ALU.add),
                             r=[("tmpf", hf), ("xt", sl)], w=[("xt", sl)])
                self.ldc(self.x1[t * 128:(t + 1) * 128, :], xt[:, sl], r=[("xt", sl)], a=["x1"])
                self.act(lambda e, sl=sl: e.activation(out=sq[:], in_=xt[:, sl], func=AF.Square, accum_out=ssx[:, 0:1]), r=[("xt", sl)], w=["sq", "ssx"])
                self.rstd(ssx[:, 0:1], ssx[:, 1:2], D, "ssx", "rsx")
                self.act(lambda e, sl=sl: e.activation(out=xs[:], in_=xt[:, sl], func=AF.Copy, scale=ssx[:, 1:2]), r=[("xt", sl), "rsx"], w=["xs"])

                def tpx(e):
                    for j in range(8):
                        ins = e.transpose(pT[:, j, :], xs[:, j * 128:(j + 1) * 128], self.identb[:])
                    return ins
                self.pe(tpx, r=["xs", "identb"], w=["pT"])
                ao = layer * 16 + 8
                a_bc = self.aT[:, ao:ao + 8].unsqueeze(2).to_broadcast([128, 8, 128])
                b_bc = self.modT[:, layer * 48 + 24:layer * 48 + 32].unsqueeze(2).to_broadcast([128, 8, 128])
                tm3 = tmpf[:].rearrange("p (j s) -> p j s", j=8)
                self.dve(lambda e, a_bc=a_bc, tm3=tm3: e.tensor_tensor(out=tm3, in0=pT[:], in1=a_bc, op=ALU.mult),
                         r=["pT", ("aT", ao)], w=[("tmpf", 0), ("tmpf", 1)])
                self.dve(lambda e, b_bc=b_bc, tm3=tm3, sl=sl: e.tensor_tensor(out=hT[:, sl], in0=tm3, in1=b_bc, op=ALU.add),
                         r=[("tmpf", 0), ("tmpf", 1), "modT"], w=[("hT", sl)])
                self.ldc(self.h2T[:, t * 128:(t + 1) * 128].rearrange("(k p) s -> p k s", p=128), hT[:, sl], r=[("hT", sl)], a=["h2T"])
            self.S.flush()

    def phaseC2(self, layer):
        sb = self.sb
        dst = self.xL if layer == 0 else self.out
        ng = max(1, self.nt // 2)
        with ExitStack() as es:
            w1 = sb(es, "w1", [128, 8, DFF], BF16)
            w2 = sb(es, "w2", [128, 32, D], BF16)
            G2 = sb(es, "G2", [128, D], F32)
            hTt = sb(es, "hTt", [128, 2, 8, 256], BF16)
            x1t = sb(es, "x1t", [128, 2, 2, D], F32)
            aT = sb(es, "aTt", [128, 32, 256], BF16)
            rl = sb(es, "rl", [128, 2, 512], F32)
            tmp = sb(es, "tmpc2", [128, 2, 512], F32)
            psa = [self.ps(es, "psa%d" % i, [128, 2, 256], F32) for i in range(2)]
            psy = [self.ps(es, "psy2_%d" % i, [128, 512], F32) for i in range(4)]
            self.load_w(w1, self.mlp_w1[layer], D, "w1")
            self.load_w(w2, self.mlp_w2[layer], DFF, "w2")
            self.ld(G2[:], self.Gd[:, (layer * 2 + 1) * D:(layer * 2 + 2) * D], w=["G2"])
            for g in range(ng):
                sl = g % 2
                self.ld(hTt[:, sl], self.h2T[:, g * 256:(g + 1) * 256].rearrange("(k p) s -> p k s", p=128), w=[("hTt", sl)])
                self.ld(x1t[:, sl], self.x1[g * 256:(g + 1) * 256, :].rearrange("(s p) d -> p s d", p=128), w=[("x1t", sl)])
                for fp in range(16):
                    pb = psa[fp % 2]

                    def mm1(e, fp=fp, pb=pb, sl=sl):
                        for ff in range(2):
                            f = fp * 2 + ff
                            for k in range(8):
                                ins = e.matmul(pb[:, ff, :], lhsT=w1[:, k, f * 128:(f + 1) * 128], rhs=hTt[:, sl, k, :], start=(k == 0), stop=(k == 7))
                        return ins
                    self.pe(mm1, r=["w1", ("hTt", sl)], w=[("psa", fp % 2)])
                    rs_ = fp % 2
                    self.act(lambda e, pb=pb, rs_=rs_: e.activation(out=rl[:, rs_, :], in_=pb[:].rearrange("p a b -> p (a b)"), func=AF.Relu),
                             r=[("psa", fp % 2)], w=[("rl", rs_)])
                    self.dve(lambda e, fp=fp, rs_=rs_: e.tensor_tensor(out=aT[:, fp * 2:fp * 2 + 2, :].rearrange("p a b -> p (a b)"), in0=rl[:, rs_, :], in1=rl[:, rs_, :], op=ALU.mult),
                             r=[("rl", rs_)], w=[("aT", fp)])
                for s_ in range(2):
                    for hf in range(2):
                        pi = s_ * 2 + hf

                        def mm2(e, s_=s_, hf=hf, pi=pi):
                            for f in range(32):
                                ins = e.matmul(psy[pi][:], lhsT=aT[:, f, s_ * 128:(s_ + 1) * 128], rhs=w2[:, f, hf * 512:(hf + 1) * 512], start=(f == 0), stop=(f == 31))
                            return ins
                        self.pe(mm2, r=["w2"] + [("aT", fp) for fp in range(16)], w=[("psy", pi)])
                        self.dve(lambda e, hf=hf, pi=pi: e.tensor_tensor(out=tmp[:, hf, :], in0=psy[pi][:], in1=G2[:, hf * 512:(hf + 1) * 512], op=ALU.mult),
                                 r=[("psy", pi), "G2"], w=[("tmp", hf)])
                        self.dve(lambda e, s_=s_, hf=hf, sl=sl: e.tensor_tensor(out=x1t[:, sl, s_, hf * 512:(hf + 1) * 512], in0=x1t[:, sl, s_, hf * 512:(hf + 1) * 512],
                                                                              in1=tmp[:, hf, :], op=ALU.add),
                                 r=[("tmp", hf), ("x1t", sl)], w=[("x1t", sl)])
                self.ldc(dst[g * 256:(g + 1) * 256, :].rearrange("(s p) d -> p s d", p=128), x1t[:, sl], r=[("x1t", sl)], a=["dst"])
            self.S.flush()


def _consts():
    ident = np.eye(128, dtype=np.float32)
    i16 = np.arange(16, dtype=np.float32) / np.float32(16)
    i32 = np.arange(32, dtype=np.float32) / np.float32(32)
    invf = np.concatenate([np.float32(10000.0) ** (-i16), np.float32(10000.0) ** (-i32)]).astype(np.float32)
    k = np.arange(128)[:, None]
    q = np.arange(512)[None, :]
    cmask = np.stack([(q >= 128 * j + k) for j in range(4)], axis=1).astype(np.float32)
    dm = []
    for (w, r) in ((128, 1), (512, 4), (2048, 16)):
        W = w // 128
        for o in range(W + 4):
            rel = q - k + 128 * (W - o)
            dm.append(((rel >= 0) & (rel <= w) & (rel % r == 0)).astype(np.float32))
    dmask = np.stack(dm, axis=1)
    kind = (np.arange(SEQ)[None, :] // 256 == np.arange(32)[:, None]).astype(np.float32)
    return dict(ident=ident, invf=invf, cmask=cmask.reshape(128, -1), dmask=dmask.reshape(128, -1), kind=kind)


def _colT(v, n):
    return np.ascontiguousarray(np.asarray(v, np.float32).reshape(n, 128).T)


def make_in_maps(inp, batches):
    f = lambda a: np.ascontiguousarray(np.asarray(a, dtype=np.float32))
    c = _consts()
    shared = dict(
        ada_w=f(inp["ada_w"]),
        ada_bT=np.ascontiguousarray(np.concatenate([_colT(inp["ada_b"][l], 48) for l in range(2)], axis=1)),
        nrmT=np.ascontiguousarray(np.concatenate(
            [_colT(inp[nm][l], 8) for l in range(2) for nm in ("norm_mix", "norm_mlp")], axis=1)),
        mlp_w1=f(inp["mlp_w1"]), mlp_w2=f(inp["mlp_w2"]),
        e_w_in=f(inp["even_w_in"][0]), e_w_out=f(inp["even_w_out"][0]),
        latT=np.ascontiguousarray(np.concatenate([_colT(inp["mla_q_lat_norm"][0], 3), _colT(inp["mla_kv_lat_norm"][0], 2)], axis=1)),
        w_uq=f(np.asarray(inp["mla_w_uq"][0]).reshape(384, 768)),
        w_ukv=f(np.asarray(inp["mla_w_ukv"][0]).reshape(256, 1024)),
        gains=f(np.concatenate([np.asarray(inp[k][0], np.float32).reshape(-1) for k in
                                ("mla_q_norm", "mla_k_norm", "dil_q_norm", "dil_k_norm",
                                 "diff_q_norm", "diff_k_norm", "moba_q_norm", "moba_k_norm")])),
        o_w_in=f(inp["odd_w_in"][0]), o_w_out=f(inp["odd_w_out"][0]),
        dlam=f(np.asarray(inp["diff_lambda"][0]).reshape(256)),
        subT=np.ascontiguousarray(np.asarray(inp["diff_subln"][0], np.float32).reshape(128, 1)),
        **c,
    )
    maps = []
    for b in batches:
        m = dict(shared)
        m["x"] = f(inp["x"][b])
        m["posT"] = np.ascontiguousarray(np.asarray(inp["positions"][b], np.int32).reshape(NT, 128).T)
        m["cT"] = _colT(inp["c"][b], 8)
        maps.append(m)
    return maps


def build(nt=NT, phases=None, dbg_out=()):
    phases = ALL_PHASES if phases is None else phases
    nc = bass.Bass("TRN2", target_bir_lowering=False)
    k = K(nc, nt=nt, phases=phases, dbg_out=dbg_out)
    k.declare()
    with ExitStack() as gs:
        k.es = gs
        sems = {}
        for e in Sched.COMPUTE:
            sems[e] = gs.enter_context(nc.semaphore("sem_" + e))
        for q in ("sp", "pq"):
            sems[q] = [gs.enter_context(nc.semaphore("sem_%s%d" % (q, i))) for i in range(k.S.ring)]
        k.S.init_emit(sems)
        k.setup()
        for ph in phases:
            getattr(k, "run_" + ph)()
        stats = k.S.finish()
    return nc, k, stats


def _add_phase_methods():
    K.run_A0 = lambda self: self.phaseA(0)
    K.run_A1 = lambda self: self.phaseA(1)
    K.run_Bmla = lambda self: self.phaseB("mla")
    K.run_C10 = lambda self: self.phaseC1(0)
    K.run_C20 = lambda self: self.phaseC2(0)
    K.run_C11 = lambda self: self.phaseC1(1)
    K.run_C21 = lambda self: self.phaseC2(1)
    K.run_Bdil = lambda self: self.phaseB("dil")
    K.run_Bdiff = lambda self: self.phaseB("diff")
    K.run_Bmoba = lambda self: self.phaseB("moba")


_add_phase_methods()


ALL_PHASES = ("A0", "Bmla", "Bdil", "C10", "C20", "A1", "Bdiff", "Bmoba", "C11", "C21")


def kernel(**inputs):
    nb = int(np.asarray(inputs["x"]).shape[0])
    nc, k, stats = build(nt=NT, phases=ALL_PHASES)
    maps = make_in_maps(inputs, list(range(nb)))
    res = run_bass_kernel_spmd(nc, maps, core_ids=list(range(nb)))
    return np.stack([np.asarray(res.results[b]["out"], dtype=np.float32) for b in range(nb)], axis=0)
```

```python
class _Op:
    __slots__ = ("eng", "fn", "deps", "needs_sig", "sem", "val", "is_dma", "idx", "ring_prev")

    def __init__(self, eng, fn, is_dma):
        self.eng = eng
        self.fn = fn
        self.deps = []
        self.needs_sig = False
        self.sem = None
        self.val = 0
        self.is_dma = is_dma
        self.ring_prev = None


class Sched:
    COMPUTE = ("pe", "act", "dve", "pool")

    def __init__(self, nc, ring=8, same_engine_sync=True):
        self.nc = nc
        self.ops = []
        self.last_w = {}
        self.readers = {}
        self.appenders = {}
        self.ring = ring
        self.same_engine_sync = same_engine_sync
        self.engobj = {"pe": nc.tensor, "act": nc.scalar, "dve": nc.vector, "pool": nc.gpsimd,
                       "sp": nc.sync, "pq": nc.gpsimd}
        self.stream = {"pe": "pe", "act": "act", "dve": "dve", "pool": "pool", "sp": "sp", "pq": "pool"}

    def add(self, eng, fn, r=(), w=(), a=()):
        is_dma = eng in ("sp", "pq")
        op = _Op(eng, fn, is_dma)
        deps = {}

        def dep(p, kind):
            if p is op:
                return
            same = (self.stream[p.eng] == self.stream[eng]) and not p.is_dma
            if same:
                if eng == "pe" or not self.same_engine_sync:
                    return
            deps[id(p)] = p

        for x in r:
            p = self.last_w.get(x)
            if p is not None:
                dep(p, "raw")
            for p in self.appenders.get(x, ()):
                dep(p, "raw")
        for x in list(w) + list(a):
            p = self.last_w.get(x)
            if p is not None:
                dep(p, "waw")
            for p in self.readers.get(x, ()):
                dep(p, "war")
        for x in w:
            for p in self.appenders.get(x, ()):
                dep(p, "waw")
        for x in r:
            self.readers.setdefault(x, []).append(op)
        for x in w:
            self.last_w[x] = op
            self.readers[x] = []
            self.appenders[x] = []
        for x in a:
            self.appenders.setdefault(x, []).append(op)
            self.readers[x] = []
        op.deps = list(deps.values())
        for p in op.deps:
            p.needs_sig = True
        self.ops.append(op)
        return op

    def init_emit(self, sems):
        self.sems = sems
        self.cnt = {e: 0 for e in self.COMPUTE}
        self.dcount = {"sp": 0, "pq": 0}
        self.dhist = {"sp": [], "pq": []}
        self.waited = {}
        self.nwaits = 0
        self.nops = 0
        self.barrier_deps = []

    def flush(self):
        ops = self.ops
        lastc = {}
        for op in ops:
            if not op.is_dma:
                lastc[op.eng] = op
        for op in lastc.values():
            op.needs_sig = True
        bd = self.barrier_deps
        first_seen = set()
        for op in ops:
            st = self.stream[op.eng]
            if st not in first_seen:
                first_seen.add(st)
                op.deps = op.deps + [p for p in bd if not (self.stream[p.eng] == st and not p.is_dma)]
            if op.is_dma:
                i = self.dcount[op.eng]
                self.dcount[op.eng] += 1
                op.sem = self.sems[op.eng][i % self.ring]
                op.val = 16 * (i // self.ring + 1)
                if i >= self.ring:
                    op.ring_prev = self.dhist[op.eng][i - self.ring]
                self.dhist[op.eng].append(op)
            elif op.needs_sig:
                self.cnt[op.eng] += 1
                op.sem = self.sems[op.eng]
                op.val = self.cnt[op.eng]
        waited = self.waited
        for op in ops:
            e = self.engobj[op.eng]
            st = self.stream[op.eng]
            need = {}
            plist = list(op.deps)
            if op.ring_prev is not None:
                plist.append(op.ring_prev)
            for p in plist:
                k = id(p.sem)
                if k not in need or need[k][1] < p.val:
                    need[k] = (p.sem, p.val)
            for k, (sem, val) in need.items():
                if waited.get((st, k), 0) < val:
                    e.wait_ge(sem, val)
                    waited[(st, k)] = val
                    self.nwaits += 1
            inst = op.fn(e)
            if op.is_dma:
                inst.then_inc(op.sem, 16)
            elif op.needs_sig:
                inst.then_inc(op.sem, 1)
        self.nops += len(ops)
        nb = list(lastc.values())
        for p in bd:
            if not p.is_dma and p.eng not in lastc:
                nb.append(p)
        for q in ("sp", "pq"):
            nb.extend(self.dhist[q][-self.ring:])
        self.barrier_deps = nb
        self.ops = []
        self.last_w = {}
        self.readers = {}
        self.appenders = {}

    def finish(self, eng="sp"):
        self.flush()
        e = self.engobj[eng]
        st = self.stream[eng]
        for p in self.barrier_deps:
            k = id(p.sem)
            if self.waited.get((st, k), 0) < p.val:
                e.wait_ge(p.sem, p.val)
                self.waited[(st, k)] = p.val
        return {"n_ops": self.nops, "n_waits": self.nwaits, "sig": dict(self.cnt), "dma": dict(self.dcount)}


import math
from contextlib import ExitStack
import numpy as np
import ml_dtypes
import concourse.bass as bass
import concourse.mybir as mybir
from concourse.bass_utils import run_bass_kernel_spmd

F32 = mybir.dt.float32
BF16 = mybir.dt.bfloat16
I32 = mybir.dt.int32
AF = mybir.ActivationFunctionType
ALU = mybir.AluOpType
AX = mybir.AxisListType

D = 1024
SEQ = 8192
NT = SEQ // 128
NCH = SEQ // 512
EPS = 1e-6
EVEN_IN = 2976
ODD_IN = 3072
DFF = 4096
NEGB = -30000.0
TWO_PI = float(2 * np.pi)


class K:
    def __init__(self, nc, nt=NT, phases=None, dbg_out=()):
        self.nc = nc
        self.S = Sched(nc)
        self.nt = nt
        self.phases = phases
        self.dbg_out = dbg_out
        self.es = ExitStack()
        self.din = {}
        self.dscr = {}
        import os as _os
        self.lim = float(_os.environ.get('KLIM', '99'))

    def inp(self, name, shape, dt=F32):
        t = self.nc.dram_tensor(name, list(shape), dt, kind="ExternalInput").ap()
        self.din[name] = t
        return t

    def scr(self, name, shape, dt):
        kind = "ExternalOutput" if name in self.dbg_out else "Internal"
        t = self.nc.dram_tensor(name, list(shape), dt, kind=kind).ap()
        self.dscr[name] = t
        return t

    def sb(self, es, name, shape, dt):
        self.uid = getattr(self, "uid", 0) + 1
        return es.enter_context(self.nc.sbuf_tensor("%s_%d" % (name, self.uid), list(shape), dt))

    def ps(self, es, name, shape, dt):
        self.uid = getattr(self, "uid", 0) + 1
        return es.enter_context(self.nc.psum_tensor("%s_%d" % (name, self.uid), list(shape), dt))

    def act(self, fn, r=(), w=(), a=()):
        return self.S.add("act", fn, r, w, a)

    def dve(self, fn, r=(), w=(), a=()):
        return self.S.add("dve", fn, r, w, a)

    def pool(self, fn, r=(), w=(), a=()):
        return self.S.add("pool", fn, r, w, a)

    def pe(self, fn, r=(), w=(), a=()):
        return self.S.add("pe", fn, r, w, a)

    def ld(self, out, in_, r=(), w=(), a=(), q="sp"):
        return self.S.add(q, lambda e: e.dma_start(out=out, in_=in_), r, w, a)

    def ldc(self, out, in_, r=(), w=(), a=()):
        return self.S.add("pq", lambda e: e.dma_start(out=out, in_=in_), r, w, a)

    def rstd(self, ss, rs, n, rn_ss, rn_rs):
        self.act(lambda e: e.activation(out=rs, in_=ss, func=AF.Sqrt, scale=1.0 / n, bias=self.eps_ap(ss)),
                 r=[rn_ss], w=[rn_rs])
        self.dve(lambda e: e.reciprocal(out=rs, in_=rs), r=[rn_rs], w=[rn_rs])

    def eps_ap(self, like):
        p = like.shape[0]
        return self.epsT[0:p, 0:1]

    def declare(self):
        i = self.inp
        self.x = i("x", [SEQ, D])
        self.posT = i("posT", [128, NT], I32)
        self.cT = i("cT", [128, 8])
        self.ada_w = i("ada_w", [2, D, 6 * D])
        self.ada_bT = i("ada_bT", [128, 96])
        self.nrmT = i("nrmT", [128, 32])
        self.mlp_w1 = i("mlp_w1", [2, D, DFF])
        self.mlp_w2 = i("mlp_w2", [2, DFF, D])
        self.e_w_in = i("e_w_in", [D, EVEN_IN])
        self.e_w_out = i("e_w_out", [768, D])
        self.latT = i("latT", [128, 5])
        self.w_uq = i("w_uq", [384, 768])
        self.w_ukv = i("w_ukv", [256, 1024])
        self.gains = i("gains", [576])
        self.o_w_in = i("o_w_in", [D, ODD_IN])
        self.o_w_out = i("o_w_out", [D, D])
        self.dlam = i("dlam", [256])
        self.subT = i("subT", [128, 1])
        self.ident = i("ident", [128, 128])
        self.invf = i("invf", [48])
        self.cmask = i("cmask", [128, 4 * 512])
        self.dmask = i("dmask", [128, 33 * 512])
        self.kind = i("kind", [32, SEQ])
        self.out = self.nc.dram_tensor("out", [SEQ, D], F32, kind="ExternalOutput").ap()
        s = self.scr
        self.trigd = s("trigd", [128, 2 * NT * 48], F32)
        self.Gd = s("Gd", [128, 4 * D], F32)
        self.x1 = s("x1", [SEQ, D], F32)
        self.xL = s("xL", [SEQ, D], F32)
        self.h2T = s("h2T", [D, SEQ], BF16)
        self.oT = s("oT", [D, SEQ], BF16)
        self.qT_mla = s("qT_mla", [8, 96, SEQ], BF16)
        self.kT_mla = s("kT_mla", [8, 96, SEQ], BF16)
        self.v_mla = s("v_mla", [SEQ, 512], BF16)
        self.qT_dil = s("qT_dil", [12, 64, SEQ], BF16)
        self.kT_dil = s("kT_dil", [12, 64, SEQ], BF16)
        self.v_dil = s("v_dil", [SEQ, 768], BF16)
        self.qT_df = s("qT_df", [8, 64, SEQ], BF16)
        self.kT_df = s("kT_df", [8, 64, SEQ], BF16)
        self.v_df = s("v_df", [SEQ, 512], BF16)
        self.qT_mb = s("qT_mb", [8, 96, SEQ], BF16)
        self.kT_mb = s("kT_mb", [8, 64, SEQ], BF16)
        self.v_mb = s("v_mb", [SEQ, 512], BF16)

    def setup(self):
        nc, S = self.nc, self.S
        g = self.es
        sb = self.sb
        self.epsT = sb(g, "epsT", [128, 1], F32)
        self.identb = sb(g, "identb", [128, 128], BF16)
        self.modT = sb(g, "modT", [128, 96], F32)
        self.aT = sb(g, "aT", [128, 32], F32)
        self.gbc = sb(g, "gbc", [128, 576], F32)
        self.cmb = sb(g, "cmb", [128, 4, 512], BF16)
        self.onesb = sb(g, "onesb", [128, 64], F32)
        self.pool(lambda e: e.memset(self.epsT[:], EPS), w=["epsT"])
        self.pool(lambda e: e.memset(self.onesb[:], 1.0), w=["onesb"])
        self.ldc(self.identb[:], self.ident, w=["identb"])
        self.ldc(self.cmb[:], self.cmask.rearrange("p (a b) -> p a b", a=4), w=["cmb"])
        self.ld(self.gbc[:], self.gains.partition_broadcast(128), w=["gbc"])
        with ExitStack() as es:
            self.trig = sb(es, "trig", [128, 2, NT, 48], F32)
            self.G = sb(es, "G", [128, 4, D], F32)
            condT = sb(es, "condT", [128, 8], F32)
            bT = sb(es, "bT", [128, 96], F32)
            nT = sb(es, "nT", [128, 32], F32)
            stage = sb(es, "adastage", [128, 2, 8, 1024], F32)
            posi = sb(es, "posi", [128, NT], I32)
            posf = sb(es, "posf", [128, NT], F32)
            invb = sb(es, "invb", [128, 48], F32)
            kf = sb(es, "kf", [128, 2, NT, 48], F32)
            ki = sb(es, "ki", [128, 2, NT, 48], I32)
            psmod = self.ps(es, "psmod", [128, 96], F32)
            self.ld(condT[:], self.cT, w=["condT"])
            self.ld(bT[:], self.ada_bT, w=["bT"])
            self.ld(nT[:], self.nrmT, w=["nT"])
            self.ld(posi[:], self.posT, w=["posi"])
            self.ld(invb[:], self.invf.partition_broadcast(128), w=["invb"])
            self.act(lambda e: e.activation(out=condT[:], in_=condT[:], func=AF.Silu), r=["condT"], w=["condT"])
            tr = self.trig
            self.dve(lambda e: e.tensor_copy(out=posf[:], in_=posi[:]), r=["posi"], w=["posf"])
            self.dve(lambda e: e.tensor_tensor(out=tr[:, 0], in0=posf[:].unsqueeze(2).to_broadcast([128, NT, 48]),
                                               in1=invb[:].unsqueeze(1).to_broadcast([128, NT, 48]), op=ALU.mult),
                     r=["posf", "invb"], w=["trig"])
            self.dve(lambda e: e.tensor_scalar(out=tr[:, 1], in0=tr[:, 0], scalar1=float(np.pi / 2), scalar2=None,
                                               op0=ALU.add), r=["trig"], w=["trig"])
            self.dve(lambda e: e.tensor_scalar(out=kf[:], in0=tr[:], scalar1=float(1 / TWO_PI), scalar2=None,
                                               op0=ALU.mult), r=["trig"], w=["kf"])
            self.dve(lambda e: e.tensor_copy(out=ki[:], in_=kf[:]), r=["kf"], w=["ki"])
            self.dve(lambda e: e.tensor_copy(out=kf[:], in_=ki[:]), r=["ki"], w=["kf"])
            self.dve(lambda e: e.scalar_tensor_tensor(out=tr[:], in0=kf[:], scalar=-TWO_PI, in1=tr[:],
                                                      op0=ALU.mult, op1=ALU.add), r=["kf", "trig"], w=["trig"])
            self.dve(lambda e: e.tensor_scalar(out=kf[:], in0=tr[:], scalar1=float(np.pi), scalar2=-TWO_PI,
                                               op0=ALU.is_gt, op1=ALU.mult), r=["trig"], w=["kf"])
            self.dve(lambda e: e.tensor_tensor(out=tr[:], in0=tr[:], in1=kf[:], op=ALU.add), r=["kf", "trig"], w=["trig"])
            self.dve(lambda e: e.tensor_scalar(out=kf[:], in0=tr[:], scalar1=float(-np.pi), scalar2=TWO_PI,
                                               op0=ALU.is_lt, op1=ALU.mult), r=["trig"], w=["kf"])
            self.dve(lambda e: e.tensor_tensor(out=tr[:], in0=tr[:], in1=kf[:], op=ALU.add), r=["kf", "trig"], w=["trig"])
            self.act(lambda e: e.activation(out=tr[:], in_=tr[:], func=AF.Sin), r=["trig"], w=["trig"])
            for l in range(2):
                for cb in range(6):
                    sl = (l * 6 + cb) % 2
                    self.ld(stage[:, sl], self.ada_w[l, :, cb * 1024:(cb + 1) * 1024].rearrange("(k p) n -> p k n", p=128),
                            w=[("adast", sl)])
                    for jj in range(8):
                        col = l * 48 + cb * 8 + jj

                        def mm(e, sl=sl, jj=jj, col=col):
                            for k in range(8):
                                ins = e.matmul(psmod[:, col:col + 1], lhsT=stage[:, sl, k, jj * 128:(jj + 1) * 128],
                                               rhs=condT[:, k:k + 1], start=(k == 0), stop=(k == 7))
                            return ins
                        self.pe(mm, r=[("adast", sl), "condT"], a=["psmod"])
            self.dve(lambda e: e.tensor_tensor(out=self.modT[:], in0=psmod[:], in1=bT[:], op=ALU.add),
                     r=["psmod", "bT"], w=["modT"])
            for l in range(2):
                m = self.modT[:, l * 48:(l + 1) * 48]
                for which, (sc0, sh0) in enumerate(((8, 0), (32, 24))):
                    o = l * 16 + which * 8
                    nsl = nT[:, l * 16 + which * 8: l * 16 + which * 8 + 8]
                    self.dve(lambda e, o=o, m=m, sc0=sc0, nsl=nsl: e.scalar_tensor_tensor(
                        out=self.aT[:, o:o + 8], in0=m[:, sc0:sc0 + 8], scalar=1.0, in1=nsl, op0=ALU.add, op1=ALU.mult),
                        r=["modT", "nT"], w=[("aT", o)])
            identf = sb(es, "identf", [128, 128], F32)
            onesf = sb(es, "onesf", [128, 128], F32)
            dg = sb(es, "dg", [128, 2, 128], F32)
            psg_ = [self.ps(es, "psgate%d" % i, [128, 512], F32) for i in range(2)]
            self.ld(identf[:], self.ident, w=["identf"])
            self.pool(lambda e: e.memset(onesf[:], 1.0), w=["onesf"])
            cnt = 0
            for l in range(2):
                for which, off in enumerate((16, 40)):
                    gi = l * 2 + which
                    for half in range(2):
                        pb = psg_[cnt % 2]
                        for jj in range(4):
                            j = half * 4 + jj
                            col = l * 48 + off + j
                            sl = (cnt * 4 + jj) % 2
                            self.dve(lambda e, sl=sl, col=col: e.tensor_scalar(out=dg[:, sl, :], in0=identf[:], scalar1=self.modT[:, col:col + 1],
                                                                               scalar2=None, op0=ALU.mult), r=["identf", "modT"], w=[("dg", sl)])
                            self.pe(lambda e, sl=sl, jj=jj, pb=pb: e.matmul(pb[:, jj * 128:(jj + 1) * 128], lhsT=onesf[:], rhs=dg[:, sl, :], start=True, stop=True),
                                    r=[("dg", sl), "onesf"], w=[("psgate", cnt % 2, jj)])
                        self.act(lambda e, gi=gi, half=half, pb=pb: e.activation(out=self.G[:, gi, half * 512:(half + 1) * 512], in_=pb[:], func=AF.Copy),
                                 r=[("psgate", cnt % 2, jj) for jj in range(4)], w=[("G", gi, half)])
                        cnt += 1
            self.ldc(self.trigd.rearrange("p (a b) -> p a b", a=2 * NT), self.trig[:].rearrange("p s t f -> p (s t) f"), r=["trig"], w=["trigd"])
            self.ldc(self.Gd.rearrange("p (a b) -> p a b", a=4), self.G[:], r=[("G", gi, hf) for gi in range(4) for hf in range(2)], w=["Gd"])
            self.S.flush()

    def head_post(self, nm, src, nh, hd, goff, rope_lo, half, tg, tg_nm, sq, sq_nm, ssh, ssh_nm, rt, rt_nm, dst_bf, nm_bf):
        n = nh * hd
        nms = list(nm) if isinstance(nm, list) else [nm]
        s3 = src.rearrange("p (h d) -> p h d", h=nh)
        sq2 = sq[:, 0:n]
        self.act(lambda e: e.activation(out=sq2, in_=src, func=AF.Square), r=nms, w=[sq_nm])
        yield
        self.dve(lambda e: e.tensor_reduce(out=ssh[:, 0:nh], in_=sq2.rearrange("p (h d) -> p h d", h=nh), axis=AX.X, op=ALU.add),
                 r=[sq_nm], w=[ssh_nm])
        yield
        self.act(lambda e: e.activation(out=ssh[:, 0:nh], in_=ssh[:, 0:nh], func=AF.Sqrt, scale=1.0 / hd, bias=self.epsT[:, 0:1]),
                 r=[ssh_nm], w=[ssh_nm])
        yield
        self.dve(lambda e: e.reciprocal(out=ssh[:, 0:nh], in_=ssh[:, 0:nh]), r=[ssh_nm], w=[ssh_nm])
        yield
        self.dve(lambda e: e.tensor_tensor(out=s3, in0=s3, in1=ssh[:, 0:nh].unsqueeze(2).to_broadcast([128, nh, hd]), op=ALU.mult),
                 r=nms + [ssh_nm], w=nms)
        yield
        gb = self.gbc[:, goff:goff + hd]
        self.dve(lambda e: e.tensor_tensor(out=s3, in0=s3, in1=gb.unsqueeze(1).to_broadcast([128, nh, hd]), op=ALU.mult),
                 r=nms + ["gbc"], w=nms)
        yield
        fo = 0 if half == 16 else 16
        sin = tg[:, 0, fo:fo + half].unsqueeze(1).to_broadcast([128, nh, half])
        cos = tg[:, 1, fo:fo + half].unsqueeze(1).to_broadcast([128, nh, half])
        x1 = s3[:, :, rope_lo:rope_lo + half]
        x2 = s3[:, :, rope_lo + half:rope_lo + 2 * half]
        m = nh * half
        tA = rt[:, 0, 0:m].rearrange("p (h d) -> p h d", h=nh)
        tB = rt[:, 1, 0:m].rearrange("p (h d) -> p h d", h=nh)
        tC = rt[:, 2, 0:m].rearrange("p (h d) -> p h d", h=nh)
        tD = rt[:, 3, 0:m].rearrange("p (h d) -> p h d", h=nh)
        rA, rB, rC, rD = [(rt_nm, i) for i in range(4)]
        self.dve(lambda e: e.tensor_tensor(out=tA, in0=x1, in1=cos, op=ALU.mult), r=nms + [tg_nm], w=[rA])
        self.dve(lambda e: e.tensor_tensor(out=tB, in0=x2, in1=sin, op=ALU.mult), r=nms + [tg_nm], w=[rB])
        yield
        self.dve(lambda e: e.tensor_tensor(out=tC, in0=x2, in1=cos, op=ALU.mult), r=nms + [tg_nm], w=[rC])
        self.dve(lambda e: e.tensor_tensor(out=tD, in0=x1, in1=sin, op=ALU.mult), r=nms + [tg_nm], w=[rD])
        yield
        self.dve(lambda e: e.tensor_tensor(out=x1, in0=tA, in1=tB, op=ALU.subtract), r=[rA, rB], w=nms)
        yield
        self.dve(lambda e: e.tensor_tensor(out=x2, in0=tC, in1=tD, op=ALU.add), r=[rC, rD], w=nms)
        yield
        self.act(lambda e: e.activation(out=dst_bf, in_=src, func=AF.Copy), r=nms, w=[nm_bf])
        yield

    def tr_store(self, nm_bf, src_bf, nh, hd, pstr, pcnt, stg, stg_nm, dram, t, hw=None, row0=0):
        hw = hw or hd
        s3 = src_bf.rearrange("p (h d) -> p h d", h=nh)
        done = 0
        while done < nh:
            nb = min(8, nh - done)
            ps = pstr[pcnt[0] % 2]
            rn = ("pstr", pcnt[0] % 2)
            pcnt[0] += 1

            def tp(e, done=done, nb=nb, ps=ps):
                for i in range(nb):
                    ins = e.transpose(ps[0:hw, i, :], s3[:, done + i, 0:hw], self.identb[:])
                return ins
            self.pe(tp, r=[nm_bf, "identb"], w=[rn])
            self.dve(lambda e, done=done, nb=nb, ps=ps: e.tensor_copy(out=stg[0:hw, done:done + nb, :], in_=ps[0:hw, 0:nb, :]),
                     r=[rn], w=[(stg_nm, done)])
            yield
            dst = dram[done:done + nb, row0:row0 + hw, t * 128:(t + 1) * 128].rearrange("h d s -> d h s")
            self.ldc(dst, stg[0:hw, done:done + nb, :], r=[(stg_nm, done)], a=[("dram", id(dram))])
            done += nb

    def load_w(self, dst, src, K, rn, n0=0, n1=None, d0=0):
        n1 = n1 if n1 is not None else src.shape[1]
        kc = K // 128
        step = 2 if (n1 - n0) > 1024 else kc
        for k0 in range(0, kc, step):
            k1 = min(kc, k0 + step)
            self.ldc(dst[:, k0:k1, d0:d0 + (n1 - n0)],
                     src[k0 * 128:k1 * 128, n0:n1].rearrange("(k p) n -> p k n", p=128), a=[rn])

    def phaseA(self, layer):
        sb = self.sb
        NI = 2
        ncol = EVEN_IN if layer == 0 else ODD_IN
        xin = self.x if layer == 0 else self.xL
        with ExitStack() as es:
            w_in = sb(es, "w_in", [128, 8, ncol], BF16)
            trg = sb(es, "trg", [128, NI, 2, 48], F32)
            xt = sb(es, "xt", [128, NI, D], F32)
            xs = sb(es, "xs", [128, NI, D], BF16)
            ssx = sb(es, "ssx", [128, NI, 4], F32)
            hT = sb(es, "hT", [128, NI, 8, 128], BF16)
            tmpf = sb(es, "tmpf", [128, NI, D], F32)
            u = sb(es, "u", [128, NI, ncol], F32)
            sq = sb(es, "sq", [128, NI, D], F32)
            ssh = sb(es, "ssh", [128, NI, 16], F32)
            rt = sb(es, "rt", [128, NI, 4, 512], F32)
            qbf = sb(es, "qbf", [128, NI, 2048], BF16)
            vbf = sb(es, "vbf", [128, NI, 1024], BF16)
            stg = [sb(es, "stg%d" % i, [96, NI, 12, 128], BF16) for i in range(4)]
            pT = self.ps(es, "pT", [128, 8, 128], BF16)
            psu = [self.ps(es, "psu%d" % i, [128, 512], F32) for i in range(2)]
            pstr = [self.ps(es, "pstr%d" % i, [128, 8, 128], BF16) for i in range(2)]
            self.load_w(w_in, self.e_w_in if layer == 0 else self.o_w_in, D, "w_in")
            if layer == 0:
                w_uq = sb(es, "w_uqb", [128, 3, 768], BF16)
                w_ukv = sb(es, "w_ukvb", [128, 2, 1024], BF16)
                latTs = sb(es, "latTs", [128, 5], F32)
                latb = sb(es, "latb", [128, NI, 640], BF16)
                latTt = sb(es, "latTt", [128, NI, 5, 128], BF16)
                qm = sb(es, "qm", [128, NI, 768], F32)
                kf_ = sb(es, "kfull", [128, NI, 768], F32)
                psup = [self.ps(es, "psup%d" % i, [128, 512], F32) for i in range(2)]
                self.ld(latTs[:], self.latT, w=["latTs"])
                self.ldc(w_uq[:], self.w_uq.rearrange("(k p) n -> p k n", p=128), w=["w_uq"])
                self.ldc(w_ukv[:], self.w_ukv.rearrange("(k p) n -> p k n", p=128), w=["w_ukv"])
                self.dve(lambda e: e.tensor_tensor(out=w_uq[:], in0=w_uq[:], in1=latTs[:, 0:3].unsqueeze(2).to_broadcast([128, 3, 768]), op=ALU.mult),
                         r=["w_uq", "latTs"], w=["w_uq"])
                self.dve(lambda e: e.tensor_tensor(out=w_ukv[:], in0=w_ukv[:], in1=latTs[:, 3:5].unsqueeze(2).to_broadcast([128, 2, 1024]), op=ALU.mult),
                         r=["w_ukv", "latTs"], w=["w_ukv"])
            else:
                kmacc = sb(es, "kmacc", [64, 8, 32], F32)
                kmb = sb(es, "kmb", [64, 8, 32], BF16)
                kpart = sb(es, "kpart", [64, NI, 8], F32)
                gsb = sb(es, "gsb", [128, NI, 8, 32], F32)
                top8 = sb(es, "top8", [128, NI, 8, 8], F32)
                biasf = sb(es, "biasf", [128, NI, 8, 32], F32)
                biasb = sb(es, "biasb", [128, NI, 8, 32], BF16)
                psg = self.ps(es, "psg", [128, 8, 32], F32)
                self.pool(lambda e: e.memset(gsb[:], -1e30), w=[("gsb", i) for i in range(NI)])
                self.pool(lambda e: e.memset(kmacc[:], 0.0), w=["kmacc"])
            pcnt = [0]
            trg_d = self.trigd.rearrange("p (s t f) -> p s t f", s=2, t=NT)

            def body(t, sl):
                R = lambda nm: (nm, sl)
                xtt = xt[:, sl]
                u_ = u[:, sl]
                sq_ = sq[:, sl]
                ssx_ = ssx[:, sl]
                ssh_ = ssh[:, sl]
                rt_ = rt[:, sl]
                qbf_ = qbf[:, sl]
                vbf_ = vbf[:, sl]
                tg = trg[:, sl]
                self.ld(xtt, xin[t * 128:(t + 1) * 128, :], w=[R("xt")])
                self.ld(tg, trg_d[:, :, t, :], w=[R("tg")])
                yield
                self.act(lambda e: e.activation(out=sq_, in_=xtt, func=AF.Square, accum_out=ssx_[:, 0:1]), r=[R("xt")], w=[R("sq"), R("ssx")])
                yield
                self.act(lambda e: e.activation(out=ssx_[:, 1:2], in_=ssx_[:, 0:1], func=AF.Sqrt, scale=1.0 / D, bias=self.epsT[:, 0:1]), r=[R("ssx")], w=[R("rsx")])
                yield
                self.dve(lambda e: e.reciprocal(out=ssx_[:, 1:2], in_=ssx_[:, 1:2]), r=[R("rsx")], w=[R("rsx")])
                yield
                self.act(lambda e: e.activation(out=xs[:, sl], in_=xtt, func=AF.Copy, scale=ssx_[:, 1:2]), r=[R("xt"), R("rsx")], w=[R("xs")])
                yield

                def tpx(e):
                    for j in range(8):
                        ins = e.transpose(pT[:, j, :], xs[:, sl, j * 128:(j + 1) * 128], self.identb[:])
                    return ins
                self.pe(tpx, r=[R("xs"), "identb"], w=["pT"])
                ao = layer * 16
                a_bc = self.aT[:, ao:ao + 8].unsqueeze(2).to_broadcast([128, 8, 128])
                b_bc = self.modT[:, layer * 48:layer * 48 + 8].unsqueeze(2).to_broadcast([128, 8, 128])
                tm3 = tmpf[:, sl].rearrange("p (j s) -> p j s", j=8)
                self.dve(lambda e: e.tensor_tensor(out=tm3, in0=pT[:], in1=a_bc, op=ALU.mult), r=["pT", ("aT", ao)], w=[R("tmpf")])
                yield
                self.dve(lambda e: e.tensor_tensor(out=hT[:, sl], in0=tm3, in1=b_bc, op=ALU.add), r=[R("tmpf"), "modT"], w=[R("hT")])
                yield
                ngrp = (ncol + 511) // 512
                for gi, c0 in enumerate(range(0, ncol, 512)):
                    c1 = min(ncol, c0 + 512)
                    pb = psu[gi % 2]

                    def mm(e, c0=c0, c1=c1, pb=pb):
                        for j in range(8):
                            ins = e.matmul(pb[:, 0:c1 - c0], lhsT=hT[:, sl, j, :], rhs=w_in[:, j, c0:c1], start=(j == 0), stop=(j == 7))
                        return ins
                    self.pe(mm, r=[R("hT"), "w_in"], w=[("psu", gi % 2)])
                    self.act(lambda e, c0=c0, c1=c1, pb=pb: e.activation(out=u_[:, c0:c1], in_=pb[:, 0:c1 - c0], func=AF.Copy),
                             r=[("psu", gi % 2)], w=[("u", sl, gi)])
                    yield
                uall = [("u", sl, gi) for gi in range(ngrp)]
                U = lambda gi: ("u", sl, gi)
                if layer == 0:
                    latb_ = latb[:, sl]
                    qm_ = qm[:, sl]
                    kfs = kf_[:, sl]
                    self.act(lambda e: e.activation(out=sq_[:, 0:384], in_=u_[:, 0:384], func=AF.Square, accum_out=ssx_[:, 2:3]), r=[U(0)], w=[R("sq"), R("ssl")])
                    self.act(lambda e: e.activation(out=sq_[:, 384:640], in_=u_[:, 384:640], func=AF.Square, accum_out=ssx_[:, 3:4]), r=[U(0), U(1), R("sq")], w=[R("sq"), R("ssl2")])
                    yield
                    self.act(lambda e: e.activation(out=ssx_[:, 2:3], in_=ssx_[:, 2:3], func=AF.Sqrt, scale=1.0 / 384, bias=self.epsT[:, 0:1]), r=[R("ssl")], w=[R("ssl")])
                    self.act(lambda e: e.activation(out=ssx_[:, 3:4], in_=ssx_[:, 3:4], func=AF.Sqrt, scale=1.0 / 256, bias=self.epsT[:, 0:1]), r=[R("ssl2")], w=[R("ssl2")])
                    yield
                    self.dve(lambda e: e.reciprocal(out=ssx_[:, 2:4], in_=ssx_[:, 2:4]), r=[R("ssl"), R("ssl2")], w=[R("ssl"), R("ssl2")])
                    yield
                    self.dve(lambda e: e.tensor_scalar(out=latb_[:, 0:384], in0=u_[:, 0:384], scalar1=ssx_[:, 2:3], scalar2=None, op0=ALU.mult), r=[U(0), R("ssl")], w=[R("latb0")])
                    self.dve(lambda e: e.tensor_scalar(out=latb_[:, 384:640], in0=u_[:, 384:640], scalar1=ssx_[:, 3:4], scalar2=None, op0=ALU.mult), r=[U(0), U(1), R("ssl2")], w=[R("latb1")])
                    yield
                    ps = pstr[pcnt[0] % 2]
                    rn = ("pstr", pcnt[0] % 2)
                    pcnt[0] += 1

                    def tpl(e, ps=ps):
                        for j in range(5):
                            ins = e.transpose(ps[:, j, :], latb_[:, j * 128:(j + 1) * 128], self.identb[:])
                        return ins
                    self.pe(tpl, r=[R("latb0"), R("latb1"), "identb"], w=[rn])
                    self.dve(lambda e, ps=ps: e.tensor_copy(out=latTt[:, sl], in_=ps[:, 0:5, :]), r=[rn], w=[R("latTt")])
                    yield
                    for gi, (c0, c1) in enumerate(((0, 512), (512, 768))):
                        def mmq(e, c0=c0, c1=c1, gi=gi):
                            for j in range(3):
                                ins = e.matmul(psup[gi][:, 0:c1 - c0], lhsT=latTt[:, sl, j, :], rhs=w_uq[:, j, c0:c1], start=(j == 0), stop=(j == 2))
                            return ins
                        self.pe(mmq, r=[R("latTt"), "w_uq"], w=[("psup", gi)])
                        self.act(lambda e, c0=c0, c1=c1, gi=gi: e.activation(out=qm_[:, c0:c1], in_=psup[gi][:, 0:c1 - c0], func=AF.Copy),
                                 r=[("psup", gi)], w=[R("qm")] if gi == 0 else [], a=[] if gi == 0 else [R("qm")])
                        yield
                    kf3 = kfs.rearrange("p (h d) -> p h d", h=8)
                    vb3 = vbf_[:, 0:512].rearrange("p (h d) -> p h d", h=8)
                    for gi in range(2):
                        def mmk(e, gi=gi):
                            for j in range(2):
                                ins = e.matmul(psup[gi][:], lhsT=latTt[:, sl, 3 + j, :], rhs=w_ukv[:, j, gi * 512:(gi + 1) * 512], start=(j == 0), stop=(j == 1))
                            return ins
                        self.pe(mmk, r=[R("latTt"), "w_ukv"], w=[("psup", gi)])
                        p3 = psup[gi][:].rearrange("p (h d) -> p h d", h=4)
                        self.act(lambda e, gi=gi, p3=p3: e.activation(out=kf3[:, gi * 4:gi * 4 + 4, 0:64], in_=p3[:, :, 0:64], func=AF.Copy),
                                 r=[("psup", gi)], w=[R("kfull")] if gi == 0 else [], a=[] if gi == 0 else [R("kfull")])
                        self.act(lambda e, gi=gi, p3=p3: e.activation(out=vb3[:, gi * 4:gi * 4 + 4, :], in_=p3[:, :, 64:128], func=AF.Copy),
                                 r=[("psup", gi)], w=[R("vbf")] if gi == 0 else [], a=[] if gi == 0 else [R("vbf")])
                        yield
                    self.dve(lambda e: e.tensor_copy(out=kf3[:, :, 64:96], in_=u_[:, 640:672].unsqueeze(1).to_broadcast([128, 8, 32])), r=[U(1)], a=[R("kfull")])
                    yield
                    yield from self.head_post(R("qm"), qm_, 8, 96, 0, 64, 16, tg, R("tg"), sq_, R("sq"), ssh_, R("ssh"), rt_, R("rt"), qbf_[:, 0:768], R("qbf0"))
                    yield from self.tr_store(R("qbf0"), qbf_[:, 0:768], 8, 96, pstr, pcnt, stg[0][:, sl], R("stg0"), self.qT_mla, t)
                    yield from self.head_post(R("kfull"), kfs, 8, 96, 96, 64, 16, tg, R("tg"), sq_, R("sq"), ssh_, R("ssh"), rt_, R("rt"), qbf_[:, 768:1536], R("qbf1"))
                    yield from self.tr_store(R("qbf1"), qbf_[:, 768:1536], 8, 96, pstr, pcnt, stg[1][:, sl], R("stg1"), self.kT_mla, t)
                    self.ldc(self.v_mla[t * 128:(t + 1) * 128, :], vbf_[:, 0:512], r=[R("vbf")], a=["v_mla"])
                    yield
                    yield from self.head_post([U(1), U(2)], u_[:, 672:1440], 12, 64, 192, 0, 32, tg, R("tg"), sq_, R("sq"), ssh_, R("ssh"), rt_, R("rt"), qbf_[:, 0:768], R("qbf0"))
                    yield from self.tr_store(R("qbf0"), qbf_[:, 0:768], 12, 64, pstr, pcnt, stg[2][:, sl], R("stg2"), self.qT_dil, t)
                    yield from self.head_post([U(2), U(3), U(4)], u_[:, 1440:2208], 12, 64, 256, 0, 32, tg, R("tg"), sq_, R("sq"), ssh_, R("ssh"), rt_, R("rt"), qbf_[:, 768:1536], R("qbf1"))
                    yield from self.tr_store(R("qbf1"), qbf_[:, 768:1536], 12, 64, pstr, pcnt, stg[3][:, sl], R("stg3"), self.kT_dil, t)
                    self.act(lambda e: e.activation(out=vbf_[:, 0:768], in_=u_[:, 2208:2976], func=AF.Copy), r=uall, w=[R("vbf")])
                    yield
                    self.ldc(self.v_dil[t * 128:(t + 1) * 128, :], vbf_[:, 0:768], r=[R("vbf")], a=["v_dil"])
                    yield
                else:
                    yield from self.head_post(U(0), u_[:, 0:512], 8, 64, 320, 0, 32, tg, R("tg"), sq_, R("sq"), ssh_, R("ssh"), rt_, R("rt"), qbf_[:, 0:512], R("qbfA"))
                    yield from self.tr_store(R("qbfA"), qbf_[:, 0:512], 8, 64, pstr, pcnt, stg[0][:, sl], R("stg0"), self.qT_df, t)
                    yield from self.head_post(U(1), u_[:, 512:1024], 8, 64, 384, 0, 32, tg, R("tg"), sq_, R("sq"), ssh_, R("ssh"), rt_, R("rt"), qbf_[:, 512:1024], R("qbfB"))
                    yield from self.tr_store(R("qbfB"), qbf_[:, 512:1024], 8, 64, pstr, pcnt, stg[1][:, sl], R("stg1"), self.kT_df, t)
                    self.act(lambda e: e.activation(out=vbf_[:, 0:512], in_=u_[:, 1024:1536], func=AF.Copy), r=uall, w=[R("vbf")])
                    yield
                    self.ldc(self.v_df[t * 128:(t + 1) * 128, :], vbf_[:, 0:512], r=[R("vbf")], a=["v_df"])
                    self.act(lambda e: e.activation(out=vbf_[:, 512:1024], in_=u_[:, 2560:3072], func=AF.Copy), r=uall, w=[R("vbf2")])
                    yield
                    self.ldc(self.v_mb[t * 128:(t + 1) * 128, :], vbf_[:, 512:1024], r=[R("vbf2")], a=["v_mb"])
                    yield
                    yield from self.head_post(U(4), u_[:, 2048:2560], 8, 64, 512, 0, 32, tg, R("tg"), sq_, R("sq"), ssh_, R("ssh"), rt_, R("rt"), qbf_[:, 1024:1536], R("qbfC"))
                    yield from self.tr_store(R("qbfC"), qbf_[:, 1024:1536], 8, 64, pstr, pcnt, stg[2][:, sl], R("stg2"), self.kT_mb, t)
                    nblk = t // 2
                    kp = kpart[:, sl]
                    self.dve(lambda e: e.tensor_reduce(out=kp, in_=stg[2][0:64, sl, 0:8, :], axis=AX.X, op=ALU.add), r=[(R("stg2"), 0)], w=[R("kpart")])
                    yield
                    self.dve(lambda e: e.tensor_tensor(out=kmacc[:, :, nblk], in0=kmacc[:, :, nblk], in1=kp, op=ALU.add), r=[R("kpart"), "kmacc"], w=["kmacc"])
                    yield
                    if t % 2 == 1:
                        self.act(lambda e: e.activation(out=kmb[:, :, nblk], in_=kmacc[:, :, nblk], func=AF.Copy, scale=1.0 / 256), r=["kmacc"], a=["kmb"])
                        yield
                    yield from self.head_post(U(3), u_[:, 1536:2048], 8, 64, 448, 0, 32, tg, R("tg"), sq_, R("sq"), ssh_, R("ssh"), rt_, R("rt"), qbf_[:, 1536:2048], R("qbfD"))
                    yield from self.tr_store(R("qbfD"), qbf_[:, 1536:2048], 8, 64, pstr, pcnt, stg[3][:, sl], R("stg3"), self.qT_mb, t)
                    gs_ = gsb[:, sl]
                    t8 = top8[:, sl]
                    bf_ = biasf[:, sl]
                    bb_ = biasb[:, sl]
                    if nblk > 0:
                        def mmg(e):
                            for h in range(8):
                                ins = e.matmul(psg[:, h, 0:nblk], lhsT=stg[3][0:64, sl, h, :], rhs=kmb[:, h, 0:nblk], start=True, stop=True)
                            return ins
                        self.pe(mmg, r=[(R("stg3"), 0), "kmb"], w=["psg"])
                        self.dve(lambda e: e.tensor_copy(out=gs_[:, :, 0:nblk], in_=psg[:, :, 0:nblk]), r=["psg"], w=[R("gsb")])
                        yield

                        def mx(e):
                            for h in range(8):
                                ins = e.max(out=t8[:, h, :], in_=gs_[:, h, :])
                            return ins
                        self.dve(mx, r=[R("gsb")], w=[R("top8")])
                        yield
                        self.dve(lambda e: e.tensor_tensor(out=bf_, in0=gs_, in1=t8[:, :, 2:3].to_broadcast([128, 8, 32]), op=ALU.is_lt), r=[R("gsb"), R("top8")], w=[R("biasf")])
                        yield
                        self.dve(lambda e: e.tensor_scalar(out=bb_, in0=bf_, scalar1=NEGB, scalar2=None, op0=ALU.mult), r=[R("biasf")], w=[R("biasb")])
                        yield
                    else:
                        self.dve(lambda e: e.memset(bb_, NEGB), w=[R("biasb")])
                        yield
                    self.dve(lambda e: e.memset(bb_[:, :, nblk:nblk + 1], 0.0), r=[R("biasb")], w=[R("biasb")])
                    yield
                    ps = pstr[pcnt[0] % 2]
                    rn = ("pstr", pcnt[0] % 2)
                    pcnt[0] += 1

                    def tpb(e, ps=ps):
                        for h in range(8):
                            ins = e.transpose(ps[0:32, h, :], bb_[:, h, :], self.identb[:])
                        return ins
                    self.pe(tpb, r=[R("biasb"), "identb"], w=[rn])
                    self.dve(lambda e, ps=ps: e.tensor_copy(out=stg[0][0:32, sl, 0:8, :], in_=ps[0:32, 0:8, :]), r=[rn], w=[(R("stg0"), 0)])
                    yield
                    dst = self.qT_mb[:, 64:96, t * 128:(t + 1) * 128].rearrange("h d s -> d h s")
                    self.ldc(dst, stg[0][0:32, sl, 0:8, :], r=[(R("stg0"), 0)], a=["qT_mb_bias"])
                    yield

            import os as _os
            STAG = int(_os.environ.get("KSTAG", "35"))
            active = []
            next_t = 0
            while next_t < self.nt or active:
                if next_t < self.nt and len(active) < NI and (not active or active[-1][1] >= STAG):
                    active.append([body(next_t, next_t % NI), 0])
                    next_t += 1
                for a_ in list(active):
                    try:
                        next(a_[0])
                        a_[1] += 1
                    except StopIteration:
                        active.remove(a_)
            self.S.flush()

    def attn_tiles(self, tiles, ps_s, P, po, po_nm, sc, cnt, hook=None, hook_at=5, acc=None, acc_nm=None, vrows=65, pair=False):
        n = len(tiles)
        ns = len(ps_s)
        npb = P.shape[1]
        import os as _os
        LA = min(ns - 1, int(_os.environ.get('KLA', '3')))
        for i in range(n + LA):
            if i < n:
                kap, qap, q0, N, mk, vs = tiles[i][:6]
                si = (cnt + i) % ns
                pi = (cnt + i) % npb
                if not pair:
                    self.pe(lambda e, kap=kap, qap=qap, N=N, si=si: e.matmul(ps_s[si][:, 0:N], lhsT=kap, rhs=qap, start=True, stop=True),
                            r=tiles[i][6], w=[("ps_s", si)])
                elif i % 2 == 0:
                    grp = [i] + ([i + 1] if i + 1 < n else [])

                    def qk2(e, grp=grp):
                        for ii in grp:
                            kap2, qap2, _, N2 = tiles[ii][:4]
                            ins = e.matmul(ps_s[(cnt + ii) % ns][:, 0:N2], lhsT=kap2, rhs=qap2, start=True, stop=True)
                        return ins
                    rr = []
                    for ii in grp:
                        rr += tiles[ii][6]
                    self.pe(qk2, r=rr, w=[("ps_s", (cnt + ii) % ns) for ii in grp])
                self.act(lambda e, N=N, si=si, pi=pi: e.activation(out=P[:, pi, 0:N], in_=ps_s[si][:, 0:N], func=AF.Exp, scale=sc),
                         r=[("ps_s", si)], w=[("P", pi)])
                if mk is not None:
                    self.dve(lambda e, N=N, pi=pi, mk=mk: e.tensor_tensor(out=P[:, pi, 0:N], in0=P[:, pi, 0:N], in1=mk, op=ALU.mult),
                             r=[("P", pi), "masks"], w=[("P", pi)])
                if acc is not None:
                    if i == 0:
                        self.dve(lambda e, N=N, pi=pi, q0=q0: e.tensor_copy(out=acc[:, q0:512], in_=P[:, pi, 0:N]), r=[("P", pi)], w=[acc_nm])
                    else:
                        self.dve(lambda e, N=N, pi=pi, q0=q0: e.tensor_tensor(out=acc[:, q0:512], in0=acc[:, q0:512], in1=P[:, pi, 0:N], op=ALU.add),
                                 r=[("P", pi), acc_nm], w=[acc_nm])
            if hook is not None and i == min(hook_at, n + LA - 1):
                hook()
                hook = None
            j = i - LA
            if j >= 0:
                kap, qap, q0, N, mk, vs = tiles[j][:6]
                pi = (cnt + j) % npb
                for f, vap in enumerate(vs):
                    self.pe(lambda e, f=f, vap=vap, q0=q0, N=N, pi=pi, j=j: e.matmul(po[f][0:vrows, q0:512], lhsT=vap, rhs=P[:, pi, 0:N],
                                                                                    start=(j == 0), stop=(j == n - 1)),
                            r=[("P", pi)] + tiles[j][7], w=[po_nm[f]] if j == 0 else [], a=[] if j == 0 else [po_nm[f]])
                nd = getattr(self, "ndummy", 0)
                if nd:
                    def dm(e):
                        for _ in range(nd):
                            ins = e.matmul(self.ps_dummy[:, 0:128], lhsT=self.identb[:], rhs=self.identb[:], start=True, stop=True)
                        return ins
                    self.pe(dm, r=["identb"], a=["psdummy"])
        if hook is not None:
            hook()
        return cnt + n

    def attn_norm(self, po, po_nm, ep, slot, dst, dst_nm):
        rrow, osb, ps_bc = ep
        self.act(lambda e: e.activation(out=rrow[64:65, slot, :], in_=po[64:65, :], func=AF.Ln), r=[po_nm], w=[("rrow", slot)])
        self.act(lambda e: e.activation(out=rrow[64:65, slot, :], in_=rrow[64:65, slot, :], func=AF.Exp, scale=-1.0), r=[("rrow", slot)], w=[("rrow", slot)])
        self.pe(lambda e: e.matmul(ps_bc[0:64, :], lhsT=self.onesb[64:65, 0:64], rhs=rrow[64:65, slot, :], start=True, stop=True),
                r=[("rrow", slot), "onesb"], w=["ps_bc"])
        self.act(lambda e: e.activation(out=osb[0:64, slot, :], in_=po[0:64, :], func=AF.Copy), r=[po_nm], w=[("osb", slot)])
        self.dve(lambda e: e.tensor_tensor(out=dst, in0=osb[0:64, slot, :], in1=ps_bc[0:64, :], op=ALU.mult),
                 r=[("osb", slot), "ps_bc"], w=[dst_nm])

    def phaseB(self, kind):
        sb = self.sb
        nch = max(1, self.nt // 4)
        ncols = nch * 512
        dil = kind == "dil"
        nheads = {"mla": 8, "dil": 4, "diff": 4, "moba": 8}[kind]
        nm = 2 if kind == "diff" else 1
        parts = 1
        isdf = kind == "diff"
        dk = {"mla": 96, "dil": 64, "diff": 64, "moba": 96}[kind]
        sc = float({"mla": 96 ** -0.5, "dil": 0.125, "diff": 0.125, "moba": 0.125}[kind])
        qsrc = {"mla": self.qT_mla, "dil": self.qT_dil, "diff": self.qT_df, "moba": self.qT_mb}[kind]
        ksrc = {"mla": self.kT_mla, "dil": self.kT_dil, "diff": self.kT_df, "moba": self.kT_mb}[kind]
        vsrc = {"mla": self.v_mla, "dil": self.v_dil, "diff": self.v_df, "moba": self.v_mb}[kind]
        row_base = {"mla": 0, "dil": 512, "diff": 0, "moba": 512}[kind]
        lam_init = 0.8 - 0.6 * math.exp(-0.3 * 1)
        with ExitStack() as es:
            P_dummy = sb(es, "Pdummy", [128, 2], F32)
            if dil:
                qT = sb(es, "qTd", [128, 2, 3, 512], BF16)
                kT = sb(es, "kTd", [128, 3, SEQ // 2], BF16)
                V = sb(es, "Vd", [128, 3, 64, 65], BF16)
                dmb = sb(es, "dmb", [128, 33, 512], BF16)
                self.ldc(dmb[:], self.dmask.rearrange("p (a b) -> p a b", a=33), w=["masks"])
                self.pool(lambda e: e.memset(V[:, :, :, 64:65], 1.0), a=["Vones"])
            else:
                if isdf:
                    qT = sb(es, "qTp", [128, 2, SEQ], BF16)
                    kT = sb(es, "kTp", [128, 2, SEQ // 2], BF16)
                else:
                    qT = sb(es, "qTa", [96, 2, SEQ], BF16)
                    kT = sb(es, "kTa", [96, 2, SEQ], BF16)
                vw = 128 if isdf else 65
                V = sb(es, "Va", [128, 2, parts, 64, vw], BF16)
                if isdf:
                    self.pool(lambda e: e.memset(P_dummy[:], 0.0), a=["Vones"])
                else:
                    self.pool(lambda e: e.memset(V[:, :, :, :, 64:65], 1.0), a=["Vones"])
                if kind == "moba":
                    for sl in range(2):
                        self.ldc(kT[64:96, sl, :], self.kind, a=["Vones"])
            import os as _os
            self.ndummy = 0
            n_po = 2
            n_s = 7 - n_po - (1 if self.ndummy else 0)
            self.ps_dummy = self.ps(es, "psdummy", [128, 512], F32) if self.ndummy else None
            P = sb(es, "Pt", [128, n_s + 2, 512], BF16)
            rrow = sb(es, "rrow", [65, 2, 512], F32)
            osb = sb(es, "osb", [64, 2, 512], F32)
            onb = sb(es, "onb", [64, 4, 512], BF16)
            onbd = sb(es, "onbd", [128, 2, 512], BF16)
            ps_s = [self.ps(es, "ps_s%d" % i, [128, 512], F32) for i in range(n_s)]
            po = [self.ps(es, "po%d" % i, [128, 512], F32) for i in range(n_po)]
            ps_bc = self.ps(es, "ps_bc", [128, 512], F32)
            ep = (rrow, osb, ps_bc)
            if isdf:
                acc = sb(es, "acc", [128, 2, 512], F32)
                rcp = sb(es, "rcp", [128, 512], F32)
                nrm = sb(es, "nrm", [128, 2, 512], F32)
                dd = sb(es, "dd", [128, 512], F32)
                sqd = sb(es, "sqd", [128, 512], F32)
                rsd = sb(es, "rsd", [128, 512], F32)
                onesf = sb(es, "onesfB", [128, 128], F32)
                lamt = sb(es, "lamt", [128, 256], F32)
                lsm = sb(es, "lsm", [128, 8], F32)
                subs = sb(es, "subs", [128, 1], F32)
                self.pool(lambda e: e.memset(onesf[:], 1.0), w=["onesfB"])
                self.ld(lamt[:], self.dlam.partition_broadcast(128), w=["lamt"])
                self.ld(subs[:], self.subT, w=["subs"])
                self.dve(lambda e: e.tensor_tensor(out=lamt[:, 0:64], in0=lamt[:, 0:64], in1=lamt[:, 64:128], op=ALU.mult), r=["lamt"], w=["lamt"])
                self.dve(lambda e: e.tensor_tensor(out=lamt[:, 128:192], in0=lamt[:, 128:192], in1=lamt[:, 192:256], op=ALU.mult), r=["lamt"], w=["lamt"])
                self.dve(lambda e: e.tensor_reduce(out=lsm[:, 0:1], in_=lamt[:, 0:64], axis=AX.X, op=ALU.add), r=["lamt"], w=["lsm0"])
                self.dve(lambda e: e.tensor_reduce(out=lsm[:, 1:2], in_=lamt[:, 128:192], axis=AX.X, op=ALU.add), r=["lamt"], w=["lsm1"])
                self.act(lambda e: e.activation(out=lsm[:, 2:4], in_=lsm[:, 0:2], func=AF.Exp), r=["lsm0", "lsm1"], w=["lsm2"])
                self.dve(lambda e: e.tensor_tensor(out=lsm[:, 4:5], in0=lsm[:, 3:4], in1=lsm[:, 2:3], op=ALU.subtract), r=["lsm2"], w=["lsm4"])
                self.dve(lambda e: e.tensor_scalar(out=lsm[:, 5:6], in0=lsm[:, 4:5], scalar1=-lam_init, scalar2=None, op0=ALU.add), r=["lsm4"], w=["neglam"])
                self.dve(lambda e: e.tensor_scalar(out=subs[:], in0=subs[:], scalar1=1.0 - lam_init, scalar2=None, op0=ALU.mult), r=["subs"], w=["subs"])
            cnt = 0
            ecnt = 0
            pending = [None]
            for h in range(nheads):
                vsl = h % 2
                if dil:
                    for g in range(3):
                        gh = g * 4 + h
                        for hf in range(2):
                            self.ld(kT[hf * 64:(hf + 1) * 64, g, 0:ncols // 2].rearrange("d (j s) -> d j s", s=128),
                                    ksrc[gh, :, 0:ncols].rearrange("d (j two s) -> d j two s", two=2, s=128)[:, :, hf, :],
                                    w=[("kT", g)] if hf == 0 else [], a=[] if hf == 0 else [("kT", g)])
                        self.ld(V[:, g, 0:nch * 4, 0:64], vsrc[0:ncols, gh * 64:gh * 64 + 64].rearrange("(t p) d -> p t d", p=128),
                                r=["Vones"], w=[("V", g)])
                else:
                    for f in range(parts):
                        vd = 128 if isdf else 64
                        c0 = h * vd
                        self.ld(V[:, vsl, f, 0:nch * 4, 0:vd], vsrc[0:ncols, c0:c0 + vd].rearrange("(t p) d -> p t d", p=128),
                                r=["Vones"], w=[("V", vsl, f)])
                    for m in range(nm):
                        u_ = h * nm + m
                        sl = u_ % 2
                        dq = 96 if kind in ("mla", "moba") else 64
                        dkk = 96 if kind == "mla" else 64
                        if isdf:
                            for hf in range(2):
                                self.ld(qT[hf * 64:(hf + 1) * 64, sl, 0:ncols], qsrc[u_, 0:64, 0:ncols], w=[("qT", sl)] if hf == 0 else [], a=[] if hf == 0 else [("qT", sl)])
                                self.ld(kT[hf * 64:(hf + 1) * 64, sl, 0:ncols // 2].rearrange("d (j s) -> d j s", s=128),
                                        ksrc[u_, 0:64, 0:ncols].rearrange("d (j two s) -> d j two s", two=2, s=128)[:, :, hf, :],
                                        w=[("kT", sl)] if hf == 0 else [], a=[] if hf == 0 else [("kT", sl)])
                        else:
                            self.ld(qT[0:dq, sl, 0:ncols], qsrc[u_, 0:dq, 0:ncols], w=[("qT", sl)])
                            self.ld(kT[0:dkk, sl, 0:ncols], ksrc[u_, 0:dkk, 0:ncols], r=["Vones"], w=[("kT", sl)])
                for c in range(nch):
                    if dil:
                        qs = c % 2
                        for g in range(3):
                            for hf in range(2):
                                self.ld(qT[hf * 64:(hf + 1) * 64, qs, g, :], qsrc[g * 4 + h, :, c * 512:(c + 1) * 512],
                                        w=[("qT", qs, g)] if hf == 0 else [], a=[] if hf == 0 else [("qT", qs, g)])
                    for m in range(nm):
                        u_ = h * nm + m
                        sl = u_ % 2
                        tiles = []
                        if dil:
                            mo = 0
                            for g, W in enumerate((1, 4, 16)):
                                for o in range(W + 4):
                                    kt = 4 * c - W + o
                                    if kt >= 0:
                                        hf = kt % 2
                                        tiles.append((kT[hf * 64:(hf + 1) * 64, g, (kt // 2) * 128:(kt // 2 + 1) * 128], qT[hf * 64:(hf + 1) * 64, qs, g, :], 0, 512, dmb[:, mo + o, :],
                                                      [V[:, g, kt, :]], [("kT", g), ("qT", qs, g)], [("V", g)]))
                                mo += W + 4
                        else:
                            for kt in range(4 * c + 4):
                                j = kt - 4 * c
                                if j < 0:
                                    q0, mk = 0, None
                                else:
                                    q0, mk = 128 * j, self.cmb[:, j, 128 * j:512]
                                N = 512 - q0
                                if isdf:
                                    hf = kt % 2
                                    kap_ = kT[hf * 64:(hf + 1) * 64, sl, (kt // 2) * 128:(kt // 2 + 1) * 128]
                                    qap_ = qT[hf * 64:(hf + 1) * 64, sl, c * 512 + q0:(c + 1) * 512]
                                else:
                                    kap_ = kT[0:dk, sl, kt * 128:(kt + 1) * 128]
                                    qap_ = qT[0:dk, sl, c * 512 + q0:(c + 1) * 512]
                                tiles.append((kap_, qap_, q0, N, mk,
                                              [V[:, vsl, f, kt, :] for f in range(parts)], [("kT", sl), ("qT", sl)],
                                              [("V", vsl, f) for f in range(parts)]))
                        pidx = [ecnt % 2]
                        pos_ = [po[i] for i in pidx]
                        po_nm = [("po", i) for i in pidx]
                        if isdf:
                            cnt = self.attn_tiles(tiles, ps_s, P, pos_, po_nm, sc, cnt, hook=pending[0], acc=acc[:, m, :], acc_nm=("acc", m), vrows=128, pair=True)
                        else:
                            cnt = self.attn_tiles(tiles, ps_s, P, pos_, po_nm, sc, cnt, hook=pending[0], pair=dil)

                        def epilogue(pos_=pos_, po_nm=po_nm, ecnt0=ecnt, m=m, c=c, h=h):
                            if not isdf:
                                es_ = ecnt0 % 2
                                osl = ecnt0 % 4
                                self.attn_norm(pos_[0], po_nm[0], ep, es_, onb[:, osl, :], ("onb", osl))
                                r0 = row_base + h * 64
                                self.ldc(self.oT[r0:r0 + 64, c * 512:(c + 1) * 512], onb[:, osl, :], r=[("onb", osl)], a=["oT"])
                                return
                            self.pe(lambda e: e.matmul(ps_bc[:, :], lhsT=onesf[:], rhs=acc[:, m, :], start=True, stop=True), r=[("acc", m), "onesfB"], w=["ps_bc"])
                            self.act(lambda e: e.activation(out=rcp[:], in_=ps_bc[:, :], func=AF.Ln), r=["ps_bc"], w=["rcp"])
                            self.act(lambda e: e.activation(out=rcp[:], in_=rcp[:], func=AF.Exp, scale=-1.0), r=["rcp"], w=["rcp"])
                            self.dve(lambda e: e.tensor_tensor(out=nrm[:, m, :], in0=pos_[0][:, :], in1=rcp[:], op=ALU.mult), r=[po_nm[0], "rcp"], w=[("nrm", m)])
                            if m == 1:
                                self.dve(lambda e: e.scalar_tensor_tensor(out=dd[:], in0=nrm[:, 1, :], scalar=lsm[:, 5:6], in1=nrm[:, 0, :], op0=ALU.mult, op1=ALU.add),
                                         r=[("nrm", 0), ("nrm", 1), "neglam"], w=["dd"])
                                self.act(lambda e: e.activation(out=sqd[:], in_=dd[:], func=AF.Square), r=["dd"], w=["sqd"])
                                self.pe(lambda e: e.matmul(ps_bc[:, :], lhsT=onesf[:], rhs=sqd[:], start=True, stop=True), r=["sqd", "onesfB"], w=["ps_bc"])
                                self.act(lambda e: e.activation(out=rsd[:], in_=ps_bc[:, :], func=AF.Ln, scale=1.0 / 128, bias=self.epsT[:, 0:1]), r=["ps_bc"], w=["rsd"])
                                self.act(lambda e: e.activation(out=rsd[:], in_=rsd[:], func=AF.Exp, scale=-0.5), r=["rsd"], w=["rsd"])
                                osl = c % 2
                                ob = onbd[:, osl, :]
                                self.dve(lambda e, ob=ob: e.scalar_tensor_tensor(out=ob, in0=dd[:], scalar=subs[:, 0:1], in1=rsd[:], op0=ALU.mult, op1=ALU.mult),
                                         r=["dd", "rsd", "subs"], w=[("onbd", osl)])
                                self.ldc(self.oT[h * 128:(h + 1) * 128, c * 512:(c + 1) * 512], ob, r=[("onbd", osl)], a=["oT"])
                        pending[0] = epilogue
                        ecnt += parts
            if pending[0] is not None:
                pending[0]()
            self.S.flush()

    def phaseC1(self, layer):
        sb = self.sb
        NI = 2
        nk = 6 if layer == 0 else 8
        xin = self.x if layer == 0 else self.xL
        wsrc = self.e_w_out if layer == 0 else self.o_w_out
        with ExitStack() as es:
            w_out = sb(es, "w_out", [128, nk, D], BF16)
            G1 = sb(es, "G1", [128, D], F32)
            xt = sb(es, "xtc", [128, NI, D], F32)
            oTt = sb(es, "oTt", [128, NI, nk, 128], BF16)
            tmpf = sb(es, "tmpfc", [128, NI, D], F32)
            sq = sb(es, "sqc", [128, NI, D], F32)
            ssx = sb(es, "ssxc", [128, NI, 2], F32)
            xs = sb(es, "xsc", [128, NI, D], BF16)
            hT = sb(es, "hTc", [128, NI, 8, 128], BF16)
            psy = [self.ps(es, "psy%d" % i, [128, 512], F32) for i in range(4)]
            pT = [self.ps(es, "pTc%d" % i, [128, 8, 128], BF16) for i in range(2)]
            self.load_w(w_out, wsrc, nk * 128, "w_out")
            self.ld(G1[:], self.Gd[:, (layer * 2) * D:(layer * 2 + 1) * D], w=["G1"])

            def body(t, sl):
                R = lambda nm: (nm, sl)
                self.ld(xt[:, sl], xin[t * 128:(t + 1) * 128, :], w=[R("xt")])
                self.ld(oTt[:, sl], self.oT[0:nk * 128, t * 128:(t + 1) * 128].rearrange("(k p) s -> p k s", p=128), w=[R("oTt")])
                yield
                for hf in range(2):
                    pb = psy[sl * 2 + hf]
                    pn = ("psy", sl * 2 + hf)

                    def mm(e, hf=hf, pb=pb):
                        for k in range(nk):
                            ins = e.matmul(pb[:], lhsT=oTt[:, sl, k, :], rhs=w_out[:, k, hf * 512:(hf + 1) * 512], start=(k == 0), stop=(k == nk - 1))
                        return ins
                    self.pe(mm, r=[R("oTt"), "w_out"], w=[pn])
                    yield
                    self.dve(lambda e, hf=hf, pb=pb: e.tensor_tensor(out=tmpf[:, sl, hf * 512:(hf + 1) * 512], in0=pb[:], in1=G1[:, hf * 512:(hf + 1) * 512], op=ALU.mult),
                             r=[pn, "G1"], w=[("tmpf", sl, hf)])
                    yield
                    self.dve(lambda e, hf=hf: e.tensor_tensor(out=xt[:, sl, hf * 512:(hf + 1) * 512], in0=xt[:, sl, hf * 512:(hf + 1) * 512],
                                                              in1=tmpf[:, sl, hf * 512:(hf + 1) * 512], op=ALU.add),
                             r=[("tmpf", sl, hf), R("xt")], w=[R("xt")])
                    yield
                self.ldc(self.x1[t * 128:(t + 1) * 128, :], xt[:, sl], r=[R("xt")], a=["x1"])
                self.act(lambda e: e.activation(out=sq[:, sl], in_=xt[:, sl], func=AF.Square, accum_out=ssx[:, sl, 0:1]), r=[R("xt")], w=[R("sq"), R("ssx")])
                yield
                self.act(lambda e: e.activation(out=ssx[:, sl, 1:2], in_=ssx[:, sl, 0:1], func=AF.Sqrt, scale=1.0 / D, bias=self.epsT[:, 0:1]), r=[R("ssx")], w=[R("rsx")])
                yield
                self.dve(lambda e: e.reciprocal(out=ssx[:, sl, 1:2], in_=ssx[:, sl, 1:2]), r=[R("rsx")], w=[R("rsx")])
                yield
                self.act(lambda e: e.activation(out=xs[:, sl], in_=xt[:, sl], func=AF.Copy, scale=ssx[:, sl, 1:2]), r=[R("xt"), R("rsx")], w=[R("xs")])
                yield

                def tpx(e):
                    for j in range(8):
                        ins = e.transpose(pT[sl][:, j, :], xs[:, sl, j * 128:(j + 1) * 128], self.identb[:])
                    return ins
                self.pe(tpx, r=[R("xs"), "identb"], w=[("pT", sl)])
                yield
                ao = layer * 16 + 8
                a_bc = self.aT[:, ao:ao + 8].unsqueeze(2).to_broadcast([128, 8, 128])
                b_bc = self.modT[:, layer * 48 + 24:layer * 48 + 32].unsqueeze(2).to_broadcast([128, 8, 128])
                tm3 = tmpf[:, sl].rearrange("p (j s) -> p j s", j=8)
                self.dve(lambda e: e.tensor_tensor(out=tm3, in0=pT[sl][:], in1=a_bc, op=ALU.mult),
                         r=[("pT", sl), ("aT", ao)], w=[("tmpf", sl, 0), ("tmpf", sl, 1)])
                yield
                self.dve(lambda e: e.tensor_tensor(out=hT[:, sl], in0=tm3, in1=b_bc, op=ALU.add),
                         r=[("tmpf", sl, 0), ("tmpf", sl, 1), "modT"], w=[R("hT")])
                yield
                self.ldc(self.h2T[:, t * 128:(t + 1) * 128].rearrange("(k p) s -> p k s", p=128), hT[:, sl], r=[R("hT")], a=["h2T"])
                yield

            STAG = 8
            active = []
            next_t = 0
            while next_t < self.nt or active:
                if next_t < self.nt and len(active) < NI and (not active or active[-1][1] >= STAG):
                    active.append([body(next_t, next_t % NI), 0])
                    next_t += 1
                for a_ in list(active):
                    try:
                        next(a_[0])
                        a_[1] += 1
                    except StopIteration:
                        active.remove(a_)
            self.S.flush()

    def phaseC2(self, layer):
        sb = self.sb
        dst = self.xL if layer == 0 else self.out
        ng = max(1, self.nt // 2)
        with ExitStack() as es:
            w1 = sb(es, "w1", [128, 8, DFF], BF16)
            w2 = sb(es, "w2", [128, 32, D], BF16)
            G2 = sb(es, "G2", [128, D], F32)
            hTt = sb(es, "hTt", [128, 2, 8, 256], BF16)
            x1t = sb(es, "x1t", [128, 2, 2, D], F32)
            aT = sb(es, "aTt", [128, 32, 256], BF16)
            rl = sb(es, "rl", [128, 2, 512], F32)
            tmp = sb(es, "tmpc2", [128, 2, 512], F32)
            psa = [self.ps(es, "psa%d" % i, [128, 2, 256], F32) for i in range(2)]
            psy = [self.ps(es, "psy2_%d" % i, [128, 512], F32) for i in range(4)]
            self.load_w(w1, self.mlp_w1[layer], D, "w1")
            self.load_w(w2, self.mlp_w2[layer], DFF, "w2")
            self.ld(G2[:], self.Gd[:, (layer * 2 + 1) * D:(layer * 2 + 2) * D], w=["G2"])
            for g in range(ng):
                sl = g % 2
                self.ld(hTt[:, sl], self.h2T[:, g * 256:(g + 1) * 256].rearrange("(k p) s -> p k s", p=128), w=[("hTt", sl)])
                self.ld(x1t[:, sl], self.x1[g * 256:(g + 1) * 256, :].rearrange("(s p) d -> p s d", p=128), w=[("x1t", sl)])
                for fp in range(16):
                    pb = psa[fp % 2]

                    def mm1(e, fp=fp, pb=pb, sl=sl):
                        for ff in range(2):
                            f = fp * 2 + ff
                            for k in range(8):
                                ins = e.matmul(pb[:, ff, :], lhsT=w1[:, k, f * 128:(f + 1) * 128], rhs=hTt[:, sl, k, :], start=(k == 0), stop=(k == 7))
                        return ins
                    self.pe(mm1, r=["w1", ("hTt", sl)], w=[("psa", fp % 2)])
                    rs_ = fp % 2
                    self.act(lambda e, pb=pb, rs_=rs_: e.activation(out=rl[:, rs_, :], in_=pb[:].rearrange("p a b -> p (a b)"), func=AF.Relu),
                             r=[("psa", fp % 2)], w=[("rl", rs_)])
                    self.dve(lambda e, fp=fp, rs_=rs_: e.tensor_tensor(out=aT[:, fp * 2:fp * 2 + 2, :].rearrange("p a b -> p (a b)"), in0=rl[:, rs_, :], in1=rl[:, rs_, :], op=ALU.mult),
                             r=[("rl", rs_)], w=[("aT", fp)])
                for s_ in range(2):
                    for hf in range(2):
                        pi = s_ * 2 + hf

                        def mm2(e, s_=s_, hf=hf, pi=pi):
                            for f in range(32):
                                ins = e.matmul(psy[pi][:], lhsT=aT[:, f, s_ * 128:(s_ + 1) * 128], rhs=w2[:, f, hf * 512:(hf + 1) * 512], start=(f == 0), stop=(f == 31))
                            return ins
                        self.pe(mm2, r=["w2"] + [("aT", fp) for fp in range(16)], w=[("psy", pi)])
                        self.dve(lambda e, hf=hf, pi=pi: e.tensor_tensor(out=tmp[:, hf, :], in0=psy[pi][:], in1=G2[:, hf * 512:(hf + 1) * 512], op=ALU.mult),
                                 r=[("psy", pi), "G2"], w=[("tmp", hf)])
                        self.dve(lambda e, s_=s_, hf=hf, sl=sl: e.tensor_tensor(out=x1t[:, sl, s_, hf * 512:(hf + 1) * 512], in0=x1t[:, sl, s_, hf * 512:(hf + 1) * 512],
                                                                              in1=tmp[:, hf, :], op=ALU.add),
                                 r=[("tmp", hf), ("x1t", sl)], w=[("x1t", sl)])
                self.ldc(dst[g * 256:(g + 1) * 256, :].rearrange("(s p) d -> p s d", p=128), x1t[:, sl], r=[("x1t", sl)], a=["dst"])
            self.S.flush()


def _consts():
    ident = np.eye(128, dtype=np.float32)
    i16 = np.arange(16, dtype=np.float32) / np.float32(16)
    i32 = np.arange(32, dtype=np.float32) / np.float32(32)
    invf = np.concatenate([np.float32(10000.0) ** (-i16), np.float32(10000.0) ** (-i32)]).astype(np.float32)
    k = np.arange(128)[:, None]
    q = np.arange(512)[None, :]
    cmask = np.stack([(q >= 128 * j + k) for j in range(4)], axis=1).astype(np.float32)
    dm = []
    for (w, r) in ((128, 1), (512, 4), (2048, 16)):
        W = w // 128
        for o in range(W + 4):
            rel = q - k + 128 * (W - o)
            dm.append(((rel >= 0) & (rel <= w) & (rel % r == 0)).astype(np.float32))
    dmask = np.stack(dm, axis=1)
    kind = (np.arange(SEQ)[None, :] // 256 == np.arange(32)[:, None]).astype(np.float32)
    return dict(ident=ident, invf=invf, cmask=cmask.reshape(128, -1), dmask=dmask.reshape(128, -1), kind=kind)


def _colT(v, n):
    return np.ascontiguousarray(np.asarray(v, np.float32).reshape(n, 128).T)


def make_in_maps(inp, batches):
    f = lambda a: np.ascontiguousarray(np.asarray(a, dtype=np.float32))
    c = _consts()
    shared = dict(
        ada_w=f(inp["ada_w"]),
        ada_bT=np.ascontiguousarray(np.concatenate([_colT(inp["ada_b"][l], 48) for l in range(2)], axis=1)),
        nrmT=np.ascontiguousarray(np.concatenate(
            [_colT(inp[nm][l], 8) for l in range(2) for nm in ("norm_mix", "norm_mlp")], axis=1)),
        mlp_w1=f(inp["mlp_w1"]), mlp_w2=f(inp["mlp_w2"]),
        e_w_in=f(inp["even_w_in"][0]), e_w_out=f(inp["even_w_out"][0]),
        latT=np.ascontiguousarray(np.concatenate([_colT(inp["mla_q_lat_norm"][0], 3), _colT(inp["mla_kv_lat_norm"][0], 2)], axis=1)),
        w_uq=f(np.asarray(inp["mla_w_uq"][0]).reshape(384, 768)),
        w_ukv=f(np.asarray(inp["mla_w_ukv"][0]).reshape(256, 1024)),
        gains=f(np.concatenate([np.asarray(inp[k][0], np.float32).reshape(-1) for k in
                                ("mla_q_norm", "mla_k_norm", "dil_q_norm", "dil_k_norm",
                                 "diff_q_norm", "diff_k_norm", "moba_q_norm", "moba_k_norm")])),
        o_w_in=f(inp["odd_w_in"][0]), o_w_out=f(inp["odd_w_out"][0]),
        dlam=f(np.asarray(inp["diff_lambda"][0]).reshape(256)),
        subT=np.ascontiguousarray(np.asarray(inp["diff_subln"][0], np.float32).reshape(128, 1)),
        **c,
    )
    maps = []
    for b in batches:
        m = dict(shared)
        m["x"] = f(inp["x"][b])
        m["posT"] = np.ascontiguousarray(np.asarray(inp["positions"][b], np.int32).reshape(NT, 128).T)
        m["cT"] = _colT(inp["c"][b], 8)
        maps.append(m)
    return maps


def build(nt=NT, phases=None, dbg_out=()):
    phases = ALL_PHASES if phases is None else phases
    nc = bass.Bass("TRN2", target_bir_lowering=False)
    k = K(nc, nt=nt, phases=phases, dbg_out=dbg_out)
    k.declare()
    with ExitStack() as gs:
        k.es = gs
        sems = {}
        for e in Sched.COMPUTE:
            sems[e] = gs.enter_context(nc.semaphore("sem_" + e))
        for q in ("sp", "pq"):
            sems[q] = [gs.enter_context(nc.semaphore("sem_%s%d" % (q, i))) for i in range(k.S.ring)]
        k.S.init_emit(sems)
        k.setup()
        for ph in phases:
            getattr(k, "run_" + ph)()
        stats = k.S.finish()
    return nc, k, stats


def _add_phase_methods():
    K.run_A0 = lambda self: self.phaseA(0)
    K.run_A1 = lambda self: self.phaseA(1)
    K.run_Bmla = lambda self: self.phaseB("mla")
    K.run_C10 = lambda self: self.phaseC1(0)
    K.run_C20 = lambda self: self.phaseC2(0)
    K.run_C11 = lambda self: self.phaseC1(1)
    K.run_C21 = lambda self: self.phaseC2(1)
    K.run_Bdil = lambda self: self.phaseB("dil")
    K.run_Bdiff = lambda self: self.phaseB("diff")
    K.run_Bmoba = lambda self: self.phaseB("moba")


_add_phase_methods()


ALL_PHASES = ("A0", "Bmla", "Bdil", "C10", "C20", "A1", "Bdiff", "Bmoba", "C11", "C21")


def kernel(**inputs):
    nb = int(np.asarray(inputs["x"]).shape[0])
    nc, k, stats = build(nt=NT, phases=ALL_PHASES)
    maps = make_in_maps(inputs, list(range(nb)))
    res = run_bass_kernel_spmd(nc, maps, core_ids=list(range(nb)))
    return np.stack([np.asarray(res.results[b]["out"], dtype=np.float32) for b in range(nb)], axis=0)
```

```python
class _Op:
    __slots__ = ("eng", "fn", "deps", "needs_sig", "sem", "val", "is_dma", "idx", "ring_prev")

    def __init__(self, eng, fn, is_dma):
        self.eng = eng
        self.fn = fn
        self.deps = []
        self.needs_sig = False
        self.sem = None
        self.val = 0
        self.is_dma = is_dma
        self.ring_prev = None


class Sched:
    COMPUTE = ("pe", "act", "dve", "pool")

    def __init__(self, nc, ring=8, same_engine_sync=True):
        self.nc = nc
        self.ops = []
        self.last_w = {}
        self.readers = {}
        self.appenders = {}
        self.ring = ring
        self.same_engine_sync = same_engine_sync
        self.engobj = {"pe": nc.tensor, "act": nc.scalar, "dve": nc.vector, "pool": nc.gpsimd,
                       "sp": nc.sync, "pq": nc.gpsimd}
        self.stream = {"pe": "pe", "act": "act", "dve": "dve", "pool": "pool", "sp": "sp", "pq": "pool"}

    def add(self, eng, fn, r=(), w=(), a=()):
        is_dma = eng in ("sp", "pq")
        op = _Op(eng, fn, is_dma)
        deps = {}

        def dep(p, kind):
            if p is op:
                return
            same = (self.stream[p.eng] == self.stream[eng]) and not p.is_dma
            if same:
                if eng == "pe" or not self.same_engine_sync:
                    return
            deps[id(p)] = p

        for x in r:
            p = self.last_w.get(x)
            if p is not None:
                dep(p, "raw")
            for p in self.appenders.get(x, ()):
                dep(p, "raw")
        for x in list(w) + list(a):
            p = self.last_w.get(x)
            if p is not None:
                dep(p, "waw")
            for p in self.readers.get(x, ()):
                dep(p, "war")
        for x in w:
            for p in self.appenders.get(x, ()):
                dep(p, "waw")
        for x in r:
            self.readers.setdefault(x, []).append(op)
        for x in w:
            self.last_w[x] = op
            self.readers[x] = []
            self.appenders[x] = []
        for x in a:
            self.appenders.setdefault(x, []).append(op)
            self.readers[x] = []
        op.deps = list(deps.values())
        for p in op.deps:
            p.needs_sig = True
        self.ops.append(op)
        return op

    def init_emit(self, sems):
        self.sems = sems
        self.cnt = {e: 0 for e in self.COMPUTE}
        self.dcount = {"sp": 0, "pq": 0}
        self.dhist = {"sp": [], "pq": []}
        self.waited = {}
        self.nwaits = 0
        self.nops = 0
        self.barrier_deps = []

    def flush(self):
        ops = self.ops
        lastc = {}
        for op in ops:
            if not op.is_dma:
                lastc[op.eng] = op
        for op in lastc.values():
            op.needs_sig = True
        bd = self.barrier_deps
        first_seen = set()
        for op in ops:
            st = self.stream[op.eng]
            if st not in first_seen:
                first_seen.add(st)
                op.deps = op.deps + [p for p in bd if not (self.stream[p.eng] == st and not p.is_dma)]
            if op.is_dma:
                i = self.dcount[op.eng]
                self.dcount[op.eng] += 1
                op.sem = self.sems[op.eng][i % self.ring]
                op.val = 16 * (i // self.ring + 1)
                if i >= self.ring:
                    op.ring_prev = self.dhist[op.eng][i - self.ring]
                self.dhist[op.eng].append(op)
            elif op.needs_sig:
                self.cnt[op.eng] += 1
                op.sem = self.sems[op.eng]
                op.val = self.cnt[op.eng]
        waited = self.waited
        for op in ops:
            e = self.engobj[op.eng]
            st = self.stream[op.eng]
            need = {}
            plist = list(op.deps)
            if op.ring_prev is not None:
                plist.append(op.ring_prev)
            for p in plist:
                k = id(p.sem)
                if k not in need or need[k][1] < p.val:
                    need[k] = (p.sem, p.val)
            for k, (sem, val) in need.items():
                if waited.get((st, k), 0) < val:
                    e.wait_ge(sem, val)
                    waited[(st, k)] = val
                    self.nwaits += 1
            inst = op.fn(e)
            if op.is_dma:
                inst.then_inc(op.sem, 16)
            elif op.needs_sig:
                inst.then_inc(op.sem, 1)
        self.nops += len(ops)
        nb = list(lastc.values())
        for p in bd:
            if not p.is_dma and p.eng not in lastc:
                nb.append(p)
        for q in ("sp", "pq"):
            nb.extend(self.dhist[q][-self.ring:])
        self.barrier_deps = nb
        self.ops = []
        self.last_w = {}
        self.readers = {}
        self.appenders = {}

    def finish(self, eng="sp"):
        self.flush()
        e = self.engobj[eng]
        st = self.stream[eng]
        for p in self.barrier_deps:
            k = id(p.sem)
            if self.waited.get((st, k), 0) < p.val:
                e.wait_ge(p.sem, p.val)
                self.waited[(st, k)] = p.val
        return {"n_ops": self.nops, "n_waits": self.nwaits, "sig": dict(self.cnt), "dma": dict(self.dcount)}


import math
from contextlib import ExitStack
import numpy as np
import ml_dtypes
import concourse.bass as bass
import concourse.mybir as mybir
from concourse.bass_utils import run_bass_kernel_spmd

F32 = mybir.dt.float32
BF16 = mybir.dt.bfloat16
I32 = mybir.dt.int32
AF = mybir.ActivationFunctionType
ALU = mybir.AluOpType
AX = mybir.AxisListType

D = 1024
SEQ = 8192
NT = SEQ // 128
NCH = SEQ // 512
EPS = 1e-6
EVEN_IN = 2976
ODD_IN = 3072
DFF = 4096
NEGB = -30000.0
TWO_PI = float(2 * np.pi)


class K:
    def __init__(self, nc, nt=NT, phases=None, dbg_out=()):
        self.nc = nc
        self.S = Sched(nc)
        self.nt = nt
        self.phases = phases
        self.dbg_out = dbg_out
        self.es = ExitStack()
        self.din = {}
        self.dscr = {}
        import os as _os
        self.lim = float(_os.environ.get('KLIM', '99'))

    def inp(self, name, shape, dt=F32):
        t = self.nc.dram_tensor(name, list(shape), dt, kind="ExternalInput").ap()
        self.din[name] = t
        return t

    def scr(self, name, shape, dt):
        kind = "ExternalOutput" if name in self.dbg_out else "Internal"
        t = self.nc.dram_tensor(name, list(shape), dt, kind=kind).ap()
        self.dscr[name] = t
        return t

    def sb(self, es, name, shape, dt):
        self.uid = getattr(self, "uid", 0) + 1
        return es.enter_context(self.nc.sbuf_tensor("%s_%d" % (name, self.uid), list(shape), dt))

    def ps(self, es, name, shape, dt):
        self.uid = getattr(self, "uid", 0) + 1
        return es.enter_context(self.nc.psum_tensor("%s_%d" % (name, self.uid), list(shape), dt))

    def act(self, fn, r=(), w=(), a=()):
        return self.S.add("act", fn, r, w, a)

    def dve(self, fn, r=(), w=(), a=()):
        return self.S.add("dve", fn, r, w, a)

    def pool(self, fn, r=(), w=(), a=()):
        return self.S.add("pool", fn, r, w, a)

    def pe(self, fn, r=(), w=(), a=()):
        return self.S.add("pe", fn, r, w, a)

    def ld(self, out, in_, r=(), w=(), a=(), q="sp"):
        return self.S.add(q, lambda e: e.dma_start(out=out, in_=in_), r, w, a)

    def ldc(self, out, in_, r=(), w=(), a=()):
        return self.S.add("pq", lambda e: e.dma_start(out=out, in_=in_), r, w, a)

    def rstd(self, ss, rs, n, rn_ss, rn_rs):
        self.act(lambda e: e.activation(out=rs, in_=ss, func=AF.Sqrt, scale=1.0 / n, bias=self.eps_ap(ss)),
                 r=[rn_ss], w=[rn_rs])
        self.dve(lambda e: e.reciprocal(out=rs, in_=rs), r=[rn_rs], w=[rn_rs])

    def eps_ap(self, like):
        p = like.shape[0]
        return self.epsT[0:p, 0:1]

    def declare(self):
        i = self.inp
        self.x = i("x", [SEQ, D])
        self.posT = i("posT", [128, NT], I32)
        self.cT = i("cT", [128, 8])
        self.ada_w = i("ada_w", [2, D, 6 * D])
        self.ada_bT = i("ada_bT", [128, 96])
        self.nrmT = i("nrmT", [128, 32])
        self.mlp_w1 = i("mlp_w1", [2, D, DFF])
        self.mlp_w2 = i("mlp_w2", [2, DFF, D])
        self.e_w_in = i("e_w_in", [D, EVEN_IN])
        self.e_w_out = i("e_w_out", [768, D])
        self.latT = i("latT", [128, 5])
        self.w_uq = i("w_uq", [384, 768])
        self.w_ukv = i("w_ukv", [256, 1024])
        self.gains = i("gains", [576])
        self.o_w_in = i("o_w_in", [D, ODD_IN])
        self.o_w_out = i("o_w_out", [D, D])
        self.dlam = i("dlam", [256])
        self.subT = i("subT", [128, 1])
        self.ident = i("ident", [128, 128])
        self.invf = i("invf", [48])
        self.cmask = i("cmask", [128, 4 * 512])
        self.dmask = i("dmask", [128, 33 * 512])
        self.kind = i("kind", [32, SEQ])
        self.out = self.nc.dram_tensor("out", [SEQ, D], F32, kind="ExternalOutput").ap()
        s = self.scr
        self.trigd = s("trigd", [128, 2 * NT * 48], F32)
        self.Gd = s("Gd", [128, 4 * D], F32)
        self.x1 = s("x1", [SEQ, D], F32)
        self.xL = s("xL", [SEQ, D], F32)
        self.h2T = s("h2T", [D, SEQ], BF16)
        self.oT = s("oT", [D, SEQ], BF16)
        self.qT_mla = s("qT_mla", [8, 96, SEQ], BF16)
        self.kT_mla = s("kT_mla", [8, 96, SEQ], BF16)
        self.v_mla = s("v_mla", [SEQ, 512], BF16)
        self.qT_dil = s("qT_dil", [12, 64, SEQ], BF16)
        self.kT_dil = s("kT_dil", [12, 64, SEQ], BF16)
        self.v_dil = s("v_dil", [SEQ, 768], BF16)
        self.qT_df = s("qT_df", [8, 64, SEQ], BF16)
        self.kT_df = s("kT_df", [8, 64, SEQ], BF16)
        self.v_df = s("v_df", [SEQ, 512], BF16)
        self.qT_mb = s("qT_mb", [8, 96, SEQ], BF16)
        self.kT_mb = s("kT_mb", [8, 64, SEQ], BF16)
        self.v_mb = s("v_mb", [SEQ, 512], BF16)

    def setup(self):
        nc, S = self.nc, self.S
        g = self.es
        sb = self.sb
        self.epsT = sb(g, "epsT", [128, 1], F32)
        self.identb = sb(g, "identb", [128, 128], BF16)
        self.modT = sb(g, "modT", [128, 96], F32)
        self.aT = sb(g, "aT", [128, 32], F32)
        self.gbc = sb(g, "gbc", [128, 576], F32)
        self.cmb = sb(g, "cmb", [128, 4, 512], BF16)
        self.onesb = sb(g, "onesb", [128, 64], F32)
        self.pool(lambda e: e.memset(self.epsT[:], EPS), w=["epsT"])
        self.pool(lambda e: e.memset(self.onesb[:], 1.0), w=["onesb"])
        self.ldc(self.identb[:], self.ident, w=["identb"])
        self.ldc(self.cmb[:], self.cmask.rearrange("p (a b) -> p a b", a=4), w=["cmb"])
        self.ld(self.gbc[:], self.gains.partition_broadcast(128), w=["gbc"])
        with ExitStack() as es:
            self.trig = sb(es, "trig", [128, 2, NT, 48], F32)
            self.G = sb(es, "G", [128, 4, D], F32)
            condT = sb(es, "condT", [128, 8], F32)
            bT = sb(es, "bT", [128, 96], F32)
            nT = sb(es, "nT", [128, 32], F32)
            stage = sb(es, "adastage", [128, 2, 8, 1024], F32)
            posi = sb(es, "posi", [128, NT], I32)
            posf = sb(es, "posf", [128, NT], F32)
            invb = sb(es, "invb", [128, 48], F32)
            kf = sb(es, "kf", [128, 2, NT, 48], F32)
            ki = sb(es, "ki", [128, 2, NT, 48], I32)
            psmod = self.ps(es, "psmod", [128, 96], F32)
            self.ld(condT[:], self.cT, w=["condT"])
            self.ld(bT[:], self.ada_bT, w=["bT"])
            self.ld(nT[:], self.nrmT, w=["nT"])
            self.ld(posi[:], self.posT, w=["posi"])
            self.ld(invb[:], self.invf.partition_broadcast(128), w=["invb"])
            self.act(lambda e: e.activation(out=condT[:], in_=condT[:], func=AF.Silu), r=["condT"], w=["condT"])
            tr = self.trig
            self.dve(lambda e: e.tensor_copy(out=posf[:], in_=posi[:]), r=["posi"], w=["posf"])
            self.dve(lambda e: e.tensor_tensor(out=tr[:, 0], in0=posf[:].unsqueeze(2).to_broadcast([128, NT, 48]),
                                               in1=invb[:].unsqueeze(1).to_broadcast([128, NT, 48]), op=ALU.mult),
                     r=["posf", "invb"], w=["trig"])
            self.dve(lambda e: e.tensor_scalar(out=tr[:, 1], in0=tr[:, 0], scalar1=float(np.pi / 2), scalar2=None,
                                               op0=ALU.add), r=["trig"], w=["trig"])
            self.dve(lambda e: e.tensor_scalar(out=kf[:], in0=tr[:], scalar1=float(1 / TWO_PI), scalar2=None,
                                               op0=ALU.mult), r=["trig"], w=["kf"])
            self.dve(lambda e: e.tensor_copy(out=ki[:], in_=kf[:]), r=["kf"], w=["ki"])
            self.dve(lambda e: e.tensor_copy(out=kf[:], in_=ki[:]), r=["ki"], w=["kf"])
            self.dve(lambda e: e.scalar_tensor_tensor(out=tr[:], in0=kf[:], scalar=-TWO_PI, in1=tr[:],
                                                      op0=ALU.mult, op1=ALU.add), r=["kf", "trig"], w=["trig"])
            self.dve(lambda e: e.tensor_scalar(out=kf[:], in0=tr[:], scalar1=float(np.pi), scalar2=-TWO_PI,
                                               op0=ALU.is_gt, op1=ALU.mult), r=["trig"], w=["kf"])
            self.dve(lambda e: e.tensor_tensor(out=tr[:], in0=tr[:], in1=kf[:], op=ALU.add), r=["kf", "trig"], w=["trig"])
            self.dve(lambda e: e.tensor_scalar(out=kf[:], in0=tr[:], scalar1=float(-np.pi), scalar2=TWO_PI,
                                               op0=ALU.is_lt, op1=ALU.mult), r=["trig"], w=["kf"])
            self.dve(lambda e: e.tensor_tensor(out=tr[:], in0=tr[:], in1=kf[:], op=ALU.add), r=["kf", "trig"], w=["trig"])
            self.act(lambda e: e.activation(out=tr[:], in_=tr[:], func=AF.Sin), r=["trig"], w=["trig"])
            for l in range(2):
                for cb in range(6):
                    sl = (l * 6 + cb) % 2
                    self.ld(stage[:, sl], self.ada_w[l, :, cb * 1024:(cb + 1) * 1024].rearrange("(k p) n -> p k n", p=128),
                            w=[("adast", sl)])
                    for jj in range(8):
                        col = l * 48 + cb * 8 + jj

                        def mm(e, sl=sl, jj=jj, col=col):
                            for k in range(8):
                                ins = e.matmul(psmod[:, col:col + 1], lhsT=stage[:, sl, k, jj * 128:(jj + 1) * 128],
                                               rhs=condT[:, k:k + 1], start=(k == 0), stop=(k == 7))
                            return ins
                        self.pe(mm, r=[("adast", sl), "condT"], a=["psmod"])
            self.dve(lambda e: e.tensor_tensor(out=self.modT[:], in0=psmod[:], in1=bT[:], op=ALU.add),
                     r=["psmod", "bT"], w=["modT"])
            for l in range(2):
                m = self.modT[:, l * 48:(l + 1) * 48]
                for which, (sc0, sh0) in enumerate(((8, 0), (32, 24))):
                    o = l * 16 + which * 8
                    nsl = nT[:, l * 16 + which * 8: l * 16 + which * 8 + 8]
                    self.dve(lambda e, o=o, m=m, sc0=sc0, nsl=nsl: e.scalar_tensor_tensor(
                        out=self.aT[:, o:o + 8], in0=m[:, sc0:sc0 + 8], scalar=1.0, in1=nsl, op0=ALU.add, op1=ALU.mult),
                        r=["modT", "nT"], w=[("aT", o)])
            identf = sb(es, "identf", [128, 128], F32)
            onesf = sb(es, "onesf", [128, 128], F32)
            dg = sb(es, "dg", [128, 2, 128], F32)
            psg_ = [self.ps(es, "psgate%d" % i, [128, 512], F32) for i in range(2)]
            self.ld(identf[:], self.ident, w=["identf"])
            self.pool(lambda e: e.memset(onesf[:], 1.0), w=["onesf"])
            cnt = 0
            for l in range(2):
                for which, off in enumerate((16, 40)):
                    gi = l * 2 + which
                    for half in range(2):
                        pb = psg_[cnt % 2]
                        for jj in range(4):
                            j = half * 4 + jj
                            col = l * 48 + off + j
                            sl = (cnt * 4 + jj) % 2
                            self.dve(lambda e, sl=sl, col=col: e.tensor_scalar(out=dg[:, sl, :], in0=identf[:], scalar1=self.modT[:, col:col + 1],
                                                                               scalar2=None, op0=ALU.mult), r=["identf", "modT"], w=[("dg", sl)])
                            self.pe(lambda e, sl=sl, jj=jj, pb=pb: e.matmul(pb[:, jj * 128:(jj + 1) * 128], lhsT=onesf[:], rhs=dg[:, sl, :], start=True, stop=True),
                                    r=[("dg", sl), "onesf"], w=[("psgate", cnt % 2, jj)])
                        self.act(lambda e, gi=gi, half=half, pb=pb: e.activation(out=self.G[:, gi, half * 512:(half + 1) * 512], in_=pb[:], func=AF.Copy),
                                 r=[("psgate", cnt % 2, jj) for jj in range(4)], w=[("G", gi, half)])
                        cnt += 1
            self.ldc(self.trigd.rearrange("p (a b) -> p a b", a=2 * NT), self.trig[:].rearrange("p s t f -> p (s t) f"), r=["trig"], w=["trigd"])
            self.ldc(self.Gd.rearrange("p (a b) -> p a b", a=4), self.G[:], r=[("G", gi, hf) for gi in range(4) for hf in range(2)], w=["Gd"])
            self.S.flush()

    def head_post(self, nm, src, nh, hd, goff, rope_lo, half, tg, tg_nm, sq, sq_nm, ssh, ssh_nm, rt, rt_nm, dst_bf, nm_bf):
        n = nh * hd
        nms = list(nm) if isinstance(nm, list) else [nm]
        s3 = src.rearrange("p (h d) -> p h d", h=nh)
        sq2 = sq[:, 0:n]
        self.act(lambda e: e.activation(out=sq2, in_=src, func=AF.Square), r=nms, w=[sq_nm])
        yield
        self.dve(lambda e: e.tensor_reduce(out=ssh[:, 0:nh], in_=sq2.rearrange("p (h d) -> p h d", h=nh), axis=AX.X, op=ALU.add),
                 r=[sq_nm], w=[ssh_nm])
        yield
        self.act(lambda e: e.activation(out=ssh[:, 0:nh], in_=ssh[:, 0:nh], func=AF.Sqrt, scale=1.0 / hd, bias=self.epsT[:, 0:1]),
                 r=[ssh_nm], w=[ssh_nm])
        yield
        self.dve(lambda e: e.reciprocal(out=ssh[:, 0:nh], in_=ssh[:, 0:nh]), r=[ssh_nm], w=[ssh_nm])
        yield
        self.dve(lambda e: e.tensor_tensor(out=s3, in0=s3, in1=ssh[:, 0:nh].unsqueeze(2).to_broadcast([128, nh, hd]), op=ALU.mult),
                 r=nms + [ssh_nm], w=nms)
        yield
        gb = self.gbc[:, goff:goff + hd]
        self.dve(lambda e: e.tensor_tensor(out=s3, in0=s3, in1=gb.unsqueeze(1).to_broadcast([128, nh, hd]), op=ALU.mult),
                 r=nms + ["gbc"], w=nms)
        yield
        fo = 0 if half == 16 else 16
        sin = tg[:, 0, fo:fo + half].unsqueeze(1).to_broadcast([128, nh, half])
        cos = tg[:, 1, fo:fo + half].unsqueeze(1).to_broadcast([128, nh, half])
        x1 = s3[:, :, rope_lo:rope_lo + half]
        x2 = s3[:, :, rope_lo + half:rope_lo + 2 * half]
        m = nh * half
        tA = rt[:, 0, 0:m].rearrange("p (h d) -> p h d", h=nh)
        tB = rt[:, 1, 0:m].rearrange("p (h d) -> p h d", h=nh)
        tC = rt[:, 2, 0:m].rearrange("p (h d) -> p h d", h=nh)
        tD = rt[:, 3, 0:m].rearrange("p (h d) -> p h d", h=nh)
        rA, rB, rC, rD = [(rt_nm, i) for i in range(4)]
        self.dve(lambda e: e.tensor_tensor(out=tA, in0=x1, in1=cos, op=ALU.mult), r=nms + [tg_nm], w=[rA])
        self.dve(lambda e: e.tensor_tensor(out=tB, in0=x2, in1=sin, op=ALU.mult), r=nms + [tg_nm], w=[rB])
        yield
        self.dve(lambda e: e.tensor_tensor(out=tC, in0=x2, in1=cos, op=ALU.mult), r=nms + [tg_nm], w=[rC])
        self.dve(lambda e: e.tensor_tensor(out=tD, in0=x1, in1=sin, op=ALU.mult), r=nms + [tg_nm], w=[rD])
        yield
        self.dve(lambda e: e.tensor_tensor(out=x1, in0=tA, in1=tB, op=ALU.subtract), r=[rA, rB], w=nms)
        yield
        self.dve(lambda e: e.tensor_tensor(out=x2, in0=tC, in1=tD, op=ALU.add), r=[rC, rD], w=nms)
        yield
        self.act(lambda e: e.activation(out=dst_bf, in_=src, func=AF.Copy), r=nms, w=[nm_bf])
        yield

    def tr_store(self, nm_bf, src_bf, nh, hd, pstr, pcnt, stg, stg_nm, dram, t, hw=None, row0=0):
        hw = hw or hd
        s3 = src_bf.rearrange("p (h d) -> p h d", h=nh)
        done = 0
        while done < nh:
            nb = min(8, nh - done)
            ps = pstr[pcnt[0] % 2]
            rn = ("pstr", pcnt[0] % 2)
            pcnt[0] += 1

            def tp(e, done=done, nb=nb, ps=ps):
                for i in range(nb):
                    ins = e.transpose(ps[0:hw, i, :], s3[:, done + i, 0:hw], self.identb[:])
                return ins
            self.pe(tp, r=[nm_bf, "identb"], w=[rn])
            self.dve(lambda e, done=done, nb=nb, ps=ps: e.tensor_copy(out=stg[0:hw, done:done + nb, :], in_=ps[0:hw, 0:nb, :]),
                     r=[rn], w=[(stg_nm, done)])
            yield
            dst = dram[done:done + nb, row0:row0 + hw, t * 128:(t + 1) * 128].rearrange("h d s -> d h s")
            self.ldc(dst, stg[0:hw, done:done + nb, :], r=[(stg_nm, done)], a=[("dram", id(dram))])
            done += nb

    def load_w(self, dst, src, K, rn, n0=0, n1=None, d0=0):
        n1 = n1 if n1 is not None else src.shape[1]
        kc = K // 128
        step = 2 if (n1 - n0) > 1024 else kc
        for k0 in range(0, kc, step):
            k1 = min(kc, k0 + step)
            self.ldc(dst[:, k0:k1, d0:d0 + (n1 - n0)],
                     src[k0 * 128:k1 * 128, n0:n1].rearrange("(k p) n -> p k n", p=128), a=[rn])

    def phaseA(self, layer):
        sb = self.sb
        NI = 2
        ncol = EVEN_IN if layer == 0 else ODD_IN
        xin = self.x if layer == 0 else self.xL
        with ExitStack() as es:
            w_in = sb(es, "w_in", [128, 8, ncol], BF16)
            trg = sb(es, "trg", [128, NI, 2, 48], F32)
            xt = sb(es, "xt", [128, NI, D], F32)
            xs = sb(es, "xs", [128, NI, D], BF16)
            ssx = sb(es, "ssx", [128, NI, 4], F32)
            hT = sb(es, "hT", [128, NI, 8, 128], BF16)
            tmpf = sb(es, "tmpf", [128, NI, D], F32)
            u = sb(es, "u", [128, NI, ncol], F32)
            sq = sb(es, "sq", [128, NI, D], F32)
            ssh = sb(es, "ssh", [128, NI, 16], F32)
            rt = sb(es, "rt", [128, NI, 4, 512], F32)
            qbf = sb(es, "qbf", [128, NI, 2048], BF16)
            vbf = sb(es, "vbf", [128, NI, 1024], BF16)
            stg = [sb(es, "stg%d" % i, [96, NI, 12, 128], BF16) for i in range(4)]
            pT = self.ps(es, "pT", [128, 8, 128], BF16)
            psu = [self.ps(es, "psu%d" % i, [128, 512], F32) for i in range(2)]
            pstr = [self.ps(es, "pstr%d" % i, [128, 8, 128], BF16) for i in range(2)]
            self.load_w(w_in, self.e_w_in if layer == 0 else self.o_w_in, D, "w_in")
            if layer == 0:
                w_uq = sb(es, "w_uqb", [128, 3, 768], BF16)
                w_ukv = sb(es, "w_ukvb", [128, 2, 1024], BF16)
                latTs = sb(es, "latTs", [128, 5], F32)
                latb = sb(es, "latb", [128, NI, 640], BF16)
                latTt = sb(es, "latTt", [128, NI, 5, 128], BF16)
                qm = sb(es, "qm", [128, NI, 768], F32)
                kf_ = sb(es, "kfull", [128, NI, 768], F32)
                psup = [self.ps(es, "psup%d" % i, [128, 512], F32) for i in range(2)]
                self.ld(latTs[:], self.latT, w=["latTs"])
                self.ldc(w_uq[:], self.w_uq.rearrange("(k p) n -> p k n", p=128), w=["w_uq"])
                self.ldc(w_ukv[:], self.w_ukv.rearrange("(k p) n -> p k n", p=128), w=["w_ukv"])
                self.dve(lambda e: e.tensor_tensor(out=w_uq[:], in0=w_uq[:], in1=latTs[:, 0:3].unsqueeze(2).to_broadcast([128, 3, 768]), op=ALU.mult),
                         r=["w_uq", "latTs"], w=["w_uq"])
                self.dve(lambda e: e.tensor_tensor(out=w_ukv[:], in0=w_ukv[:], in1=latTs[:, 3:5].unsqueeze(2).to_broadcast([128, 2, 1024]), op=ALU.mult),
                         r=["w_ukv", "latTs"], w=["w_ukv"])
            else:
                kmacc = sb(es, "kmacc", [64, 8, 32], F32)
                kmb = sb(es, "kmb", [64, 8, 32], BF16)
                kpart = sb(es, "kpart", [64, NI, 8], F32)
                gsb = sb(es, "gsb", [128, NI, 8, 32], F32)
                top8 = sb(es, "top8", [128, NI, 8, 8], F32)
                biasf = sb(es, "biasf", [128, NI, 8, 32], F32)
                biasb = sb(es, "biasb", [128, NI, 8, 32], BF16)
                psg = self.ps(es, "psg", [128, 8, 32], F32)
                self.pool(lambda e: e.memset(gsb[:], -1e30), w=[("gsb", i) for i in range(NI)])
                self.pool(lambda e: e.memset(kmacc[:], 0.0), w=["kmacc"])
            pcnt = [0]
            trg_d = self.trigd.rearrange("p (s t f) -> p s t f", s=2, t=NT)

            def body(t, sl):
                R = lambda nm: (nm, sl)
                xtt = xt[:, sl]
                u_ = u[:, sl]
                sq_ = sq[:, sl]
                ssx_ = ssx[:, sl]
                ssh_ = ssh[:, sl]
                rt_ = rt[:, sl]
                qbf_ = qbf[:, sl]
                vbf_ = vbf[:, sl]
                tg = trg[:, sl]
                self.ld(xtt, xin[t * 128:(t + 1) * 128, :], w=[R("xt")])
                self.ld(tg, trg_d[:, :, t, :], w=[R("tg")])
                yield
                self.act(lambda e: e.activation(out=sq_, in_=xtt, func=AF.Square, accum_out=ssx_[:, 0:1]), r=[R("xt")], w=[R("sq"), R("ssx")])
                yield
                self.act(lambda e: e.activation(out=ssx_[:, 1:2], in_=ssx_[:, 0:1], func=AF.Sqrt, scale=1.0 / D, bias=self.epsT[:, 0:1]), r=[R("ssx")], w=[R("rsx")])
                yield
                self.dve(lambda e: e.reciprocal(out=ssx_[:, 1:2], in_=ssx_[:, 1:2]), r=[R("rsx")], w=[R("rsx")])
                yield
                self.act(lambda e: e.activation(out=xs[:, sl], in_=xtt, func=AF.Copy, scale=ssx_[:, 1:2]), r=[R("xt"), R("rsx")], w=[R("xs")])
                yield

                def tpx(e):
                    for j in range(8):
                        ins = e.transpose(pT[:, j, :], xs[:, sl, j * 128:(j + 1) * 128], self.identb[:])
                    return ins
                self.pe(tpx, r=[R("xs"), "identb"], w=["pT"])
                ao = layer * 16
                a_bc = self.aT[:, ao:ao + 8].unsqueeze(2).to_broadcast([128, 8, 128])
                b_bc = self.modT[:, layer * 48:layer * 48 + 8].unsqueeze(2).to_broadcast([128, 8, 128])
                tm3 = tmpf[:, sl].rearrange("p (j s) -> p j s", j=8)
                self.dve(lambda e: e.tensor_tensor(out=tm3, in0=pT[:], in1=a_bc, op=ALU.mult), r=["pT", ("aT", ao)], w=[R("tmpf")])
                yield
                self.dve(lambda e: e.tensor_tensor(out=hT[:, sl], in0=tm3, in1=b_bc, op=ALU.add), r=[R("tmpf"), "modT"], w=[R("hT")])
                yield
                ngrp = (ncol + 511) // 512
                for gi, c0 in enumerate(range(0, ncol, 512)):
                    c1 = min(ncol, c0 + 512)
                    pb = psu[gi % 2]

                    def mm(e, c0=c0, c1=c1, pb=pb):
                        for j in range(8):
                            ins = e.matmul(pb[:, 0:c1 - c0], lhsT=hT[:, sl, j, :], rhs=w_in[:, j, c0:c1], start=(j == 0), stop=(j == 7))
                        return ins
                    self.pe(mm, r=[R("hT"), "w_in"], w=[("psu", gi % 2)])
                    self.act(lambda e, c0=c0, c1=c1, pb=pb: e.activation(out=u_[:, c0:c1], in_=pb[:, 0:c1 - c0], func=AF.Copy),
                             r=[("psu", gi % 2)], w=[("u", sl, gi)])
                    yield
                uall = [("u", sl, gi) for gi in range(ngrp)]
                U = lambda gi: ("u", sl, gi)
                if layer == 0:
                    latb_ = latb[:, sl]
                    qm_ = qm[:, sl]
                    kfs = kf_[:, sl]
                    self.act(lambda e: e.activation(out=sq_[:, 0:384], in_=u_[:, 0:384], func=AF.Square, accum_out=ssx_[:, 2:3]), r=[U(0)], w=[R("sq"), R("ssl")])
                    self.act(lambda e: e.activation(out=sq_[:, 384:640], in_=u_[:, 384:640], func=AF.Square, accum_out=ssx_[:, 3:4]), r=[U(0), U(1), R("sq")], w=[R("sq"), R("ssl2")])
                    yield
                    self.act(lambda e: e.activation(out=ssx_[:, 2:3], in_=ssx_[:, 2:3], func=AF.Sqrt, scale=1.0 / 384, bias=self.epsT[:, 0:1]), r=[R("ssl")], w=[R("ssl")])
                    self.act(lambda e: e.activation(out=ssx_[:, 3:4], in_=ssx_[:, 3:4], func=AF.Sqrt, scale=1.0 / 256, bias=self.epsT[:, 0:1]), r=[R("ssl2")], w=[R("ssl2")])
                    yield
                    self.dve(lambda e: e.reciprocal(out=ssx_[:, 2:4], in_=ssx_[:, 2:4]), r=[R("ssl"), R("ssl2")], w=[R("ssl"), R("ssl2")])
                    yield
                    self.dve(lambda e: e.tensor_scalar(out=latb_[:, 0:384], in0=u_[:, 0:384], scalar1=ssx_[:, 2:3], scalar2=None, op0=ALU.mult), r=[U(0), R("ssl")], w=[R("latb0")])
                    self.dve(lambda e: e.tensor_scalar(out=latb_[:, 384:640], in0=u_[:, 384:640], scalar1=ssx_[:, 3:4], scalar2=None, op0=ALU.mult), r=[U(0), U(1), R("ssl2")], w=[R("latb1")])
                    yield
                    ps = pstr[pcnt[0] % 2]
                    rn = ("pstr", pcnt[0] % 2)
                    pcnt[0] += 1

                    def tpl(e, ps=ps):
                        for j in range(5):
                            ins = e.transpose(ps[:, j, :], latb_[:, j * 128:(j + 1) * 128], self.identb[:])
                        return ins
                    self.pe(tpl, r=[R("latb0"), R("latb1"), "identb"], w=[rn])
                    self.dve(lambda e, ps=ps: e.tensor_copy(out=latTt[:, sl], in_=ps[:, 0:5, :]), r=[rn], w=[R("latTt")])
                    yield
                    for gi, (c0, c1) in enumerate(((0, 512), (512, 768))):
                        def mmq(e, c0=c0, c1=c1, gi=gi):
                            for j in range(3):
                                ins = e.matmul(psup[gi][:, 0:c1 - c0], lhsT=latTt[:, sl, j, :], rhs=w_uq[:, j, c0:c1], start=(j == 0), stop=(j == 2))
                            return ins
                        self.pe(mmq, r=[R("latTt"), "w_uq"], w=[("psup", gi)])
                        self.act(lambda e, c0=c0, c1=c1, gi=gi: e.activation(out=qm_[:, c0:c1], in_=psup[gi][:, 0:c1 - c0], func=AF.Copy),
                                 r=[("psup", gi)], w=[R("qm")] if gi == 0 else [], a=[] if gi == 0 else [R("qm")])
                        yield
                    kf3 = kfs.rearrange("p (h d) -> p h d", h=8)
                    vb3 = vbf_[:, 0:512].rearrange("p (h d) -> p h d", h=8)
                    for gi in range(2):
                        def mmk(e, gi=gi):
                            for j in range(2):
                                ins = e.matmul(psup[gi][:], lhsT=latTt[:, sl, 3 + j, :], rhs=w_ukv[:, j, gi * 512:(gi + 1) * 512], start=(j == 0), stop=(j == 1))
                            return ins
                        self.pe(mmk, r=[R("latTt"), "w_ukv"], w=[("psup", gi)])
                        p3 = psup[gi][:].rearrange("p (h d) -> p h d", h=4)
                        self.act(lambda e, gi=gi, p3=p3: e.activation(out=kf3[:, gi * 4:gi * 4 + 4, 0:64], in_=p3[:, :, 0:64], func=AF.Copy),
                                 r=[("psup", gi)], w=[R("kfull")] if gi == 0 else [], a=[] if gi == 0 else [R("kfull")])
                        self.act(lambda e, gi=gi, p3=p3: e.activation(out=vb3[:, gi * 4:gi * 4 + 4, :], in_=p3[:, :, 64:128], func=AF.Copy),
                                 r=[("psup", gi)], w=[R("vbf")] if gi == 0 else [], a=[] if gi == 0 else [R("vbf")])
                        yield
                    self.dve(lambda e: e.tensor_copy(out=kf3[:, :, 64:96], in_=u_[:, 640:672].unsqueeze(1).to_broadcast([128, 8, 32])), r=[U(1)], a=[R("kfull")])
                    yield
                    yield from self.head_post(R("qm"), qm_, 8, 96, 0, 64, 16, tg, R("tg"), sq_, R("sq"), ssh_, R("ssh"), rt_, R("rt"), qbf_[:, 0:768], R("qbf0"))
                    yield from self.tr_store(R("qbf0"), qbf_[:, 0:768], 8, 96, pstr, pcnt, stg[0][:, sl], R("stg0"), self.qT_mla, t)
                    yield from self.head_post(R("kfull"), kfs, 8, 96, 96, 64, 16, tg, R("tg"), sq_, R("sq"), ssh_, R("ssh"), rt_, R("rt"), qbf_[:, 768:1536], R("qbf1"))
                    yield from self.tr_store(R("qbf1"), qbf_[:, 768:1536], 8, 96, pstr, pcnt, stg[1][:, sl], R("stg1"), self.kT_mla, t)
                    self.ldc(self.v_mla[t * 128:(t + 1) * 128, :], vbf_[:, 0:512], r=[R("vbf")], a=["v_mla"])
                    yield
                    yield from self.head_post([U(1), U(2)], u_[:, 672:1440], 12, 64, 192, 0, 32, tg, R("tg"), sq_, R("sq"), ssh_, R("ssh"), rt_, R("rt"), qbf_[:, 0:768], R("qbf0"))
                    yield from self.tr_store(R("qbf0"), qbf_[:, 0:768], 12, 64, pstr, pcnt, stg[2][:, sl], R("stg2"), self.qT_dil, t)
                    yield from self.head_post([U(2), U(3), U(4)], u_[:, 1440:2208], 12, 64, 256, 0, 32, tg, R("tg"), sq_, R("sq"), ssh_, R("ssh"), rt_, R("rt"), qbf_[:, 768:1536], R("qbf1"))
                    yield from self.tr_store(R("qbf1"), qbf_[:, 768:1536], 12, 64, pstr, pcnt, stg[3][:, sl], R("stg3"), self.kT_dil, t)
                    self.act(lambda e: e.activation(out=vbf_[:, 0:768], in_=u_[:, 2208:2976], func=AF.Copy), r=uall, w=[R("vbf")])
                    yield
                    self.ldc(self.v_dil[t * 128:(t + 1) * 128, :], vbf_[:, 0:768], r=[R("vbf")], a=["v_dil"])
                    yield
                else:
                    yield from self.head_post(U(0), u_[:, 0:512], 8, 64, 320, 0, 32, tg, R("tg"), sq_, R("sq"), ssh_, R("ssh"), rt_, R("rt"), qbf_[:, 0:512], R("qbfA"))
                    yield from self.tr_store(R("qbfA"), qbf_[:, 0:512], 8, 64, pstr, pcnt, stg[0][:, sl], R("stg0"), self.qT_df, t)
                    yield from self.head_post(U(1), u_[:, 512:1024], 8, 64, 384, 0, 32, tg, R("tg"), sq_, R("sq"), ssh_, R("ssh"), rt_, R("rt"), qbf_[:, 512:1024], R("qbfB"))
                    yield from self.tr_store(R("qbfB"), qbf_[:, 512:1024], 8, 64, pstr, pcnt, stg[1][:, sl], R("stg1"), self.kT_df, t)
                    self.act(lambda e: e.activation(out=vbf_[:, 0:512], in_=u_[:, 1024:1536], func=AF.Copy), r=uall, w=[R("vbf")])
                    yield
                    self.ldc(self.v_df[t * 128:(t + 1) * 128, :], vbf_[:, 0:512], r=[R("vbf")], a=["v_df"])
                    self.act(lambda e: e.activation(out=vbf_[:, 512:1024], in_=u_[:, 2560:3072], func=AF.Copy), r=uall, w=[R("vbf2")])
                    yield
                    self.ldc(self.v_mb[t * 128:(t + 1) * 128, :], vbf_[:, 512:1024], r=[R("vbf2")], a=["v_mb"])
                    yield
                    yield from self.head_post(U(4), u_[:, 2048:2560], 8, 64, 512, 0, 32, tg, R("tg"), sq_, R("sq"), ssh_, R("ssh"), rt_, R("rt"), qbf_[:, 1024:1536], R("qbfC"))
                    yield from self.tr_store(R("qbfC"), qbf_[:, 1024:1536], 8, 64, pstr, pcnt, stg[2][:, sl], R("stg2"), self.kT_mb, t)
                    nblk = t // 2
                    kp = kpart[:, sl]
                    self.dve(lambda e: e.tensor_reduce(out=kp, in_=stg[2][0:64, sl, 0:8, :], axis=AX.X, op=ALU.add), r=[(R("stg2"), 0)], w=[R("kpart")])
                    yield
                    self.dve(lambda e: e.tensor_tensor(out=kmacc[:, :, nblk], in0=kmacc[:, :, nblk], in1=kp, op=ALU.add), r=[R("kpart"), "kmacc"], w=["kmacc"])
                    yield
                    if t % 2 == 1:
                        self.act(lambda e: e.activation(out=kmb[:, :, nblk], in_=kmacc[:, :, nblk], func=AF.Copy, scale=1.0 / 256), r=["kmacc"], a=["kmb"])
                        yield
                    yield from self.head_post(U(3), u_[:, 1536:2048], 8, 64, 448, 0, 32, tg, R("tg"), sq_, R("sq"), ssh_, R("ssh"), rt_, R("rt"), qbf_[:, 1536:2048], R("qbfD"))
                    yield from self.tr_store(R("qbfD"), qbf_[:, 1536:2048], 8, 64, pstr, pcnt, stg[3][:, sl], R("stg3"), self.qT_mb, t)
                    gs_ = gsb[:, sl]
                    t8 = top8[:, sl]
                    bf_ = biasf[:, sl]
                    bb_ = biasb[:, sl]
                    if nblk > 0:
                        def mmg(e):
                            for h in range(8):
                                ins = e.matmul(psg[:, h, 0:nblk], lhsT=stg[3][0:64, sl, h, :], rhs=kmb[:, h, 0:nblk], start=True, stop=True)
                            return ins
                        self.pe(mmg, r=[(R("stg3"), 0), "kmb"], w=["psg"])
                        self.dve(lambda e: e.tensor_copy(out=gs_[:, :, 0:nblk], in_=psg[:, :, 0:nblk]), r=["psg"], w=[R("gsb")])
                        yield

                        def mx(e):
                            for h in range(8):
                                ins = e.max(out=t8[:, h, :], in_=gs_[:, h, :])
                            return ins
                        self.dve(mx, r=[R("gsb")], w=[R("top8")])
                        yield
                        self.dve(lambda e: e.tensor_tensor(out=bf_, in0=gs_, in1=t8[:, :, 2:3].to_broadcast([128, 8, 32]), op=ALU.is_lt), r=[R("gsb"), R("top8")], w=[R("biasf")])
                        yield
                        self.dve(lambda e: e.tensor_scalar(out=bb_, in0=bf_, scalar1=NEGB, scalar2=None, op0=ALU.mult), r=[R("biasf")], w=[R("biasb")])
                        yield
                    else:
                        self.dve(lambda e: e.memset(bb_, NEGB), w=[R("biasb")])
                        yield
                    self.dve(lambda e: e.memset(bb_[:, :, nblk:nblk + 1], 0.0), r=[R("biasb")], w=[R("biasb")])
                    yield
                    ps = pstr[pcnt[0] % 2]
                    rn = ("pstr", pcnt[0] % 2)
                    pcnt[0] += 1

                    def tpb(e, ps=ps):
                        for h in range(8):
                            ins = e.transpose(ps[0:32, h, :], bb_[:, h, :], self.identb[:])
                        return ins
                    self.pe(tpb, r=[R("biasb"), "identb"], w=[rn])
                    self.dve(lambda e, ps=ps: e.tensor_copy(out=stg[0][0:32, sl, 0:8, :], in_=ps[0:32, 0:8, :]), r=[rn], w=[(R("stg0"), 0)])
                    yield
                    dst = self.qT_mb[:, 64:96, t * 128:(t + 1) * 128].rearrange("h d s -> d h s")
                    self.ldc(dst, stg[0][0:32, sl, 0:8, :], r=[(R("stg0"), 0)], a=["qT_mb_bias"])
                    yield

            import os as _os
            STAG = int(_os.environ.get("KSTAG", "35"))
            active = []
            next_t = 0
            while next_t < self.nt or active:
                if next_t < self.nt and len(active) < NI and (not active or active[-1][1] >= STAG):
                    active.append([body(next_t, next_t % NI), 0])
                    next_t += 1
                for a_ in list(active):
                    try:
                        next(a_[0])
                        a_[1] += 1
                    except StopIteration:
                        active.remove(a_)
            self.S.flush()

    def attn_tiles(self, tiles, ps_s, P, po, po_nm, sc, cnt, hook=None, hook_at=5, acc=None, acc_nm=None, vrows=65, pair=False):
        n = len(tiles)
        ns = len(ps_s)
        npb = P.shape[1]
        import os as _os
        LA = min(ns - 1, int(_os.environ.get('KLA', '3')))
        for i in range(n + LA):
            if i < n:
                kap, qap, q0, N, mk, vs = tiles[i][:6]
                si = (cnt + i) % ns
                pi = (cnt + i) % npb
                if not pair:
                    self.pe(lambda e, kap=kap, qap=qap, N=N, si=si: e.matmul(ps_s[si][:, 0:N], lhsT=kap, rhs=qap, start=True, stop=True),
                            r=tiles[i][6], w=[("ps_s", si)])
                elif i % 2 == 0:
                    grp = [i] + ([i + 1] if i + 1 < n else [])

                    def qk2(e, grp=grp):
                        for ii in grp:
                            kap2, qap2, _, N2 = tiles[ii][:4]
                            ins = e.matmul(ps_s[(cnt + ii) % ns][:, 0:N2], lhsT=kap2, rhs=qap2, start=True, stop=True)
                        return ins
                    rr = []
                    for ii in grp:
                        rr += tiles[ii][6]
                    self.pe(qk2, r=rr, w=[("ps_s", (cnt + ii) % ns) for ii in grp])
                self.act(lambda e, N=N, si=si, pi=pi: e.activation(out=P[:, pi, 0:N], in_=ps_s[si][:, 0:N], func=AF.Exp, scale=sc),
                         r=[("ps_s", si)], w=[("P", pi)])
                if mk is not None:
                    self.dve(lambda e, N=N, pi=pi, mk=mk: e.tensor_tensor(out=P[:, pi, 0:N], in0=P[:, pi, 0:N], in1=mk, op=ALU.mult),
                             r=[("P", pi), "masks"], w=[("P", pi)])
                if acc is not None:
                    if i == 0:
                        self.dve(lambda e, N=N, pi=pi, q0=q0: e.tensor_copy(out=acc[:, q0:512], in_=P[:, pi, 0:N]), r=[("P", pi)], w=[acc_nm])
                    else:
                        self.dve(lambda e, N=N, pi=pi, q0=q0: e.tensor_tensor(out=acc[:, q0:512], in0=acc[:, q0:512], in1=P[:, pi, 0:N], op=ALU.add),
                                 r=[("P", pi), acc_nm], w=[acc_nm])
            if hook is not None and i == min(hook_at, n + LA - 1):
                hook()
                hook = None
            j = i - LA
            if j >= 0:
                kap, qap, q0, N, mk, vs = tiles[j][:6]
                pi = (cnt + j) % npb
                for f, vap in enumerate(vs):
                    self.pe(lambda e, f=f, vap=vap, q0=q0, N=N, pi=pi, j=j: e.matmul(po[f][0:vrows, q0:512], lhsT=vap, rhs=P[:, pi, 0:N],
                                                                                    start=(j == 0), stop=(j == n - 1)),
                            r=[("P", pi)] + tiles[j][7], w=[po_nm[f]] if j == 0 else [], a=[] if j == 0 else [po_nm[f]])
                nd = getattr(self, "ndummy", 0)
                if nd:
                    def dm(e):
                        for _ in range(nd):
                            ins = e.matmul(self.ps_dummy[:, 0:128], lhsT=self.identb[:], rhs=self.identb[:], start=True, stop=True)
                        return ins
                    self.pe(dm, r=["identb"], a=["psdummy"])
        if hook is not None:
            hook()
        return cnt + n

    def attn_norm(self, po, po_nm, ep, slot, dst, dst_nm):
        rrow, osb, ps_bc = ep
        self.act(lambda e: e.activation(out=rrow[64:65, slot, :], in_=po[64:65, :], func=AF.Ln), r=[po_nm], w=[("rrow", slot)])
        self.act(lambda e: e.activation(out=rrow[64:65, slot, :], in_=rrow[64:65, slot, :], func=AF.Exp, scale=-1.0), r=[("rrow", slot)], w=[("rrow", slot)])
        self.pe(lambda e: e.matmul(ps_bc[0:64, :], lhsT=self.onesb[64:65, 0:64], rhs=rrow[64:65, slot, :], start=True, stop=True),
                r=[("rrow", slot), "onesb"], w=["ps_bc"])
        self.act(lambda e: e.activation(out=osb[0:64, slot, :], in_=po[0:64, :], func=AF.Copy), r=[po_nm], w=[("osb", slot)])
        self.dve(lambda e: e.tensor_tensor(out=dst, in0=osb[0:64, slot, :], in1=ps_bc[0:64, :], op=ALU.mult),
                 r=[("osb", slot), "ps_bc"], w=[dst_nm])

    def phaseB(self, kind):
        sb = self.sb
        nch = max(1, self.nt // 4)
        ncols = nch * 512
        dil = kind == "dil"
        nheads = {"mla": 8, "dil": 4, "diff": 4, "moba": 8}[kind]
        nm = 2 if kind == "diff" else 1
        parts = 1
        isdf = kind == "diff"
        dk = {"mla": 96, "dil": 64, "diff": 64, "moba": 96}[kind]
        sc = float({"mla": 96 ** -0.5, "dil": 0.125, "diff": 0.125, "moba": 0.125}[kind])
        qsrc = {"mla": self.qT_mla, "dil": self.qT_dil, "diff": self.qT_df, "moba": self.qT_mb}[kind]
        ksrc = {"mla": self.kT_mla, "dil": self.kT_dil, "diff": self.kT_df, "moba": self.kT_mb}[kind]
        vsrc = {"mla": self.v_mla, "dil": self.v_dil, "diff": self.v_df, "moba": self.v_mb}[kind]
        row_base = {"mla": 0, "dil": 512, "diff": 0, "moba": 512}[kind]
        lam_init = 0.8 - 0.6 * math.exp(-0.3 * 1)
        with ExitStack() as es:
            P_dummy = sb(es, "Pdummy", [128, 2], F32)
            if dil:
                qT = sb(es, "qTd", [128, 2, 3, 512], BF16)
                kT = sb(es, "kTd", [128, 3, SEQ // 2], BF16)
                V = sb(es, "Vd", [128, 3, 64, 65], BF16)
                dmb = sb(es, "dmb", [128, 33, 512], BF16)
                self.ldc(dmb[:], self.dmask.rearrange("p (a b) -> p a b", a=33), w=["masks"])
                self.pool(lambda e: e.memset(V[:, :, :, 64:65], 1.0), a=["Vones"])
            else:
                if isdf:
                    qT = sb(es, "qTp", [128, 2, SEQ], BF16)
                    kT = sb(es, "kTp", [128, 2, SEQ // 2], BF16)
                else:
                    qT = sb(es, "qTa", [96, 2, SEQ], BF16)
                    kT = sb(es, "kTa", [96, 2, SEQ], BF16)
                vw = 128 if isdf else 65
                V = sb(es, "Va", [128, 2, parts, 64, vw], BF16)
                if isdf:
                    self.pool(lambda e: e.memset(P_dummy[:], 0.0), a=["Vones"])
                else:
                    self.pool(lambda e: e.memset(V[:, :, :, :, 64:65], 1.0), a=["Vones"])
                if kind == "moba":
                    for sl in range(2):
                        self.ldc(kT[64:96, sl, :], self.kind, a=["Vones"])
            import os as _os
            self.ndummy = 0
            n_po = 2
            n_s = 7 - n_po - (1 if self.ndummy else 0)
            self.ps_dummy = self.ps(es, "psdummy", [128, 512], F32) if self.ndummy else None
            P = sb(es, "Pt", [128, n_s + 2, 512], BF16)
            rrow = sb(es, "rrow", [65, 2, 512], F32)
            osb = sb(es, "osb", [64, 2, 512], F32)
            onb = sb(es, "onb", [64, 4, 512], BF16)
            onbd = sb(es, "onbd", [128, 2, 512], BF16)
            ps_s = [self.ps(es, "ps_s%d" % i, [128, 512], F32) for i in range(n_s)]
            po = [self.ps(es, "po%d" % i, [128, 512], F32) for i in range(n_po)]
            ps_bc = self.ps(es, "ps_bc", [128, 512], F32)
            ep = (rrow, osb, ps_bc)
            if isdf:
                acc = sb(es, "acc", [128, 2, 512], F32)
                rcp = sb(es, "rcp", [128, 512], F32)
                nrm = sb(es, "nrm", [128, 2, 512], F32)
                dd = sb(es, "dd", [128, 512], F32)
                sqd = sb(es, "sqd", [128, 512], F32)
                rsd = sb(es, "rsd", [128, 512], F32)
                onesf = sb(es, "onesfB", [128, 128], F32)
                lamt = sb(es, "lamt", [128, 256], F32)
                lsm = sb(es, "lsm", [128, 8], F32)
                subs = sb(es, "subs", [128, 1], F32)
                self.pool(lambda e: e.memset(onesf[:], 1.0), w=["onesfB"])
                self.ld(lamt[:], self.dlam.partition_broadcast(128), w=["lamt"])
                self.ld(subs[:], self.subT, w=["subs"])
                self.dve(lambda e: e.tensor_tensor(out=lamt[:, 0:64], in0=lamt[:, 0:64], in1=lamt[:, 64:128], op=ALU.mult), r=["lamt"], w=["lamt"])
                self.dve(lambda e: e.tensor_tensor(out=lamt[:, 128:192], in0=lamt[:, 128:192], in1=lamt[:, 192:256], op=ALU.mult), r=["lamt"], w=["lamt"])
                self.dve(lambda e: e.tensor_reduce(out=lsm[:, 0:1], in_=lamt[:, 0:64], axis=AX.X, op=ALU.add), r=["lamt"], w=["lsm0"])
                self.dve(lambda e: e.tensor_reduce(out=lsm[:, 1:2], in_=lamt[:, 128:192], axis=AX.X, op=ALU.add), r=["lamt"], w=["lsm1"])
                self.act(lambda e: e.activation(out=lsm[:, 2:4], in_=lsm[:, 0:2], func=AF.Exp), r=["lsm0", "lsm1"], w=["lsm2"])
                self.dve(lambda e: e.tensor_tensor(out=lsm[:, 4:5], in0=lsm[:, 3:4], in1=lsm[:, 2:3], op=ALU.subtract), r=["lsm2"], w=["lsm4"])
                self.dve(lambda e: e.tensor_scalar(out=lsm[:, 5:6], in0=lsm[:, 4:5], scalar1=-lam_init, scalar2=None, op0=ALU.add), r=["lsm4"], w=["neglam"])
                self.dve(lambda e: e.tensor_scalar(out=subs[:], in0=subs[:], scalar1=1.0 - lam_init, scalar2=None, op0=ALU.mult), r=["subs"], w=["subs"])
            cnt = 0
            ecnt = 0
            pending = [None]
            for h in range(nheads):
                vsl = h % 2
                if dil:
                    for g in range(3):
                        gh = g * 4 + h
                        for hf in range(2):
                            self.ld(kT[hf * 64:(hf + 1) * 64, g, 0:ncols // 2].rearrange("d (j s) -> d j s", s=128),
                                    ksrc[gh, :, 0:ncols].rearrange("d (j two s) -> d j two s", two=2, s=128)[:, :, hf, :],
                                    w=[("kT", g)] if hf == 0 else [], a=[] if hf == 0 else [("kT", g)])
                        self.ld(V[:, g, 0:nch * 4, 0:64], vsrc[0:ncols, gh * 64:gh * 64 + 64].rearrange("(t p) d -> p t d", p=128),
                                r=["Vones"], w=[("V", g)])
                else:
                    for f in range(parts):
                        vd = 128 if isdf else 64
                        c0 = h * vd
                        self.ld(V[:, vsl, f, 0:nch * 4, 0:vd], vsrc[0:ncols, c0:c0 + vd].rearrange("(t p) d -> p t d", p=128),
                                r=["Vones"], w=[("V", vsl, f)])
                    for m in range(nm):
                        u_ = h * nm + m
                        sl = u_ % 2
                        dq = 96 if kind in ("mla", "moba") else 64
                        dkk = 96 if kind == "mla" else 64
                        if isdf:
                            for hf in range(2):
                                self.ld(qT[hf * 64:(hf + 1) * 64, sl, 0:ncols], qsrc[u_, 0:64, 0:ncols], w=[("qT", sl)] if hf == 0 else [], a=[] if hf == 0 else [("qT", sl)])
                                self.ld(kT[hf * 64:(hf + 1) * 64, sl, 0:ncols // 2].rearrange("d (j s) -> d j s", s=128),
                                        ksrc[u_, 0:64, 0:ncols].rearrange("d (j two s) -> d j two s", two=2, s=128)[:, :, hf, :],
                                        w=[("kT", sl)] if hf == 0 else [], a=[] if hf == 0 else [("kT", sl)])
                        else:
                            self.ld(qT[0:dq, sl, 0:ncols], qsrc[u_, 0:dq, 0:ncols], w=[("qT", sl)])
                            self.ld(kT[0:dkk, sl, 0:ncols], ksrc[u_, 0:dkk, 0:ncols], r=["Vones"], w=[("kT", sl)])
                for c in range(nch):
                    if dil:
                        qs = c % 2
                        for g in range(3):
                            for hf in range(2):
                                self.ld(qT[hf * 64:(hf + 1) * 64, qs, g, :], qsrc[g * 4 + h, :, c * 512:(c + 1) * 512],
                                        w=[("qT", qs, g)] if hf == 0 else [], a=[] if hf == 0 else [("qT", qs, g)])
                    for m in range(nm):
                        u_ = h * nm + m
                        sl = u_ % 2
                        tiles = []
                        if dil:
                            mo = 0
                            for g, W in enumerate((1, 4, 16)):
                                for o in range(W + 4):
                                    kt = 4 * c - W + o
                                    if kt >= 0:
                                        hf = kt % 2
                                        tiles.append((kT[hf * 64:(hf + 1) * 64, g, (kt // 2) * 128:(kt // 2 + 1) * 128], qT[hf * 64:(hf + 1) * 64, qs, g, :], 0, 512, dmb[:, mo + o, :],
                                                      [V[:, g, kt, :]], [("kT", g), ("qT", qs, g)], [("V", g)]))
                                mo += W + 4
                        else:
                            for kt in range(4 * c + 4):
                                j = kt - 4 * c
                                if j < 0:
                                    q0, mk = 0, None
                                else:
                                    q0, mk = 128 * j, self.cmb[:, j, 128 * j:512]
                                N = 512 - q0
                                if isdf:
                                    hf = kt % 2
                                    kap_ = kT[hf * 64:(hf + 1) * 64, sl, (kt // 2) * 128:(kt // 2 + 1) * 128]
                                    qap_ = qT[hf * 64:(hf + 1) * 64, sl, c * 512 + q0:(c + 1) * 512]
                                else:
                                    kap_ = kT[0:dk, sl, kt * 128:(kt + 1) * 128]
                                    qap_ = qT[0:dk, sl, c * 512 + q0:(c + 1) * 512]
                                tiles.append((kap_, qap_, q0, N, mk,
                                              [V[:, vsl, f, kt, :] for f in range(parts)], [("kT", sl), ("qT", sl)],
                                              [("V", vsl, f) for f in range(parts)]))
                        pidx = [ecnt % 2]
                        pos_ = [po[i] for i in pidx]
                        po_nm = [("po", i) for i in pidx]
                        if isdf:
                            cnt = self.attn_tiles(tiles, ps_s, P, pos_, po_nm, sc, cnt, hook=pending[0], acc=acc[:, m, :], acc_nm=("acc", m), vrows=128, pair=True)
                        else:
                            cnt = self.attn_tiles(tiles, ps_s, P, pos_, po_nm, sc, cnt, hook=pending[0], pair=dil)

                        def epilogue(pos_=pos_, po_nm=po_nm, ecnt0=ecnt, m=m, c=c, h=h):
                            if not isdf:
                                es_ = ecnt0 % 2
                                osl = ecnt0 % 4
                                self.attn_norm(pos_[0], po_nm[0], ep, es_, onb[:, osl, :], ("onb", osl))
                                r0 = row_base + h * 64
                                self.ldc(self.oT[r0:r0 + 64, c * 512:(c + 1) * 512], onb[:, osl, :], r=[("onb", osl)], a=["oT"])
                                return
                            self.pe(lambda e: e.matmul(ps_bc[:, :], lhsT=onesf[:], rhs=acc[:, m, :], start=True, stop=True), r=[("acc", m), "onesfB"], w=["ps_bc"])
                            self.act(lambda e: e.activation(out=rcp[:], in_=ps_bc[:, :], func=AF.Ln), r=["ps_bc"], w=["rcp"])
                            self.act(lambda e: e.activation(out=rcp[:], in_=rcp[:], func=AF.Exp, scale=-1.0), r=["rcp"], w=["rcp"])
                            self.dve(lambda e: e.tensor_tensor(out=nrm[:, m, :], in0=pos_[0][:, :], in1=rcp[:], op=ALU.mult), r=[po_nm[0], "rcp"], w=[("nrm", m)])
                            if m == 1:
                                self.dve(lambda e: e.scalar_tensor_tensor(out=dd[:], in0=nrm[:, 1, :], scalar=lsm[:, 5:6], in1=nrm[:, 0, :], op0=ALU.mult, op1=ALU.add),
                                         r=[("nrm", 0), ("nrm", 1), "neglam"], w=["dd"])
                                self.act(lambda e: e.activation(out=sqd[:], in_=dd[:], func=AF.Square), r=["dd"], w=["sqd"])
                                self.pe(lambda e: e.matmul(ps_bc[:, :], lhsT=onesf[:], rhs=sqd[:], start=True, stop=True), r=["sqd", "onesfB"], w=["ps_bc"])
                                self.act(lambda e: e.activation(out=rsd[:], in_=ps_bc[:, :], func=AF.Ln, scale=1.0 / 128, bias=self.epsT[:, 0:1]), r=["ps_bc"], w=["rsd"])
                                self.act(lambda e: e.activation(out=rsd[:], in_=rsd[:], func=AF.Exp, scale=-0.5), r=["rsd"], w=["rsd"])
                                osl = c % 2
                                ob = onbd[:, osl, :]
                                self.dve(lambda e, ob=ob: e.scalar_tensor_tensor(out=ob, in0=dd[:], scalar=subs[:, 0:1], in1=rsd[:], op0=ALU.mult, op1=ALU.mult),
                                         r=["dd", "rsd", "subs"], w=[("onbd", osl)])
                                self.ldc(self.oT[h * 128:(h + 1) * 128, c * 512:(c + 1) * 512], ob, r=[("onbd", osl)], a=["oT"])
                        pending[0] = epilogue
                        ecnt += parts
            if pending[0] is not None:
                pending[0]()
            self.S.flush()

    def phaseC1(self, layer):
        sb = self.sb
        NI = 2
        nk = 6 if layer == 0 else 8
        xin = self.x if layer == 0 else self.xL
        wsrc = self.e_w_out if layer == 0 else self.o_w_out
        with ExitStack() as es:
            w_out = sb(es, "w_out", [128, nk, D], BF16)
            G1 = sb(es, "G1", [128, D], F32)
            xt = sb(es, "xtc", [128, NI, D], F32)
            oTt = sb(es, "oTt", [128, NI, nk, 128], BF16)
            tmpf = sb(es, "tmpfc", [128, NI, D], F32)
            sq = sb(es, "sqc", [128, NI, D], F32)
            ssx = sb(es, "ssxc", [128, NI, 2], F32)
            xs = sb(es, "xsc", [128, NI, D], BF16)
            hT = sb(es, "hTc", [128, NI, 8, 128], BF16)
            psy = [self.ps(es, "psy%d" % i, [128, 512], F32) for i in range(4)]
            pT = [self.ps(es, "pTc%d" % i, [128, 8, 128], BF16) for i in range(2)]
            self.load_w(w_out, wsrc, nk * 128, "w_out")
            self.ld(G1[:], self.Gd[:, (layer * 2) * D:(layer * 2 + 1) * D], w=["G1"])

            def body(t, sl):
                R = lambda nm: (nm, sl)
                self.ld(xt[:, sl], xin[t * 128:(t + 1) * 128, :], w=[R("xt")])
                self.ld(oTt[:, sl], self.oT[0:nk * 128, t * 128:(t + 1) * 128].rearrange("(k p) s -> p k s", p=128), w=[R("oTt")])
                yield
                for hf in range(2):
                    pb = psy[sl * 2 + hf]
                    pn = ("psy", sl * 2 + hf)

                    def mm(e, hf=hf, pb=pb):
                        for k in range(nk):
                            ins = e.matmul(pb[:], lhsT=oTt[:, sl, k, :], rhs=w_out[:, k, hf * 512:(hf + 1) * 512], start=(k == 0), stop=(k == nk - 1))
                        return ins
                    self.pe(mm, r=[R("oTt"), "w_out"], w=[pn])
                    yield
                    self.dve(lambda e, hf=hf, pb=pb: e.tensor_tensor(out=tmpf[:, sl, hf * 512:(hf + 1) * 512], in0=pb[:], in1=G1[:, hf * 512:(hf + 1) * 512], op=ALU.mult),
                             r=[pn, "G1"], w=[("tmpf", sl, hf)])
                    yield
                    self.dve(lambda e, hf=hf: e.tensor_tensor(out=xt[:, sl, hf * 512:(hf + 1) * 512], in0=xt[:, sl, hf * 512:(hf + 1) * 512],
                                                              in1=tmpf[:, sl, hf * 512:(hf + 1) * 512], op=ALU.add),
                             r=[("tmpf", sl, hf), R("xt")], w=[R("xt")])
                    yield
                self.ldc(self.x1[t * 128:(t + 1) * 128, :], xt[:, sl], r=[R("xt")], a=["x1"])
                self.act(lambda e: e.activation(out=sq[:, sl], in_=xt[:, sl], func=AF.Square, accum_out=ssx[:, sl, 0:1]), r=[R("xt")], w=[R("sq"), R("ssx")])
                yield
                self.act(lambda e: e.activation(out=ssx[:, sl, 1:2], in_=ssx[:, sl, 0:1], func=AF.Sqrt, scale=1.0 / D, bias=self.epsT[:, 0:1]), r=[R("ssx")], w=[R("rsx")])
                yield
                self.dve(lambda e: e.reciprocal(out=ssx[:, sl, 1:2], in_=ssx[:, sl, 1:2]), r=[R("rsx")], w=[R("rsx")])
                yield
                self.act(lambda e: e.activation(out=xs[:, sl], in_=xt[:, sl], func=AF.Copy, scale=ssx[:, sl, 1:2]), r=[R("xt"), R("rsx")], w=[R("xs")])
                yield

                def tpx(e):
                    for j in range(8):
                        ins = e.transpose(pT[sl][:, j, :], xs[:, sl, j * 128:(j + 1) * 128], self.identb[:])
                    return ins
                self.pe(tpx, r=[R("xs"), "identb"], w=[("pT", sl)])
                yield
                ao = layer * 16 + 8
                a_bc = self.aT[:, ao:ao + 8].unsqueeze(2).to_broadcast([128, 8, 128])
                b_bc = self.modT[:, layer * 48 + 24:layer * 48 + 32].unsqueeze(2).to_broadcast([128, 8, 128])
                tm3 = tmpf[:, sl].rearrange("p (j s) -> p j s", j=8)
                self.dve(lambda e: e.tensor_tensor(out=tm3, in0=pT[sl][:], in1=a_bc, op=ALU.mult),
                         r=[("pT", sl), ("aT", ao)], w=[("tmpf", sl, 0), ("tmpf", sl, 1)])
                yield
                self.dve(lambda e: e.tensor_tensor(out=hT[:, sl], in0=tm3, in1=b_bc, op=ALU.add),
                         r=[("tmpf", sl, 0), ("tmpf", sl, 1), "modT"], w=[R("hT")])
                yield
                self.ldc(self.h2T[:, t * 128:(t + 1) * 128].rearrange("(k p) s -> p k s", p=128), hT[:, sl], r=[R("hT")], a=["h2T"])
                yield

            STAG = 8
            active = []
            next_t = 0
            while next_t < self.nt or active:
                if next_t < self.nt and len(active) < NI and (not active or active[-1][1] >= STAG):
                    active.append([body(next_t, next_t % NI), 0])
                    next_t += 1
                for a_ in list(active):
                    try:
                        next(a_[0])
                        a_[1] += 1
                    except StopIteration:
                        active.remove(a_)
            self.S.flush()

    def phaseC2(self, layer):
        sb = self.sb
        dst = self.xL if layer == 0 else self.out
        ng = max(1, self.nt // 2)
        with ExitStack() as es:
            w1 = sb(es, "w1", [128, 8, DFF], BF16)
            w2 = sb(es, "w2", [128, 32, D], BF16)
            G2 = sb(es, "G2", [128, D], F32)
            hTt = sb(es, "hTt", [128, 2, 8, 256], BF16)
            x1t = sb(es, "x1t", [128, 2, 2, D], F32)
            aT = sb(es, "aTt", [128, 32, 256], BF16)
            rl = sb(es, "rl", [128, 2, 512], F32)
            tmp = sb(es, "tmpc2", [128, 2, 512], F32)
            psa = [self.ps(es, "psa%d" % i, [128, 2, 256], F32) for i in range(2)]
            psy = [self.ps(es, "psy2_%d" % i, [128, 512], F32) for i in range(4)]
            self.load_w(w1, self.mlp_w1[layer], D, "w1")
            self.load_w(w2, self.mlp_w2[layer], DFF, "w2")
            self.ld(G2[:], self.Gd[:, (layer * 2 + 1) * D:(layer * 2 + 2) * D], w=["G2"])
            for g in range(ng):
                sl = g % 2
                self.ld(hTt[:, sl], self.h2T[:, g * 256:(g + 1) * 256].rearrange("(k p) s -> p k s", p=128), w=[("hTt", sl)])
                self.ld(x1t[:, sl], self.x1[g * 256:(g + 1) * 256, :].rearrange("(s p) d -> p s d", p=128), w=[("x1t", sl)])
                for fp in range(16):
                    pb = psa[fp % 2]

                    def mm1(e, fp=fp, pb=pb, sl=sl):
                        for ff in range(2):
                            f = fp * 2 + ff
                            for k in range(8):
                                ins = e.matmul(pb[:, ff, :], lhsT=w1[:, k, f * 128:(f + 1) * 128], rhs=hTt[:, sl, k, :], start=(k == 0), stop=(k == 7))
                        return ins
                    self.pe(mm1, r=["w1", ("hTt", sl)], w=[("psa", fp % 2)])
                    rs_ = fp % 2
                    self.act(lambda e, pb=pb, rs_=rs_: e.activation(out=rl[:, rs_, :], in_=pb[:].rearrange("p a b -> p (a b)"), func=AF.Relu),
                             r=[("psa", fp % 2)], w=[("rl", rs_)])
                    self.dve(lambda e, fp=fp, rs_=rs_: e.tensor_tensor(out=aT[:, fp * 2:fp * 2 + 2, :].rearrange("p a b -> p (a b)"), in0=rl[:, rs_, :], in1=rl[:, rs_, :], op=ALU.mult),
                             r=[("rl", rs_)], w=[("aT", fp)])
                for s_ in range(2):
                    for hf in range(2):
                        pi = s_ * 2 + hf

                        def mm2(e, s_=s_, hf=hf, pi=pi):
                            for f in range(32):
                                ins = e.matmul(psy[pi][:], lhsT=aT[:, f, s_ * 128:(s_ + 1) * 128], rhs=w2[:, f, hf * 512:(hf + 1) * 512], start=(f == 0), stop=(f == 31))
                            return ins
                        self.pe(mm2, r=["w2"] + [("aT", fp) for fp in range(16)], w=[("psy", pi)])
                        self.dve(lambda e, hf=hf, pi=pi: e.tensor_tensor(out=tmp[:, hf, :], in0=psy[pi][:], in1=G2[:, hf * 512:(hf + 1) * 512], op=ALU.mult),
                                 r=[("psy", pi), "G2"], w=[("tmp", hf)])
                        self.dve(lambda e, s_=s_, hf=hf, sl=sl: e.tensor_tensor(out=x1t[:, sl, s_, hf * 512:(hf + 1) * 512], in0=x1t[:, sl, s_, hf * 512:(hf + 1) * 512],
                                                                              in1=tmp[:, hf, :], op=ALU.add),
                                 r=[("tmp", hf), ("x1t", sl)], w=[("x1t", sl)])
                self.ldc(dst[g * 256:(g + 1) * 256, :].rearrange("(s p) d -> p s d", p=128), x1t[:, sl], r=[("x1t", sl)], a=["dst"])
            self.S.flush()


def _consts():
    ident = np.eye(128, dtype=np.float32)
    i16 = np.arange(16, dtype=np.float32) / np.float32(16)
    i32 = np.arange(32, dtype=np.float32) / np.float32(32)
    invf = np.concatenate([np.float32(10000.0) ** (-i16), np.float32(10000.0) ** (-i32)]).astype(np.float32)
    k = np.arange(128)[:, None]
    q = np.arange(512)[None, :]
    cmask = np.stack([(q >= 128 * j + k) for j in range(4)], axis=1).astype(np.float32)
    dm = []
    for (w, r) in ((128, 1), (512, 4), (2048, 16)):
        W = w // 128
        for o in range(W + 4):
            rel = q - k + 128 * (W - o)
            dm.append(((rel >= 0) & (rel <= w) & (rel % r == 0)).astype(np.float32))
    dmask = np.stack(dm, axis=1)
    kind = (np.arange(SEQ)[None, :] // 256 == np.arange(32)[:, None]).astype(np.float32)
    return dict(ident=ident, invf=invf, cmask=cmask.reshape(128, -1), dmask=dmask.reshape(128, -1), kind=kind)


def _colT(v, n):
    return np.ascontiguousarray(np.asarray(v, np.float32).reshape(n, 128).T)


def make_in_maps(inp, batches):
    f = lambda a: np.ascontiguousarray(np.asarray(a, dtype=np.float32))
    c = _consts()
    shared = dict(
        ada_w=f(inp["ada_w"]),
        ada_bT=np.ascontiguousarray(np.concatenate([_colT(inp["ada_b"][l], 48) for l in range(2)], axis=1)),
        nrmT=np.ascontiguousarray(np.concatenate(
            [_colT(inp[nm][l], 8) for l in range(2) for nm in ("norm_mix", "norm_mlp")], axis=1)),
        mlp_w1=f(inp["mlp_w1"]), mlp_w2=f(inp["mlp_w2"]),
        e_w_in=f(inp["even_w_in"][0]), e_w_out=f(inp["even_w_out"][0]),
        latT=np.ascontiguousarray(np.concatenate([_colT(inp["mla_q_lat_norm"][0], 3), _colT(inp["mla_kv_lat_norm"][0], 2)], axis=1)),
        w_uq=f(np.asarray(inp["mla_w_uq"][0]).reshape(384, 768)),
        w_ukv=f(np.asarray(inp["mla_w_ukv"][0]).reshape(256, 1024)),
        gains=f(np.concatenate([np.asarray(inp[k][0], np.float32).reshape(-1) for k in
                                ("mla_q_norm", "mla_k_norm", "dil_q_norm", "dil_k_norm",
                                 "diff_q_norm", "diff_k_norm", "moba_q_norm", "moba_k_norm")])),
        o_w_in=f(inp["odd_w_in"][0]), o_w_out=f(inp["odd_w_out"][0]),
        dlam=f(np.asarray(inp["diff_lambda"][0]).reshape(256)),
        subT=np.ascontiguousarray(np.asarray(inp["diff_subln"][0], np.float32).reshape(128, 1)),
        **c,
    )
    maps = []
    for b in batches:
        m = dict(shared)
        m["x"] = f(inp["x"][b])
        m["posT"] = np.ascontiguousarray(np.asarray(inp["positions"][b], np.int32).reshape(NT, 128).T)
        m["cT"] = _colT(inp["c"][b], 8)
        maps.append(m)
    return maps


def build(nt=NT, phases=None, dbg_out=()):
    phases = ALL_PHASES if phases is None else phases
    nc = bass.Bass("TRN2", target_bir_lowering=False)
    k = K(nc, nt=nt, phases=phases, dbg_out=dbg_out)
    k.declare()
    with ExitStack() as gs:
        k.es = gs
        sems = {}
        for e in Sched.COMPUTE:
            sems[e] = gs.enter_context(nc.semaphore("sem_" + e))
        for q in ("sp", "pq"):
            sems[q] = [gs.enter_context(nc.semaphore("sem_%s%d" % (q, i))) for i in range(k.S.ring)]
        k.S.init_emit(sems)
        k.setup()
        for ph in phases:
            getattr(k, "run_" + ph)()
        stats = k.S.finish()
    return nc, k, stats


def _add_phase_methods():
    K.run_A0 = lambda self: self.phaseA(0)
    K.run_A1 = lambda self: self.phaseA(1)
    K.run_Bmla = lambda self: self.phaseB("mla")
    K.run_C10 = lambda self: self.phaseC1(0)
    K.run_C20 = lambda self: self.phaseC2(0)
    K.run_C11 = lambda self: self.phaseC1(1)
    K.run_C21 = lambda self: self.phaseC2(1)
    K.run_Bdil = lambda self: self.phaseB("dil")
    K.run_Bdiff = lambda self: self.phaseB("diff")
    K.run_Bmoba = lambda self: self.phaseB("moba")


_add_phase_methods()


ALL_PHASES = ("A0", "Bmla", "Bdil", "C10", "C20", "A1", "Bdiff", "Bmoba", "C11", "C21")


def kernel(**inputs):
    nb = int(np.asarray(inputs["x"]).shape[0])
    nc, k, stats = build(nt=NT, phases=ALL_PHASES)
    real = make_in_maps(inputs, list(range(nb)))
    keep = ("ident", "invf", "cmask", "dmask", "kind")
    zero = {k_: (v if k_ in keep else np.zeros_like(v)) for k_, v in real[0].items()}
    slots = [0, 1, None, None, 2, 3, None, None]
    maps = [real[b] if b is not None else zero for b in slots]
    res = run_bass_kernel_spmd(nc, maps, core_ids=list(range(8)))
    where = {b: i for i, b in enumerate(slots) if b is not None}
    return np.stack([np.asarray(res.results[where[b]]["out"], dtype=np.float32) for b in range(nb)], axis=0)
```
